# Optimizing a Trainium2 kernel written in Bass

```python
import jax, jax.numpy as jnp
from jax import lax
import numpy as np

D_MODEL = 1024
BATCH = 2
SEQ = 8192
DEPTH = 2
DEC_BATCH = 16
DEC_SEQ = 4096
PAST_LEN = 128

N_EVEN = (DEPTH + 1) // 2
N_ODD = DEPTH // 2
EPS = 1e-6
MLA_HEADS = 8
Q_LORA = 256
KV_LORA = 128
QK_NOPE = 64
QK_ROPE = 32
V_HEAD = 64
ROPE_THETA = 10000.0
Q_BLOCK = 128
MLA_WIDTH = MLA_HEADS * V_HEAD
M_HEADS = 4
M_HEAD_DIM = 128
M_WIDTH = M_HEADS * M_HEAD_DIM
M_CHUNK = 128
AB_IN = Q_LORA + KV_LORA + QK_ROPE + 4 * M_WIDTH + 4 * M_HEADS
AB_CUTS = [int(c) for c in np.cumsum([Q_LORA, KV_LORA, QK_ROPE, 2 * M_WIDTH, M_WIDTH, M_WIDTH])]
G_GROUPS = 8
G_CHUNK = 128
G_WIDTH = D_MODEL
G_CH = G_WIDTH // G_GROUPS
D_FF = 2816
CONV_W = 3

kernel_name = 'hybrid_mla_mlstm_gmlp_encoder'


def rmsnorm(x, g):
    xf = x.astype(jnp.float32)
    y = xf * lax.rsqrt(jnp.mean(xf * xf, axis=-1, keepdims=True) + EPS)
    return (y * g.astype(jnp.float32)).astype(x.dtype)


def dwconv3(x, w, b):
    xp = jnp.pad(x, ((0, 0), (1, 1), (0, 0)))
    return xp[:, :-2] * w[0] + xp[:, 1:-1] * w[1] + xp[:, 2:] * w[2] + b


def rope_tables(S):
    pos = jnp.arange(S, dtype=jnp.float32)
    inv = 1.0 / (ROPE_THETA ** (jnp.arange(0, QK_ROPE, 2, dtype=jnp.float32) / QK_ROPE))
    ang = pos[:, None] * inv[None, :]
    return jnp.cos(ang), jnp.sin(ang)


def apply_rope(x, cos, sin):
    x1, x2 = jnp.split(x, 2, axis=-1)
    cos = cos.astype(x.dtype)
    sin = sin.astype(x.dtype)
    return jnp.concatenate([x1 * cos - x2 * sin, x2 * cos + x1 * sin], axis=-1)


def mla_attention(q_nope, q_rope, k_nope, k_rope, v):
    B, S, H, _ = q_nope.shape
    nq = S // Q_BLOCK
    scale = (QK_NOPE + QK_ROPE) ** -0.5
    qn_b = q_nope.reshape(B, nq, Q_BLOCK, H, QK_NOPE).transpose(1, 0, 2, 3, 4)
    qr_b = q_rope.reshape(B, nq, Q_BLOCK, H, QK_ROPE).transpose(1, 0, 2, 3, 4)

    def block(args):
        qn_i, qr_i = args
        s = jnp.einsum('bqhd,bkhd->bhqk', qn_i, k_nope) + jnp.einsum('bqhr,bkr->bhqk', qr_i, k_rope)
        p = jax.nn.softmax(s.astype(jnp.float32) * scale, axis=-1).astype(v.dtype)
        return jnp.einsum('bhqk,bkhd->bqhd', p, v)

    o = lax.map(block, (qn_b, qr_b))
    return o.transpose(1, 0, 2, 3, 4).reshape(B, S, H * V_HEAD)


def mlstm_chunkwise(q, k, v, li, lf):
    B, H, S, dh = q.shape
    L = M_CHUNK
    nc = S // L
    q = q.reshape(B, H, nc, L, dh)
    k = k.reshape(B, H, nc, L, dh)
    v = v.reshape(B, H, nc, L, dh)
    li = li.reshape(B, H, nc, L)
    lf = lf.reshape(B, H, nc, L)
    b = jnp.cumsum(lf, axis=-1)
    bL = b[..., -1]
    mask = jnp.tril(jnp.ones((L, L), dtype=bool))
    dlog = jnp.where(mask, b[..., :, None] - b[..., None, :] + li[..., None, :], -jnp.inf)
    g = bL[..., None] - b + li
    m_loc = jnp.max(g, axis=-1)
    w_loc = jnp.exp(g - m_loc[..., None])
    C_loc = jnp.einsum('bhnl,bhnld,bhnle->bhnde', w_loc, v, k)
    n_loc = jnp.einsum('bhnl,bhnld->bhnd', w_loc, k)

    def step(carry, xs):
        C, n, m = carry
        bL_c, C_l, n_l, m_l = xs
        m_new = jnp.maximum(bL_c + m, m_l)
        a = jnp.exp(bL_c + m - m_new)
        c = jnp.exp(m_l - m_new)
        C_new = a[..., None, None] * C + c[..., None, None] * C_l
        n_new = a[..., None] * n + c[..., None] * n_l
        return (C_new, n_new, m_new), (C, n, m)

    init = (jnp.zeros((B, H, dh, dh), jnp.float32), jnp.zeros((B, H, dh), jnp.float32), jnp.zeros((B, H), jnp.float32))
    xs = (jnp.moveaxis(bL, 2, 0), jnp.moveaxis(C_loc, 2, 0), jnp.moveaxis(n_loc, 2, 0), jnp.moveaxis(m_loc, 2, 0))
    _, (C_prev, n_prev, m_prev) = lax.scan(step, init, xs)
    C_prev = jnp.moveaxis(C_prev, 0, 2)
    n_prev = jnp.moveaxis(n_prev, 0, 2)
    m_prev = jnp.moveaxis(m_prev, 0, 2)
    a_t = b + m_prev[..., None]
    m_t = jnp.maximum(a_t, jnp.max(dlog, axis=-1))
    inter = jnp.exp(a_t - m_t)
    qk = jnp.einsum('bhntd,bhnsd->bhnts', q, k) * jnp.exp(dlog - m_t[..., None])
    num = inter[..., None] * jnp.einsum('bhnde,bhnte->bhntd', C_prev, q) + jnp.einsum('bhnts,bhnsd->bhntd', qk, v)
    den = inter * jnp.einsum('bhnd,bhntd->bhnt', n_prev, q) + jnp.sum(qk, axis=-1)
    h = num / jnp.maximum(jnp.abs(den), jnp.exp(-m_t))[..., None]
    return h.reshape(B, H, S, dh)


def mixer_ab(xn, w_in, q_norm, w_uq, kv_norm, w_ukv, conv_w, conv_b, gate_bias, out_norm, w_out):
    B, S, _ = xn.shape
    h = xn @ w_in
    c_q, c_kv, k_r, m_qk, m_v, m_o, m_g = jnp.split(h, AB_CUTS, axis=-1)
    cos, sin = rope_tables(S)
    q = (rmsnorm(c_q, q_norm) @ w_uq).reshape(B, S, MLA_HEADS, QK_NOPE + QK_ROPE)
    q_nope = q[..., :QK_NOPE]
    q_rope = apply_rope(q[..., QK_NOPE:], cos[:, None], sin[:, None])
    kv = (rmsnorm(c_kv, kv_norm) @ w_ukv).reshape(B, S, MLA_HEADS, QK_NOPE + V_HEAD)
    k_nope = kv[..., :QK_NOPE]
    v_att = kv[..., QK_NOPE:]
    k_rope = apply_rope(k_r, cos, sin)
    y_att = mla_attention(q_nope, q_rope, k_nope, k_rope, v_att)
    qk = jax.nn.silu(dwconv3(m_qk, conv_w, conv_b))

    def heads(t):
        return t.reshape(B, S, M_HEADS, M_HEAD_DIM).transpose(0, 2, 1, 3).astype(jnp.float32)

    mq = heads(qk[..., :M_WIDTH])
    mk = heads(qk[..., M_WIDTH:]) * (M_HEAD_DIM ** -0.5)
    mv = heads(m_v)
    g = (m_g.astype(jnp.float32) + gate_bias.astype(jnp.float32)).transpose(0, 2, 1)
    i_f, f_f, i_b, f_b = jnp.split(g, 4, axis=1)
    h_f = mlstm_chunkwise(mq, mk, mv, i_f, jax.nn.log_sigmoid(f_f))
    fl = lambda t: jnp.flip(t, axis=2)
    h_b = fl(mlstm_chunkwise(fl(mq), fl(mk), fl(mv), fl(i_b), fl(jax.nn.log_sigmoid(f_b))))
    hs = (h_f + h_b).transpose(0, 2, 1, 3)
    hs = hs * lax.rsqrt(jnp.mean(hs * hs, axis=-1, keepdims=True) + EPS)
    hs = hs.reshape(B, S, M_WIDTH) * out_norm.astype(jnp.float32)
    y_m = (jax.nn.sigmoid(m_o.astype(jnp.float32)) * hs).astype(xn.dtype)
    return jnp.concatenate([y_att, y_m], axis=-1) @ w_out


def mixer_c(xn, w_in, v_norm, w_s, b_s, w_out):
    B, S, _ = xn.shape
    h = jax.nn.gelu(xn @ w_in, approximate=False)
    u, v = jnp.split(h, 2, axis=-1)
    v = rmsnorm(v, v_norm).reshape(B, S // G_CHUNK, G_CHUNK, G_GROUPS, G_CH)
    sv = jnp.einsum('gts,bnsgc->bntgc', w_s, v) + b_s.T[:, :, None]
    return (u * sv.reshape(B, S, G_WIDTH)) @ w_out


def conv_ffn(xn, w_up, conv_w, conv_b, w_down):
    a = dwconv3(xn @ w_up, conv_w, conv_b)
    gate, val = jnp.split(a, 2, axis=-1)
    return (jax.nn.silu(gate) * val) @ w_down


def forward(x, p):
    for layer in range(DEPTH):
        if layer % 2 == 0:
            i = layer // 2
            x = x + mixer_ab(rmsnorm(x, p['ab_norm'][i]), p['ab_w_in'][i], p['mla_q_norm'][i], p['mla_w_uq'][i],
                             p['mla_kv_norm'][i], p['mla_w_ukv'][i], p['mlstm_conv_w'][i], p['mlstm_conv_b'][i],
                             p['mlstm_gate_bias'][i], p['mlstm_out_norm'][i], p['ab_w_out'][i])
        else:
            i = layer // 2
            x = x + mixer_c(rmsnorm(x, p['c_norm'][i]), p['c_w_in'][i], p['c_v_norm'][i], p['c_w_spatial'][i],
                            p['c_b_spatial'][i], p['c_w_out'][i])
        x = x + conv_ffn(rmsnorm(x, p['ffn_norm'][layer]), p['ffn_w_up'][layer], p['ffn_conv_w'][layer],
                         p['ffn_conv_b'][layer], p['ffn_w_down'][layer])
    return rmsnorm(x, p['final_norm'])


def setup_inputs(seed: int = 0) -> dict:
    key = jax.random.key(seed)
    ks = jax.random.split(key, 40)
    f32 = jnp.float32

    def nrm(k, shape, scale):
        return jax.random.normal(k, shape, f32) * scale

    def gain(k, shape):
        return 1.0 + 0.05 * jax.random.normal(k, shape, f32)

    f_lin = jnp.linspace(3.0, 6.0, M_HEADS, dtype=f32)[None, :]
    gate_bias = jnp.concatenate([
        nrm(ks[10], (N_EVEN, M_HEADS), 0.1),
        f_lin + nrm(ks[11], (N_EVEN, M_HEADS), 0.1),
        nrm(ks[12], (N_EVEN, M_HEADS), 0.1),
        f_lin + nrm(ks[13], (N_EVEN, M_HEADS), 0.1)], axis=-1)
    return {
        'x_prompt': nrm(ks[0], (BATCH, SEQ, D_MODEL), 1.0),
        'x_sample': nrm(ks[1], (DEC_BATCH, DEC_SEQ, D_MODEL), 1.0),
        'ab_norm': gain(ks[2], (N_EVEN, D_MODEL)),
        'ab_w_in': nrm(ks[3], (N_EVEN, D_MODEL, AB_IN), D_MODEL ** -0.5),
        'mla_q_norm': gain(ks[4], (N_EVEN, Q_LORA)),
        'mla_w_uq': nrm(ks[5], (N_EVEN, Q_LORA, MLA_HEADS * (QK_NOPE + QK_ROPE)), Q_LORA ** -0.5),
        'mla_kv_norm': gain(ks[6], (N_EVEN, KV_LORA)),
        'mla_w_ukv': nrm(ks[7], (N_EVEN, KV_LORA, MLA_HEADS * (QK_NOPE + V_HEAD)), KV_LORA ** -0.5),
        'mlstm_conv_w': nrm(ks[8], (N_EVEN, CONV_W, 2 * M_WIDTH), CONV_W ** -0.5),
        'mlstm_conv_b': nrm(ks[9], (N_EVEN, 2 * M_WIDTH), 0.01),
        'mlstm_gate_bias': gate_bias,
        'mlstm_out_norm': gain(ks[14], (N_EVEN, M_WIDTH)),
        'ab_w_out': nrm(ks[15], (N_EVEN, MLA_WIDTH + M_WIDTH, D_MODEL), (MLA_WIDTH + M_WIDTH) ** -0.5),
        'c_norm': gain(ks[16], (N_ODD, D_MODEL)),
        'c_w_in': nrm(ks[17], (N_ODD, D_MODEL, 2 * G_WIDTH), D_MODEL ** -0.5),
        'c_v_norm': gain(ks[18], (N_ODD, G_WIDTH)),
        'c_w_spatial': nrm(ks[19], (N_ODD, G_GROUPS, G_CHUNK, G_CHUNK), G_CHUNK ** -0.5),
        'c_b_spatial': nrm(ks[20], (N_ODD, G_GROUPS, G_CHUNK), 0.01),
        'c_w_out': nrm(ks[21], (N_ODD, G_WIDTH, D_MODEL), G_WIDTH ** -0.5),
        'ffn_norm': gain(ks[22], (DEPTH, D_MODEL)),
        'ffn_w_up': nrm(ks[23], (DEPTH, D_MODEL, 2 * D_FF), D_MODEL ** -0.5),
        'ffn_conv_w': nrm(ks[24], (DEPTH, CONV_W, 2 * D_FF), CONV_W ** -0.5),
        'ffn_conv_b': nrm(ks[25], (DEPTH, 2 * D_FF), 0.01),
        'ffn_w_down': nrm(ks[26], (DEPTH, D_FF, D_MODEL), D_FF ** -0.5),
        'final_norm': gain(ks[27], (D_MODEL,)),
    }


def reference(x_prompt, x_sample, ab_norm, ab_w_in, mla_q_norm, mla_w_uq, mla_kv_norm, mla_w_ukv, mlstm_conv_w,
              mlstm_conv_b, mlstm_gate_bias, mlstm_out_norm, ab_w_out, c_norm, c_w_in, c_v_norm, c_w_spatial,
              c_b_spatial, c_w_out, ffn_norm, ffn_w_up, ffn_conv_w, ffn_conv_b, ffn_w_down, final_norm):
    p = {
        'ab_norm': ab_norm, 'ab_w_in': ab_w_in, 'mla_q_norm': mla_q_norm, 'mla_w_uq': mla_w_uq,
        'mla_kv_norm': mla_kv_norm, 'mla_w_ukv': mla_w_ukv, 'mlstm_conv_w': mlstm_conv_w,
        'mlstm_conv_b': mlstm_conv_b, 'mlstm_gate_bias': mlstm_gate_bias, 'mlstm_out_norm': mlstm_out_norm,
        'ab_w_out': ab_w_out, 'c_norm': c_norm, 'c_w_in': c_w_in, 'c_v_norm': c_v_norm,
        'c_w_spatial': c_w_spatial, 'c_b_spatial': c_b_spatial, 'c_w_out': c_w_out, 'ffn_norm': ffn_norm,
        'ffn_w_up': ffn_w_up, 'ffn_conv_w': ffn_conv_w, 'ffn_conv_b': ffn_conv_b, 'ffn_w_down': ffn_w_down,
        'final_norm': final_norm,
    }
    y_prompt = forward(x_prompt, p)
    y_sample = forward(x_sample, p)
    return (y_prompt, y_sample)
```

```python
import math
import numpy as np
import concourse.bass as bass
import concourse.mybir as mybir
from concourse.bass_utils import run_bass_kernel_spmd

F32 = mybir.dt.float32
BF16 = mybir.dt.bfloat16
ALU = mybir.AluOpType
AF = mybir.ActivationFunctionType
AX = mybir.AxisListType

D = 1024
EPS = 1e-6
DFF = 2816
NFC = 22
SC_ATT = 96 ** -0.5
SC_M = 128 ** -0.5
LNSC_M = math.log(SC_M)

PARAMS = ['ab_norm', 'ab_w_in', 'mla_q_norm', 'mla_w_uq', 'mla_kv_norm', 'mla_w_ukv', 'mlstm_conv_w',
          'mlstm_conv_b', 'mlstm_gate_bias', 'mlstm_out_norm', 'ab_w_out', 'c_norm', 'c_w_in', 'c_v_norm',
          'c_w_spatial', 'c_b_spatial', 'c_w_out', 'ffn_norm', 'ffn_w_up', 'ffn_conv_w', 'ffn_conv_b',
          'ffn_w_down', 'final_norm']
PSHAPES = {
    'ab_norm': [1, 1024], 'ab_w_in': [1, 1024, 2480], 'mla_q_norm': [1, 256], 'mla_w_uq': [1, 256, 768],
    'mla_kv_norm': [1, 128], 'mla_w_ukv': [1, 128, 1024], 'mlstm_conv_w': [1, 3, 1024], 'mlstm_conv_b': [1, 1024],
    'mlstm_gate_bias': [1, 16], 'mlstm_out_norm': [1, 512], 'ab_w_out': [1, 1024, 1024], 'c_norm': [1, 1024],
    'c_w_in': [1, 1024, 2048], 'c_v_norm': [1, 1024], 'c_w_spatial': [1, 8, 128, 128], 'c_b_spatial': [1, 8, 128],
    'c_w_out': [1, 1024, 1024], 'ffn_norm': [2, 1024], 'ffn_w_up': [2, 1024, 5632], 'ffn_conv_w': [2, 3, 5632],
    'ffn_conv_b': [2, 5632], 'ffn_w_down': [2, 2816, 1024], 'final_norm': [1024],
}


class Buf:
    __slots__ = ("name", "w", "r", "excl")

    def __init__(self, name, excl=False):
        self.name = name
        self.w = None
        self.r = []
        self.excl = excl


class Prog:
    ENGS = ("pe", "act", "dve", "pool", "sp")

    def __init__(self, nc, n_dma_sems=80):
        self.nc = nc
        self.ops = {e: [] for e in self.ENGS}
        self.count = {}
        self.seen = {e: {} for e in self.ENGS}
        self.pending = {e: [] for e in self.ENGS}
        self.dma_sems = ["d%d" % i for i in range(n_dma_sems)]
        self.nops = 0
        self.disabled = False
        self.sweep_no = 0
        import os
        self.maxops = int(os.environ.get("K_OPS", "1000000000"))

    def op(self, eng, fn, reads=(), writes=(), dsem=None):
        if self.disabled or self.nops >= self.maxops:
            return None
        deps = []
        for b in reads:
            if b.w is not None:
                deps.append(b.w)
            if b.excl:
                for t in b.r:
                    if t[0] != eng:
                        deps.append(t)
        for b in writes:
            if b.w is not None:
                deps.append(b.w)
            deps.extend(b.r)
        if dsem is None:
            key, inc = eng, 1
        else:
            key, inc = dsem, 16
        self.count[key] = self.count.get(key, 0) + inc
        tok = (key, self.count[key])
        seen = self.seen[eng]
        waits = []
        if self.pending[eng]:
            for k, v in self.pending[eng]:
                if seen.get(k, 0) < v:
                    seen[k] = v
                    waits.append((k, v))
            self.pending[eng] = []
        for k, v in deps:
            if k == "pe" and eng == "pe" and dsem is None:
                continue
            if seen.get(k, 0) >= v:
                continue
            seen[k] = v
            waits.append((k, v))
        self.ops[eng].append((fn, waits, key, inc))
        for b in reads:
            b.r.append(tok)
        for b in writes:
            b.w = tok
            b.r = []
        self.nops += 1
        return tok

    def barrier(self):
        snap = [(k, v) for k, v in self.count.items()]
        for e in self.ENGS:
            self.pending[e] = list(snap)

    def emit(self):
        from contextlib import ExitStack
        nc = self.nc
        self.barrier()
        with ExitStack() as st:
            sems = {}
            for k in list(self.ENGS) + self.dma_sems:
                sems[k] = st.enter_context(nc.semaphore("s_" + k))
            block = st.enter_context(nc.Block())

            def mk(e):
                def section(eng):
                    for fn, waits, key, inc in self.ops[e]:
                        for k, v in waits:
                            eng.wait_ge(sems[k], v)
                        ins = fn(eng)
                        ins.then_inc(sems[key], inc)
                    for k, v in self.pending[e]:
                        if self.seen[e].get(k, 0) < v:
                            eng.wait_ge(sems[k], v)
                return section

            block.tensor(mk("pe"))
            block.scalar(mk("act"))
            block.vector(mk("dve"))
            block.gpsimd(mk("pool"))
            block.sync(mk("sp"))


class TL:
    __slots__ = ("t", "b", "sem")

    def __init__(self, t, name, sem=None):
        self.t = t
        self.b = Buf(name)
        self.sem = sem


class Ring:
    def __init__(self, tiles):
        self.tiles = tiles
        self.i = 0

    def next(self):
        t = self.tiles[self.i % len(self.tiles)]
        self.i += 1
        return t


def dsize(dt):
    return 4 if dt == F32 else 2


class Ctx:
    def __init__(self, nc, P):
        self.nc = nc
        self.P = P
        self.uid = 0
        self.pers_off = 16512
        self.off = 16512
        self.semi = 0
        self.banks = [TL(nc.alloc_psum_tensor("pb%d" % i, [128, 512], F32), "pb%d" % i) for i in range(8)]
        for b in self.banks:
            b.b.excl = True
        self.pgen = Ring(self.banks[0:6])
        self.pacc = Ring(self.banks[6:8])

    def _alloc(self, shape, dt, dma, off):
        self.uid += 1
        name = "t%d" % self.uid
        t = self.nc.alloc_sbuf_tensor_at(name, list(shape), dt, offset=off)
        sem = None
        if dma:
            sem = self.P.dma_sems[self.semi % (len(self.P.dma_sems) - 1)]
            self.semi += 1
        return TL(t, name, sem)

    @staticmethod
    def _bytes(shape, dt):
        n = 1
        for s in shape[1:]:
            n *= s
        return (n * dsize(dt) + 63) // 64 * 64

    def pers(self, shape, dt, dma=False):
        tl = self._alloc(shape, dt, dma, self.pers_off)
        self.pers_off += self._bytes(shape, dt)
        return tl

    def begin_sweep(self):
        import os
        self.P.sweep_no += 1
        lim = int(os.environ.get("K_LIMIT", "100000"))
        self.P.disabled = self.P.sweep_no > lim
        if os.environ.get("K_VERBOSE"):
            print("sweep", self.P.sweep_no, "ops so far", self.P.nops, flush=True)
        self.P.barrier()
        self.off = self.pers_off

    def tile(self, shape, dt, dma=False):
        tl = self._alloc(shape, dt, dma, self.off)
        self.off += self._bytes(shape, dt)
        assert self.off <= 229376, ("SBUF overflow", self.off)
        return tl

    def ring(self, n, shape, dt, dma=False):
        return Ring([self.tile(shape, dt, dma) for _ in range(n)])

    def ld(self, dst, dst_ap, src_ap, eng="sp", slow=False):
        if slow:
            fn = lambda e: e.dma_start(out=dst_ap, in_=src_ap, allow_slow_non_contiguous=True)
        else:
            fn = lambda e: e.dma_start(out=dst_ap, in_=src_ap)
        self.P.op(eng, fn, writes=[dst.b], dsem=dst.sem)

    def st(self, src, dst_ap, src_ap, eng="pool", slow=False):
        if slow:
            fn = lambda e: e.dma_start(out=dst_ap, in_=src_ap, allow_slow_non_contiguous=True)
        else:
            fn = lambda e: e.dma_start(out=dst_ap, in_=src_ap)
        self.P.op(eng, fn, reads=[src.b], dsem=src.sem)

    def mm(self, out, out_ap, lt, lhsT, rt, rhs, start=True, stop=True, skip=False):
        if skip:
            fn = lambda e: e.matmul(out_ap, lhsT=lhsT, rhs=rhs, start=start, stop=stop, skip_group_check=True)
        else:
            fn = lambda e: e.matmul(out_ap, lhsT=lhsT, rhs=rhs, start=start, stop=stop)
        self.P.op("pe", fn, reads=[lt.b, rt.b], writes=[out.b])

    def tr(self, out, out_ap, it, in_ap, idt, id_ap):
        self.P.op("pe", lambda e: e.transpose(out=out_ap, in_=in_ap, identity=id_ap),
                  reads=[it.b, idt.b], writes=[out.b])

    def act(self, out, out_ap, ins, in_ap, func, scale=None, bias=None, accum=None, extra_w=()):
        kw = {}
        if scale is not None:
            kw["scale"] = scale
        if bias is not None:
            kw["bias"] = bias
        if accum is not None:
            kw["accum_out"] = accum
        self.P.op("act", lambda e: e.activation(out=out_ap, in_=in_ap, func=func, **kw),
                  reads=[t.b for t in ins], writes=[out.b] + [t.b for t in extra_w])

    def v(self, eng, fn, ins, outs):
        self.P.op(eng, fn, reads=[t.b for t in ins], writes=[t.b for t in outs])


def build_program(S_list):
    nc = bass.Bass("TRN2", target_bir_lowering=False)
    P = Prog(nc)
    K = Ctx(nc, P)
    SM = max(S_list)
    NJ = len(S_list)

    import os
    DEBUG = bool(os.environ.get("K_DEBUG"))
    dbg_names = []

    def dram(name, shape, dt, kind="Internal"):
        if DEBUG and kind == "Internal" and (not name.startswith("wb_") or os.environ.get("K_DEBUGW")):
            kind = "ExternalOutput"
            dbg_names.append(name)
        return nc.dram_tensor(name, list(shape), dt, kind=kind).ap()

    xin = [dram("x%d" % j, [S_list[j], D], F32, "ExternalInput") for j in range(NJ)]
    yout = [dram("y%d" % j, [S_list[j], D], F32, "ExternalOutput") for j in range(NJ)]
    W = {n: dram(n, PSHAPES[n], F32, "ExternalInput") for n in PARAMS}
    c_ident = dram("c_ident", [128, 128], F32, "ExternalInput")
    c_triu = dram("c_triu", [128, 128], F32, "ExternalInput")
    c_tril = dram("c_tril", [128, 128], F32, "ExternalInput")
    c_ones = dram("c_ones", [128, 128], F32, "ExternalInput")
    c_cos = dram("c_cos", [SM, 16], F32, "ExternalInput")
    c_sin = dram("c_sin", [SM, 16], F32, "ExternalInput")

    wb_in = dram("wb_in", [1024, 2480], BF16)
    wb_uq = dram("wb_uq", [256, 768], BF16)
    wb_ukv = dram("wb_ukv", [128, 1024], BF16)
    wb_out = dram("wb_out", [1024, 1024], BF16)
    wb_cin = dram("wb_cin", [1024, 2048], BF16)
    wb_cout = dram("wb_cout", [1024, 1024], BF16)
    wb_up = dram("wb_up", [2, 1024, 5632], BF16)
    wb_dn = dram("wb_dn", [2, 2816, 1024], BF16)
    XT = [dram("XT%d" % i, [8, 128, SM + 2], BF16) for i in range(2)]
    X1 = dram("X1", [SM, D], F32)
    QT = dram("QT", [8, 96, SM], BF16)
    KT = dram("KT", [8, 96, SM], BF16)
    VV = dram("VV", [SM, 8, 65], BF16)
    QKT = dram("QKT", [8, 128, SM], BF16)
    VM = dram("VM", [SM, 4, 129], BF16)
    SO = dram("SO", [SM, 512], BF16)
    GG = dram("GG", [SM, 16], F32)
    HF = dram("HF", [SM, 512], F32)
    YY = dram("YY", [SM, D], BF16)
    if DEBUG:
        DBG_dm = dram("DBG_dm", [128, 128], F32)
        DBG_pt = dram("DBG_pt", [128, 128], BF16)
        DBG_s = dram("DBG_s", [128, 24], F32)
        DBG_tot = dram("DBG_tot", [128, 129], F32)
        DBG_kk = dram("DBG_kk", [128, 128], BF16)
        DBG_wsT = dram("DBG_wsT", [128, 1024], BF16)
        DBG_bsT = dram("DBG_bsT", [128, 8], F32)
        DBG_uv = dram("DBG_uv", [128, 2048], F32)
        DBG_vn = dram("DBG_vn", [128, 1024], BF16)
        DBG_z = dram("DBG_z", [128, 1024], BF16)

    wsem = P.dma_sems[-1]

    def cast(dst, src, rows):
        for r0 in range(0, rows, 128):
            r1 = min(rows, r0 + 128)
            P.op("pool", lambda e, r0=r0, r1=r1: e.dma_start(out=dst[r0:r1, :], in_=src[r0:r1, :]),
                 writes=[], dsem=wsem)

    cast(wb_in, W['ab_w_in'][0], 1024)
    cast(wb_uq, W['mla_w_uq'][0], 256)
    cast(wb_ukv, W['mla_w_ukv'][0], 128)
    cast(wb_out, W['ab_w_out'][0], 1024)
    cast(wb_cin, W['c_w_in'][0], 1024)
    cast(wb_cout, W['c_w_out'][0], 1024)
    for l in range(2):
        cast(wb_up[l], W['ffn_w_up'][l], 1024)
        cast(wb_dn[l], W['ffn_w_down'][l], 2816)

    identf = K.pers([128, 128], F32, dma=True)
    ident = K.pers([128, 128], BF16)
    triu = K.pers([128, 128], F32, dma=True)
    tril = K.pers([128, 128], F32, dma=True)
    ones = K.pers([128, 128], F32, dma=True)
    cw_m = K.pers([128, 4, 8], F32)
    cw_f = [K.pers([128, 4, 44], F32) for _ in range(2)]
    bsT = K.pers([128, 8], F32)
    wsT = K.pers([128, 8, 128], BF16)
    zero = K.pers([128, 16], BF16, dma=True)
    K.begin_sweep()
    K.ld(identf, identf.t[:], c_ident)
    K.ld(triu, triu.t[:], c_triu)
    K.ld(tril, tril.t[:], c_tril)
    K.ld(ones, ones.t[:], c_ones)
    K.v("dve", lambda e: e.tensor_copy(out=ident.t[:], in_=identf.t[:]), [identf], [ident])
    K.v("dve", lambda e: e.memset(zero.t[:], 0.0), [], [zero])

    def load_T(dst, dst_view, raws, nrow):
        raw = K.tile([nrow, len(raws), 128], F32, dma=True)
        for j, src in enumerate(raws):
            K.ld(raw, raw.t[:, j, :], src)
        for j in range(len(raws)):
            pb = K.pgen.next()
            K.tr(pb, pb.t[:, 0:nrow], raw, raw.t[:, j, :], identf, identf.t[0:nrow, 0:nrow])
            K.v("dve", lambda e, j=j, pb=pb: e.tensor_copy(out=dst_view(j), in_=pb.t[:, 0:nrow]), [pb], [dst])

    mcw = W['mlstm_conv_w'][0]
    load_T(cw_m, lambda j: cw_m.t[:, j, :],
           [mcw[j].rearrange("(c p) -> c p", p=128) for j in range(3)] +
           [W['mlstm_conv_b'][0].rearrange("(c p) -> c p", p=128)], 8)
    for l in range(2):
        load_T(cw_f[l], lambda j, l=l: cw_f[l].t[:, j, :],
               [W['ffn_conv_w'][l, j].rearrange("(c p) -> c p", p=128) for j in range(3)] +
               [W['ffn_conv_b'][l].rearrange("(c p) -> c p", p=128)], 44)
    load_T(bsT, lambda j: bsT.t[:, :], [W['c_b_spatial'][0]], 8)
    wsr = K.tile([128, 8, 128], F32, dma=True)
    wsrb = K.tile([128, 8, 128], BF16)
    K.ld(wsr, wsr.t[:], W['c_w_spatial'][0].rearrange("g t s -> t g s"))
    K.v("dve", lambda e: e.tensor_copy(out=wsrb.t[:], in_=wsr.t[:]), [wsr], [wsrb])
    pb = K.pgen.next()
    pbv = pb.t[:].bitcast(BF16).rearrange("p (c t) -> p c t", c=8)
    for g in range(8):
        K.tr(pb, pbv[:, g, :], wsrb, wsrb.t[:, g, :], ident, ident.t[:])
    K.v("dve", lambda e, pbv=pbv: e.tensor_copy(out=wsT.t[:], in_=pbv), [pb], [wsT])

    def bc_load(dst, src1d, n):
        K.ld(dst, dst.t[:, 0:n], src1d.partition_broadcast(128))

    class NormT:
        def __init__(self):
            self.junk = K.tile([128, 1024], BF16)
            self.ss = K.ring(2, [128, 4], F32)
            self.xn = K.ring(2, [128, 1024], BF16)
            self.xnT = K.ring(2, [128, 8, 128], BF16, dma=True)
            self.g = K.tile([128, 1024], F32, dma=True)

        def load_gain(self, src1d):
            bc_load(self.g, src1d, 1024)

        def stats(self, x, x_ap, n):
            ss = self.ss.next()
            junk = self.junk
            K.v("dve", lambda e: e.memset(ss.t[:, 0:1], 0.0), [], [ss])
            K.act(junk, junk.t[:, 0:n], [x, ss], x_ap, AF.Square, accum=ss.t[:, 0:1], extra_w=[ss])
            K.v("dve", lambda e: e.tensor_scalar(out=ss.t[:, 1:2], in0=ss.t[:, 0:1], scalar1=1.0 / n, scalar2=EPS,
                                                 op0=ALU.mult, op1=ALU.add), [ss], [ss])
            K.act(ss, ss.t[:, 2:3], [ss], ss.t[:, 1:2], AF.Sqrt)
            K.v("dve", lambda e: e.reciprocal(out=ss.t[:, 3:4], in_=ss.t[:, 2:3]), [ss], [ss])
            return ss

        def run(self, x, x_ap, t, XTd):
            ss = self.stats(x, x_ap, 1024)
            xn = self.xn.next()
            g = self.g
            K.v("dve", lambda e: e.scalar_tensor_tensor(out=xn.t[:], in0=x_ap, scalar=ss.t[:, 3:4], in1=g.t[:],
                                                        op0=ALU.mult, op1=ALU.mult), [x, ss, g], [xn])
            pb = K.pgen.next()
            pbv = pb.t[:].bitcast(BF16).rearrange("p (c t) -> p c t", c=8)
            for c in range(8):
                K.tr(pb, pbv[:, c, :], xn, xn.t[:, c * 128:(c + 1) * 128], ident, ident.t[:])
            xnT = self.xnT.next()
            K.act(xnT, xnT.t[:], [pb], pbv, AF.Copy)
            K.st(xnT, XTd[:, :, 1 + t * 128:1 + (t + 1) * 128].rearrange("c p t -> p c t"), xnT.t[:])

    def conv_chunk(pm, ph, hsl, cw, ci, dst):
        w0 = cw.t[:, 0, ci:ci + 1]
        w1 = cw.t[:, 1, ci:ci + 1]
        w2 = cw.t[:, 2, ci:ci + 1]
        bb = cw.t[:, 3, ci:ci + 1]
        K.act(dst, dst.t[:], [pm, cw], pm.t[:], AF.Identity, scale=w1, bias=bb)
        K.v("dve", lambda e: e.scalar_tensor_tensor(out=dst.t[:, 1:512], in0=pm.t[:, 0:511], scalar=w0,
                                                    in1=dst.t[:, 1:512], op0=ALU.mult, op1=ALU.add),
            [pm, cw, dst], [dst])
        K.v("dve", lambda e: e.scalar_tensor_tensor(out=dst.t[:, 0:511], in0=pm.t[:, 1:512], scalar=w2,
                                                    in1=dst.t[:, 0:511], op0=ALU.mult, op1=ALU.add),
            [pm, cw, dst], [dst])
        K.v("dve", lambda e: e.scalar_tensor_tensor(out=dst.t[:, 0:1], in0=ph.t[:, hsl:hsl + 1], scalar=w0,
                                                    in1=dst.t[:, 0:1], op0=ALU.mult, op1=ALU.add),
            [ph, cw, dst], [dst])
        K.v("dve", lambda e: e.scalar_tensor_tensor(out=dst.t[:, 511:512], in0=ph.t[:, hsl + 1:hsl + 2], scalar=w2,
                                                    in1=dst.t[:, 511:512], op0=ALU.mult, op1=ALU.add),
            [ph, cw, dst], [dst])

    for jb in range(NJ):
        S = S_list[jb]
        NT = S // 128
        NB = S // 512
        x_in = xin[jb]
        y_out = yout[jb]

        K.begin_sweep()
        nt = NormT()
        nt.load_gain(W['ab_norm'][0])
        for i in range(2):
            K.st(zero, XT[i][:, :, 0:1].rearrange("c p t -> p c t"), zero.t[:, 0:8].rearrange("p (c t) -> p c t", t=1),
                 slow=True)
            K.st(zero, XT[i][:, :, S + 1:S + 2].rearrange("c p t -> p c t"),
                 zero.t[:, 0:8].rearrange("p (c t) -> p c t", t=1), slow=True)
        xr = K.ring(2, [128, 1024], F32, dma=True)
        for t in range(NT):
            x = xr.next()
            K.ld(x, x.t[:], x_in[t * 128:(t + 1) * 128, :])
            nt.run(x, x.t[:], t, XT[0])

        K.begin_sweep()
        w_in = K.tile([128, 8, 2480], BF16, dma=True)
        K.ld(w_in, w_in.t[:], wb_in.rearrange("(k p) n -> p k n", p=128))
        w_uq = K.tile([128, 2, 768], BF16, dma=True)
        K.ld(w_uq, w_uq.t[:], wb_uq.rearrange("(k p) n -> p k n", p=128))
        w_ukv = K.tile([128, 1024], BF16, dma=True)
        K.ld(w_ukv, w_ukv.t[:], wb_ukv)
        qg = K.tile([128, 256], F32, dma=True)
        bc_load(qg, W['mla_q_norm'][0], 256)
        kvg = K.tile([128, 128], F32, dma=True)
        bc_load(kvg, W['mla_kv_norm'][0], 128)
        gbias = K.tile([128, 16], F32, dma=True)
        bc_load(gbias, W['mlstm_gate_bias'][0], 16)
        winr = K.ring(2, [128, 8, 514], BF16, dma=True)
        cgr = K.ring(2, [128, 512], F32)
        qkr = K.ring(2, [128, 512], BF16, dma=True)
        junk = K.tile([128, 256], BF16)
        ssr = K.ring(2, [128, 8], F32)
        cqn = K.ring(2, [128, 384], BF16)
        cTr = K.ring(2, [128, 3, 128], BF16)
        cosr = K.ring(2, [128, 32], F32, dma=True)
        krr = K.ring(2, [128, 32], BF16)
        tmp = K.ring(2, [128, 8, 64], F32)
        qst = K.ring(2, [128, 8, 96], BF16)
        qfr = K.ring(2, [128, 768], F32)
        kst = K.ring(2, [128, 8, 96], BF16)
        vst = K.ring(2, [128, 8, 65], BF16, dma=True)
        qTs = K.ring(2, [96, 8, 128], BF16, dma=True)
        kTs = K.ring(2, [96, 8, 128], BF16, dma=True)
        vmst = K.ring(2, [128, 4, 129], BF16, dma=True)
        sor = K.ring(2, [128, 512], BF16, dma=True)
        gtr = K.ring(2, [128, 16], F32, dma=True)
        ge = K.ring(2, [128, 2, 4], F32)
        for tl in vst.tiles + vmst.tiles:
            K.v("dve", lambda e, tl=tl: e.memset(tl.t[:], 1.0), [], [tl])
        K.pgen = Ring(K.banks[0:6])
        for b in range(NB):
            win = winr.next()
            K.ld(win, win.t[:], XT[0][:, :, 512 * b:512 * b + 514].rearrange("c p t -> p c t"))
            for c in range(8):
                col0 = 416 + c * 128
                pm = K.pgen.next()
                hsl = 0
                halo = K.banks[6 + (c % 2)]
                for k in range(8):
                    K.mm(pm, pm.t[:, :], w_in, w_in.t[:, k, col0:col0 + 128], win, win.t[:, k, 1:513],
                         start=(k == 0), stop=(k == 7))
                for k in range(8):
                    K.mm(halo, halo.t[:, hsl:hsl + 2], w_in, w_in.t[:, k, col0:col0 + 128], win,
                         win.t[:, k, 0:514:513], start=(k == 0), stop=(k == 7))
                cg = cgr.next()
                conv_chunk(pm, halo, hsl, cw_m, c, cg)
                qk = qkr.next()
                K.act(qk, qk.t[:], [cg], cg.t[:], AF.Silu)
                K.st(qk, QKT[c, :, 512 * b:512 * b + 512], qk.t[:])
            for tt in range(4):
                t = b * 4 + tt
                lo = 1 + tt * 128
                pa = K.pgen.next()
                pv = K.pgen.next()
                po = K.pgen.next()
                for k in range(8):
                    K.mm(pa, pa.t[:, 0:416], win, win.t[:, k, lo:lo + 128], w_in, w_in.t[:, k, 0:416],
                         start=(k == 0), stop=(k == 7))
                for k in range(8):
                    K.mm(pa, pa.t[:, 416:432], win, win.t[:, k, lo:lo + 128], w_in, w_in.t[:, k, 2464:2480],
                         start=(k == 0), stop=(k == 7))
                for k in range(8):
                    K.mm(pv, pv.t[:, :], win, win.t[:, k, lo:lo + 128], w_in, w_in.t[:, k, 1440:1952],
                         start=(k == 0), stop=(k == 7))
                for k in range(8):
                    K.mm(po, po.t[:, :], win, win.t[:, k, lo:lo + 128], w_in, w_in.t[:, k, 1952:2464],
                         start=(k == 0), stop=(k == 7))
                vm = vmst.next()
                K.act(vm, vm.t[:, :, 0:128], [pv], pv.t[:, :].rearrange("p (h d) -> p h d", d=128), AF.Copy)
                K.st(vm, VM[t * 128:(t + 1) * 128, :, :], vm.t[:])
                so = sor.next()
                K.act(so, so.t[:], [po], po.t[:, :], AF.Sigmoid)
                K.st(so, SO[t * 128:(t + 1) * 128, :], so.t[:])
                gt = gtr.next()
                g2 = ge.next()
                K.v("dve", lambda e, gt=gt, pa=pa: e.tensor_tensor(out=gt.t[:], in0=pa.t[:, 416:432], in1=gbias.t[:],
                                                                   op=ALU.add), [pa, gbias], [gt])
                gtv = gt.t[:, :].rearrange("p (a b) -> p a b", b=8)[:, :, 4:8]
                K.act(g2, g2.t[:], [gt], gtv, AF.Exp, scale=-1.0)
                K.v("dve", lambda e, g2=g2: e.tensor_scalar_add(out=g2.t[:], in0=g2.t[:], scalar1=1.0), [g2], [g2])
                K.act(g2, g2.t[:], [g2], g2.t[:], AF.Ln)
                K.v("dve", lambda e, g2=g2, gtv=gtv: e.tensor_scalar_mul(out=gtv, in0=g2.t[:], scalar1=-1.0), [g2], [gt])
                K.st(gt, GG[t * 128:(t + 1) * 128, :], gt.t[:])
                ss = ssr.next()
                K.v("dve", lambda e, ss=ss: e.memset(ss.t[:, 0:2], 0.0), [], [ss])
                K.act(junk, junk.t[:, 0:256], [pa, ss], pa.t[:, 0:256], AF.Square, accum=ss.t[:, 0:1], extra_w=[ss])
                K.act(junk, junk.t[:, 0:128], [pa, ss], pa.t[:, 256:384], AF.Square, accum=ss.t[:, 1:2], extra_w=[ss])
                K.v("dve", lambda e, ss=ss: e.tensor_scalar(out=ss.t[:, 2:3], in0=ss.t[:, 0:1], scalar1=1.0 / 256,
                                                            scalar2=EPS, op0=ALU.mult, op1=ALU.add), [ss], [ss])
                K.v("dve", lambda e, ss=ss: e.tensor_scalar(out=ss.t[:, 3:4], in0=ss.t[:, 1:2], scalar1=1.0 / 128,
                                                            scalar2=EPS, op0=ALU.mult, op1=ALU.add), [ss], [ss])
                K.act(ss, ss.t[:, 4:6], [ss], ss.t[:, 2:4], AF.Sqrt)
                K.v("dve", lambda e, ss=ss: e.reciprocal(out=ss.t[:, 6:8], in_=ss.t[:, 4:6]), [ss], [ss])
                cq = cqn.next()
                K.v("dve", lambda e, ss=ss, cq=cq, pa=pa: e.scalar_tensor_tensor(
                    out=cq.t[:, 0:256], in0=pa.t[:, 0:256], scalar=ss.t[:, 6:7], in1=qg.t[:], op0=ALU.mult,
                    op1=ALU.mult), [pa, ss, qg], [cq])
                K.v("dve", lambda e, ss=ss, cq=cq, pa=pa: e.scalar_tensor_tensor(
                    out=cq.t[:, 256:384], in0=pa.t[:, 256:384], scalar=ss.t[:, 7:8], in1=kvg.t[:], op0=ALU.mult,
                    op1=ALU.mult), [pa, ss, kvg], [cq])
                cs = cosr.next()
                K.ld(cs, cs.t[:, 0:16], c_cos[t * 128:(t + 1) * 128, :])
                K.ld(cs, cs.t[:, 16:32], c_sin[t * 128:(t + 1) * 128, :])
                kr = krr.next()
                tm = tmp.next()

                def rope(dst1, dst2, x1, x2, cosb, sinb, ta, tb, ins, outs, tmt):
                    K.v("dve", lambda e: e.tensor_tensor(out=ta, in0=x1, in1=cosb, op=ALU.mult), ins, [tmt])
                    K.v("dve", lambda e: e.tensor_tensor(out=tb, in0=x2, in1=sinb, op=ALU.mult), ins, [tmt])
                    K.v("dve", lambda e: e.tensor_tensor(out=dst1, in0=ta, in1=tb, op=ALU.subtract), [tmt], outs)
                    K.v("dve", lambda e: e.tensor_tensor(out=ta, in0=x2, in1=cosb, op=ALU.mult), ins + outs, [tmt])
                    K.v("dve", lambda e: e.tensor_tensor(out=tb, in0=x1, in1=sinb, op=ALU.mult), ins, [tmt])
                    K.v("dve", lambda e: e.tensor_tensor(out=dst2, in0=ta, in1=tb, op=ALU.add), [tmt], outs)

                rope(kr.t[:, 0:16], kr.t[:, 16:32], pa.t[:, 384:400], pa.t[:, 400:416], cs.t[:, 0:16], cs.t[:, 16:32],
                     tm.t[:, 0, 0:16], tm.t[:, 0, 16:32], [pa, cs], [kr], tm)
                pb = K.pgen.next()
                pbv = pb.t[:].bitcast(BF16).rearrange("p (c t) -> p c t", c=8)
                for c in range(3):
                    K.tr(pb, pbv[:, c, :], cq, cq.t[:, c * 128:(c + 1) * 128], ident, ident.t[:])
                cT = cTr.next()
                K.act(cT, cT.t[:], [pb], pbv[:, 0:3, :], AF.Copy)
                pq1 = K.pgen.next()
                pq2 = K.pgen.next()
                for k in range(2):
                    K.mm(pq1, pq1.t[:, 0:480], cT, cT.t[:, k, :], w_uq, w_uq.t[:, k, 0:480], start=(k == 0), stop=(k == 1))
                for k in range(2):
                    K.mm(pq2, pq2.t[:, 0:288], cT, cT.t[:, k, :], w_uq, w_uq.t[:, k, 480:768], start=(k == 0), stop=(k == 1))
                qs = qst.next()
                qf = qfr.next()
                K.act(qf, qf.t[:, 0:480], [pq1], pq1.t[:, 0:480], AF.Copy)
                K.act(qf, qf.t[:, 480:768], [pq2], pq2.t[:, 0:288], AF.Copy)
                qfv = qf.t[:, :].rearrange("p (h d) -> p h d", d=96)
                K.v("dve", lambda e, qs=qs, qfv=qfv: e.tensor_copy(out=qs.t[:, :, 0:64], in_=qfv[:, :, 0:64]), [qf], [qs])
                cosb = cs.t[:, 0:16].unsqueeze(1).to_broadcast([128, 8, 16])
                sinb = cs.t[:, 16:32].unsqueeze(1).to_broadcast([128, 8, 16])
                rope(qs.t[:, :, 64:80], qs.t[:, :, 80:96], qfv[:, :, 64:80], qfv[:, :, 80:96],
                     cosb, sinb, tm.t[:, :, 0:16], tm.t[:, :, 16:32], [qf, cs], [qs], tm)
                pk1 = K.pgen.next()
                pk2 = K.pgen.next()
                K.mm(pk1, pk1.t[:, :], cT, cT.t[:, 2, :], w_ukv, w_ukv.t[:, 0:512])
                K.mm(pk2, pk2.t[:, :], cT, cT.t[:, 2, :], w_ukv, w_ukv.t[:, 512:1024])
                ks = kst.next()
                vs = vst.next()
                for (pk, h0) in ((pk1, 0), (pk2, 4)):
                    pkv = pk.t[:, :].rearrange("p (h d) -> p h d", d=128)
                    K.act(ks, ks.t[:, h0:h0 + 4, 0:64], [pk], pkv[:, :, 0:64], AF.Copy)
                    K.v("dve", lambda e, vs=vs, pkv=pkv, h0=h0: e.tensor_copy(out=vs.t[:, h0:h0 + 4, 0:64],
                                                                             in_=pkv[:, :, 64:128]), [pk], [vs])
                K.v("dve", lambda e, ks=ks, kr=kr: e.tensor_copy(
                    out=ks.t[:, :, 64:96], in_=kr.t[:, :].unsqueeze(1).to_broadcast([128, 8, 32])), [kr], [ks])
                K.st(vs, VV[t * 128:(t + 1) * 128, :, :], vs.t[:])
                for (src, dstr, dd) in ((qs, qTs, QT), (ks, kTs, KT)):
                    pb = K.pgen.next()
                    pbv = pb.t[:].bitcast(BF16).rearrange("p (c t) -> p c t", c=8)
                    for h in range(8):
                        K.tr(pb, pbv[0:96, h, :], src, src.t[:, h, :], ident, ident.t[:])
                    dT = dstr.next()
                    K.act(dT, dT.t[:], [pb], pbv[0:96, :, :], AF.Copy)
                    K.st(dT, dd[:, :, t * 128:(t + 1) * 128].rearrange("h p t -> p h t"), dT.t[:])
        K.pgen = Ring(K.banks[0:6])

        K.begin_sweep()
        kth = K.ring(2, [96, S], BF16, dma=True)
        vh = K.ring(2, [128, NT, 65], BF16, dma=True)
        qbr = K.ring(2, [96, 512], BF16, dma=True)
        ptr = K.ring(3, [128, 512], BF16)
        rcr = K.ring(2, [128, 4], F32)
        yar = K.ring(2, [128, 4, 64], BF16, dma=True)
        for h in range(8):
            kt = kth.next()
            K.ld(kt, kt.t[:], KT[h, :, 0:S])
            vv = vh.next()
            K.ld(vv, vv.t[:], VV[0:S, h, :].rearrange("(n p) d -> p n d", p=128))
            for qb in range(NB):
                qt = qbr.next()
                K.ld(qt, qt.t[:], QT[h, :, qb * 512:(qb + 1) * 512])
                acc = K.pacc.next()
                for kb in range(NT):
                    ps = K.pgen.next()
                    K.mm(ps, ps.t[:, :], kt, kt.t[:, kb * 128:(kb + 1) * 128], qt, qt.t[:, :])
                    pt = ptr.next()
                    K.act(pt, pt.t[:], [ps], ps.t[:, :], AF.Exp, scale=SC_ATT)
                    for j in range(4):
                        K.mm(acc, acc.t[:, j * 65:(j + 1) * 65], pt, pt.t[:, j * 128:(j + 1) * 128], vv, vv.t[:, kb, :],
                             start=(kb == 0 and j == 0), stop=(kb == NT - 1 and j == 3), skip=True)
                accv = acc.t[:, 0:260].rearrange("p (j d) -> p j d", d=65)
                rc = rcr.next()
                K.v("dve", lambda e, rc=rc, accv=accv: e.reciprocal(out=rc.t[:, :].unsqueeze(2), in_=accv[:, :, 64:65]),
                    [acc], [rc])
                ya = yar.next()
                K.v("dve", lambda e, rc=rc, accv=accv, ya=ya: e.tensor_tensor(
                    out=ya.t[:], in0=accv[:, :, 0:64], in1=rc.t[:, :].unsqueeze(2).to_broadcast([128, 4, 64]),
                    op=ALU.mult), [acc, rc], [ya])
                K.st(ya, YY[qb * 512:(qb + 1) * 512, h * 64:(h + 1) * 64].rearrange("(j p) d -> p j d", p=128), ya.t[:])

        for direction in (0, 1):
            K.begin_sweep()
            tri = triu if direction == 0 else tril
            gi0 = 0 if direction == 0 else 8
            qkl = K.ring(2, [128, 8, 128], BF16, dma=True)
            vml = K.ring(2, [128, 4, 129], BF16, dma=True)
            gl = K.ring(2, [128, 16], F32, dma=True)
            ktok = K.ring(2, [128, 4, 128], BF16)
            sm = K.ring(2, [128, 6, 4], F32)
            dmr = K.ring(2, [128, 128], F32)
            ptm = K.ring(2, [128, 128], BF16)
            itr = K.ring(2, [128, 129], F32)
            totr = K.ring(2, [128, 129], F32)
            dnr = K.ring(2, [128, 2], F32)
            vwr = K.ring(2, [128, 129], BF16)
            ST = [K.tile([128, 129], F32) for _ in range(4)]
            STb = [K.tile([128, 129], BF16) for _ in range(4)]
            hfr = K.ring(2, [128, 512], F32, dma=True)
            if direction == 1:
                sol = K.ring(2, [128, 512], BF16, dma=True)
                og = K.tile([128, 512], F32, dma=True)
                bc_load(og, W['mlstm_out_norm'][0], 512)
                sq = K.tile([128, 512], F32)
                ms = K.ring(2, [128, 3, 4], F32)
                ymr = K.ring(2, [128, 512], BF16, dma=True)
            for h in range(4):
                K.v("dve", lambda e, h=h: e.memset(ST[h].t[:], 0.0), [], [ST[h]])
                K.v("dve", lambda e, h=h: e.memset(STb[h].t[:], 0.0), [], [STb[h]])
            order = range(NT) if direction == 0 else range(NT - 1, -1, -1)
            for n in order:
                r0, r1 = n * 128, (n + 1) * 128
                qk = qkl.next()
                K.ld(qk, qk.t[:], QKT[:, :, r0:r1].rearrange("c p t -> p c t"))
                vm = vml.next()
                K.ld(vm, vm.t[:], VM[r0:r1, :, :])
                g = gl.next()
                K.ld(g, g.t[:], GG[r0:r1, :])
                hf = hfr.next()
                if direction == 1:
                    K.ld(hf, hf.t[:], HF[r0:r1, :])
                    so = sol.next()
                    K.ld(so, so.t[:], SO[r0:r1, :])
                li = g.t[:, gi0:gi0 + 4]
                lf = g.t[:, gi0 + 4:gi0 + 8]
                pb = K.pgen.next()
                pbv = pb.t[:].bitcast(BF16).rearrange("p (c t) -> p c t", c=8)
                for h in range(4):
                    K.tr(pb, pbv[:, h, :], qk, qk.t[:, 4 + h, :], ident, ident.t[:])
                kk = ktok.next()
                K.act(kk, kk.t[:], [pb], pbv[:, 0:4, :], AF.Copy)
                pc = K.pgen.next()
                K.mm(pc, pc.t[:, 0:4], tri, tri.t[:], g, lf)
                K.mm(pc, pc.t[:, 4:8], ones, ones.t[:], g, lf)
                s_ = sm.next()
                K.v("dve", lambda e, s_=s_, pc=pc, li=li: e.tensor_tensor(out=s_.t[:, 0, :], in0=li, in1=pc.t[:, 0:4],
                                                                         op=ALU.subtract), [g, pc], [s_])
                K.v("dve", lambda e, s_=s_, pc=pc: e.scalar_tensor_tensor(
                    out=s_.t[:, 1, :], in0=pc.t[:, 4:8], scalar=LNSC_M, in1=s_.t[:, 0, :], op0=ALU.add, op1=ALU.add),
                    [pc, s_], [s_])
                K.act(s_, s_.t[:, 2:4, :], [pc], pc.t[:, 0:8].rearrange("p (a b) -> p a b", b=4), AF.Exp)
                K.act(s_, s_.t[:, 4, :], [s_], s_.t[:, 1, :], AF.Exp)
                for h in range(4):
                    pB = K.pgen.next()
                    K.mm(pB, pB.t[:, 0:128], g, lf[:, h:h + 1].to_broadcast([128, 128]), tri, tri.t[:])
                    dm = dmr.next()
                    K.act(dm, dm.t[:], [pB, s_], pB.t[:, 0:128], AF.Exp, bias=s_.t[:, 0, h:h + 1])
                    K.v("dve", lambda e, dm=dm, tri=tri: e.tensor_tensor(out=dm.t[:], in0=dm.t[:], in1=tri.t[:], op=ALU.mult),
                        [dm, tri], [dm])
                    pS = K.pgen.next()
                    K.mm(pS, pS.t[:, 0:128], qk, qk.t[:, 4 + h, :], qk, qk.t[:, h, :])
                    pt = ptm.next()
                    K.v("dve", lambda e, pS=pS, dm=dm, pt=pt: e.scalar_tensor_tensor(
                        out=pt.t[:], in0=pS.t[:, 0:128], scalar=SC_M, in1=dm.t[:], op0=ALU.mult, op1=ALU.mult),
                        [pS, dm], [pt])
                    if DEBUG and direction == 0 and n == 0 and h == 0:
                        P.op("sp", lambda e, dm=dm: e.dma_start(out=DBG_dm, in_=dm.t[:]), reads=[dm.b], dsem=wsem)
                        P.op("sp", lambda e, pt=pt: e.dma_start(out=DBG_pt, in_=pt.t[:]), reads=[pt.b], dsem=wsem)
                        P.op("sp", lambda e, s_=s_: e.dma_start(out=DBG_s, in_=s_.t[:, :, :].rearrange("p a b -> p (a b)")), reads=[s_.b], dsem=wsem)
                        P.op("sp", lambda e, kk=kk: e.dma_start(out=DBG_kk, in_=kk.t[:, 0, :]), reads=[kk.b], dsem=wsem)
                    pN = K.pgen.next()
                    K.mm(pN, pN.t[:, 0:129], pt, pt.t[:], vm, vm.t[:, h, :])
                    pI = K.pgen.next()
                    K.mm(pI, pI.t[:, 0:129], qk, qk.t[:, h, :], STb[h], STb[h].t[:])
                    it = itr.next()
                    K.act(it, it.t[:], [pI, s_], pI.t[:, 0:129], AF.Copy, scale=s_.t[:, 2, h:h + 1])
                    tot = totr.next()
                    K.v("dve", lambda e, tot=tot, pN=pN, it=it: e.tensor_tensor(out=tot.t[:], in0=pN.t[:, 0:129],
                                                                               in1=it.t[:], op=ALU.add), [pN, it], [tot])
                    if DEBUG and direction == 0 and n == 0 and h == 0:
                        P.op("sp", lambda e, tot=tot: e.dma_start(out=DBG_tot, in_=tot.t[:]), reads=[tot.b], dsem=wsem)
                    dn = dnr.next()
                    K.v("dve", lambda e, tot=tot, dn=dn: e.tensor_scalar_mul(out=dn.t[:, 1:2], in0=tot.t[:, 128:129],
                                                                             scalar1=-1.0), [tot], [dn])
                    K.v("dve", lambda e, tot=tot, dn=dn: e.tensor_tensor(out=dn.t[:, 0:1], in0=tot.t[:, 128:129],
                                                                         in1=dn.t[:, 1:2], op=ALU.max), [tot, dn], [dn])
                    K.v("dve", lambda e, dn=dn: e.tensor_scalar_max(out=dn.t[:, 0:1], in0=dn.t[:, 0:1], scalar1=1.0),
                        [dn], [dn])
                    K.v("dve", lambda e, dn=dn: e.reciprocal(out=dn.t[:, 1:2], in_=dn.t[:, 0:1]), [dn], [dn])
                    hs_ap = hf.t[:, h * 128:(h + 1) * 128]
                    if direction == 0:
                        K.v("dve", lambda e, tot=tot, dn=dn, hs_ap=hs_ap: e.tensor_scalar_mul(
                            out=hs_ap, in0=tot.t[:, 0:128], scalar1=dn.t[:, 1:2]), [tot, dn], [hf])
                    else:
                        K.v("dve", lambda e, tot=tot, dn=dn, hs_ap=hs_ap: e.scalar_tensor_tensor(
                            out=hs_ap, in0=tot.t[:, 0:128], scalar=dn.t[:, 1:2], in1=hs_ap, op0=ALU.mult, op1=ALU.add),
                            [tot, dn, hf], [hf])
                    vw = vwr.next()
                    K.v("dve", lambda e, vw=vw, vm=vm, s_=s_, h=h: e.tensor_scalar_mul(
                        out=vw.t[:], in0=vm.t[:, h, :], scalar1=s_.t[:, 4, h:h + 1]), [vm, s_], [vw])
                    pC = K.pgen.next()
                    K.mm(pC, pC.t[:, 0:129], kk, kk.t[:, h, :], vw, vw.t[:])
                    K.v("dve", lambda e, pC=pC, s_=s_, h=h: e.scalar_tensor_tensor(
                        out=ST[h].t[:], in0=ST[h].t[:], scalar=s_.t[:, 3, h:h + 1], in1=pC.t[:, 0:129], op0=ALU.mult,
                        op1=ALU.add), [ST[h], s_, pC], [ST[h]])
                    K.act(STb[h], STb[h].t[:], [ST[h]], ST[h].t[:], AF.Copy)
                if direction == 0:
                    K.st(hf, HF[r0:r1, :], hf.t[:])
                else:
                    m_ = ms.next()
                    K.v("dve", lambda e, hf=hf: e.tensor_tensor(out=sq.t[:], in0=hf.t[:], in1=hf.t[:], op=ALU.mult),
                        [hf], [sq])
                    K.v("dve", lambda e, m_=m_: e.tensor_reduce(out=m_.t[:, 0, :],
                                                                 in_=sq.t[:, :].rearrange("p (h d) -> p h d", d=128),
                                                                 axis=AX.X, op=ALU.add), [sq], [m_])
                    K.v("dve", lambda e, m_=m_: e.tensor_scalar(out=m_.t[:, 1, :], in0=m_.t[:, 0, :], scalar1=1.0 / 128,
                                                                scalar2=EPS, op0=ALU.mult, op1=ALU.add), [m_], [m_])
                    K.act(m_, m_.t[:, 2, :], [m_], m_.t[:, 1, :], AF.Sqrt)
                    K.v("dve", lambda e, m_=m_: e.reciprocal(out=m_.t[:, 0, :], in_=m_.t[:, 2, :]), [m_], [m_])
                    K.v("dve", lambda e, m_=m_, hf=hf: e.tensor_tensor(
                        out=sq.t[:, :].rearrange("p (h d) -> p h d", d=128),
                        in0=hf.t[:, :].rearrange("p (h d) -> p h d", d=128),
                        in1=m_.t[:, 0, :].unsqueeze(2).to_broadcast([128, 4, 128]), op=ALU.mult), [hf, m_], [sq])
                    K.v("dve", lambda e: e.tensor_tensor(out=sq.t[:], in0=sq.t[:], in1=og.t[:], op=ALU.mult), [sq, og], [sq])
                    ym = ymr.next()
                    K.v("dve", lambda e, ym=ym, so=so: e.tensor_tensor(out=ym.t[:], in0=sq.t[:], in1=so.t[:], op=ALU.mult),
                        [sq, so], [ym])
                    K.st(ym, YY[r0:r1, 512:1024], ym.t[:])

        def tail(x, pr, t, nt, XTd, final_g=None):
            for half in range(2):
                K.v("dve", lambda e, half=half: e.tensor_tensor(
                    out=x.t[:, half * 512:(half + 1) * 512], in0=x.t[:, half * 512:(half + 1) * 512],
                    in1=pr[half].t[:, :], op=ALU.add), [x, pr[half]], [x])
            if final_g is None:
                K.st(x, X1[t * 128:(t + 1) * 128, :], x.t[:])
                nt.run(x, x.t[:], t, XTd)
            else:
                ss = nt.stats(x, x.t[:], 1024)
                K.v("dve", lambda e, ss=ss: e.scalar_tensor_tensor(out=x.t[:], in0=x.t[:], scalar=ss.t[:, 3:4],
                                                                   in1=final_g.t[:], op0=ALU.mult, op1=ALU.mult),
                    [x, ss, final_g], [x])
                K.st(x, y_out[t * 128:(t + 1) * 128, :], x.t[:])

        def project_tm(src_tile, src_ap_k, wt, w_ap_k, nk):
            pr = [K.pgen.next(), K.pgen.next()]
            for half in range(2):
                for k in range(nk):
                    K.mm(pr[half], pr[half].t[:, :], src_tile, src_ap_k(k), wt, w_ap_k(k, half), start=(k == 0),
                         stop=(k == nk - 1))
            return pr

        K.begin_sweep()
        nt = NormT()
        nt.load_gain(W['ffn_norm'][0])
        w_o = K.tile([128, 8, 1024], BF16, dma=True)
        K.ld(w_o, w_o.t[:], wb_out.rearrange("(k p) n -> p k n", p=128))
        xr = K.ring(2, [128, 1024], F32, dma=True)
        yr = K.ring(2, [128, 1024], BF16, dma=True)
        yTr = K.ring(2, [128, 8, 128], BF16)
        for t in range(NT):
            x = xr.next()
            K.ld(x, x.t[:], x_in[t * 128:(t + 1) * 128, :])
            y = yr.next()
            K.ld(y, y.t[:], YY[t * 128:(t + 1) * 128, :])
            pb = K.pgen.next()
            pbv = pb.t[:].bitcast(BF16).rearrange("p (c t) -> p c t", c=8)
            for c in range(8):
                K.tr(pb, pbv[:, c, :], y, y.t[:, c * 128:(c + 1) * 128], ident, ident.t[:])
            yT = yTr.next()
            K.act(yT, yT.t[:], [pb], pbv, AF.Copy)
            pr = project_tm(yT, lambda k: yT.t[:, k, :], w_o, lambda k, half: w_o.t[:, k, half * 512:(half + 1) * 512], 8)
            tail(x, pr, t, nt, XT[1])

        def ffn_sweep(l, XTs, XTd, gain_next, final):
            K.begin_sweep()
            wup = K.tile([128, 8, 5632], BF16, dma=True)
            for k in range(8):
                K.ld(wup, wup.t[:, k, :], wb_up[l, k * 128:(k + 1) * 128, :])
            wdn = K.tile([128, NFC, 1024], BF16, dma=True)
            K.ld(wdn, wdn.t[:], wb_dn[l].rearrange("(i p) n -> p i n", p=128))
            nt = NormT()
            fg = None
            if final:
                fg = nt.g
            nt.load_gain(gain_next)
            win = K.tile([128, 8, 514], BF16, dma=True)
            hT = K.tile([128, NFC, 512], BF16)
            cgr = K.ring(2, [128, 512], F32)
            cvr = K.ring(2, [128, 512], F32)
            sgr = K.ring(2, [128, 512], BF16)
            xr = K.ring(2, [128, 1024], F32, dma=True)
            K.pgen = Ring(K.banks[0:6])
            cw = cw_f[l]
            hs_i = 0
            for b in range(NB):
                K.ld(win, win.t[:], XTs[:, :, 512 * b:512 * b + 514].rearrange("c p t -> p c t"))
                for i in range(NFC):
                    outs = []
                    for (col0, ci, rr) in ((i * 128, i, cgr), (DFF + i * 128, NFC + i, cvr)):
                        pm = K.pgen.next()
                        hsl = 0
                        halo = K.banks[6 + (hs_i % 2)]
                        hs_i += 1
                        for k in range(8):
                            K.mm(pm, pm.t[:, :], wup, wup.t[:, k, col0:col0 + 128], win, win.t[:, k, 1:513],
                                 start=(k == 0), stop=(k == 7))
                        for k in range(8):
                            K.mm(halo, halo.t[:, hsl:hsl + 2], wup, wup.t[:, k, col0:col0 + 128], win,
                                 win.t[:, k, 0:514:513], start=(k == 0), stop=(k == 7))
                        cc = rr.next()
                        conv_chunk(pm, halo, hsl, cw, ci, cc)
                        outs.append(cc)
                    cg, cv = outs
                    sg = sgr.next()
                    K.act(sg, sg.t[:], [cg], cg.t[:], AF.Silu)
                    K.v("dve", lambda e, sg=sg, cv=cv, i=i: e.tensor_tensor(out=hT.t[:, i, :], in0=sg.t[:], in1=cv.t[:],
                                                                            op=ALU.mult), [sg, cv], [hT])
                for tt in range(4):
                    t = b * 4 + tt
                    x = xr.next()
                    K.ld(x, x.t[:], X1[t * 128:(t + 1) * 128, :])
                    pr = project_tm(hT, lambda k: hT.t[:, k, tt * 128:(tt + 1) * 128], wdn,
                                    lambda k, half: wdn.t[:, k, half * 512:(half + 1) * 512], NFC)
                    tail(x, pr, t, nt, XTd, final_g=fg)
            K.pgen = Ring(K.banks[0:6])

        ffn_sweep(0, XT[1], XT[0], W['c_norm'][0], False)

        K.begin_sweep()
        nt = NormT()
        nt.load_gain(W['ffn_norm'][1])
        cwin = K.tile([128, 8, 2048], BF16, dma=True)
        K.ld(cwin, cwin.t[:], wb_cin.rearrange("(k p) n -> p k n", p=128))
        cwo = K.tile([128, 8, 1024], BF16, dma=True)
        K.ld(cwo, cwo.t[:], wb_cout.rearrange("(k p) n -> p k n", p=128))
        vg = K.tile([128, 1024], F32, dma=True)
        bc_load(vg, W['c_v_norm'][0], 1024)
        xTl = K.ring(2, [128, 8, 128], BF16, dma=True)
        xr = K.ring(2, [128, 1024], F32, dma=True)
        ur = K.ring(2, [128, 2048], F32)
        vnr = K.ring(2, [128, 1024], BF16)
        zr = K.ring(2, [128, 1024], BF16)
        zTr = K.ring(2, [128, 8, 128], BF16)
        for t in range(NT):
            xT = xTl.next()
            K.ld(xT, xT.t[:], XT[0][:, :, 1 + t * 128:1 + (t + 1) * 128].rearrange("c p t -> p c t"))
            x = xr.next()
            K.ld(x, x.t[:], X1[t * 128:(t + 1) * 128, :])
            uv = ur.next()
            for j in range(4):
                pu = K.pgen.next()
                for k in range(8):
                    K.mm(pu, pu.t[:, :], xT, xT.t[:, k, :], cwin, cwin.t[:, k, j * 512:(j + 1) * 512], start=(k == 0),
                         stop=(k == 7))
                K.act(uv, uv.t[:, j * 512:(j + 1) * 512], [pu], pu.t[:, :], AF.Gelu)
            ss = nt.stats(uv, uv.t[:, 1024:2048], 1024)
            vn = vnr.next()
            K.v("dve", lambda e, ss=ss, uv=uv, vn=vn: e.scalar_tensor_tensor(
                out=vn.t[:], in0=uv.t[:, 1024:2048], scalar=ss.t[:, 3:4], in1=vg.t[:], op0=ALU.mult, op1=ALU.mult),
                [uv, ss, vg], [vn])
            z = zr.next()
            psv = [K.pgen.next(), K.pgen.next()]
            for g in range(8):
                pp = psv[g // 4]
                K.mm(pp, pp.t[:, (g % 4) * 128:(g % 4 + 1) * 128], wsT, wsT.t[:, g, :], vn, vn.t[:, g * 128:(g + 1) * 128])
            for g in range(8):
                pp = psv[g // 4]
                K.v("dve", lambda e, pp=pp, g=g, uv=uv, z=z: e.scalar_tensor_tensor(
                    out=z.t[:, g * 128:(g + 1) * 128], in0=pp.t[:, (g % 4) * 128:(g % 4 + 1) * 128],
                    scalar=bsT.t[:, g:g + 1], in1=uv.t[:, g * 128:(g + 1) * 128], op0=ALU.add, op1=ALU.mult),
                    [pp, bsT, uv], [z])
            if DEBUG and t == 0:
                P.op("sp", lambda e: e.dma_start(out=DBG_wsT, in_=wsT.t[:, :, :].rearrange("p a b -> p (a b)")), reads=[wsT.b], dsem=wsem)
                P.op("sp", lambda e: e.dma_start(out=DBG_bsT, in_=bsT.t[:]), reads=[bsT.b], dsem=wsem)
                P.op("sp", lambda e, uv=uv: e.dma_start(out=DBG_uv, in_=uv.t[:]), reads=[uv.b], dsem=wsem)
                P.op("sp", lambda e, vn=vn: e.dma_start(out=DBG_vn, in_=vn.t[:]), reads=[vn.b], dsem=wsem)
                P.op("sp", lambda e, z=z: e.dma_start(out=DBG_z, in_=z.t[:]), reads=[z.b], dsem=wsem)
            pb = K.pgen.next()
            pbv = pb.t[:].bitcast(BF16).rearrange("p (c t) -> p c t", c=8)
            for c in range(8):
                K.tr(pb, pbv[:, c, :], z, z.t[:, c * 128:(c + 1) * 128], ident, ident.t[:])
            zT = zTr.next()
            K.act(zT, zT.t[:], [pb], pbv, AF.Copy)
            pr = project_tm(zT, lambda k: zT.t[:, k, :], cwo, lambda k, half: cwo.t[:, k, half * 512:(half + 1) * 512], 8)
            tail(x, pr, t, nt, XT[1])

        ffn_sweep(1, XT[1], XT[0], W['final_norm'], True)

    P.emit()
    nc._dbg_names = dbg_names
    return nc


def make_consts(SM):
    pos = np.arange(SM, dtype=np.float32)
    inv = (1.0 / (np.float32(10000.0) ** (np.arange(0, 32, 2, dtype=np.float32) / np.float32(32)))).astype(np.float32)
    ang = pos[:, None] * inv[None, :]
    return {
        "c_ident": np.eye(128, dtype=np.float32),
        "c_triu": np.triu(np.ones((128, 128), np.float32)),
        "c_tril": np.tril(np.ones((128, 128), np.float32)),
        "c_ones": np.ones((128, 128), np.float32),
        "c_cos": np.cos(ang).astype(np.float32),
        "c_sin": np.sin(ang).astype(np.float32),
    }


_CACHE = {}


def run(jobs_per_core, params, S_list):
    key = tuple(S_list)
    if key not in _CACHE:
        _CACHE[key] = build_program(S_list)
    nc = _CACHE[key]
    consts = make_consts(max(S_list))
    in_maps = []
    for xs in jobs_per_core:
        m = {}
        for j, x in enumerate(xs):
            m["x%d" % j] = np.ascontiguousarray(x, dtype=np.float32)
        for n in PARAMS:
            m[n] = np.ascontiguousarray(params[n], dtype=np.float32)
        m.update(consts)
        in_maps.append(m)
    res = run_bass_kernel_spmd(nc, in_maps, core_ids=list(range(len(jobs_per_core))))
    global LAST_RES
    LAST_RES = res.results
    return [[r["y%d" % j] for j in range(len(S_list))] for r in res.results]


def kernel(**inputs):
    xp = np.asarray(inputs['x_prompt'], dtype=np.float32)
    xs = np.asarray(inputs['x_sample'], dtype=np.float32)
    params = {n: np.asarray(inputs[n], dtype=np.float32) for n in PARAMS}
    S_list = [4096, 4096, 8192]
    zeros = np.zeros((8192, D), np.float32)
    jobs = []
    for c in range(8):
        jobs.append([xs[2 * c], xs[2 * c + 1], xp[c] if c < 2 else zeros])
    outs = run(jobs, params, S_list)
    y_prompt = np.stack([outs[0][2], outs[1][2]], axis=0)
    y_sample = np.stack([outs[c][j] for c in range(8) for j in range(2)], axis=0)
    return (y_prompt, y_sample)
```

```python
import math
import numpy as np
import concourse.bass as bass
import concourse.mybir as mybir
from concourse.bass_utils import run_bass_kernel_spmd

F32 = mybir.dt.float32
BF16 = mybir.dt.bfloat16
ALU = mybir.AluOpType
AF = mybir.ActivationFunctionType
AX = mybir.AxisListType

D = 1024
EPS = 1e-6
DFF = 2816
NFC = 22
SC_ATT = 96 ** -0.5
SC_M = 128 ** -0.5
LNSC_M = math.log(SC_M)

PARAMS = ['ab_norm', 'ab_w_in', 'mla_q_norm', 'mla_w_uq', 'mla_kv_norm', 'mla_w_ukv', 'mlstm_conv_w',
          'mlstm_conv_b', 'mlstm_gate_bias', 'mlstm_out_norm', 'ab_w_out', 'c_norm', 'c_w_in', 'c_v_norm',
          'c_w_spatial', 'c_b_spatial', 'c_w_out', 'ffn_norm', 'ffn_w_up', 'ffn_conv_w', 'ffn_conv_b',
          'ffn_w_down', 'final_norm']
PSHAPES = {
    'ab_norm': [1, 1024], 'ab_w_in': [1, 1024, 2480], 'mla_q_norm': [1, 256], 'mla_w_uq': [1, 256, 768],
    'mla_kv_norm': [1, 128], 'mla_w_ukv': [1, 128, 1024], 'mlstm_conv_w': [1, 3, 1024], 'mlstm_conv_b': [1, 1024],
    'mlstm_gate_bias': [1, 16], 'mlstm_out_norm': [1, 512], 'ab_w_out': [1, 1024, 1024], 'c_norm': [1, 1024],
    'c_w_in': [1, 1024, 2048], 'c_v_norm': [1, 1024], 'c_w_spatial': [1, 8, 128, 128], 'c_b_spatial': [1, 8, 128],
    'c_w_out': [1, 1024, 1024], 'ffn_norm': [2, 1024], 'ffn_w_up': [2, 1024, 5632], 'ffn_conv_w': [2, 3, 5632],
    'ffn_conv_b': [2, 5632], 'ffn_w_down': [2, 2816, 1024], 'final_norm': [1024],
}


class Buf:
    __slots__ = ("name", "w", "r", "excl")

    def __init__(self, name, excl=False):
        self.name = name
        self.w = None
        self.r = []
        self.excl = excl


class Prog:
    ENGS = ("pe", "act", "dve", "pool", "sp")

    def __init__(self, nc, n_dma_sems=80):
        self.nc = nc
        self.ops = {e: [] for e in self.ENGS}
        self.count = {}
        self.seen = {e: {} for e in self.ENGS}
        self.pending = {e: [] for e in self.ENGS}
        self.dma_sems = ["d%d" % i for i in range(n_dma_sems)]
        self.nops = 0
        self.disabled = False
        self.sweep_no = 0
        import os
        self.maxops = int(os.environ.get("K_OPS", "1000000000"))

    def op(self, eng, fn, reads=(), writes=(), dsem=None):
        if self.disabled or self.nops >= self.maxops:
            return None
        deps = []
        for b in reads:
            if b.w is not None:
                deps.append(b.w)
            if b.excl:
                for t in b.r:
                    if t[0] != eng:
                        deps.append(t)
        for b in writes:
            if b.w is not None:
                deps.append(b.w)
            deps.extend(b.r)
        if dsem is None:
            key, inc = eng, 1
        else:
            key, inc = dsem, 16
        self.count[key] = self.count.get(key, 0) + inc
        tok = (key, self.count[key])
        seen = self.seen[eng]
        waits = []
        if self.pending[eng]:
            for k, v in self.pending[eng]:
                if seen.get(k, 0) < v:
                    seen[k] = v
                    waits.append((k, v))
            self.pending[eng] = []
        for k, v in deps:
            if k == "pe" and eng == "pe" and dsem is None:
                continue
            if seen.get(k, 0) >= v:
                continue
            seen[k] = v
            waits.append((k, v))
        self.ops[eng].append((fn, waits, key, inc))
        for b in reads:
            b.r.append(tok)
        for b in writes:
            b.w = tok
            b.r = []
        self.nops += 1
        return tok

    def barrier(self):
        snap = [(k, v) for k, v in self.count.items()]
        for e in self.ENGS:
            self.pending[e] = list(snap)

    def emit(self):
        from contextlib import ExitStack
        nc = self.nc
        self.barrier()
        with ExitStack() as st:
            sems = {}
            for k in list(self.ENGS) + self.dma_sems:
                sems[k] = st.enter_context(nc.semaphore("s_" + k))
            block = st.enter_context(nc.Block())

            def mk(e):
                def section(eng):
                    for fn, waits, key, inc in self.ops[e]:
                        for k, v in waits:
                            eng.wait_ge(sems[k], v)
                        ins = fn(eng)
                        ins.then_inc(sems[key], inc)
                    for k, v in self.pending[e]:
                        if self.seen[e].get(k, 0) < v:
                            eng.wait_ge(sems[k], v)
                return section

            block.tensor(mk("pe"))
            block.scalar(mk("act"))
            block.vector(mk("dve"))
            block.gpsimd(mk("pool"))
            block.sync(mk("sp"))


class TL:
    __slots__ = ("t", "b", "sem")

    def __init__(self, t, name, sem=None):
        self.t = t
        self.b = Buf(name)
        self.sem = sem


class Ring:
    def __init__(self, tiles):
        self.tiles = tiles
        self.i = 0

    def next(self):
        t = self.tiles[self.i % len(self.tiles)]
        self.i += 1
        return t


def dsize(dt):
    return 4 if dt == F32 else 2


class Ctx:
    def __init__(self, nc, P):
        self.nc = nc
        self.P = P
        self.uid = 0
        self.pers_off = 16512
        self.off = 16512
        self.semi = 0
        self.banks = [TL(nc.alloc_psum_tensor("pb%d" % i, [128, 512], F32), "pb%d" % i) for i in range(8)]
        for b in self.banks:
            b.b.excl = True
        self.pgen = Ring(self.banks[0:6])
        self.pacc = Ring(self.banks[6:8])

    def _alloc(self, shape, dt, dma, off):
        self.uid += 1
        name = "t%d" % self.uid
        t = self.nc.alloc_sbuf_tensor_at(name, list(shape), dt, offset=off)
        sem = None
        if dma:
            sem = self.P.dma_sems[self.semi % (len(self.P.dma_sems) - 1)]
            self.semi += 1
        return TL(t, name, sem)

    @staticmethod
    def _bytes(shape, dt):
        n = 1
        for s in shape[1:]:
            n *= s
        return (n * dsize(dt) + 63) // 64 * 64

    def pers(self, shape, dt, dma=False):
        tl = self._alloc(shape, dt, dma, self.pers_off)
        self.pers_off += self._bytes(shape, dt)
        return tl

    def begin_sweep(self):
        import os
        self.P.sweep_no += 1
        lim = int(os.environ.get("K_LIMIT", "100000"))
        self.P.disabled = self.P.sweep_no > lim
        if os.environ.get("K_VERBOSE"):
            print("sweep", self.P.sweep_no, "ops so far", self.P.nops, flush=True)
        self.P.barrier()
        self.off = self.pers_off

    def tile(self, shape, dt, dma=False):
        tl = self._alloc(shape, dt, dma, self.off)
        self.off += self._bytes(shape, dt)
        assert self.off <= 229376, ("SBUF overflow", self.off)
        return tl

    def ring(self, n, shape, dt, dma=False):
        return Ring([self.tile(shape, dt, dma) for _ in range(n)])

    def ld(self, dst, dst_ap, src_ap, eng="sp", slow=False):
        if slow:
            fn = lambda e: e.dma_start(out=dst_ap, in_=src_ap, allow_slow_non_contiguous=True)
        else:
            fn = lambda e: e.dma_start(out=dst_ap, in_=src_ap)
        self.P.op(eng, fn, writes=[dst.b], dsem=dst.sem)

    def st(self, src, dst_ap, src_ap, eng="pool", slow=False):
        if slow:
            fn = lambda e: e.dma_start(out=dst_ap, in_=src_ap, allow_slow_non_contiguous=True)
        else:
            fn = lambda e: e.dma_start(out=dst_ap, in_=src_ap)
        self.P.op(eng, fn, reads=[src.b], dsem=src.sem)

    def mm(self, out, out_ap, lt, lhsT, rt, rhs, start=True, stop=True, skip=False):
        if skip:
            fn = lambda e: e.matmul(out_ap, lhsT=lhsT, rhs=rhs, start=start, stop=stop, skip_group_check=True)
        else:
            fn = lambda e: e.matmul(out_ap, lhsT=lhsT, rhs=rhs, start=start, stop=stop)
        self.P.op("pe", fn, reads=[lt.b, rt.b], writes=[out.b])

    def tr(self, out, out_ap, it, in_ap, idt, id_ap):
        self.P.op("pe", lambda e: e.transpose(out=out_ap, in_=in_ap, identity=id_ap),
                  reads=[it.b, idt.b], writes=[out.b])

    def act(self, out, out_ap, ins, in_ap, func, scale=None, bias=None, accum=None, extra_w=()):
        kw = {}
        if scale is not None:
            kw["scale"] = scale
        if bias is not None:
            kw["bias"] = bias
        if accum is not None:
            kw["accum_out"] = accum
        self.P.op("act", lambda e: e.activation(out=out_ap, in_=in_ap, func=func, **kw),
                  reads=[t.b for t in ins], writes=[out.b] + [t.b for t in extra_w])

    def v(self, eng, fn, ins, outs):
        self.P.op(eng, fn, reads=[t.b for t in ins], writes=[t.b for t in outs])


def build_program(S_list):
    nc = bass.Bass("TRN2", target_bir_lowering=False)
    P = Prog(nc)
    K = Ctx(nc, P)
    SM = max(S_list)
    NJ = len(S_list)

    import os
    DEBUG = bool(os.environ.get("K_DEBUG"))
    dbg_names = []

    def dram(name, shape, dt, kind="Internal"):
        if DEBUG and kind == "Internal" and (not name.startswith("wb_") or os.environ.get("K_DEBUGW")):
            kind = "ExternalOutput"
            dbg_names.append(name)
        return nc.dram_tensor(name, list(shape), dt, kind=kind).ap()

    xin = [dram("x%d" % j, [S_list[j], D], F32, "ExternalInput") for j in range(NJ)]
    yout = [dram("y%d" % j, [S_list[j], D], F32, "ExternalOutput") for j in range(NJ)]
    W = {n: dram(n, PSHAPES[n], F32, "ExternalInput") for n in PARAMS}
    c_ident = dram("c_ident", [128, 128], F32, "ExternalInput")
    c_triu = dram("c_triu", [128, 128], F32, "ExternalInput")
    c_tril = dram("c_tril", [128, 128], F32, "ExternalInput")
    c_ones = dram("c_ones", [128, 128], F32, "ExternalInput")
    c_cos = dram("c_cos", [SM, 16], F32, "ExternalInput")
    c_sin = dram("c_sin", [SM, 16], F32, "ExternalInput")

    wb_in = dram("wb_in", [1024, 2480], BF16)
    wb_uq = dram("wb_uq", [256, 768], BF16)
    wb_ukv = dram("wb_ukv", [128, 1024], BF16)
    wb_out = dram("wb_out", [1024, 1024], BF16)
    wb_cin = dram("wb_cin", [1024, 2048], BF16)
    wb_cout = dram("wb_cout", [1024, 1024], BF16)
    wb_up = dram("wb_up", [2, 1024, 5632], BF16)
    wb_dn = dram("wb_dn", [2, 2816, 1024], BF16)
    XT = [dram("XT%d" % i, [8, 128, SM + 2], BF16) for i in range(2)]
    X1 = dram("X1", [SM, D], F32)
    QT = dram("QT", [8, 96, SM], BF16)
    KT = dram("KT", [8, 96, SM], BF16)
    VV = dram("VV", [SM, 8, 65], BF16)
    QKT = dram("QKT", [8, 128, SM], BF16)
    VM = dram("VM", [SM, 4, 129], BF16)
    SO = dram("SO", [SM, 512], BF16)
    GG = dram("GG", [SM, 16], F32)
    HF = dram("HF", [SM, 512], F32)
    YY = dram("YY", [SM, D], BF16)
    if DEBUG:
        DBG_dm = dram("DBG_dm", [128, 128], F32)
        DBG_pt = dram("DBG_pt", [128, 128], BF16)
        DBG_s = dram("DBG_s", [128, 24], F32)
        DBG_tot = dram("DBG_tot", [128, 129], F32)
        DBG_kk = dram("DBG_kk", [128, 128], BF16)
        DBG_wsT = dram("DBG_wsT", [128, 1024], BF16)
        DBG_bsT = dram("DBG_bsT", [128, 8], F32)
        DBG_uv = dram("DBG_uv", [128, 2048], F32)
        DBG_vn = dram("DBG_vn", [128, 1024], BF16)
        DBG_z = dram("DBG_z", [128, 1024], BF16)

    wsem = P.dma_sems[-1]

    def cast(dst, src, rows):
        for r0 in range(0, rows, 128):
            r1 = min(rows, r0 + 128)
            P.op("pool", lambda e, r0=r0, r1=r1: e.dma_start(out=dst[r0:r1, :], in_=src[r0:r1, :]),
                 writes=[], dsem=wsem)

    cast(wb_in, W['ab_w_in'][0], 1024)
    cast(wb_uq, W['mla_w_uq'][0], 256)
    cast(wb_ukv, W['mla_w_ukv'][0], 128)
    cast(wb_out, W['ab_w_out'][0], 1024)
    cast(wb_cin, W['c_w_in'][0], 1024)
    cast(wb_cout, W['c_w_out'][0], 1024)
    for l in range(2):
        cast(wb_up[l], W['ffn_w_up'][l], 1024)
        cast(wb_dn[l], W['ffn_w_down'][l], 2816)

    identf = K.pers([128, 128], F32, dma=True)
    ident = K.pers([128, 128], BF16)
    triu = K.pers([128, 128], F32, dma=True)
    tril = K.pers([128, 128], F32, dma=True)
    ones = K.pers([128, 128], F32, dma=True)
    cw_m = K.pers([128, 4, 8], F32)
    cw_f = [K.pers([128, 4, 44], F32) for _ in range(2)]
    bsT = K.pers([128, 8], F32)
    wsT = K.pers([128, 8, 128], BF16)
    zero = K.pers([128, 16], BF16, dma=True)
    K.begin_sweep()
    K.ld(identf, identf.t[:], c_ident)
    K.ld(triu, triu.t[:], c_triu)
    K.ld(tril, tril.t[:], c_tril)
    K.ld(ones, ones.t[:], c_ones)
    K.v("dve", lambda e: e.tensor_copy(out=ident.t[:], in_=identf.t[:]), [identf], [ident])
    K.v("dve", lambda e: e.memset(zero.t[:], 0.0), [], [zero])

    def load_T(dst, dst_view, raws, nrow):
        raw = K.tile([nrow, len(raws), 128], F32, dma=True)
        for j, src in enumerate(raws):
            K.ld(raw, raw.t[:, j, :], src)
        for j in range(len(raws)):
            pb = K.pgen.next()
            K.tr(pb, pb.t[:, 0:nrow], raw, raw.t[:, j, :], identf, identf.t[0:nrow, 0:nrow])
            K.v("dve", lambda e, j=j, pb=pb: e.tensor_copy(out=dst_view(j), in_=pb.t[:, 0:nrow]), [pb], [dst])

    mcw = W['mlstm_conv_w'][0]
    load_T(cw_m, lambda j: cw_m.t[:, j, :],
           [mcw[j].rearrange("(c p) -> c p", p=128) for j in range(3)] +
           [W['mlstm_conv_b'][0].rearrange("(c p) -> c p", p=128)], 8)
    for l in range(2):
        load_T(cw_f[l], lambda j, l=l: cw_f[l].t[:, j, :],
               [W['ffn_conv_w'][l, j].rearrange("(c p) -> c p", p=128) for j in range(3)] +
               [W['ffn_conv_b'][l].rearrange("(c p) -> c p", p=128)], 44)
    load_T(bsT, lambda j: bsT.t[:, :], [W['c_b_spatial'][0]], 8)
    wsr = K.tile([128, 8, 128], F32, dma=True)
    wsrb = K.tile([128, 8, 128], BF16)
    K.ld(wsr, wsr.t[:], W['c_w_spatial'][0].rearrange("g t s -> t g s"))
    K.v("dve", lambda e: e.tensor_copy(out=wsrb.t[:], in_=wsr.t[:]), [wsr], [wsrb])
    pb = K.pgen.next()
    pbv = pb.t[:].bitcast(BF16).rearrange("p (c t) -> p c t", c=8)
    for g in range(8):
        K.tr(pb, pbv[:, g, :], wsrb, wsrb.t[:, g, :], ident, ident.t[:])
    K.v("dve", lambda e, pbv=pbv: e.tensor_copy(out=wsT.t[:], in_=pbv), [pb], [wsT])

    def bc_load(dst, src1d, n):
        K.ld(dst, dst.t[:, 0:n], src1d.partition_broadcast(128))

    class NormT:
        def __init__(self):
            self.junk = K.tile([128, 1024], BF16)
            self.ss = K.ring(2, [128, 4], F32)
            self.xn = K.ring(2, [128, 1024], BF16)
            self.xnT = K.ring(2, [128, 8, 128], BF16, dma=True)
            self.g = K.tile([128, 1024], F32, dma=True)

        def load_gain(self, src1d):
            bc_load(self.g, src1d, 1024)

        def stats(self, x, x_ap, n):
            ss = self.ss.next()
            junk = self.junk
            K.v("dve", lambda e: e.memset(ss.t[:, 0:1], 0.0), [], [ss])
            K.act(junk, junk.t[:, 0:n], [x, ss], x_ap, AF.Square, accum=ss.t[:, 0:1], extra_w=[ss])
            K.v("dve", lambda e: e.tensor_scalar(out=ss.t[:, 1:2], in0=ss.t[:, 0:1], scalar1=1.0 / n, scalar2=EPS,
                                                 op0=ALU.mult, op1=ALU.add), [ss], [ss])
            K.act(ss, ss.t[:, 2:3], [ss], ss.t[:, 1:2], AF.Sqrt)
            K.v("dve", lambda e: e.reciprocal(out=ss.t[:, 3:4], in_=ss.t[:, 2:3]), [ss], [ss])
            return ss

        def run(self, x, x_ap, t, XTd):
            ss = self.stats(x, x_ap, 1024)
            xn = self.xn.next()
            g = self.g
            K.v("dve", lambda e: e.scalar_tensor_tensor(out=xn.t[:], in0=x_ap, scalar=ss.t[:, 3:4], in1=g.t[:],
                                                        op0=ALU.mult, op1=ALU.mult), [x, ss, g], [xn])
            pb = K.pgen.next()
            pbv = pb.t[:].bitcast(BF16).rearrange("p (c t) -> p c t", c=8)
            for c in range(8):
                K.tr(pb, pbv[:, c, :], xn, xn.t[:, c * 128:(c + 1) * 128], ident, ident.t[:])
            xnT = self.xnT.next()
            K.act(xnT, xnT.t[:], [pb], pbv, AF.Copy)
            K.st(xnT, XTd[:, :, 1 + t * 128:1 + (t + 1) * 128].rearrange("c p t -> p c t"), xnT.t[:])

    def conv_chunk(pm, ph, hsl, cw, ci, dst):
        w0 = cw.t[:, 0, ci:ci + 1]
        w1 = cw.t[:, 1, ci:ci + 1]
        w2 = cw.t[:, 2, ci:ci + 1]
        bb = cw.t[:, 3, ci:ci + 1]
        K.act(dst, dst.t[:], [pm, cw], pm.t[:], AF.Identity, scale=w1, bias=bb)
        K.v("dve", lambda e: e.scalar_tensor_tensor(out=dst.t[:, 1:512], in0=pm.t[:, 0:511], scalar=w0,
                                                    in1=dst.t[:, 1:512], op0=ALU.mult, op1=ALU.add),
            [pm, cw, dst], [dst])
        K.v("dve", lambda e: e.scalar_tensor_tensor(out=dst.t[:, 0:511], in0=pm.t[:, 1:512], scalar=w2,
                                                    in1=dst.t[:, 0:511], op0=ALU.mult, op1=ALU.add),
            [pm, cw, dst], [dst])
        K.v("dve", lambda e: e.scalar_tensor_tensor(out=dst.t[:, 0:1], in0=ph.t[:, hsl:hsl + 1], scalar=w0,
                                                    in1=dst.t[:, 0:1], op0=ALU.mult, op1=ALU.add),
            [ph, cw, dst], [dst])
        K.v("dve", lambda e: e.scalar_tensor_tensor(out=dst.t[:, 511:512], in0=ph.t[:, hsl + 1:hsl + 2], scalar=w2,
                                                    in1=dst.t[:, 511:512], op0=ALU.mult, op1=ALU.add),
            [ph, cw, dst], [dst])

    for jb in range(NJ):
        S = S_list[jb]
        NT = S // 128
        NB = S // 512
        x_in = xin[jb]
        y_out = yout[jb]

        K.begin_sweep()
        nt = NormT()
        nt.load_gain(W['ab_norm'][0])
        for i in range(2):
            K.st(zero, XT[i][:, :, 0:1].rearrange("c p t -> p c t"), zero.t[:, 0:8].rearrange("p (c t) -> p c t", t=1),
                 slow=True)
            K.st(zero, XT[i][:, :, S + 1:S + 2].rearrange("c p t -> p c t"),
                 zero.t[:, 0:8].rearrange("p (c t) -> p c t", t=1), slow=True)
        xr = K.ring(2, [128, 1024], F32, dma=True)
        for t in range(NT):
            x = xr.next()
            K.ld(x, x.t[:], x_in[t * 128:(t + 1) * 128, :])
            nt.run(x, x.t[:], t, XT[0])

        K.begin_sweep()
        w_in = K.tile([128, 8, 2480], BF16, dma=True)
        K.ld(w_in, w_in.t[:], wb_in.rearrange("(k p) n -> p k n", p=128))
        w_uq = K.tile([128, 2, 768], BF16, dma=True)
        K.ld(w_uq, w_uq.t[:], wb_uq.rearrange("(k p) n -> p k n", p=128))
        w_ukv = K.tile([128, 1024], BF16, dma=True)
        K.ld(w_ukv, w_ukv.t[:], wb_ukv)
        qg = K.tile([128, 256], F32, dma=True)
        bc_load(qg, W['mla_q_norm'][0], 256)
        kvg = K.tile([128, 128], F32, dma=True)
        bc_load(kvg, W['mla_kv_norm'][0], 128)
        gbias = K.tile([128, 16], F32, dma=True)
        bc_load(gbias, W['mlstm_gate_bias'][0], 16)
        winr = K.ring(2, [128, 8, 514], BF16, dma=True)
        cgr = K.ring(2, [128, 512], F32)
        qkr = K.ring(2, [128, 512], BF16, dma=True)
        junk = K.tile([128, 256], BF16)
        ssr = K.ring(2, [128, 8], F32)
        cqn = K.ring(2, [128, 384], BF16)
        cTr = K.ring(2, [128, 3, 128], BF16)
        cosr = K.ring(2, [128, 32], F32, dma=True)
        krr = K.ring(2, [128, 32], BF16)
        tmp = K.ring(2, [128, 8, 64], F32)
        qst = K.ring(2, [128, 8, 96], BF16)
        qfr = K.ring(2, [128, 768], F32)
        kst = K.ring(2, [128, 8, 96], BF16)
        vst = K.ring(2, [128, 8, 65], BF16, dma=True)
        qTs = K.ring(2, [96, 8, 128], BF16, dma=True)
        kTs = K.ring(2, [96, 8, 128], BF16, dma=True)
        vmst = K.ring(2, [128, 4, 129], BF16, dma=True)
        sor = K.ring(2, [128, 512], BF16, dma=True)
        gtr = K.ring(2, [128, 16], F32, dma=True)
        ge = K.ring(2, [128, 2, 4], F32)
        for tl in vst.tiles + vmst.tiles:
            K.v("dve", lambda e, tl=tl: e.memset(tl.t[:], 1.0), [], [tl])
        K.pgen = Ring(K.banks[0:6])
        for b in range(NB):
            win = winr.next()
            K.ld(win, win.t[:], XT[0][:, :, 512 * b:512 * b + 514].rearrange("c p t -> p c t"))
            for c in range(8):
                col0 = 416 + c * 128
                pm = K.pgen.next()
                hsl = 0
                halo = K.banks[6 + (c % 2)]
                for k in range(8):
                    K.mm(pm, pm.t[:, :], w_in, w_in.t[:, k, col0:col0 + 128], win, win.t[:, k, 1:513],
                         start=(k == 0), stop=(k == 7))
                for k in range(8):
                    K.mm(halo, halo.t[:, hsl:hsl + 2], w_in, w_in.t[:, k, col0:col0 + 128], win,
                         win.t[:, k, 0:514:513], start=(k == 0), stop=(k == 7))
                cg = cgr.next()
                conv_chunk(pm, halo, hsl, cw_m, c, cg)
                qk = qkr.next()
                K.act(qk, qk.t[:], [cg], cg.t[:], AF.Silu)
                K.st(qk, QKT[c, :, 512 * b:512 * b + 512], qk.t[:])
            for tt in range(4):
                t = b * 4 + tt
                lo = 1 + tt * 128
                pa = K.pgen.next()
                pv = K.pgen.next()
                po = K.pgen.next()
                for k in range(8):
                    K.mm(pa, pa.t[:, 0:416], win, win.t[:, k, lo:lo + 128], w_in, w_in.t[:, k, 0:416],
                         start=(k == 0), stop=(k == 7))
                for k in range(8):
                    K.mm(pa, pa.t[:, 416:432], win, win.t[:, k, lo:lo + 128], w_in, w_in.t[:, k, 2464:2480],
                         start=(k == 0), stop=(k == 7))
                for k in range(8):
                    K.mm(pv, pv.t[:, :], win, win.t[:, k, lo:lo + 128], w_in, w_in.t[:, k, 1440:1952],
                         start=(k == 0), stop=(k == 7))
                for k in range(8):
                    K.mm(po, po.t[:, :], win, win.t[:, k, lo:lo + 128], w_in, w_in.t[:, k, 1952:2464],
                         start=(k == 0), stop=(k == 7))
                vm = vmst.next()
                K.act(vm, vm.t[:, :, 0:128], [pv], pv.t[:, :].rearrange("p (h d) -> p h d", d=128), AF.Copy)
                K.st(vm, VM[t * 128:(t + 1) * 128, :, :], vm.t[:])
                so = sor.next()
                K.act(so, so.t[:], [po], po.t[:, :], AF.Sigmoid)
                K.st(so, SO[t * 128:(t + 1) * 128, :], so.t[:])
                gt = gtr.next()
                g2 = ge.next()
                K.v("dve", lambda e, gt=gt, pa=pa: e.tensor_tensor(out=gt.t[:], in0=pa.t[:, 416:432], in1=gbias.t[:],
                                                                   op=ALU.add), [pa, gbias], [gt])
                gtv = gt.t[:, :].rearrange("p (a b) -> p a b", b=8)[:, :, 4:8]
                K.act(g2, g2.t[:], [gt], gtv, AF.Exp, scale=-1.0)
                K.v("dve", lambda e, g2=g2: e.tensor_scalar_add(out=g2.t[:], in0=g2.t[:], scalar1=1.0), [g2], [g2])
                K.act(g2, g2.t[:], [g2], g2.t[:], AF.Ln)
                K.v("dve", lambda e, g2=g2, gtv=gtv: e.tensor_scalar_mul(out=gtv, in0=g2.t[:], scalar1=-1.0), [g2], [gt])
                K.st(gt, GG[t * 128:(t + 1) * 128, :], gt.t[:])
                ss = ssr.next()
                K.v("dve", lambda e, ss=ss: e.memset(ss.t[:, 0:2], 0.0), [], [ss])
                K.act(junk, junk.t[:, 0:256], [pa, ss], pa.t[:, 0:256], AF.Square, accum=ss.t[:, 0:1], extra_w=[ss])
                K.act(junk, junk.t[:, 0:128], [pa, ss], pa.t[:, 256:384], AF.Square, accum=ss.t[:, 1:2], extra_w=[ss])
                K.v("dve", lambda e, ss=ss: e.tensor_scalar(out=ss.t[:, 2:3], in0=ss.t[:, 0:1], scalar1=1.0 / 256,
                                                            scalar2=EPS, op0=ALU.mult, op1=ALU.add), [ss], [ss])
                K.v("dve", lambda e, ss=ss: e.tensor_scalar(out=ss.t[:, 3:4], in0=ss.t[:, 1:2], scalar1=1.0 / 128,
                                                            scalar2=EPS, op0=ALU.mult, op1=ALU.add), [ss], [ss])
                K.act(ss, ss.t[:, 4:6], [ss], ss.t[:, 2:4], AF.Sqrt)
                K.v("dve", lambda e, ss=ss: e.reciprocal(out=ss.t[:, 6:8], in_=ss.t[:, 4:6]), [ss], [ss])
                cq = cqn.next()
                K.v("dve", lambda e, ss=ss, cq=cq, pa=pa: e.scalar_tensor_tensor(
                    out=cq.t[:, 0:256], in0=pa.t[:, 0:256], scalar=ss.t[:, 6:7], in1=qg.t[:], op0=ALU.mult,
                    op1=ALU.mult), [pa, ss, qg], [cq])
                K.v("dve", lambda e, ss=ss, cq=cq, pa=pa: e.scalar_tensor_tensor(
                    out=cq.t[:, 256:384], in0=pa.t[:, 256:384], scalar=ss.t[:, 7:8], in1=kvg.t[:], op0=ALU.mult,
                    op1=ALU.mult), [pa, ss, kvg], [cq])
                cs = cosr.next()
                K.ld(cs, cs.t[:, 0:16], c_cos[t * 128:(t + 1) * 128, :])
                K.ld(cs, cs.t[:, 16:32], c_sin[t * 128:(t + 1) * 128, :])
                kr = krr.next()
                tm = tmp.next()

                def rope(dst1, dst2, x1, x2, cosb, sinb, ta, tb, ins, outs, tmt):
                    K.v("dve", lambda e: e.tensor_tensor(out=ta, in0=x1, in1=cosb, op=ALU.mult), ins, [tmt])
                    K.v("dve", lambda e: e.tensor_tensor(out=tb, in0=x2, in1=sinb, op=ALU.mult), ins, [tmt])
                    K.v("dve", lambda e: e.tensor_tensor(out=dst1, in0=ta, in1=tb, op=ALU.subtract), [tmt], outs)
                    K.v("dve", lambda e: e.tensor_tensor(out=ta, in0=x2, in1=cosb, op=ALU.mult), ins + outs, [tmt])
                    K.v("dve", lambda e: e.tensor_tensor(out=tb, in0=x1, in1=sinb, op=ALU.mult), ins, [tmt])
                    K.v("dve", lambda e: e.tensor_tensor(out=dst2, in0=ta, in1=tb, op=ALU.add), [tmt], outs)

                rope(kr.t[:, 0:16], kr.t[:, 16:32], pa.t[:, 384:400], pa.t[:, 400:416], cs.t[:, 0:16], cs.t[:, 16:32],
                     tm.t[:, 0, 0:16], tm.t[:, 0, 16:32], [pa, cs], [kr], tm)
                pb = K.pgen.next()
                pbv = pb.t[:].bitcast(BF16).rearrange("p (c t) -> p c t", c=8)
                for c in range(3):
                    K.tr(pb, pbv[:, c, :], cq, cq.t[:, c * 128:(c + 1) * 128], ident, ident.t[:])
                cT = cTr.next()
                K.act(cT, cT.t[:], [pb], pbv[:, 0:3, :], AF.Copy)
                pq1 = K.pgen.next()
                pq2 = K.pgen.next()
                for k in range(2):
                    K.mm(pq1, pq1.t[:, 0:480], cT, cT.t[:, k, :], w_uq, w_uq.t[:, k, 0:480], start=(k == 0), stop=(k == 1))
                for k in range(2):
                    K.mm(pq2, pq2.t[:, 0:288], cT, cT.t[:, k, :], w_uq, w_uq.t[:, k, 480:768], start=(k == 0), stop=(k == 1))
                qs = qst.next()
                qf = qfr.next()
                K.act(qf, qf.t[:, 0:480], [pq1], pq1.t[:, 0:480], AF.Copy)
                K.act(qf, qf.t[:, 480:768], [pq2], pq2.t[:, 0:288], AF.Copy)
                qfv = qf.t[:, :].rearrange("p (h d) -> p h d", d=96)
                K.v("dve", lambda e, qs=qs, qfv=qfv: e.tensor_copy(out=qs.t[:, :, 0:64], in_=qfv[:, :, 0:64]), [qf], [qs])
                cosb = cs.t[:, 0:16].unsqueeze(1).to_broadcast([128, 8, 16])
                sinb = cs.t[:, 16:32].unsqueeze(1).to_broadcast([128, 8, 16])
                rope(qs.t[:, :, 64:80], qs.t[:, :, 80:96], qfv[:, :, 64:80], qfv[:, :, 80:96],
                     cosb, sinb, tm.t[:, :, 0:16], tm.t[:, :, 16:32], [qf, cs], [qs], tm)
                pk1 = K.pgen.next()
                pk2 = K.pgen.next()
                K.mm(pk1, pk1.t[:, :], cT, cT.t[:, 2, :], w_ukv, w_ukv.t[:, 0:512])
                K.mm(pk2, pk2.t[:, :], cT, cT.t[:, 2, :], w_ukv, w_ukv.t[:, 512:1024])
                ks = kst.next()
                vs = vst.next()
                for (pk, h0) in ((pk1, 0), (pk2, 4)):
                    pkv = pk.t[:, :].rearrange("p (h d) -> p h d", d=128)
                    K.act(ks, ks.t[:, h0:h0 + 4, 0:64], [pk], pkv[:, :, 0:64], AF.Copy)
                    K.v("dve", lambda e, vs=vs, pkv=pkv, h0=h0: e.tensor_copy(out=vs.t[:, h0:h0 + 4, 0:64],
                                                                             in_=pkv[:, :, 64:128]), [pk], [vs])
                K.v("dve", lambda e, ks=ks, kr=kr: e.tensor_copy(
                    out=ks.t[:, :, 64:96], in_=kr.t[:, :].unsqueeze(1).to_broadcast([128, 8, 32])), [kr], [ks])
                K.st(vs, VV[t * 128:(t + 1) * 128, :, :], vs.t[:])
                for (src, dstr, dd) in ((qs, qTs, QT), (ks, kTs, KT)):
                    pb = K.pgen.next()
                    pbv = pb.t[:].bitcast(BF16).rearrange("p (c t) -> p c t", c=8)
                    for h in range(8):
                        K.tr(pb, pbv[0:96, h, :], src, src.t[:, h, :], ident, ident.t[:])
                    dT = dstr.next()
                    K.act(dT, dT.t[:], [pb], pbv[0:96, :, :], AF.Copy)
                    K.st(dT, dd[:, :, t * 128:(t + 1) * 128].rearrange("h p t -> p h t"), dT.t[:])
        K.pgen = Ring(K.banks[0:6])

        K.begin_sweep()
        kth = K.ring(2, [96, S], BF16, dma=True)
        vh = K.ring(2, [128, NT, 65], BF16, dma=True)
        qbr = K.ring(2, [96, 512], BF16, dma=True)
        ptr = K.ring(3, [128, 512], BF16)
        rcr = K.ring(2, [128, 4], F32)
        yar = K.ring(2, [128, 4, 64], BF16, dma=True)
        for h in range(8):
            kt = kth.next()
            K.ld(kt, kt.t[:], KT[h, :, 0:S])
            vv = vh.next()
            K.ld(vv, vv.t[:], VV[0:S, h, :].rearrange("(n p) d -> p n d", p=128))
            for qb in range(NB):
                qt = qbr.next()
                K.ld(qt, qt.t[:], QT[h, :, qb * 512:(qb + 1) * 512])
                acc = K.pacc.next()

                def scores(kb, kt=kt, qt=qt):
                    ps = K.pgen.next()
                    K.mm(ps, ps.t[:, :], kt, kt.t[:, kb * 128:(kb + 1) * 128], qt, qt.t[:, :])
                    return ps

                LOOK = 2
                pend = [scores(kb) for kb in range(min(LOOK, NT))]
                for kb in range(NT):
                    ps = pend.pop(0)
                    if kb + LOOK < NT:
                        pend.append(scores(kb + LOOK))
                    pt = ptr.next()
                    K.act(pt, pt.t[:], [ps], ps.t[:, :], AF.Exp, scale=SC_ATT)
                    for j in range(4):
                        K.mm(acc, acc.t[:, j * 65:(j + 1) * 65], pt, pt.t[:, j * 128:(j + 1) * 128], vv, vv.t[:, kb, :],
                             start=(kb == 0 and j == 0), stop=(kb == NT - 1 and j == 3), skip=True)
                accv = acc.t[:, 0:260].rearrange("p (j d) -> p j d", d=65)
                rc = rcr.next()
                K.v("dve", lambda e, rc=rc, accv=accv: e.reciprocal(out=rc.t[:, :].unsqueeze(2), in_=accv[:, :, 64:65]),
                    [acc], [rc])
                ya = yar.next()
                K.v("dve", lambda e, rc=rc, accv=accv, ya=ya: e.tensor_tensor(
                    out=ya.t[:], in0=accv[:, :, 0:64], in1=rc.t[:, :].unsqueeze(2).to_broadcast([128, 4, 64]),
                    op=ALU.mult), [acc, rc], [ya])
                K.st(ya, YY[qb * 512:(qb + 1) * 512, h * 64:(h + 1) * 64].rearrange("(j p) d -> p j d", p=128), ya.t[:])

        for direction in (0, 1):
            K.begin_sweep()
            tri = triu if direction == 0 else tril
            gi0 = 0 if direction == 0 else 8
            qkl = K.ring(2, [128, 8, 128], BF16, dma=True)
            vml = K.ring(2, [128, 4, 129], BF16, dma=True)
            gl = K.ring(2, [128, 16], F32, dma=True)
            ktok = K.ring(2, [128, 4, 128], BF16)
            sm = K.ring(2, [128, 6, 4], F32)
            dmr = K.ring(2, [128, 128], F32)
            ptm = K.ring(2, [128, 128], BF16)
            itr = K.ring(2, [128, 129], F32)
            totr = K.ring(2, [128, 129], F32)
            dnr = K.ring(2, [128, 2], F32)
            vwr = K.ring(2, [128, 129], BF16)
            ST = [K.tile([128, 129], F32) for _ in range(4)]
            STb = [K.tile([128, 129], BF16) for _ in range(4)]
            hfr = K.ring(2, [128, 512], F32, dma=True)
            if direction == 1:
                sol = K.ring(2, [128, 512], BF16, dma=True)
                og = K.tile([128, 512], F32, dma=True)
                bc_load(og, W['mlstm_out_norm'][0], 512)
                sq = K.tile([128, 512], F32)
                ms = K.ring(2, [128, 3, 4], F32)
                ymr = K.ring(2, [128, 512], BF16, dma=True)
            for h in range(4):
                K.v("dve", lambda e, h=h: e.memset(ST[h].t[:], 0.0), [], [ST[h]])
                K.v("dve", lambda e, h=h: e.memset(STb[h].t[:], 0.0), [], [STb[h]])
            order = range(NT) if direction == 0 else range(NT - 1, -1, -1)
            for n in order:
                r0, r1 = n * 128, (n + 1) * 128
                qk = qkl.next()
                K.ld(qk, qk.t[:], QKT[:, :, r0:r1].rearrange("c p t -> p c t"))
                vm = vml.next()
                K.ld(vm, vm.t[:], VM[r0:r1, :, :])
                g = gl.next()
                K.ld(g, g.t[:], GG[r0:r1, :])
                hf = hfr.next()
                if direction == 1:
                    K.ld(hf, hf.t[:], HF[r0:r1, :])
                    so = sol.next()
                    K.ld(so, so.t[:], SO[r0:r1, :])
                li = g.t[:, gi0:gi0 + 4]
                lf = g.t[:, gi0 + 4:gi0 + 8]
                pb = K.pgen.next()
                pbv = pb.t[:].bitcast(BF16).rearrange("p (c t) -> p c t", c=8)
                for h in range(4):
                    K.tr(pb, pbv[:, h, :], qk, qk.t[:, 4 + h, :], ident, ident.t[:])
                kk = ktok.next()
                K.act(kk, kk.t[:], [pb], pbv[:, 0:4, :], AF.Copy)
                pc = K.pgen.next()
                K.mm(pc, pc.t[:, 0:4], tri, tri.t[:], g, lf)
                K.mm(pc, pc.t[:, 4:8], ones, ones.t[:], g, lf)
                s_ = sm.next()
                K.v("dve", lambda e, s_=s_, pc=pc, li=li: e.tensor_tensor(out=s_.t[:, 0, :], in0=li, in1=pc.t[:, 0:4],
                                                                         op=ALU.subtract), [g, pc], [s_])
                K.v("dve", lambda e, s_=s_, pc=pc: e.scalar_tensor_tensor(
                    out=s_.t[:, 1, :], in0=pc.t[:, 4:8], scalar=LNSC_M, in1=s_.t[:, 0, :], op0=ALU.add, op1=ALU.add),
                    [pc, s_], [s_])
                K.act(s_, s_.t[:, 2:4, :], [pc], pc.t[:, 0:8].rearrange("p (a b) -> p a b", b=4), AF.Exp)
                K.act(s_, s_.t[:, 4, :], [s_], s_.t[:, 1, :], AF.Exp)
                for h in range(4):
                    pB = K.pgen.next()
                    K.mm(pB, pB.t[:, 0:128], g, lf[:, h:h + 1].to_broadcast([128, 128]), tri, tri.t[:])
                    dm = dmr.next()
                    K.act(dm, dm.t[:], [pB, s_], pB.t[:, 0:128], AF.Exp, bias=s_.t[:, 0, h:h + 1])
                    K.v("dve", lambda e, dm=dm, tri=tri: e.tensor_tensor(out=dm.t[:], in0=dm.t[:], in1=tri.t[:], op=ALU.mult),
                        [dm, tri], [dm])
                    pS = K.pgen.next()
                    K.mm(pS, pS.t[:, 0:128], qk, qk.t[:, 4 + h, :], qk, qk.t[:, h, :])
                    pt = ptm.next()
                    K.v("dve", lambda e, pS=pS, dm=dm, pt=pt: e.scalar_tensor_tensor(
                        out=pt.t[:], in0=pS.t[:, 0:128], scalar=SC_M, in1=dm.t[:], op0=ALU.mult, op1=ALU.mult),
                        [pS, dm], [pt])
                    if DEBUG and direction == 0 and n == 0 and h == 0:
                        P.op("sp", lambda e, dm=dm: e.dma_start(out=DBG_dm, in_=dm.t[:]), reads=[dm.b], dsem=wsem)
                        P.op("sp", lambda e, pt=pt: e.dma_start(out=DBG_pt, in_=pt.t[:]), reads=[pt.b], dsem=wsem)
                        P.op("sp", lambda e, s_=s_: e.dma_start(out=DBG_s, in_=s_.t[:, :, :].rearrange("p a b -> p (a b)")), reads=[s_.b], dsem=wsem)
                        P.op("sp", lambda e, kk=kk: e.dma_start(out=DBG_kk, in_=kk.t[:, 0, :]), reads=[kk.b], dsem=wsem)
                    pN = K.pgen.next()
                    K.mm(pN, pN.t[:, 0:129], pt, pt.t[:], vm, vm.t[:, h, :])
                    pI = K.pgen.next()
                    K.mm(pI, pI.t[:, 0:129], qk, qk.t[:, h, :], STb[h], STb[h].t[:])
                    it = itr.next()
                    K.act(it, it.t[:], [pI, s_], pI.t[:, 0:129], AF.Copy, scale=s_.t[:, 2, h:h + 1])
                    tot = totr.next()
                    K.v("dve", lambda e, tot=tot, pN=pN, it=it: e.tensor_tensor(out=tot.t[:], in0=pN.t[:, 0:129],
                                                                               in1=it.t[:], op=ALU.add), [pN, it], [tot])
                    if DEBUG and direction == 0 and n == 0 and h == 0:
                        P.op("sp", lambda e, tot=tot: e.dma_start(out=DBG_tot, in_=tot.t[:]), reads=[tot.b], dsem=wsem)
                    dn = dnr.next()
                    K.v("dve", lambda e, tot=tot, dn=dn: e.tensor_scalar_mul(out=dn.t[:, 1:2], in0=tot.t[:, 128:129],
                                                                             scalar1=-1.0), [tot], [dn])
                    K.v("dve", lambda e, tot=tot, dn=dn: e.tensor_tensor(out=dn.t[:, 0:1], in0=tot.t[:, 128:129],
                                                                         in1=dn.t[:, 1:2], op=ALU.max), [tot, dn], [dn])
                    K.v("dve", lambda e, dn=dn: e.tensor_scalar_max(out=dn.t[:, 0:1], in0=dn.t[:, 0:1], scalar1=1.0),
                        [dn], [dn])
                    K.v("dve", lambda e, dn=dn: e.reciprocal(out=dn.t[:, 1:2], in_=dn.t[:, 0:1]), [dn], [dn])
                    hs_ap = hf.t[:, h * 128:(h + 1) * 128]
                    if direction == 0:
                        K.v("dve", lambda e, tot=tot, dn=dn, hs_ap=hs_ap: e.tensor_scalar_mul(
                            out=hs_ap, in0=tot.t[:, 0:128], scalar1=dn.t[:, 1:2]), [tot, dn], [hf])
                    else:
                        K.v("dve", lambda e, tot=tot, dn=dn, hs_ap=hs_ap: e.scalar_tensor_tensor(
                            out=hs_ap, in0=tot.t[:, 0:128], scalar=dn.t[:, 1:2], in1=hs_ap, op0=ALU.mult, op1=ALU.add),
                            [tot, dn, hf], [hf])
                    vw = vwr.next()
                    K.v("dve", lambda e, vw=vw, vm=vm, s_=s_, h=h: e.tensor_scalar_mul(
                        out=vw.t[:], in0=vm.t[:, h, :], scalar1=s_.t[:, 4, h:h + 1]), [vm, s_], [vw])
                    pC = K.pgen.next()
                    K.mm(pC, pC.t[:, 0:129], kk, kk.t[:, h, :], vw, vw.t[:])
                    K.v("dve", lambda e, pC=pC, s_=s_, h=h: e.scalar_tensor_tensor(
                        out=ST[h].t[:], in0=ST[h].t[:], scalar=s_.t[:, 3, h:h + 1], in1=pC.t[:, 0:129], op0=ALU.mult,
                        op1=ALU.add), [ST[h], s_, pC], [ST[h]])
                    K.act(STb[h], STb[h].t[:], [ST[h]], ST[h].t[:], AF.Copy)
                if direction == 0:
                    K.st(hf, HF[r0:r1, :], hf.t[:])
                else:
                    m_ = ms.next()
                    K.v("dve", lambda e, hf=hf: e.tensor_tensor(out=sq.t[:], in0=hf.t[:], in1=hf.t[:], op=ALU.mult),
                        [hf], [sq])
                    K.v("dve", lambda e, m_=m_: e.tensor_reduce(out=m_.t[:, 0, :],
                                                                 in_=sq.t[:, :].rearrange("p (h d) -> p h d", d=128),
                                                                 axis=AX.X, op=ALU.add), [sq], [m_])
                    K.v("dve", lambda e, m_=m_: e.tensor_scalar(out=m_.t[:, 1, :], in0=m_.t[:, 0, :], scalar1=1.0 / 128,
                                                                scalar2=EPS, op0=ALU.mult, op1=ALU.add), [m_], [m_])
                    K.act(m_, m_.t[:, 2, :], [m_], m_.t[:, 1, :], AF.Sqrt)
                    K.v("dve", lambda e, m_=m_: e.reciprocal(out=m_.t[:, 0, :], in_=m_.t[:, 2, :]), [m_], [m_])
                    K.v("dve", lambda e, m_=m_, hf=hf: e.tensor_tensor(
                        out=sq.t[:, :].rearrange("p (h d) -> p h d", d=128),
                        in0=hf.t[:, :].rearrange("p (h d) -> p h d", d=128),
                        in1=m_.t[:, 0, :].unsqueeze(2).to_broadcast([128, 4, 128]), op=ALU.mult), [hf, m_], [sq])
                    K.v("dve", lambda e: e.tensor_tensor(out=sq.t[:], in0=sq.t[:], in1=og.t[:], op=ALU.mult), [sq, og], [sq])
                    ym = ymr.next()
                    K.v("dve", lambda e, ym=ym, so=so: e.tensor_tensor(out=ym.t[:], in0=sq.t[:], in1=so.t[:], op=ALU.mult),
                        [sq, so], [ym])
                    K.st(ym, YY[r0:r1, 512:1024], ym.t[:])

        def tail(x, pr, t, nt, XTd, final_g=None):
            for half in range(2):
                K.v("dve", lambda e, half=half: e.tensor_tensor(
                    out=x.t[:, half * 512:(half + 1) * 512], in0=x.t[:, half * 512:(half + 1) * 512],
                    in1=pr[half].t[:, :], op=ALU.add), [x, pr[half]], [x])
            if final_g is None:
                K.st(x, X1[t * 128:(t + 1) * 128, :], x.t[:])
                nt.run(x, x.t[:], t, XTd)
            else:
                ss = nt.stats(x, x.t[:], 1024)
                K.v("dve", lambda e, ss=ss: e.scalar_tensor_tensor(out=x.t[:], in0=x.t[:], scalar=ss.t[:, 3:4],
                                                                   in1=final_g.t[:], op0=ALU.mult, op1=ALU.mult),
                    [x, ss, final_g], [x])
                K.st(x, y_out[t * 128:(t + 1) * 128, :], x.t[:])

        def project_tm(src_tile, src_ap_k, wt, w_ap_k, nk):
            pr = [K.pgen.next(), K.pgen.next()]
            for half in range(2):
                for k in range(nk):
                    K.mm(pr[half], pr[half].t[:, :], src_tile, src_ap_k(k), wt, w_ap_k(k, half), start=(k == 0),
                         stop=(k == nk - 1))
            return pr

        K.begin_sweep()
        nt = NormT()
        nt.load_gain(W['ffn_norm'][0])
        w_o = K.tile([128, 8, 1024], BF16, dma=True)
        K.ld(w_o, w_o.t[:], wb_out.rearrange("(k p) n -> p k n", p=128))
        xr = K.ring(2, [128, 1024], F32, dma=True)
        yr = K.ring(2, [128, 1024], BF16, dma=True)
        yTr = K.ring(2, [128, 8, 128], BF16)
        for t in range(NT):
            x = xr.next()
            K.ld(x, x.t[:], x_in[t * 128:(t + 1) * 128, :])
            y = yr.next()
            K.ld(y, y.t[:], YY[t * 128:(t + 1) * 128, :])
            pb = K.pgen.next()
            pbv = pb.t[:].bitcast(BF16).rearrange("p (c t) -> p c t", c=8)
            for c in range(8):
                K.tr(pb, pbv[:, c, :], y, y.t[:, c * 128:(c + 1) * 128], ident, ident.t[:])
            yT = yTr.next()
            K.act(yT, yT.t[:], [pb], pbv, AF.Copy)
            pr = project_tm(yT, lambda k: yT.t[:, k, :], w_o, lambda k, half: w_o.t[:, k, half * 512:(half + 1) * 512], 8)
            tail(x, pr, t, nt, XT[1])

        def ffn_sweep(l, XTs, XTd, gain_next, final):
            K.begin_sweep()
            wup = K.tile([128, 8, 5632], BF16, dma=True)
            for k in range(8):
                K.ld(wup, wup.t[:, k, :], wb_up[l, k * 128:(k + 1) * 128, :])
            wdn = K.tile([128, NFC, 1024], BF16, dma=True)
            K.ld(wdn, wdn.t[:], wb_dn[l].rearrange("(i p) n -> p i n", p=128))
            nt = NormT()
            fg = None
            if final:
                fg = nt.g
            nt.load_gain(gain_next)
            win = K.tile([128, 8, 514], BF16, dma=True)
            hT = K.tile([128, NFC, 512], BF16)
            cgr = K.ring(2, [128, 512], F32)
            cvr = K.ring(2, [128, 512], F32)
            sgr = K.ring(2, [128, 512], BF16)
            xr = K.ring(2, [128, 1024], F32, dma=True)
            K.pgen = Ring(K.banks[0:6])
            cw = cw_f[l]
            hs_i = 0
            for b in range(NB):
                K.ld(win, win.t[:], XTs[:, :, 512 * b:512 * b + 514].rearrange("c p t -> p c t"))
                for i in range(NFC):
                    outs = []
                    for (col0, ci, rr) in ((i * 128, i, cgr), (DFF + i * 128, NFC + i, cvr)):
                        pm = K.pgen.next()
                        hsl = 0
                        halo = K.banks[6 + (hs_i % 2)]
                        hs_i += 1
                        for k in range(8):
                            K.mm(pm, pm.t[:, :], wup, wup.t[:, k, col0:col0 + 128], win, win.t[:, k, 1:513],
                                 start=(k == 0), stop=(k == 7))
                        for k in range(8):
                            K.mm(halo, halo.t[:, hsl:hsl + 2], wup, wup.t[:, k, col0:col0 + 128], win,
                                 win.t[:, k, 0:514:513], start=(k == 0), stop=(k == 7))
                        cc = rr.next()
                        conv_chunk(pm, halo, hsl, cw, ci, cc)
                        outs.append(cc)
                    cg, cv = outs
                    sg = sgr.next()
                    K.act(sg, sg.t[:], [cg], cg.t[:], AF.Silu)
                    K.v("dve", lambda e, sg=sg, cv=cv, i=i: e.tensor_tensor(out=hT.t[:, i, :], in0=sg.t[:], in1=cv.t[:],
                                                                            op=ALU.mult), [sg, cv], [hT])
                for tt in range(4):
                    t = b * 4 + tt
                    x = xr.next()
                    K.ld(x, x.t[:], X1[t * 128:(t + 1) * 128, :])
                    pr = project_tm(hT, lambda k: hT.t[:, k, tt * 128:(tt + 1) * 128], wdn,
                                    lambda k, half: wdn.t[:, k, half * 512:(half + 1) * 512], NFC)
                    tail(x, pr, t, nt, XTd, final_g=fg)
            K.pgen = Ring(K.banks[0:6])

        ffn_sweep(0, XT[1], XT[0], W['c_norm'][0], False)

        K.begin_sweep()
        nt = NormT()
        nt.load_gain(W['ffn_norm'][1])
        cwin = K.tile([128, 8, 2048], BF16, dma=True)
        K.ld(cwin, cwin.t[:], wb_cin.rearrange("(k p) n -> p k n", p=128))
        cwo = K.tile([128, 8, 1024], BF16, dma=True)
        K.ld(cwo, cwo.t[:], wb_cout.rearrange("(k p) n -> p k n", p=128))
        vg = K.tile([128, 1024], F32, dma=True)
        bc_load(vg, W['c_v_norm'][0], 1024)
        xTl = K.ring(2, [128, 8, 128], BF16, dma=True)
        xr = K.ring(2, [128, 1024], F32, dma=True)
        ur = K.ring(2, [128, 2048], F32)
        vnr = K.ring(2, [128, 1024], BF16)
        zr = K.ring(2, [128, 1024], BF16)
        zTr = K.ring(2, [128, 8, 128], BF16)
        for t in range(NT):
            xT = xTl.next()
            K.ld(xT, xT.t[:], XT[0][:, :, 1 + t * 128:1 + (t + 1) * 128].rearrange("c p t -> p c t"))
            x = xr.next()
            K.ld(x, x.t[:], X1[t * 128:(t + 1) * 128, :])
            uv = ur.next()
            for j in range(4):
                pu = K.pgen.next()
                for k in range(8):
                    K.mm(pu, pu.t[:, :], xT, xT.t[:, k, :], cwin, cwin.t[:, k, j * 512:(j + 1) * 512], start=(k == 0),
                         stop=(k == 7))
                K.act(uv, uv.t[:, j * 512:(j + 1) * 512], [pu], pu.t[:, :], AF.Gelu)
            ss = nt.stats(uv, uv.t[:, 1024:2048], 1024)
            vn = vnr.next()
            K.v("dve", lambda e, ss=ss, uv=uv, vn=vn: e.scalar_tensor_tensor(
                out=vn.t[:], in0=uv.t[:, 1024:2048], scalar=ss.t[:, 3:4], in1=vg.t[:], op0=ALU.mult, op1=ALU.mult),
                [uv, ss, vg], [vn])
            z = zr.next()
            psv = [K.pgen.next(), K.pgen.next()]
            for g in range(8):
                pp = psv[g // 4]
                K.mm(pp, pp.t[:, (g % 4) * 128:(g % 4 + 1) * 128], wsT, wsT.t[:, g, :], vn, vn.t[:, g * 128:(g + 1) * 128])
            for g in range(8):
                pp = psv[g // 4]
                K.v("dve", lambda e, pp=pp, g=g, uv=uv, z=z: e.scalar_tensor_tensor(
                    out=z.t[:, g * 128:(g + 1) * 128], in0=pp.t[:, (g % 4) * 128:(g % 4 + 1) * 128],
                    scalar=bsT.t[:, g:g + 1], in1=uv.t[:, g * 128:(g + 1) * 128], op0=ALU.add, op1=ALU.mult),
                    [pp, bsT, uv], [z])
            if DEBUG and t == 0:
                P.op("sp", lambda e: e.dma_start(out=DBG_wsT, in_=wsT.t[:, :, :].rearrange("p a b -> p (a b)")), reads=[wsT.b], dsem=wsem)
                P.op("sp", lambda e: e.dma_start(out=DBG_bsT, in_=bsT.t[:]), reads=[bsT.b], dsem=wsem)
                P.op("sp", lambda e, uv=uv: e.dma_start(out=DBG_uv, in_=uv.t[:]), reads=[uv.b], dsem=wsem)
                P.op("sp", lambda e, vn=vn: e.dma_start(out=DBG_vn, in_=vn.t[:]), reads=[vn.b], dsem=wsem)
                P.op("sp", lambda e, z=z: e.dma_start(out=DBG_z, in_=z.t[:]), reads=[z.b], dsem=wsem)
            pb = K.pgen.next()
            pbv = pb.t[:].bitcast(BF16).rearrange("p (c t) -> p c t", c=8)
            for c in range(8):
                K.tr(pb, pbv[:, c, :], z, z.t[:, c * 128:(c + 1) * 128], ident, ident.t[:])
            zT = zTr.next()
            K.act(zT, zT.t[:], [pb], pbv, AF.Copy)
            pr = project_tm(zT, lambda k: zT.t[:, k, :], cwo, lambda k, half: cwo.t[:, k, half * 512:(half + 1) * 512], 8)
            tail(x, pr, t, nt, XT[1])

        ffn_sweep(1, XT[1], XT[0], W['final_norm'], True)

    P.emit()
    nc._dbg_names = dbg_names
    return nc


def make_consts(SM):
    pos = np.arange(SM, dtype=np.float32)
    inv = (1.0 / (np.float32(10000.0) ** (np.arange(0, 32, 2, dtype=np.float32) / np.float32(32)))).astype(np.float32)
    ang = pos[:, None] * inv[None, :]
    return {
        "c_ident": np.eye(128, dtype=np.float32),
        "c_triu": np.triu(np.ones((128, 128), np.float32)),
        "c_tril": np.tril(np.ones((128, 128), np.float32)),
        "c_ones": np.ones((128, 128), np.float32),
        "c_cos": np.cos(ang).astype(np.float32),
        "c_sin": np.sin(ang).astype(np.float32),
    }


_CACHE = {}


def run(jobs_per_core, params, S_list):
    key = tuple(S_list)
    if key not in _CACHE:
        _CACHE[key] = build_program(S_list)
    nc = _CACHE[key]
    consts = make_consts(max(S_list))
    in_maps = []
    for xs in jobs_per_core:
        m = {}
        for j, x in enumerate(xs):
            m["x%d" % j] = np.ascontiguousarray(x, dtype=np.float32)
        for n in PARAMS:
            m[n] = np.ascontiguousarray(params[n], dtype=np.float32)
        m.update(consts)
        in_maps.append(m)
    res = run_bass_kernel_spmd(nc, in_maps, core_ids=list(range(len(jobs_per_core))))
    global LAST_RES
    LAST_RES = res.results
    return [[r["y%d" % j] for j in range(len(S_list))] for r in res.results]


def kernel(**inputs):
    xp = np.asarray(inputs['x_prompt'], dtype=np.float32)
    xs = np.asarray(inputs['x_sample'], dtype=np.float32)
    params = {n: np.asarray(inputs[n], dtype=np.float32) for n in PARAMS}
    S_list = [4096, 4096, 8192]
    zeros = np.zeros((8192, D), np.float32)
    jobs = []
    for c in range(8):
        jobs.append([xs[2 * c], xs[2 * c + 1], xp[c] if c < 2 else zeros])
    outs = run(jobs, params, S_list)
    y_prompt = np.stack([outs[0][2], outs[1][2]], axis=0)
    y_sample = np.stack([outs[c][j] for c in range(8) for j in range(2)], axis=0)
    return (y_prompt, y_sample)
```

```python
import math
import numpy as np
import concourse.bass as bass
import concourse.mybir as mybir
from concourse.bass_utils import run_bass_kernel_spmd

F32 = mybir.dt.float32
BF16 = mybir.dt.bfloat16
ALU = mybir.AluOpType
AF = mybir.ActivationFunctionType
AX = mybir.AxisListType

D = 1024
EPS = 1e-6
DFF = 2816
NFC = 22
SC_ATT = 96 ** -0.5
SC_M = 128 ** -0.5
LNSC_M = math.log(SC_M)

PARAMS = ['ab_norm', 'ab_w_in', 'mla_q_norm', 'mla_w_uq', 'mla_kv_norm', 'mla_w_ukv', 'mlstm_conv_w',
          'mlstm_conv_b', 'mlstm_gate_bias', 'mlstm_out_norm', 'ab_w_out', 'c_norm', 'c_w_in', 'c_v_norm',
          'c_w_spatial', 'c_b_spatial', 'c_w_out', 'ffn_norm', 'ffn_w_up', 'ffn_conv_w', 'ffn_conv_b',
          'ffn_w_down', 'final_norm']
PSHAPES = {
    'ab_norm': [1, 1024], 'ab_w_in': [1, 1024, 2480], 'mla_q_norm': [1, 256], 'mla_w_uq': [1, 256, 768],
    'mla_kv_norm': [1, 128], 'mla_w_ukv': [1, 128, 1024], 'mlstm_conv_w': [1, 3, 1024], 'mlstm_conv_b': [1, 1024],
    'mlstm_gate_bias': [1, 16], 'mlstm_out_norm': [1, 512], 'ab_w_out': [1, 1024, 1024], 'c_norm': [1, 1024],
    'c_w_in': [1, 1024, 2048], 'c_v_norm': [1, 1024], 'c_w_spatial': [1, 8, 128, 128], 'c_b_spatial': [1, 8, 128],
    'c_w_out': [1, 1024, 1024], 'ffn_norm': [2, 1024], 'ffn_w_up': [2, 1024, 5632], 'ffn_conv_w': [2, 3, 5632],
    'ffn_conv_b': [2, 5632], 'ffn_w_down': [2, 2816, 1024], 'final_norm': [1024],
}


class Buf:
    __slots__ = ("name", "w", "r", "excl")

    def __init__(self, name, excl=False):
        self.name = name
        self.w = None
        self.r = []
        self.excl = excl


class Prog:
    ENGS = ("pe", "act", "dve", "pool", "sp")

    def __init__(self, nc, n_dma_sems=80):
        self.nc = nc
        self.ops = {e: [] for e in self.ENGS}
        self.count = {}
        self.seen = {e: {} for e in self.ENGS}
        self.pending = {e: [] for e in self.ENGS}
        self.dma_sems = ["d%d" % i for i in range(n_dma_sems)]
        self.nops = 0
        self.disabled = False
        self.sweep_no = 0
        import os
        self.maxops = int(os.environ.get("K_OPS", "1000000000"))

    def op(self, eng, fn, reads=(), writes=(), dsem=None):
        if self.disabled or self.nops >= self.maxops:
            return None
        deps = []
        for b in reads:
            if b.w is not None:
                deps.append(b.w)
            if b.excl:
                for t in b.r:
                    if t[0] != eng:
                        deps.append(t)
        for b in writes:
            if b.w is not None:
                deps.append(b.w)
            deps.extend(b.r)
        if dsem is None:
            key, inc = eng, 1
        else:
            key, inc = dsem, 16
        self.count[key] = self.count.get(key, 0) + inc
        tok = (key, self.count[key])
        seen = self.seen[eng]
        waits = []
        if self.pending[eng]:
            for k, v in self.pending[eng]:
                if seen.get(k, 0) < v:
                    seen[k] = v
                    waits.append((k, v))
            self.pending[eng] = []
        for k, v in deps:
            if k == "pe" and eng == "pe" and dsem is None:
                continue
            if seen.get(k, 0) >= v:
                continue
            seen[k] = v
            waits.append((k, v))
        self.ops[eng].append((fn, waits, key, inc))
        for b in reads:
            b.r.append(tok)
        for b in writes:
            b.w = tok
            b.r = []
        self.nops += 1
        return tok

    def barrier(self):
        snap = [(k, v) for k, v in self.count.items()]
        for e in self.ENGS:
            self.pending[e] = list(snap)

    def emit(self):
        from contextlib import ExitStack
        nc = self.nc
        self.barrier()
        with ExitStack() as st:
            sems = {}
            for k in list(self.ENGS) + self.dma_sems:
                sems[k] = st.enter_context(nc.semaphore("s_" + k))
            block = st.enter_context(nc.Block())

            def mk(e):
                def section(eng):
                    for fn, waits, key, inc in self.ops[e]:
                        for k, v in waits:
                            eng.wait_ge(sems[k], v)
                        ins = fn(eng)
                        ins.then_inc(sems[key], inc)
                    for k, v in self.pending[e]:
                        if self.seen[e].get(k, 0) < v:
                            eng.wait_ge(sems[k], v)
                return section

            block.tensor(mk("pe"))
            block.scalar(mk("act"))
            block.vector(mk("dve"))
            block.gpsimd(mk("pool"))
            block.sync(mk("sp"))


class TL:
    __slots__ = ("t", "b", "sem")

    def __init__(self, t, name, sem=None):
        self.t = t
        self.b = Buf(name)
        self.sem = sem


class Ring:
    def __init__(self, tiles):
        self.tiles = tiles
        self.i = 0

    def next(self):
        t = self.tiles[self.i % len(self.tiles)]
        self.i += 1
        return t


def dsize(dt):
    return 4 if dt == F32 else 2


class Ctx:
    def __init__(self, nc, P):
        self.nc = nc
        self.P = P
        self.uid = 0
        self.pers_off = 16512
        self.off = 16512
        self.semi = 0
        self.banks = [TL(nc.alloc_psum_tensor("pb%d" % i, [128, 512], F32), "pb%d" % i) for i in range(8)]
        for b in self.banks:
            b.b.excl = True
        self.pgen = Ring(self.banks[0:6])
        self.pacc = Ring(self.banks[6:8])

    def _alloc(self, shape, dt, dma, off):
        self.uid += 1
        name = "t%d" % self.uid
        t = self.nc.alloc_sbuf_tensor_at(name, list(shape), dt, offset=off)
        sem = None
        if dma:
            sem = self.P.dma_sems[self.semi % (len(self.P.dma_sems) - 1)]
            self.semi += 1
        return TL(t, name, sem)

    @staticmethod
    def _bytes(shape, dt):
        n = 1
        for s in shape[1:]:
            n *= s
        return (n * dsize(dt) + 63) // 64 * 64

    def pers(self, shape, dt, dma=False):
        tl = self._alloc(shape, dt, dma, self.pers_off)
        self.pers_off += self._bytes(shape, dt)
        return tl

    def begin_sweep(self):
        import os
        self.P.sweep_no += 1
        lim = int(os.environ.get("K_LIMIT", "100000"))
        self.P.disabled = self.P.sweep_no > lim
        if os.environ.get("K_VERBOSE"):
            print("sweep", self.P.sweep_no, "ops so far", self.P.nops, flush=True)
        self.P.barrier()
        self.off = self.pers_off

    def tile(self, shape, dt, dma=False):
        tl = self._alloc(shape, dt, dma, self.off)
        self.off += self._bytes(shape, dt)
        assert self.off <= 229376, ("SBUF overflow", self.off)
        return tl

    def ring(self, n, shape, dt, dma=False):
        return Ring([self.tile(shape, dt, dma) for _ in range(n)])

    def ld(self, dst, dst_ap, src_ap, eng="sp", slow=False, reads=()):
        if slow:
            fn = lambda e: e.dma_start(out=dst_ap, in_=src_ap, allow_slow_non_contiguous=True)
        else:
            fn = lambda e: e.dma_start(out=dst_ap, in_=src_ap)
        self.P.op(eng, fn, reads=list(reads), writes=[dst.b], dsem=dst.sem)

    def st(self, src, dst_ap, src_ap, eng="pool", slow=False):
        if slow:
            fn = lambda e: e.dma_start(out=dst_ap, in_=src_ap, allow_slow_non_contiguous=True)
        else:
            fn = lambda e: e.dma_start(out=dst_ap, in_=src_ap)
        self.P.op(eng, fn, reads=[src.b], dsem=src.sem)

    def mm(self, out, out_ap, lt, lhsT, rt, rhs, start=True, stop=True, skip=False):
        if skip:
            fn = lambda e: e.matmul(out_ap, lhsT=lhsT, rhs=rhs, start=start, stop=stop, skip_group_check=True)
        else:
            fn = lambda e: e.matmul(out_ap, lhsT=lhsT, rhs=rhs, start=start, stop=stop)
        self.P.op("pe", fn, reads=[lt.b, rt.b], writes=[out.b])

    def tr(self, out, out_ap, it, in_ap, idt, id_ap):
        self.P.op("pe", lambda e: e.transpose(out=out_ap, in_=in_ap, identity=id_ap),
                  reads=[it.b, idt.b], writes=[out.b])

    def act(self, out, out_ap, ins, in_ap, func, scale=None, bias=None, accum=None, extra_w=()):
        kw = {}
        if scale is not None:
            kw["scale"] = scale
        if bias is not None:
            kw["bias"] = bias
        if accum is not None:
            kw["accum_out"] = accum
        self.P.op("act", lambda e: e.activation(out=out_ap, in_=in_ap, func=func, **kw),
                  reads=[t.b for t in ins], writes=[out.b] + [t.b for t in extra_w])

    def v(self, eng, fn, ins, outs):
        self.P.op(eng, fn, reads=[t.b for t in ins], writes=[t.b for t in outs])


def norm_jobs(jobs):
    return [dict(S=j, NTo=j // 128, out=(0, j // 128), prompt=False) if isinstance(j, int) else j for j in jobs]


def build_program(jobs):
    jobs = norm_jobs(jobs)
    S_list = [j['S'] for j in jobs]
    nc = bass.Bass("TRN2", target_bir_lowering=False)
    P = Prog(nc)
    K = Ctx(nc, P)
    SM = max(S_list)
    NJ = len(S_list)

    import os
    DEBUG = bool(os.environ.get("K_DEBUG"))
    dbg_names = []

    def dram(name, shape, dt, kind="Internal"):
        if DEBUG and kind == "Internal" and (not name.startswith("wb_") or os.environ.get("K_DEBUGW")):
            kind = "ExternalOutput"
            dbg_names.append(name)
        return nc.dram_tensor(name, list(shape), dt, kind=kind).ap()

    xin = [dram("x%d" % j, [S_list[j], D], F32, "ExternalInput") for j in range(NJ)]
    yout = [dram("y%d" % j, [(jobs[j]['out'][1] - jobs[j]['out'][0]) * 128, D], F32, "ExternalOutput") for j in range(NJ)]
    csin = [dram("cs%d" % j, [S_list[j], 32], F32, "ExternalInput") for j in range(NJ)]
    mkin = [dram("mk%d" % j, [S_list[j], 4], F32, "ExternalInput") for j in range(NJ)]
    sfin = dram("sf", [1, 4], F32, "ExternalInput")
    W = {n: dram(n, PSHAPES[n], F32, "ExternalInput") for n in PARAMS}
    c_ident = dram("c_ident", [128, 128], F32, "ExternalInput")
    c_triu = dram("c_triu", [128, 128], F32, "ExternalInput")
    c_tril = dram("c_tril", [128, 128], F32, "ExternalInput")
    c_ones = dram("c_ones", [128, 128], F32, "ExternalInput")

    wb_in = dram("wb_in", [1024, 2480], BF16)
    wb_uq = dram("wb_uq", [256, 768], BF16)
    wb_ukv = dram("wb_ukv", [128, 1024], BF16)
    wb_out = dram("wb_out", [1024, 1024], BF16)
    wb_cin = dram("wb_cin", [1024, 2048], BF16)
    wb_cout = dram("wb_cout", [1024, 1024], BF16)
    wb_up = dram("wb_up", [2, 1024, 5632], BF16)
    wb_dn = dram("wb_dn", [2, 2816, 1024], BF16)
    XT = [dram("XT%d" % i, [8, 128, SM + 2], BF16) for i in range(2)]
    X1 = dram("X1", [SM, D], F32)
    QT = dram("QT", [8, 96, SM], BF16)
    KT = dram("KT", [8, 96, SM], BF16)
    VV = dram("VV", [SM, 8, 65], BF16)
    QKT = dram("QKT", [8, 128, SM], BF16)
    VM = dram("VM", [SM, 4, 129], BF16)
    SO = dram("SO", [SM, 512], BF16)
    GG = dram("GG", [SM, 16], F32)
    HF = dram("HF", [SM, 512], F32)
    YY = dram("YY", [SM, D], BF16)
    if DEBUG:
        DBG_dm = dram("DBG_dm", [128, 128], F32)
        DBG_pt = dram("DBG_pt", [128, 128], BF16)
        DBG_s = dram("DBG_s", [128, 24], F32)
        DBG_tot = dram("DBG_tot", [128, 129], F32)
        DBG_kk = dram("DBG_kk", [128, 128], BF16)
        DBG_wsT = dram("DBG_wsT", [128, 1024], BF16)
        DBG_bsT = dram("DBG_bsT", [128, 8], F32)
        DBG_uv = dram("DBG_uv", [128, 2048], F32)
        DBG_vn = dram("DBG_vn", [128, 1024], BF16)
        DBG_z = dram("DBG_z", [128, 1024], BF16)

    wsem = P.dma_sems[-1]

    def cast(dst, src, rows):
        for r0 in range(0, rows, 128):
            r1 = min(rows, r0 + 128)
            P.op("pool", lambda e, r0=r0, r1=r1: e.dma_start(out=dst[r0:r1, :], in_=src[r0:r1, :]),
                 writes=[], dsem=wsem)

    cast(wb_in, W['ab_w_in'][0], 1024)
    cast(wb_uq, W['mla_w_uq'][0], 256)
    cast(wb_ukv, W['mla_w_ukv'][0], 128)
    cast(wb_out, W['ab_w_out'][0], 1024)
    cast(wb_cin, W['c_w_in'][0], 1024)
    cast(wb_cout, W['c_w_out'][0], 1024)
    for l in range(2):
        cast(wb_up[l], W['ffn_w_up'][l], 1024)
        cast(wb_dn[l], W['ffn_w_down'][l], 2816)

    identf = K.pers([128, 128], F32, dma=True)
    ident = K.pers([128, 128], BF16)
    triu = K.pers([128, 128], F32, dma=True)
    tril = K.pers([128, 128], F32, dma=True)
    ones = K.pers([128, 128], F32, dma=True)
    cw_m = K.pers([128, 4, 8], F32)
    cw_f = [K.pers([128, 4, 44], F32) for _ in range(2)]
    bsT = K.pers([128, 8], F32)
    wsT = K.pers([128, 8, 128], BF16)
    zero = K.pers([128, 16], BF16, dma=True)
    K.begin_sweep()
    K.ld(identf, identf.t[:], c_ident)
    K.ld(triu, triu.t[:], c_triu)
    K.ld(tril, tril.t[:], c_tril)
    K.ld(ones, ones.t[:], c_ones)
    K.v("dve", lambda e: e.tensor_copy(out=ident.t[:], in_=identf.t[:]), [identf], [ident])
    K.v("dve", lambda e: e.memset(zero.t[:], 0.0), [], [zero])

    def load_T(dst, dst_view, raws, nrow):
        raw = K.tile([nrow, len(raws), 128], F32, dma=True)
        for j, src in enumerate(raws):
            K.ld(raw, raw.t[:, j, :], src)
        for j in range(len(raws)):
            pb = K.pgen.next()
            K.tr(pb, pb.t[:, 0:nrow], raw, raw.t[:, j, :], identf, identf.t[0:nrow, 0:nrow])
            K.v("dve", lambda e, j=j, pb=pb: e.tensor_copy(out=dst_view(j), in_=pb.t[:, 0:nrow]), [pb], [dst])

    mcw = W['mlstm_conv_w'][0]
    load_T(cw_m, lambda j: cw_m.t[:, j, :],
           [mcw[j].rearrange("(c p) -> c p", p=128) for j in range(3)] +
           [W['mlstm_conv_b'][0].rearrange("(c p) -> c p", p=128)], 8)
    for l in range(2):
        load_T(cw_f[l], lambda j, l=l: cw_f[l].t[:, j, :],
               [W['ffn_conv_w'][l, j].rearrange("(c p) -> c p", p=128) for j in range(3)] +
               [W['ffn_conv_b'][l].rearrange("(c p) -> c p", p=128)], 44)
    load_T(bsT, lambda j: bsT.t[:, :], [W['c_b_spatial'][0]], 8)
    wsr = K.tile([128, 8, 128], F32, dma=True)
    wsrb = K.tile([128, 8, 128], BF16)
    K.ld(wsr, wsr.t[:], W['c_w_spatial'][0].rearrange("g t s -> t g s"))
    K.v("dve", lambda e: e.tensor_copy(out=wsrb.t[:], in_=wsr.t[:]), [wsr], [wsrb])
    pb = K.pgen.next()
    pbv = pb.t[:].bitcast(BF16).rearrange("p (c t) -> p c t", c=8)
    for g in range(8):
        K.tr(pb, pbv[:, g, :], wsrb, wsrb.t[:, g, :], ident, ident.t[:])
    K.v("dve", lambda e, pbv=pbv: e.tensor_copy(out=wsT.t[:], in_=pbv), [pb], [wsT])

    def bc_load(dst, src1d, n):
        K.ld(dst, dst.t[:, 0:n], src1d.partition_broadcast(128))

    class NormT:
        def __init__(self):
            self.junk = K.tile([128, 1024], BF16)
            self.ss = K.ring(2, [128, 4], F32)
            self.xn = K.ring(2, [128, 1024], BF16)
            self.xnT = K.ring(2, [128, 8, 128], BF16, dma=True)
            self.g = K.tile([128, 1024], F32, dma=True)

        def load_gain(self, src1d):
            bc_load(self.g, src1d, 1024)

        def stats(self, x, x_ap, n):
            ss = self.ss.next()
            junk = self.junk
            K.v("dve", lambda e: e.memset(ss.t[:, 0:1], 0.0), [], [ss])
            K.act(junk, junk.t[:, 0:n], [x, ss], x_ap, AF.Square, accum=ss.t[:, 0:1], extra_w=[ss])
            K.v("dve", lambda e: e.tensor_scalar(out=ss.t[:, 1:2], in0=ss.t[:, 0:1], scalar1=1.0 / n, scalar2=EPS,
                                                 op0=ALU.mult, op1=ALU.add), [ss], [ss])
            K.act(ss, ss.t[:, 2:3], [ss], ss.t[:, 1:2], AF.Sqrt)
            K.v("dve", lambda e: e.reciprocal(out=ss.t[:, 3:4], in_=ss.t[:, 2:3]), [ss], [ss])
            return ss

        def run(self, x, x_ap, t, XTd):
            ss = self.stats(x, x_ap, 1024)
            xn = self.xn.next()
            g = self.g
            K.v("dve", lambda e: e.scalar_tensor_tensor(out=xn.t[:], in0=x_ap, scalar=ss.t[:, 3:4], in1=g.t[:],
                                                        op0=ALU.mult, op1=ALU.mult), [x, ss, g], [xn])
            pb = K.pgen.next()
            pbv = pb.t[:].bitcast(BF16).rearrange("p (c t) -> p c t", c=8)
            for c in range(8):
                K.tr(pb, pbv[:, c, :], xn, xn.t[:, c * 128:(c + 1) * 128], ident, ident.t[:])
            xnT = self.xnT.next()
            K.act(xnT, xnT.t[:], [pb], pbv, AF.Copy)
            K.st(xnT, XTd[:, :, 1 + t * 128:1 + (t + 1) * 128].rearrange("c p t -> p c t"), xnT.t[:])

    def conv_chunk(pm, ph, hsl, cw, ci, dst):
        w0 = cw.t[:, 0, ci:ci + 1]
        w1 = cw.t[:, 1, ci:ci + 1]
        w2 = cw.t[:, 2, ci:ci + 1]
        bb = cw.t[:, 3, ci:ci + 1]
        K.act(dst, dst.t[:], [pm, cw], pm.t[:], AF.Identity, scale=w1, bias=bb)
        K.v("dve", lambda e: e.scalar_tensor_tensor(out=dst.t[:, 1:512], in0=pm.t[:, 0:511], scalar=w0,
                                                    in1=dst.t[:, 1:512], op0=ALU.mult, op1=ALU.add),
            [pm, cw, dst], [dst])
        K.v("dve", lambda e: e.scalar_tensor_tensor(out=dst.t[:, 0:511], in0=pm.t[:, 1:512], scalar=w2,
                                                    in1=dst.t[:, 0:511], op0=ALU.mult, op1=ALU.add),
            [pm, cw, dst], [dst])
        K.v("dve", lambda e: e.scalar_tensor_tensor(out=dst.t[:, 0:1], in0=ph.t[:, hsl:hsl + 1], scalar=w0,
                                                    in1=dst.t[:, 0:1], op0=ALU.mult, op1=ALU.add),
            [ph, cw, dst], [dst])
        K.v("dve", lambda e: e.scalar_tensor_tensor(out=dst.t[:, 511:512], in0=ph.t[:, hsl + 1:hsl + 2], scalar=w2,
                                                    in1=dst.t[:, 511:512], op0=ALU.mult, op1=ALU.add),
            [ph, cw, dst], [dst])

    for jb in range(NJ):
        S = S_list[jb]
        NT = S // 128
        NB = S // 512
        x_in = xin[jb]
        y_out = yout[jb]
        NTo = jobs[jb]['NTo']
        NBo = NTo // 4
        out_lo, out_hi = jobs[jb]['out']
        prompt = jobs[jb]['prompt']
        SEAM_BLOCKS = (0, 4, 8, 12) if prompt else ()

        def seam_setup(cw, nch, ncand):
            sfb = K.tile([128, 4], F32, dma=True)
            bc_load(sfb, sfin[0], 4)
            nfw = K.tile([128, ncand, 2, nch], F32)
            for c in range(ncand):
                for jj, wi in ((0, 0), (1, 2)):
                    K.v("dve", lambda e, c=c, jj=jj, wi=wi, nfw=nfw, sfb=sfb, cw=cw: e.tensor_scalar_mul(
                        out=nfw.t[:, c, jj, :], in0=cw.t[:, wi, :], scalar1=sfb.t[:, c:c + 1]), [cw, sfb], [nfw])
            return nfw

        def seam_fix(pm, dst, nfw, c, ci):
            K.v("dve", lambda e: e.scalar_tensor_tensor(out=dst.t[:, 256:257], in0=pm.t[:, 255:256],
                                                        scalar=nfw.t[:, c, 0, ci:ci + 1], in1=dst.t[:, 256:257],
                                                        op0=ALU.mult, op1=ALU.add), [pm, nfw, dst], [dst])
            K.v("dve", lambda e: e.scalar_tensor_tensor(out=dst.t[:, 255:256], in0=pm.t[:, 256:257],
                                                        scalar=nfw.t[:, c, 1, ci:ci + 1], in1=dst.t[:, 255:256],
                                                        op0=ALU.mult, op1=ALU.add), [pm, nfw, dst], [dst])

        K.begin_sweep()
        nt = NormT()
        nt.load_gain(W['ab_norm'][0])
        for i in range(2):
            K.st(zero, XT[i][:, :, 0:1].rearrange("c p t -> p c t"), zero.t[:, 0:8].rearrange("p (c t) -> p c t", t=1),
                 slow=True)
            K.st(zero, XT[i][:, :, S + 1:S + 2].rearrange("c p t -> p c t"),
                 zero.t[:, 0:8].rearrange("p (c t) -> p c t", t=1), slow=True)
        xr = K.ring(2, [128, 1024], F32, dma=True)
        for t in range(NT):
            x = xr.next()
            K.ld(x, x.t[:], x_in[t * 128:(t + 1) * 128, :])
            nt.run(x, x.t[:], t, XT[0])

        K.begin_sweep()
        w_in = K.tile([128, 8, 2480], BF16, dma=True)
        K.ld(w_in, w_in.t[:], wb_in.rearrange("(k p) n -> p k n", p=128))
        w_uq = K.tile([128, 2, 768], BF16, dma=True)
        K.ld(w_uq, w_uq.t[:], wb_uq.rearrange("(k p) n -> p k n", p=128))
        w_ukv = K.tile([128, 1024], BF16, dma=True)
        K.ld(w_ukv, w_ukv.t[:], wb_ukv)
        qg = K.tile([128, 256], F32, dma=True)
        bc_load(qg, W['mla_q_norm'][0], 256)
        kvg = K.tile([128, 128], F32, dma=True)
        bc_load(kvg, W['mla_kv_norm'][0], 128)
        gbias = K.tile([128, 16], F32, dma=True)
        bc_load(gbias, W['mlstm_gate_bias'][0], 16)
        winr = K.ring(2, [128, 8, 514], BF16, dma=True)
        cgr = K.ring(2, [128, 512], F32)
        qkr = K.ring(2, [128, 512], BF16, dma=True)
        junk = K.tile([128, 256], BF16)
        ssr = K.ring(2, [128, 8], F32)
        cqn = K.ring(2, [128, 384], BF16)
        cTr = K.ring(2, [128, 3, 128], BF16)
        cosr = K.ring(2, [128, 32], F32, dma=True)
        krr = K.ring(2, [128, 32], BF16)
        tmp = K.ring(2, [128, 8, 64], F32)
        qst = K.ring(2, [128, 8, 96], BF16)
        qfr = K.ring(2, [128, 768], F32)
        kst = K.ring(2, [128, 8, 96], BF16)
        vst = K.ring(2, [128, 8, 65], BF16, dma=True)
        qTs = K.ring(2, [96, 8, 128], BF16, dma=True)
        kTs = K.ring(2, [96, 8, 128], BF16, dma=True)
        vmst = K.ring(2, [128, 4, 129], BF16, dma=True)
        sor = K.ring(2, [128, 512], BF16, dma=True)
        gtr = K.ring(2, [128, 16], F32, dma=True)
        ge = K.ring(2, [128, 2, 4], F32)
        for tl in vst.tiles + vmst.tiles:
            K.v("dve", lambda e, tl=tl: e.memset(tl.t[:], 1.0), [], [tl])
        mkr = K.ring(2, [128, 4], F32, dma=True)
        padb = Buf("xtpad")
        nfw_m = None
        if prompt:
            P.op("pool", lambda e, S=S: e.dma_start(out=XT[0][:, :, 0:1], in_=XT[0][:, :, S:S + 1],
                                                    allow_slow_non_contiguous=True), writes=[padb], dsem=wsem)
            P.op("pool", lambda e, S=S: e.dma_start(out=XT[0][:, :, S + 1:S + 2], in_=XT[0][:, :, 1:2],
                                                    allow_slow_non_contiguous=True), writes=[padb], dsem=wsem)
            nfw_m = seam_setup(cw_m, 8, 4)
        K.pgen = Ring(K.banks[0:6])
        for b in range(NB):
            win = winr.next()
            K.ld(win, win.t[:], XT[0][:, :, 512 * b:512 * b + 514].rearrange("c p t -> p c t"), reads=[padb])
            for c in range(8):
                if c < 4 and b >= NBo:
                    continue
                col0 = 416 + c * 128
                pm = K.pgen.next()
                hsl = 0
                halo = K.banks[6 + (c % 2)]
                for k in range(8):
                    K.mm(pm, pm.t[:, :], w_in, w_in.t[:, k, col0:col0 + 128], win, win.t[:, k, 1:513],
                         start=(k == 0), stop=(k == 7))
                for k in range(8):
                    K.mm(halo, halo.t[:, hsl:hsl + 2], w_in, w_in.t[:, k, col0:col0 + 128], win,
                         win.t[:, k, 0:514:513], start=(k == 0), stop=(k == 7))
                cg = cgr.next()
                conv_chunk(pm, halo, hsl, cw_m, c, cg)
                if b in SEAM_BLOCKS:
                    seam_fix(pm, cg, nfw_m, b // 4, c)
                qk = qkr.next()
                K.act(qk, qk.t[:], [cg], cg.t[:], AF.Silu)
                K.st(qk, QKT[c, :, 512 * b:512 * b + 512], qk.t[:])
            for tt in range(4):
                t = b * 4 + tt
                lo = 1 + tt * 128
                pa = K.pgen.next()
                pv = K.pgen.next()
                po = K.pgen.next()
                for k in range(8):
                    K.mm(pa, pa.t[:, 0:416], win, win.t[:, k, lo:lo + 128], w_in, w_in.t[:, k, 0:416],
                         start=(k == 0), stop=(k == 7))
                for k in range(8):
                    K.mm(pa, pa.t[:, 416:432], win, win.t[:, k, lo:lo + 128], w_in, w_in.t[:, k, 2464:2480],
                         start=(k == 0), stop=(k == 7))
                for k in range(8):
                    K.mm(pv, pv.t[:, :], win, win.t[:, k, lo:lo + 128], w_in, w_in.t[:, k, 1440:1952],
                         start=(k == 0), stop=(k == 7))
                is_own = t < NTo
                if is_own:
                    for k in range(8):
                        K.mm(po, po.t[:, :], win, win.t[:, k, lo:lo + 128], w_in, w_in.t[:, k, 1952:2464],
                             start=(k == 0), stop=(k == 7))
                vm = vmst.next()
                K.act(vm, vm.t[:, :, 0:128], [pv], pv.t[:, :].rearrange("p (h d) -> p h d", d=128), AF.Copy)
                K.st(vm, VM[t * 128:(t + 1) * 128, :, :], vm.t[:])
                if is_own:
                    so = sor.next()
                    K.act(so, so.t[:], [po], po.t[:, :], AF.Sigmoid)
                    K.st(so, SO[t * 128:(t + 1) * 128, :], so.t[:])
                gt = gtr.next()
                g2 = ge.next()
                K.v("dve", lambda e, gt=gt, pa=pa: e.tensor_tensor(out=gt.t[:], in0=pa.t[:, 416:432], in1=gbias.t[:],
                                                                   op=ALU.add), [pa, gbias], [gt])
                gtv = gt.t[:, :].rearrange("p (a b) -> p a b", b=8)[:, :, 4:8]
                K.act(g2, g2.t[:], [gt], gtv, AF.Exp, scale=-1.0)
                K.v("dve", lambda e, g2=g2: e.tensor_scalar_add(out=g2.t[:], in0=g2.t[:], scalar1=1.0), [g2], [g2])
                K.act(g2, g2.t[:], [g2], g2.t[:], AF.Ln)
                K.v("dve", lambda e, g2=g2, gtv=gtv: e.tensor_scalar_mul(out=gtv, in0=g2.t[:], scalar1=-1.0), [g2], [gt])
                mk = mkr.next()
                K.ld(mk, mk.t[:], mkin[jb][t * 128:(t + 1) * 128, :])
                for (c0, mi) in ((0, 0), (8, 2)):
                    K.v("dve", lambda e, gt=gt, mk=mk, c0=c0, mi=mi: e.scalar_tensor_tensor(
                        out=gt.t[:, c0:c0 + 4], in0=gt.t[:, c0:c0 + 4], scalar=mk.t[:, mi:mi + 1],
                        in1=mk.t[:, mi + 1:mi + 2].to_broadcast([128, 4]), op0=ALU.mult, op1=ALU.add), [gt, mk], [gt])
                    K.v("dve", lambda e, gt=gt, mk=mk, c0=c0, mi=mi: e.tensor_scalar_mul(
                        out=gt.t[:, c0 + 4:c0 + 8], in0=gt.t[:, c0 + 4:c0 + 8], scalar1=mk.t[:, mi:mi + 1]), [gt, mk], [gt])
                K.st(gt, GG[t * 128:(t + 1) * 128, :], gt.t[:])
                ss = ssr.next()
                K.v("dve", lambda e, ss=ss: e.memset(ss.t[:, 0:2], 0.0), [], [ss])
                K.act(junk, junk.t[:, 0:256], [pa, ss], pa.t[:, 0:256], AF.Square, accum=ss.t[:, 0:1], extra_w=[ss])
                K.act(junk, junk.t[:, 0:128], [pa, ss], pa.t[:, 256:384], AF.Square, accum=ss.t[:, 1:2], extra_w=[ss])
                K.v("dve", lambda e, ss=ss: e.tensor_scalar(out=ss.t[:, 2:3], in0=ss.t[:, 0:1], scalar1=1.0 / 256,
                                                            scalar2=EPS, op0=ALU.mult, op1=ALU.add), [ss], [ss])
                K.v("dve", lambda e, ss=ss: e.tensor_scalar(out=ss.t[:, 3:4], in0=ss.t[:, 1:2], scalar1=1.0 / 128,
                                                            scalar2=EPS, op0=ALU.mult, op1=ALU.add), [ss], [ss])
                K.act(ss, ss.t[:, 4:6], [ss], ss.t[:, 2:4], AF.Sqrt)
                K.v("dve", lambda e, ss=ss: e.reciprocal(out=ss.t[:, 6:8], in_=ss.t[:, 4:6]), [ss], [ss])
                cq = cqn.next()
                K.v("dve", lambda e, ss=ss, cq=cq, pa=pa: e.scalar_tensor_tensor(
                    out=cq.t[:, 0:256], in0=pa.t[:, 0:256], scalar=ss.t[:, 6:7], in1=qg.t[:], op0=ALU.mult,
                    op1=ALU.mult), [pa, ss, qg], [cq])
                K.v("dve", lambda e, ss=ss, cq=cq, pa=pa: e.scalar_tensor_tensor(
                    out=cq.t[:, 256:384], in0=pa.t[:, 256:384], scalar=ss.t[:, 7:8], in1=kvg.t[:], op0=ALU.mult,
                    op1=ALU.mult), [pa, ss, kvg], [cq])
                cs = cosr.next()
                K.ld(cs, cs.t[:, 0:32], csin[jb][t * 128:(t + 1) * 128, :])
                kr = krr.next()
                tm = tmp.next()

                def rope(dst1, dst2, x1, x2, cosb, sinb, ta, tb, ins, outs, tmt):
                    K.v("dve", lambda e: e.tensor_tensor(out=ta, in0=x1, in1=cosb, op=ALU.mult), ins, [tmt])
                    K.v("dve", lambda e: e.tensor_tensor(out=tb, in0=x2, in1=sinb, op=ALU.mult), ins, [tmt])
                    K.v("dve", lambda e: e.tensor_tensor(out=dst1, in0=ta, in1=tb, op=ALU.subtract), [tmt], outs)
                    K.v("dve", lambda e: e.tensor_tensor(out=ta, in0=x2, in1=cosb, op=ALU.mult), ins + outs, [tmt])
                    K.v("dve", lambda e: e.tensor_tensor(out=tb, in0=x1, in1=sinb, op=ALU.mult), ins, [tmt])
                    K.v("dve", lambda e: e.tensor_tensor(out=dst2, in0=ta, in1=tb, op=ALU.add), [tmt], outs)

                rope(kr.t[:, 0:16], kr.t[:, 16:32], pa.t[:, 384:400], pa.t[:, 400:416], cs.t[:, 0:16], cs.t[:, 16:32],
                     tm.t[:, 0, 0:16], tm.t[:, 0, 16:32], [pa, cs], [kr], tm)
                pb = K.pgen.next()
                pbv = pb.t[:].bitcast(BF16).rearrange("p (c t) -> p c t", c=8)
                for c in range(3):
                    K.tr(pb, pbv[:, c, :], cq, cq.t[:, c * 128:(c + 1) * 128], ident, ident.t[:])
                cT = cTr.next()
                K.act(cT, cT.t[:], [pb], pbv[:, 0:3, :], AF.Copy)
                if is_own:
                    pq1 = K.pgen.next()
                    pq2 = K.pgen.next()
                    for k in range(2):
                        K.mm(pq1, pq1.t[:, 0:480], cT, cT.t[:, k, :], w_uq, w_uq.t[:, k, 0:480], start=(k == 0), stop=(k == 1))
                    for k in range(2):
                        K.mm(pq2, pq2.t[:, 0:288], cT, cT.t[:, k, :], w_uq, w_uq.t[:, k, 480:768], start=(k == 0), stop=(k == 1))
                    qs = qst.next()
                    qf = qfr.next()
                    K.act(qf, qf.t[:, 0:480], [pq1], pq1.t[:, 0:480], AF.Copy)
                    K.act(qf, qf.t[:, 480:768], [pq2], pq2.t[:, 0:288], AF.Copy)
                    qfv = qf.t[:, :].rearrange("p (h d) -> p h d", d=96)
                    K.v("dve", lambda e, qs=qs, qfv=qfv: e.tensor_copy(out=qs.t[:, :, 0:64], in_=qfv[:, :, 0:64]), [qf], [qs])
                    cosb = cs.t[:, 0:16].unsqueeze(1).to_broadcast([128, 8, 16])
                    sinb = cs.t[:, 16:32].unsqueeze(1).to_broadcast([128, 8, 16])
                    rope(qs.t[:, :, 64:80], qs.t[:, :, 80:96], qfv[:, :, 64:80], qfv[:, :, 80:96],
                         cosb, sinb, tm.t[:, :, 0:16], tm.t[:, :, 16:32], [qf, cs], [qs], tm)
                pk1 = K.pgen.next()
                pk2 = K.pgen.next()
                K.mm(pk1, pk1.t[:, :], cT, cT.t[:, 2, :], w_ukv, w_ukv.t[:, 0:512])
                K.mm(pk2, pk2.t[:, :], cT, cT.t[:, 2, :], w_ukv, w_ukv.t[:, 512:1024])
                ks = kst.next()
                vs = vst.next()
                for (pk, h0) in ((pk1, 0), (pk2, 4)):
                    pkv = pk.t[:, :].rearrange("p (h d) -> p h d", d=128)
                    K.act(ks, ks.t[:, h0:h0 + 4, 0:64], [pk], pkv[:, :, 0:64], AF.Copy)
                    K.v("dve", lambda e, vs=vs, pkv=pkv, h0=h0: e.tensor_copy(out=vs.t[:, h0:h0 + 4, 0:64],
                                                                             in_=pkv[:, :, 64:128]), [pk], [vs])
                K.v("dve", lambda e, ks=ks, kr=kr: e.tensor_copy(
                    out=ks.t[:, :, 64:96], in_=kr.t[:, :].unsqueeze(1).to_broadcast([128, 8, 32])), [kr], [ks])
                K.st(vs, VV[t * 128:(t + 1) * 128, :, :], vs.t[:])
                for (src, dstr, dd) in (((qs, qTs, QT), (ks, kTs, KT)) if is_own else ((ks, kTs, KT),)):
                    pb = K.pgen.next()
                    pbv = pb.t[:].bitcast(BF16).rearrange("p (c t) -> p c t", c=8)
                    for h in range(8):
                        K.tr(pb, pbv[0:96, h, :], src, src.t[:, h, :], ident, ident.t[:])
                    dT = dstr.next()
                    K.act(dT, dT.t[:], [pb], pbv[0:96, :, :], AF.Copy)
                    K.st(dT, dd[:, :, t * 128:(t + 1) * 128].rearrange("h p t -> p h t"), dT.t[:])
        K.pgen = Ring(K.banks[0:6])

        K.begin_sweep()
        kth = K.ring(2, [96, S], BF16, dma=True)
        vh = K.ring(2, [128, NT, 65], BF16, dma=True)
        qbr = K.ring(2, [96, 512], BF16, dma=True)
        ptr = K.ring(3, [128, 512], BF16)
        rcr = K.ring(2, [128, 4], F32)
        yar = K.ring(2, [128, 4, 64], BF16, dma=True)
        for h in range(8):
            kt = kth.next()
            K.ld(kt, kt.t[:], KT[h, :, 0:S])
            vv = vh.next()
            K.ld(vv, vv.t[:], VV[0:S, h, :].rearrange("(n p) d -> p n d", p=128))
            for qb in range(NBo):
                qt = qbr.next()
                K.ld(qt, qt.t[:], QT[h, :, qb * 512:(qb + 1) * 512])
                acc = K.pacc.next()

                def scores(kb, kt=kt, qt=qt):
                    ps = K.pgen.next()
                    K.mm(ps, ps.t[:, :], kt, kt.t[:, kb * 128:(kb + 1) * 128], qt, qt.t[:, :])
                    return ps

                LOOK = 2
                pend = [scores(kb) for kb in range(min(LOOK, NT))]
                for kb in range(NT):
                    ps = pend.pop(0)
                    if kb + LOOK < NT:
                        pend.append(scores(kb + LOOK))
                    pt = ptr.next()
                    K.act(pt, pt.t[:], [ps], ps.t[:, :], AF.Exp, scale=SC_ATT)
                    for j in range(4):
                        K.mm(acc, acc.t[:, j * 65:(j + 1) * 65], pt, pt.t[:, j * 128:(j + 1) * 128], vv, vv.t[:, kb, :],
                             start=(kb == 0 and j == 0), stop=(kb == NT - 1 and j == 3), skip=True)
                accv = acc.t[:, 0:260].rearrange("p (j d) -> p j d", d=65)
                rc = rcr.next()
                K.v("dve", lambda e, rc=rc, accv=accv: e.reciprocal(out=rc.t[:, :].unsqueeze(2), in_=accv[:, :, 64:65]),
                    [acc], [rc])
                ya = yar.next()
                K.v("dve", lambda e, rc=rc, accv=accv, ya=ya: e.tensor_tensor(
                    out=ya.t[:], in0=accv[:, :, 0:64], in1=rc.t[:, :].unsqueeze(2).to_broadcast([128, 4, 64]),
                    op=ALU.mult), [acc, rc], [ya])
                K.st(ya, YY[qb * 512:(qb + 1) * 512, h * 64:(h + 1) * 64].rearrange("(j p) d -> p j d", p=128), ya.t[:])

        for direction in (0, 1):
            K.begin_sweep()
            tri = triu if direction == 0 else tril
            gi0 = 0 if direction == 0 else 8
            qkl = K.ring(2, [128, 8, 128], BF16, dma=True)
            vml = K.ring(2, [128, 4, 129], BF16, dma=True)
            gl = K.ring(2, [128, 16], F32, dma=True)
            ktok = K.ring(2, [128, 4, 128], BF16)
            sm = K.ring(2, [128, 6, 4], F32)
            dmr = K.ring(2, [128, 128], F32)
            ptm = K.ring(2, [128, 128], BF16)
            itr = K.ring(2, [128, 129], F32)
            totr = K.ring(2, [128, 129], F32)
            dnr = K.ring(2, [128, 2], F32)
            vwr = K.ring(2, [128, 129], BF16)
            ST = [K.tile([128, 129], F32) for _ in range(4)]
            STb = [K.tile([128, 129], BF16) for _ in range(4)]
            hfr = K.ring(2, [128, 512], F32, dma=True)
            if direction == 1:
                sol = K.ring(2, [128, 512], BF16, dma=True)
                og = K.tile([128, 512], F32, dma=True)
                bc_load(og, W['mlstm_out_norm'][0], 512)
                sq = K.tile([128, 512], F32)
                ms = K.ring(2, [128, 3, 4], F32)
                ymr = K.ring(2, [128, 512], BF16, dma=True)
            for h in range(4):
                K.v("dve", lambda e, h=h: e.memset(ST[h].t[:], 0.0), [], [ST[h]])
                K.v("dve", lambda e, h=h: e.memset(STb[h].t[:], 0.0), [], [STb[h]])
            own_c = list(range(NTo))
            ctx_c = list(range(NTo, NT))
            if direction == 0:
                order = [(n, True) for n in ctx_c] + [(n, False) for n in own_c]
            else:
                order = [(n, True) for n in reversed(ctx_c)] + [(n, False) for n in reversed(own_c)]
            for n, ctx_only in order:
                r0, r1 = n * 128, (n + 1) * 128
                qk = qkl.next()
                K.ld(qk, qk.t[:], QKT[:, :, r0:r1].rearrange("c p t -> p c t"))
                vm = vml.next()
                K.ld(vm, vm.t[:], VM[r0:r1, :, :])
                g = gl.next()
                K.ld(g, g.t[:], GG[r0:r1, :])
                hf = hfr.next()
                if direction == 1 and not ctx_only:
                    K.ld(hf, hf.t[:], HF[r0:r1, :])
                    so = sol.next()
                    K.ld(so, so.t[:], SO[r0:r1, :])
                if ctx_only:
                    pass
                li = g.t[:, gi0:gi0 + 4]
                lf = g.t[:, gi0 + 4:gi0 + 8]
                pb = K.pgen.next()
                pbv = pb.t[:].bitcast(BF16).rearrange("p (c t) -> p c t", c=8)
                for h in range(4):
                    K.tr(pb, pbv[:, h, :], qk, qk.t[:, 4 + h, :], ident, ident.t[:])
                kk = ktok.next()
                K.act(kk, kk.t[:], [pb], pbv[:, 0:4, :], AF.Copy)
                pc = K.pgen.next()
                K.mm(pc, pc.t[:, 0:4], tri, tri.t[:], g, lf)
                K.mm(pc, pc.t[:, 4:8], ones, ones.t[:], g, lf)
                s_ = sm.next()
                K.v("dve", lambda e, s_=s_, pc=pc, li=li: e.tensor_tensor(out=s_.t[:, 0, :], in0=li, in1=pc.t[:, 0:4],
                                                                         op=ALU.subtract), [g, pc], [s_])
                K.v("dve", lambda e, s_=s_, pc=pc: e.scalar_tensor_tensor(
                    out=s_.t[:, 1, :], in0=pc.t[:, 4:8], scalar=LNSC_M, in1=s_.t[:, 0, :], op0=ALU.add, op1=ALU.add),
                    [pc, s_], [s_])
                K.act(s_, s_.t[:, 2:4, :], [pc], pc.t[:, 0:8].rearrange("p (a b) -> p a b", b=4), AF.Exp)
                K.act(s_, s_.t[:, 4, :], [s_], s_.t[:, 1, :], AF.Exp)
                for h in range(4):
                    if not ctx_only:
                        pB = K.pgen.next()
                        K.mm(pB, pB.t[:, 0:128], g, lf[:, h:h + 1].to_broadcast([128, 128]), tri, tri.t[:])
                        dm = dmr.next()
                        K.act(dm, dm.t[:], [pB, s_], pB.t[:, 0:128], AF.Exp, bias=s_.t[:, 0, h:h + 1])
                        K.v("dve", lambda e, dm=dm, tri=tri: e.tensor_tensor(out=dm.t[:], in0=dm.t[:], in1=tri.t[:], op=ALU.mult),
                            [dm, tri], [dm])
                        pS = K.pgen.next()
                        K.mm(pS, pS.t[:, 0:128], qk, qk.t[:, 4 + h, :], qk, qk.t[:, h, :])
                        pt = ptm.next()
                        K.v("dve", lambda e, pS=pS, dm=dm, pt=pt: e.scalar_tensor_tensor(
                            out=pt.t[:], in0=pS.t[:, 0:128], scalar=SC_M, in1=dm.t[:], op0=ALU.mult, op1=ALU.mult),
                            [pS, dm], [pt])
                        if DEBUG and direction == 0 and n == 0 and h == 0:
                            P.op("sp", lambda e, dm=dm: e.dma_start(out=DBG_dm, in_=dm.t[:]), reads=[dm.b], dsem=wsem)
                            P.op("sp", lambda e, pt=pt: e.dma_start(out=DBG_pt, in_=pt.t[:]), reads=[pt.b], dsem=wsem)
                            P.op("sp", lambda e, s_=s_: e.dma_start(out=DBG_s, in_=s_.t[:, :, :].rearrange("p a b -> p (a b)")), reads=[s_.b], dsem=wsem)
                            P.op("sp", lambda e, kk=kk: e.dma_start(out=DBG_kk, in_=kk.t[:, 0, :]), reads=[kk.b], dsem=wsem)
                        pN = K.pgen.next()
                        K.mm(pN, pN.t[:, 0:129], pt, pt.t[:], vm, vm.t[:, h, :])
                        pI = K.pgen.next()
                        K.mm(pI, pI.t[:, 0:129], qk, qk.t[:, h, :], STb[h], STb[h].t[:])
                        it = itr.next()
                        K.act(it, it.t[:], [pI, s_], pI.t[:, 0:129], AF.Copy, scale=s_.t[:, 2, h:h + 1])
                        tot = totr.next()
                        K.v("dve", lambda e, tot=tot, pN=pN, it=it: e.tensor_tensor(out=tot.t[:], in0=pN.t[:, 0:129],
                                                                                   in1=it.t[:], op=ALU.add), [pN, it], [tot])
                        if DEBUG and direction == 0 and n == 0 and h == 0:
                            P.op("sp", lambda e, tot=tot: e.dma_start(out=DBG_tot, in_=tot.t[:]), reads=[tot.b], dsem=wsem)
                        dn = dnr.next()
                        K.v("dve", lambda e, tot=tot, dn=dn: e.tensor_scalar_mul(out=dn.t[:, 1:2], in0=tot.t[:, 128:129],
                                                                                 scalar1=-1.0), [tot], [dn])
                        K.v("dve", lambda e, tot=tot, dn=dn: e.tensor_tensor(out=dn.t[:, 0:1], in0=tot.t[:, 128:129],
                                                                             in1=dn.t[:, 1:2], op=ALU.max), [tot, dn], [dn])
                        K.v("dve", lambda e, dn=dn: e.tensor_scalar_max(out=dn.t[:, 0:1], in0=dn.t[:, 0:1], scalar1=1.0),
                            [dn], [dn])
                        K.v("dve", lambda e, dn=dn: e.reciprocal(out=dn.t[:, 1:2], in_=dn.t[:, 0:1]), [dn], [dn])
                        hs_ap = hf.t[:, h * 128:(h + 1) * 128]
                        if direction == 0:
                            K.v("dve", lambda e, tot=tot, dn=dn, hs_ap=hs_ap: e.tensor_scalar_mul(
                                out=hs_ap, in0=tot.t[:, 0:128], scalar1=dn.t[:, 1:2]), [tot, dn], [hf])
                        else:
                            K.v("dve", lambda e, tot=tot, dn=dn, hs_ap=hs_ap: e.scalar_tensor_tensor(
                                out=hs_ap, in0=tot.t[:, 0:128], scalar=dn.t[:, 1:2], in1=hs_ap, op0=ALU.mult, op1=ALU.add),
                                [tot, dn, hf], [hf])
                    vw = vwr.next()
                    K.v("dve", lambda e, vw=vw, vm=vm, s_=s_, h=h: e.tensor_scalar_mul(
                        out=vw.t[:], in0=vm.t[:, h, :], scalar1=s_.t[:, 4, h:h + 1]), [vm, s_], [vw])
                    pC = K.pgen.next()
                    K.mm(pC, pC.t[:, 0:129], kk, kk.t[:, h, :], vw, vw.t[:])
                    K.v("dve", lambda e, pC=pC, s_=s_, h=h: e.scalar_tensor_tensor(
                        out=ST[h].t[:], in0=ST[h].t[:], scalar=s_.t[:, 3, h:h + 1], in1=pC.t[:, 0:129], op0=ALU.mult,
                        op1=ALU.add), [ST[h], s_, pC], [ST[h]])
                    K.act(STb[h], STb[h].t[:], [ST[h]], ST[h].t[:], AF.Copy)
                if ctx_only:
                    continue
                if direction == 0:
                    K.st(hf, HF[r0:r1, :], hf.t[:])
                else:
                    m_ = ms.next()
                    K.v("dve", lambda e, hf=hf: e.tensor_tensor(out=sq.t[:], in0=hf.t[:], in1=hf.t[:], op=ALU.mult),
                        [hf], [sq])
                    K.v("dve", lambda e, m_=m_: e.tensor_reduce(out=m_.t[:, 0, :],
                                                                 in_=sq.t[:, :].rearrange("p (h d) -> p h d", d=128),
                                                                 axis=AX.X, op=ALU.add), [sq], [m_])
                    K.v("dve", lambda e, m_=m_: e.tensor_scalar(out=m_.t[:, 1, :], in0=m_.t[:, 0, :], scalar1=1.0 / 128,
                                                                scalar2=EPS, op0=ALU.mult, op1=ALU.add), [m_], [m_])
                    K.act(m_, m_.t[:, 2, :], [m_], m_.t[:, 1, :], AF.Sqrt)
                    K.v("dve", lambda e, m_=m_: e.reciprocal(out=m_.t[:, 0, :], in_=m_.t[:, 2, :]), [m_], [m_])
                    K.v("dve", lambda e, m_=m_, hf=hf: e.tensor_tensor(
                        out=sq.t[:, :].rearrange("p (h d) -> p h d", d=128),
                        in0=hf.t[:, :].rearrange("p (h d) -> p h d", d=128),
                        in1=m_.t[:, 0, :].unsqueeze(2).to_broadcast([128, 4, 128]), op=ALU.mult), [hf, m_], [sq])
                    K.v("dve", lambda e: e.tensor_tensor(out=sq.t[:], in0=sq.t[:], in1=og.t[:], op=ALU.mult), [sq, og], [sq])
                    ym = ymr.next()
                    K.v("dve", lambda e, ym=ym, so=so: e.tensor_tensor(out=ym.t[:], in0=sq.t[:], in1=so.t[:], op=ALU.mult),
                        [sq, so], [ym])
                    K.st(ym, YY[r0:r1, 512:1024], ym.t[:])

        def tail(x, pr, t, nt, XTd, final_g=None):
            for half in range(2):
                K.v("dve", lambda e, half=half: e.tensor_tensor(
                    out=x.t[:, half * 512:(half + 1) * 512], in0=x.t[:, half * 512:(half + 1) * 512],
                    in1=pr[half].t[:, :], op=ALU.add), [x, pr[half]], [x])
            if final_g is None:
                K.st(x, X1[t * 128:(t + 1) * 128, :], x.t[:])
                nt.run(x, x.t[:], t, XTd)
            else:
                ss = nt.stats(x, x.t[:], 1024)
                K.v("dve", lambda e, ss=ss: e.scalar_tensor_tensor(out=x.t[:], in0=x.t[:], scalar=ss.t[:, 3:4],
                                                                   in1=final_g.t[:], op0=ALU.mult, op1=ALU.mult),
                    [x, ss, final_g], [x])
                if out_lo <= t < out_hi:
                    K.st(x, y_out[(t - out_lo) * 128:(t - out_lo + 1) * 128, :], x.t[:])

        def project_tm(src_tile, src_ap_k, wt, w_ap_k, nk):
            pr = [K.pgen.next(), K.pgen.next()]
            for half in range(2):
                for k in range(nk):
                    K.mm(pr[half], pr[half].t[:, :], src_tile, src_ap_k(k), wt, w_ap_k(k, half), start=(k == 0),
                         stop=(k == nk - 1))
            return pr

        K.begin_sweep()
        nt = NormT()
        nt.load_gain(W['ffn_norm'][0])
        w_o = K.tile([128, 8, 1024], BF16, dma=True)
        K.ld(w_o, w_o.t[:], wb_out.rearrange("(k p) n -> p k n", p=128))
        xr = K.ring(2, [128, 1024], F32, dma=True)
        yr = K.ring(2, [128, 1024], BF16, dma=True)
        yTr = K.ring(2, [128, 8, 128], BF16)
        for t in range(NTo):
            x = xr.next()
            K.ld(x, x.t[:], x_in[t * 128:(t + 1) * 128, :])
            y = yr.next()
            K.ld(y, y.t[:], YY[t * 128:(t + 1) * 128, :])
            pb = K.pgen.next()
            pbv = pb.t[:].bitcast(BF16).rearrange("p (c t) -> p c t", c=8)
            for c in range(8):
                K.tr(pb, pbv[:, c, :], y, y.t[:, c * 128:(c + 1) * 128], ident, ident.t[:])
            yT = yTr.next()
            K.act(yT, yT.t[:], [pb], pbv, AF.Copy)
            pr = project_tm(yT, lambda k: yT.t[:, k, :], w_o, lambda k, half: w_o.t[:, k, half * 512:(half + 1) * 512], 8)
            tail(x, pr, t, nt, XT[1])

        def ffn_sweep(l, XTs, XTd, gain_next, final):
            K.begin_sweep()
            wup = K.tile([128, 8, 5632], BF16, dma=True)
            for k in range(8):
                K.ld(wup, wup.t[:, k, :], wb_up[l, k * 128:(k + 1) * 128, :])
            wdn = K.tile([128, NFC, 1024], BF16, dma=True)
            K.ld(wdn, wdn.t[:], wb_dn[l].rearrange("(i p) n -> p i n", p=128))
            nt = NormT()
            fg = None
            if final:
                fg = nt.g
            nt.load_gain(gain_next)
            win = K.tile([128, 8, 514], BF16, dma=True)
            hT = K.tile([128, NFC, 512], BF16)
            cgr = K.ring(2, [128, 512], F32)
            cvr = K.ring(2, [128, 512], F32)
            sgr = K.ring(2, [128, 512], BF16)
            xr = K.ring(2, [128, 1024], F32, dma=True)
            K.pgen = Ring(K.banks[0:6])
            cw = cw_f[l]
            hs_i = 0
            nfw_f = seam_setup(cw, 44, 2) if prompt else None
            for b in range(NBo):
                K.ld(win, win.t[:], XTs[:, :, 512 * b:512 * b + 514].rearrange("c p t -> p c t"))
                for i in range(NFC):
                    outs = []
                    for (col0, ci, rr) in ((i * 128, i, cgr), (DFF + i * 128, NFC + i, cvr)):
                        pm = K.pgen.next()
                        hsl = 0
                        halo = K.banks[6 + (hs_i % 2)]
                        hs_i += 1
                        for k in range(8):
                            K.mm(pm, pm.t[:, :], wup, wup.t[:, k, col0:col0 + 128], win, win.t[:, k, 1:513],
                                 start=(k == 0), stop=(k == 7))
                        for k in range(8):
                            K.mm(halo, halo.t[:, hsl:hsl + 2], wup, wup.t[:, k, col0:col0 + 128], win,
                                 win.t[:, k, 0:514:513], start=(k == 0), stop=(k == 7))
                        cc = rr.next()
                        conv_chunk(pm, halo, hsl, cw, ci, cc)
                        if prompt and b in (0, 4):
                            seam_fix(pm, cc, nfw_f, b // 4, ci)
                        outs.append(cc)
                    cg, cv = outs
                    sg = sgr.next()
                    K.act(sg, sg.t[:], [cg], cg.t[:], AF.Silu)
                    K.v("dve", lambda e, sg=sg, cv=cv, i=i: e.tensor_tensor(out=hT.t[:, i, :], in0=sg.t[:], in1=cv.t[:],
                                                                            op=ALU.mult), [sg, cv], [hT])
                for tt in range(4):
                    t = b * 4 + tt
                    x = xr.next()
                    K.ld(x, x.t[:], X1[t * 128:(t + 1) * 128, :])
                    pr = project_tm(hT, lambda k: hT.t[:, k, tt * 128:(tt + 1) * 128], wdn,
                                    lambda k, half: wdn.t[:, k, half * 512:(half + 1) * 512], NFC)
                    tail(x, pr, t, nt, XTd, final_g=fg)
            K.pgen = Ring(K.banks[0:6])

        ffn_sweep(0, XT[1], XT[0], W['c_norm'][0], False)

        K.begin_sweep()
        nt = NormT()
        nt.load_gain(W['ffn_norm'][1])
        cwin = K.tile([128, 8, 2048], BF16, dma=True)
        K.ld(cwin, cwin.t[:], wb_cin.rearrange("(k p) n -> p k n", p=128))
        cwo = K.tile([128, 8, 1024], BF16, dma=True)
        K.ld(cwo, cwo.t[:], wb_cout.rearrange("(k p) n -> p k n", p=128))
        vg = K.tile([128, 1024], F32, dma=True)
        bc_load(vg, W['c_v_norm'][0], 1024)
        xTl = K.ring(2, [128, 8, 128], BF16, dma=True)
        xr = K.ring(2, [128, 1024], F32, dma=True)
        ur = K.ring(2, [128, 2048], F32)
        vnr = K.ring(2, [128, 1024], BF16)
        zr = K.ring(2, [128, 1024], BF16)
        zTr = K.ring(2, [128, 8, 128], BF16)
        for t in range(NTo):
            xT = xTl.next()
            K.ld(xT, xT.t[:], XT[0][:, :, 1 + t * 128:1 + (t + 1) * 128].rearrange("c p t -> p c t"))
            x = xr.next()
            K.ld(x, x.t[:], X1[t * 128:(t + 1) * 128, :])
            uv = ur.next()
            for j in range(4):
                pu = K.pgen.next()
                for k in range(8):
                    K.mm(pu, pu.t[:, :], xT, xT.t[:, k, :], cwin, cwin.t[:, k, j * 512:(j + 1) * 512], start=(k == 0),
                         stop=(k == 7))
                K.act(uv, uv.t[:, j * 512:(j + 1) * 512], [pu], pu.t[:, :], AF.Gelu)
            ss = nt.stats(uv, uv.t[:, 1024:2048], 1024)
            vn = vnr.next()
            K.v("dve", lambda e, ss=ss, uv=uv, vn=vn: e.scalar_tensor_tensor(
                out=vn.t[:], in0=uv.t[:, 1024:2048], scalar=ss.t[:, 3:4], in1=vg.t[:], op0=ALU.mult, op1=ALU.mult),
                [uv, ss, vg], [vn])
            z = zr.next()
            psv = [K.pgen.next(), K.pgen.next()]
            for g in range(8):
                pp = psv[g // 4]
                K.mm(pp, pp.t[:, (g % 4) * 128:(g % 4 + 1) * 128], wsT, wsT.t[:, g, :], vn, vn.t[:, g * 128:(g + 1) * 128])
            for g in range(8):
                pp = psv[g // 4]
                K.v("dve", lambda e, pp=pp, g=g, uv=uv, z=z: e.scalar_tensor_tensor(
                    out=z.t[:, g * 128:(g + 1) * 128], in0=pp.t[:, (g % 4) * 128:(g % 4 + 1) * 128],
                    scalar=bsT.t[:, g:g + 1], in1=uv.t[:, g * 128:(g + 1) * 128], op0=ALU.add, op1=ALU.mult),
                    [pp, bsT, uv], [z])
            if DEBUG and t == 0:
                P.op("sp", lambda e: e.dma_start(out=DBG_wsT, in_=wsT.t[:, :, :].rearrange("p a b -> p (a b)")), reads=[wsT.b], dsem=wsem)
                P.op("sp", lambda e: e.dma_start(out=DBG_bsT, in_=bsT.t[:]), reads=[bsT.b], dsem=wsem)
                P.op("sp", lambda e, uv=uv: e.dma_start(out=DBG_uv, in_=uv.t[:]), reads=[uv.b], dsem=wsem)
                P.op("sp", lambda e, vn=vn: e.dma_start(out=DBG_vn, in_=vn.t[:]), reads=[vn.b], dsem=wsem)
                P.op("sp", lambda e, z=z: e.dma_start(out=DBG_z, in_=z.t[:]), reads=[z.b], dsem=wsem)
            pb = K.pgen.next()
            pbv = pb.t[:].bitcast(BF16).rearrange("p (c t) -> p c t", c=8)
            for c in range(8):
                K.tr(pb, pbv[:, c, :], z, z.t[:, c * 128:(c + 1) * 128], ident, ident.t[:])
            zT = zTr.next()
            K.act(zT, zT.t[:], [pb], pbv, AF.Copy)
            pr = project_tm(zT, lambda k: zT.t[:, k, :], cwo, lambda k, half: cwo.t[:, k, half * 512:(half + 1) * 512], 8)
            tail(x, pr, t, nt, XT[1])

        ffn_sweep(1, XT[1], XT[0], W['final_norm'], True)

    P.emit()
    nc._dbg_names = dbg_names
    return nc


BIG = 30000.0
OWN_EXT = 2560
HALO = 256


def make_consts():
    return {
        "c_ident": np.eye(128, dtype=np.float32),
        "c_triu": np.triu(np.ones((128, 128), np.float32)),
        "c_tril": np.tril(np.ones((128, 128), np.float32)),
        "c_ones": np.ones((128, 128), np.float32),
    }


def rope_table(pos):
    inv = (1.0 / (np.float32(10000.0) ** (np.arange(0, 32, 2, dtype=np.float32) / np.float32(32)))).astype(np.float32)
    ang = pos.astype(np.float32)[:, None] * inv[None, :]
    return np.concatenate([np.cos(ang), np.sin(ang)], axis=1).astype(np.float32)


def mask_table(mf, mb):
    mf = mf.astype(np.float32)
    mb = mb.astype(np.float32)
    return np.stack([mf, (mf - 1) * BIG, mb, (mb - 1) * BIG], axis=1).astype(np.float32)


def aux_full(S):
    ones = np.ones(S, np.float32)
    return rope_table(np.arange(S)), mask_table(ones, ones)


def aux_quarter(S, j):
    q = S // 4
    shift = j * q - HALO
    r = np.arange(S)
    t = r + shift
    pos = t % S
    own = r < OWN_EXT
    valid = (t >= 0) & (t < S)
    t_start = max(shift, 0)
    t_end = min(shift + OWN_EXT, S)
    mf = np.where(own, valid, pos < t_start)
    mb = np.where(own, valid, pos >= t_end)
    sf = np.zeros((1, 4), np.float32)
    sf[0, (4 - j) % 4] = -1.0
    return shift, rope_table(pos), mask_table(mf, mb), sf


_CACHE = {}


def run(core_inputs, params, jobs):
    jobs = norm_jobs(jobs)
    key = repr(jobs)
    if key not in _CACHE:
        _CACHE[key] = build_program(jobs)
    nc = _CACHE[key]
    consts = make_consts()
    in_maps = []
    for ci in core_inputs:
        m = {k: np.ascontiguousarray(v, dtype=np.float32) for k, v in ci.items()}
        for j, jd in enumerate(jobs):
            if ("cs%d" % j) not in m:
                cs, mk = aux_full(jd['S'])
                m["cs%d" % j] = cs
                m["mk%d" % j] = mk
        if "sf" not in m:
            m["sf"] = np.zeros((1, 4), np.float32)
        for n in PARAMS:
            m[n] = np.ascontiguousarray(params[n], dtype=np.float32)
        m.update(consts)
        in_maps.append(m)
    res = run_bass_kernel_spmd(nc, in_maps, core_ids=list(range(len(core_inputs))))
    global LAST_RES
    LAST_RES = res.results
    return [[r["y%d" % j] for j in range(len(jobs))] for r in res.results]


def kernel(**inputs):
    xp = np.asarray(inputs['x_prompt'], dtype=np.float32)
    xs = np.asarray(inputs['x_sample'], dtype=np.float32)
    params = {n: np.asarray(inputs[n], dtype=np.float32) for n in PARAMS}
    SP = xp.shape[1]
    jobs = [4096, 4096, dict(S=SP, NTo=OWN_EXT // 128, out=(HALO // 128, (OWN_EXT - HALO) // 128), prompt=True)]
    core_inputs = []
    for c in range(8):
        p, j = c // 4, c % 4
        shift, cs, mk, sf = aux_quarter(SP, j)
        core_inputs.append({"x0": xs[2 * c], "x1": xs[2 * c + 1], "x2": np.roll(xp[p], -shift, axis=0),
                            "cs2": cs, "mk2": mk, "sf": sf})
    outs = run(core_inputs, params, jobs)
    q = SP // 4
    y_prompt = np.zeros_like(xp)
    for c in range(8):
        p, j = c // 4, c % 4
        y_prompt[p, j * q:(j + 1) * q] = outs[c][2]
    y_sample = np.stack([outs[c][j] for c in range(8) for j in range(2)], axis=0)
    return (y_prompt, y_sample)
```

```python
import math
import numpy as np
import concourse.bass as bass
import concourse.mybir as mybir
from concourse.bass_utils import run_bass_kernel_spmd

F32 = mybir.dt.float32
BF16 = mybir.dt.bfloat16
ALU = mybir.AluOpType
AF = mybir.ActivationFunctionType
AX = mybir.AxisListType

D = 1024
EPS = 1e-6
DFF = 2816
NFC = 22
SC_ATT = 96 ** -0.5
SC_M = 128 ** -0.5
LNSC_M = math.log(SC_M)

PARAMS = ['ab_norm', 'ab_w_in', 'mla_q_norm', 'mla_w_uq', 'mla_kv_norm', 'mla_w_ukv', 'mlstm_conv_w',
          'mlstm_conv_b', 'mlstm_gate_bias', 'mlstm_out_norm', 'ab_w_out', 'c_norm', 'c_w_in', 'c_v_norm',
          'c_w_spatial', 'c_b_spatial', 'c_w_out', 'ffn_norm', 'ffn_w_up', 'ffn_conv_w', 'ffn_conv_b',
          'ffn_w_down', 'final_norm']
PSHAPES = {
    'ab_norm': [1, 1024], 'ab_w_in': [1, 1024, 2480], 'mla_q_norm': [1, 256], 'mla_w_uq': [1, 256, 768],
    'mla_kv_norm': [1, 128], 'mla_w_ukv': [1, 128, 1024], 'mlstm_conv_w': [1, 3, 1024], 'mlstm_conv_b': [1, 1024],
    'mlstm_gate_bias': [1, 16], 'mlstm_out_norm': [1, 512], 'ab_w_out': [1, 1024, 1024], 'c_norm': [1, 1024],
    'c_w_in': [1, 1024, 2048], 'c_v_norm': [1, 1024], 'c_w_spatial': [1, 8, 128, 128], 'c_b_spatial': [1, 8, 128],
    'c_w_out': [1, 1024, 1024], 'ffn_norm': [2, 1024], 'ffn_w_up': [2, 1024, 5632], 'ffn_conv_w': [2, 3, 5632],
    'ffn_conv_b': [2, 5632], 'ffn_w_down': [2, 2816, 1024], 'final_norm': [1024],
}


class Buf:
    __slots__ = ("name", "w", "r", "excl")

    def __init__(self, name, excl=False):
        self.name = name
        self.w = None
        self.r = []
        self.excl = excl


class Prog:
    ENGS = ("pe", "act", "dve", "pool", "sp")

    def __init__(self, nc, n_dma_sems=80):
        self.nc = nc
        self.ops = {e: [] for e in self.ENGS}
        self.count = {}
        self.seen = {e: {} for e in self.ENGS}
        self.pending = {e: [] for e in self.ENGS}
        self.dma_sems = ["d%d" % i for i in range(n_dma_sems)]
        self.nops = 0
        self.disabled = False
        self.sweep_no = 0
        import os
        self.maxops = int(os.environ.get("K_OPS", "1000000000"))

    def op(self, eng, fn, reads=(), writes=(), dsem=None):
        if self.disabled or self.nops >= self.maxops:
            return None
        deps = []
        for b in reads:
            if b.w is not None:
                deps.append(b.w)
            if b.excl:
                for t in b.r:
                    if t[0] != eng:
                        deps.append(t)
        for b in writes:
            if b.w is not None:
                deps.append(b.w)
            deps.extend(b.r)
        if dsem is None:
            key, inc = eng, 1
        else:
            key, inc = dsem, 16
        self.count[key] = self.count.get(key, 0) + inc
        tok = (key, self.count[key])
        seen = self.seen[eng]
        waits = []
        if self.pending[eng]:
            for k, v in self.pending[eng]:
                if seen.get(k, 0) < v:
                    seen[k] = v
                    waits.append((k, v))
            self.pending[eng] = []
        for k, v in deps:
            if k == "pe" and eng == "pe" and dsem is None:
                continue
            if seen.get(k, 0) >= v:
                continue
            seen[k] = v
            waits.append((k, v))
        self.ops[eng].append((fn, waits, key, inc))
        for b in reads:
            b.r.append(tok)
        for b in writes:
            b.w = tok
            b.r = []
        self.nops += 1
        return tok

    def barrier(self):
        snap = [(k, v) for k, v in self.count.items()]
        for e in self.ENGS:
            self.pending[e] = list(snap)

    def emit(self):
        from contextlib import ExitStack
        nc = self.nc
        self.barrier()
        with ExitStack() as st:
            sems = {}
            for k in list(self.ENGS) + self.dma_sems:
                sems[k] = st.enter_context(nc.semaphore("s_" + k))
            block = st.enter_context(nc.Block())

            def mk(e):
                def section(eng):
                    for fn, waits, key, inc in self.ops[e]:
                        for k, v in waits:
                            eng.wait_ge(sems[k], v)
                        ins = fn(eng)
                        ins.then_inc(sems[key], inc)
                    for k, v in self.pending[e]:
                        if self.seen[e].get(k, 0) < v:
                            eng.wait_ge(sems[k], v)
                return section

            block.tensor(mk("pe"))
            block.scalar(mk("act"))
            block.vector(mk("dve"))
            block.gpsimd(mk("pool"))
            block.sync(mk("sp"))


class TL:
    __slots__ = ("t", "b", "sem")

    def __init__(self, t, name, sem=None):
        self.t = t
        self.b = Buf(name)
        self.sem = sem


class Ring:
    def __init__(self, tiles):
        self.tiles = tiles
        self.i = 0

    def next(self):
        t = self.tiles[self.i % len(self.tiles)]
        self.i += 1
        return t


def dsize(dt):
    return 4 if dt == F32 else 2


class Ctx:
    def __init__(self, nc, P):
        self.nc = nc
        self.P = P
        self.uid = 0
        self.pers_off = 16512
        self.off = 16512
        self.semi = 0
        self.banks = [TL(nc.alloc_psum_tensor("pb%d" % i, [128, 512], F32), "pb%d" % i) for i in range(8)]
        for b in self.banks:
            b.b.excl = True
        self.pgen = Ring(self.banks[0:6])
        self.pacc = Ring(self.banks[6:8])

    def _alloc(self, shape, dt, dma, off):
        self.uid += 1
        name = "t%d" % self.uid
        t = self.nc.alloc_sbuf_tensor_at(name, list(shape), dt, offset=off)
        sem = None
        if dma:
            sem = self.P.dma_sems[self.semi % (len(self.P.dma_sems) - 1)]
            self.semi += 1
        return TL(t, name, sem)

    @staticmethod
    def _bytes(shape, dt):
        n = 1
        for s in shape[1:]:
            n *= s
        return (n * dsize(dt) + 63) // 64 * 64

    def pers(self, shape, dt, dma=False):
        tl = self._alloc(shape, dt, dma, self.pers_off)
        self.pers_off += self._bytes(shape, dt)
        return tl

    def begin_sweep(self):
        import os
        self.P.sweep_no += 1
        lim = int(os.environ.get("K_LIMIT", "100000"))
        self.P.disabled = self.P.sweep_no > lim
        if os.environ.get("K_VERBOSE"):
            print("sweep", self.P.sweep_no, "ops so far", self.P.nops, flush=True)
        self.P.barrier()
        self.off = self.pers_off

    def tile(self, shape, dt, dma=False):
        tl = self._alloc(shape, dt, dma, self.off)
        self.off += self._bytes(shape, dt)
        assert self.off <= 229376, ("SBUF overflow", self.off)
        return tl

    def ring(self, n, shape, dt, dma=False):
        return Ring([self.tile(shape, dt, dma) for _ in range(n)])

    def ld(self, dst, dst_ap, src_ap, eng="sp", slow=False, reads=()):
        if slow:
            fn = lambda e: e.dma_start(out=dst_ap, in_=src_ap, allow_slow_non_contiguous=True)
        else:
            fn = lambda e: e.dma_start(out=dst_ap, in_=src_ap)
        self.P.op(eng, fn, reads=list(reads), writes=[dst.b], dsem=dst.sem)

    def st(self, src, dst_ap, src_ap, eng="pool", slow=False):
        if slow:
            fn = lambda e: e.dma_start(out=dst_ap, in_=src_ap, allow_slow_non_contiguous=True)
        else:
            fn = lambda e: e.dma_start(out=dst_ap, in_=src_ap)
        self.P.op(eng, fn, reads=[src.b], dsem=src.sem)

    def mm(self, out, out_ap, lt, lhsT, rt, rhs, start=True, stop=True, skip=False):
        if skip:
            fn = lambda e: e.matmul(out_ap, lhsT=lhsT, rhs=rhs, start=start, stop=stop, skip_group_check=True)
        else:
            fn = lambda e: e.matmul(out_ap, lhsT=lhsT, rhs=rhs, start=start, stop=stop)
        self.P.op("pe", fn, reads=[lt.b, rt.b], writes=[out.b])

    def tr(self, out, out_ap, it, in_ap, idt, id_ap):
        self.P.op("pe", lambda e: e.transpose(out=out_ap, in_=in_ap, identity=id_ap),
                  reads=[it.b, idt.b], writes=[out.b])

    def act(self, out, out_ap, ins, in_ap, func, scale=None, bias=None, accum=None, extra_w=()):
        kw = {}
        if scale is not None:
            kw["scale"] = scale
        if bias is not None:
            kw["bias"] = bias
        if accum is not None:
            kw["accum_out"] = accum
        self.P.op("act", lambda e: e.activation(out=out_ap, in_=in_ap, func=func, **kw),
                  reads=[t.b for t in ins], writes=[out.b] + [t.b for t in extra_w])

    def v(self, eng, fn, ins, outs):
        self.P.op(eng, fn, reads=[t.b for t in ins], writes=[t.b for t in outs])


def norm_jobs(jobs):
    return [dict(S=j, NTo=j // 128, out=(0, j // 128), prompt=False) if isinstance(j, int) else j for j in jobs]


def build_program(jobs):
    jobs = norm_jobs(jobs)
    S_list = [j['S'] for j in jobs]
    nc = bass.Bass("TRN2", target_bir_lowering=False)
    P = Prog(nc)
    K = Ctx(nc, P)
    SM = max(S_list)
    NJ = len(S_list)

    import os
    DEBUG = bool(os.environ.get("K_DEBUG"))
    dbg_names = []

    def dram(name, shape, dt, kind="Internal"):
        if DEBUG and kind == "Internal" and (not name.startswith("wb_") or os.environ.get("K_DEBUGW")):
            kind = "ExternalOutput"
            dbg_names.append(name)
        return nc.dram_tensor(name, list(shape), dt, kind=kind).ap()

    xin = [dram("x%d" % j, [S_list[j], D], F32, "ExternalInput") for j in range(NJ)]
    yout = [dram("y%d" % j, [(jobs[j]['out'][1] - jobs[j]['out'][0]) * 128, D], F32, "ExternalOutput") for j in range(NJ)]
    csin = [dram("cs%d" % j, [S_list[j], 32], F32, "ExternalInput") for j in range(NJ)]
    mkin = [dram("mk%d" % j, [S_list[j], 4], F32, "ExternalInput") for j in range(NJ)]
    sfin = dram("sf", [1, 4], F32, "ExternalInput")
    W = {n: dram(n, PSHAPES[n], F32, "ExternalInput") for n in PARAMS}
    c_ident = dram("c_ident", [128, 128], F32, "ExternalInput")
    c_triu = dram("c_triu", [128, 128], F32, "ExternalInput")
    c_tril = dram("c_tril", [128, 128], F32, "ExternalInput")
    c_ones = dram("c_ones", [128, 128], F32, "ExternalInput")

    wb_in = dram("wb_in", [1024, 2480], BF16)
    wb_uq = dram("wb_uq", [256, 768], BF16)
    wb_ukv = dram("wb_ukv", [128, 1024], BF16)
    wb_out = dram("wb_out", [1024, 1024], BF16)
    wb_cin = dram("wb_cin", [1024, 2048], BF16)
    wb_cout = dram("wb_cout", [1024, 1024], BF16)
    wb_up = dram("wb_up", [2, 1024, 5632], BF16)
    wb_dn = dram("wb_dn", [2, 2816, 1024], BF16)
    XT = [dram("XT%d" % i, [8, 128, SM + 2], BF16) for i in range(2)]
    X1 = dram("X1", [SM, D], F32)
    QT = dram("QT", [8, 96, SM], BF16)
    KT = dram("KT", [8, 96, SM], BF16)
    VV = dram("VV", [SM, 8, 65], BF16)
    QKT = dram("QKT", [8, 128, SM], BF16)
    VM = dram("VM", [SM, 4, 129], BF16)
    SO = dram("SO", [SM, 512], BF16)
    GG = dram("GG", [SM, 16], F32)
    HF = dram("HF", [SM, 512], F32)
    YY = dram("YY", [SM, D], BF16)
    if DEBUG:
        DBG_dm = dram("DBG_dm", [128, 128], F32)
        DBG_pt = dram("DBG_pt", [128, 128], BF16)
        DBG_s = dram("DBG_s", [128, 24], F32)
        DBG_tot = dram("DBG_tot", [128, 129], F32)
        DBG_kk = dram("DBG_kk", [128, 128], BF16)
        DBG_wsT = dram("DBG_wsT", [128, 1024], BF16)
        DBG_bsT = dram("DBG_bsT", [128, 8], F32)
        DBG_uv = dram("DBG_uv", [128, 2048], F32)
        DBG_vn = dram("DBG_vn", [128, 1024], BF16)
        DBG_z = dram("DBG_z", [128, 1024], BF16)

    wsem = P.dma_sems[-1]

    def cast(dst, src, rows):
        for r0 in range(0, rows, 128):
            r1 = min(rows, r0 + 128)
            P.op("pool", lambda e, r0=r0, r1=r1: e.dma_start(out=dst[r0:r1, :], in_=src[r0:r1, :]),
                 writes=[], dsem=wsem)

    cast(wb_in, W['ab_w_in'][0], 1024)
    cast(wb_uq, W['mla_w_uq'][0], 256)
    cast(wb_ukv, W['mla_w_ukv'][0], 128)
    cast(wb_out, W['ab_w_out'][0], 1024)
    cast(wb_cin, W['c_w_in'][0], 1024)
    cast(wb_cout, W['c_w_out'][0], 1024)
    for l in range(2):
        cast(wb_up[l], W['ffn_w_up'][l], 1024)
        cast(wb_dn[l], W['ffn_w_down'][l], 2816)

    identf = K.pers([128, 128], F32, dma=True)
    ident = K.pers([128, 128], BF16)
    triu = K.pers([128, 128], F32, dma=True)
    tril = K.pers([128, 128], F32, dma=True)
    ones = K.pers([128, 128], F32, dma=True)
    cw_m = K.pers([128, 4, 8], F32)
    cw_f = [K.pers([128, 4, 44], F32) for _ in range(2)]
    bsT = K.pers([128, 8], F32)
    wsT = K.pers([128, 8, 128], BF16)
    zero = K.pers([128, 16], BF16, dma=True)
    K.begin_sweep()
    K.ld(identf, identf.t[:], c_ident)
    K.ld(triu, triu.t[:], c_triu)
    K.ld(tril, tril.t[:], c_tril)
    K.ld(ones, ones.t[:], c_ones)
    K.v("dve", lambda e: e.tensor_copy(out=ident.t[:], in_=identf.t[:]), [identf], [ident])
    K.v("dve", lambda e: e.memset(zero.t[:], 0.0), [], [zero])

    def load_T(dst, dst_view, raws, nrow):
        raw = K.tile([nrow, len(raws), 128], F32, dma=True)
        for j, src in enumerate(raws):
            K.ld(raw, raw.t[:, j, :], src)
        for j in range(len(raws)):
            pb = K.pgen.next()
            K.tr(pb, pb.t[:, 0:nrow], raw, raw.t[:, j, :], identf, identf.t[0:nrow, 0:nrow])
            K.v("dve", lambda e, j=j, pb=pb: e.tensor_copy(out=dst_view(j), in_=pb.t[:, 0:nrow]), [pb], [dst])

    mcw = W['mlstm_conv_w'][0]
    load_T(cw_m, lambda j: cw_m.t[:, j, :],
           [mcw[j].rearrange("(c p) -> c p", p=128) for j in range(3)] +
           [W['mlstm_conv_b'][0].rearrange("(c p) -> c p", p=128)], 8)
    for l in range(2):
        load_T(cw_f[l], lambda j, l=l: cw_f[l].t[:, j, :],
               [W['ffn_conv_w'][l, j].rearrange("(c p) -> c p", p=128) for j in range(3)] +
               [W['ffn_conv_b'][l].rearrange("(c p) -> c p", p=128)], 44)
    load_T(bsT, lambda j: bsT.t[:, :], [W['c_b_spatial'][0]], 8)
    wsr = K.tile([128, 8, 128], F32, dma=True)
    wsrb = K.tile([128, 8, 128], BF16)
    K.ld(wsr, wsr.t[:], W['c_w_spatial'][0].rearrange("g t s -> t g s"))
    K.v("dve", lambda e: e.tensor_copy(out=wsrb.t[:], in_=wsr.t[:]), [wsr], [wsrb])
    pb = K.pgen.next()
    pbv = pb.t[:].bitcast(BF16).rearrange("p (c t) -> p c t", c=8)
    for g in range(8):
        K.tr(pb, pbv[:, g, :], wsrb, wsrb.t[:, g, :], ident, ident.t[:])
    K.v("dve", lambda e, pbv=pbv: e.tensor_copy(out=wsT.t[:], in_=pbv), [pb], [wsT])

    def bc_load(dst, src1d, n):
        K.ld(dst, dst.t[:, 0:n], src1d.partition_broadcast(128))

    class NormT:
        def __init__(self):
            self.junk = K.tile([128, 1024], BF16)
            self.ss = K.ring(2, [128, 4], F32)
            self.xn = K.ring(2, [128, 1024], BF16)
            self.xnT = K.ring(2, [128, 8, 128], BF16, dma=True)
            self.g = K.tile([128, 1024], F32, dma=True)

        def load_gain(self, src1d):
            bc_load(self.g, src1d, 1024)

        def stats(self, x, x_ap, n):
            ss = self.ss.next()
            junk = self.junk
            K.v("dve", lambda e: e.memset(ss.t[:, 0:1], 0.0), [], [ss])
            K.act(junk, junk.t[:, 0:n], [x, ss], x_ap, AF.Square, accum=ss.t[:, 0:1], extra_w=[ss])
            K.v("dve", lambda e: e.tensor_scalar(out=ss.t[:, 1:2], in0=ss.t[:, 0:1], scalar1=1.0 / n, scalar2=EPS,
                                                 op0=ALU.mult, op1=ALU.add), [ss], [ss])
            K.act(ss, ss.t[:, 2:3], [ss], ss.t[:, 1:2], AF.Sqrt)
            K.v("dve", lambda e: e.reciprocal(out=ss.t[:, 3:4], in_=ss.t[:, 2:3]), [ss], [ss])
            return ss

        def run(self, x, x_ap, t, XTd):
            ss = self.stats(x, x_ap, 1024)
            xn = self.xn.next()
            g = self.g
            K.v("dve", lambda e: e.scalar_tensor_tensor(out=xn.t[:], in0=x_ap, scalar=ss.t[:, 3:4], in1=g.t[:],
                                                        op0=ALU.mult, op1=ALU.mult), [x, ss, g], [xn])
            pb = K.pgen.next()
            pbv = pb.t[:].bitcast(BF16).rearrange("p (c t) -> p c t", c=8)
            for c in range(8):
                K.tr(pb, pbv[:, c, :], xn, xn.t[:, c * 128:(c + 1) * 128], ident, ident.t[:])
            xnT = self.xnT.next()
            K.act(xnT, xnT.t[:], [pb], pbv, AF.Copy)
            K.st(xnT, XTd[:, :, 1 + t * 128:1 + (t + 1) * 128].rearrange("c p t -> p c t"), xnT.t[:])

    def conv_chunk(pm, ph, hsl, cw, ci, dst):
        w0 = cw.t[:, 0, ci:ci + 1]
        w1 = cw.t[:, 1, ci:ci + 1]
        w2 = cw.t[:, 2, ci:ci + 1]
        bb = cw.t[:, 3, ci:ci + 1]
        K.act(dst, dst.t[:], [pm, cw], pm.t[:], AF.Identity, scale=w1, bias=bb)
        K.v("dve", lambda e: e.scalar_tensor_tensor(out=dst.t[:, 1:512], in0=pm.t[:, 0:511], scalar=w0,
                                                    in1=dst.t[:, 1:512], op0=ALU.mult, op1=ALU.add),
            [pm, cw, dst], [dst])
        K.v("dve", lambda e: e.scalar_tensor_tensor(out=dst.t[:, 0:511], in0=pm.t[:, 1:512], scalar=w2,
                                                    in1=dst.t[:, 0:511], op0=ALU.mult, op1=ALU.add),
            [pm, cw, dst], [dst])
        K.v("dve", lambda e: e.scalar_tensor_tensor(out=dst.t[:, 0:1], in0=ph.t[:, hsl:hsl + 1], scalar=w0,
                                                    in1=dst.t[:, 0:1], op0=ALU.mult, op1=ALU.add),
            [ph, cw, dst], [dst])
        K.v("dve", lambda e: e.scalar_tensor_tensor(out=dst.t[:, 511:512], in0=ph.t[:, hsl + 1:hsl + 2], scalar=w2,
                                                    in1=dst.t[:, 511:512], op0=ALU.mult, op1=ALU.add),
            [ph, cw, dst], [dst])

    for jb in range(NJ):
        S = S_list[jb]
        NT = S // 128
        NB = S // 512
        x_in = xin[jb]
        y_out = yout[jb]
        NTo = jobs[jb]['NTo']
        NBo = NTo // 4
        out_lo, out_hi = jobs[jb]['out']
        prompt = jobs[jb]['prompt']
        SEAM_BLOCKS = (0, 4, 8, 12) if prompt else ()

        def seam_setup(cw, nch, ncand):
            sfb = K.tile([128, 4], F32, dma=True)
            bc_load(sfb, sfin[0], 4)
            nfw = K.tile([128, ncand, 2, nch], F32)
            for c in range(ncand):
                for jj, wi in ((0, 0), (1, 2)):
                    K.v("dve", lambda e, c=c, jj=jj, wi=wi, nfw=nfw, sfb=sfb, cw=cw: e.tensor_scalar_mul(
                        out=nfw.t[:, c, jj, :], in0=cw.t[:, wi, :], scalar1=sfb.t[:, c:c + 1]), [cw, sfb], [nfw])
            return nfw

        def seam_fix(pm, dst, nfw, c, ci):
            K.v("dve", lambda e: e.scalar_tensor_tensor(out=dst.t[:, 256:257], in0=pm.t[:, 255:256],
                                                        scalar=nfw.t[:, c, 0, ci:ci + 1], in1=dst.t[:, 256:257],
                                                        op0=ALU.mult, op1=ALU.add), [pm, nfw, dst], [dst])
            K.v("dve", lambda e: e.scalar_tensor_tensor(out=dst.t[:, 255:256], in0=pm.t[:, 256:257],
                                                        scalar=nfw.t[:, c, 1, ci:ci + 1], in1=dst.t[:, 255:256],
                                                        op0=ALU.mult, op1=ALU.add), [pm, nfw, dst], [dst])

        K.begin_sweep()
        nt = NormT()
        nt.load_gain(W['ab_norm'][0])
        for i in range(2):
            K.st(zero, XT[i][:, :, 0:1].rearrange("c p t -> p c t"), zero.t[:, 0:8].rearrange("p (c t) -> p c t", t=1),
                 slow=True)
            K.st(zero, XT[i][:, :, S + 1:S + 2].rearrange("c p t -> p c t"),
                 zero.t[:, 0:8].rearrange("p (c t) -> p c t", t=1), slow=True)
        xr = K.ring(2, [128, 1024], F32, dma=True)
        for t in range(NT):
            x = xr.next()
            K.ld(x, x.t[:], x_in[t * 128:(t + 1) * 128, :])
            nt.run(x, x.t[:], t, XT[0])

        K.begin_sweep()
        w_in = K.tile([128, 8, 2480], BF16, dma=True)
        K.ld(w_in, w_in.t[:], wb_in.rearrange("(k p) n -> p k n", p=128))
        w_uq = K.tile([128, 2, 768], BF16, dma=True)
        K.ld(w_uq, w_uq.t[:], wb_uq.rearrange("(k p) n -> p k n", p=128))
        w_ukv = K.tile([128, 1024], BF16, dma=True)
        K.ld(w_ukv, w_ukv.t[:], wb_ukv)
        qg = K.tile([128, 256], F32, dma=True)
        bc_load(qg, W['mla_q_norm'][0], 256)
        kvg = K.tile([128, 128], F32, dma=True)
        bc_load(kvg, W['mla_kv_norm'][0], 128)
        gbias = K.tile([128, 16], F32, dma=True)
        bc_load(gbias, W['mlstm_gate_bias'][0], 16)
        winr = K.ring(2, [128, 8, 514], BF16, dma=True)
        cgr = K.ring(2, [128, 512], F32)
        qkr = K.ring(2, [128, 512], BF16, dma=True)
        junk = K.tile([128, 256], BF16)
        ssr = K.ring(2, [128, 8], F32)
        cqn = K.ring(2, [128, 384], BF16)
        cTr = K.ring(2, [128, 3, 128], BF16)
        cosr = K.ring(2, [128, 32], F32, dma=True)
        krr = K.ring(2, [128, 32], BF16)
        tmp = K.ring(2, [128, 8, 64], F32)
        qst = K.ring(2, [128, 8, 96], BF16)
        qfr = K.ring(2, [128, 768], F32)
        kst = K.ring(2, [128, 8, 96], BF16)
        vst = K.ring(2, [128, 8, 65], BF16, dma=True)
        qTs = K.ring(2, [96, 8, 128], BF16, dma=True)
        kTs = K.ring(2, [96, 8, 128], BF16, dma=True)
        vmst = K.ring(2, [128, 4, 129], BF16, dma=True)
        sor = K.ring(2, [128, 512], BF16, dma=True)
        gtr = K.ring(2, [128, 16], F32, dma=True)
        ge = K.ring(2, [128, 2, 4], F32)
        for tl in vst.tiles + vmst.tiles:
            K.v("dve", lambda e, tl=tl: e.memset(tl.t[:], 1.0), [], [tl])
        mkr = K.ring(2, [128, 4], F32, dma=True)
        padb = Buf("xtpad")
        nfw_m = None
        if prompt:
            P.op("pool", lambda e, S=S: e.dma_start(out=XT[0][:, :, 0:1], in_=XT[0][:, :, S:S + 1],
                                                    allow_slow_non_contiguous=True), writes=[padb], dsem=wsem)
            P.op("pool", lambda e, S=S: e.dma_start(out=XT[0][:, :, S + 1:S + 2], in_=XT[0][:, :, 1:2],
                                                    allow_slow_non_contiguous=True), writes=[padb], dsem=wsem)
            nfw_m = seam_setup(cw_m, 8, 4)
        K.pgen = Ring(K.banks[0:6])
        for b in range(NB):
            win = winr.next()
            K.ld(win, win.t[:], XT[0][:, :, 512 * b:512 * b + 514].rearrange("c p t -> p c t"), reads=[padb])
            for c in range(8):
                if c < 4 and b >= NBo:
                    continue
                col0 = 416 + c * 128
                pm = K.pgen.next()
                hsl = 0
                halo = K.banks[6 + (c % 2)]
                for k in range(8):
                    K.mm(pm, pm.t[:, :], w_in, w_in.t[:, k, col0:col0 + 128], win, win.t[:, k, 1:513],
                         start=(k == 0), stop=(k == 7))
                for k in range(8):
                    K.mm(halo, halo.t[:, hsl:hsl + 2], w_in, w_in.t[:, k, col0:col0 + 128], win,
                         win.t[:, k, 0:514:513], start=(k == 0), stop=(k == 7))
                cg = cgr.next()
                conv_chunk(pm, halo, hsl, cw_m, c, cg)
                if b in SEAM_BLOCKS:
                    seam_fix(pm, cg, nfw_m, b // 4, c)
                qk = qkr.next()
                K.act(qk, qk.t[:], [cg], cg.t[:], AF.Silu)
                K.st(qk, QKT[c, :, 512 * b:512 * b + 512], qk.t[:])
            for tt in range(4):
                t = b * 4 + tt
                lo = 1 + tt * 128
                pa = K.pgen.next()
                pv = K.pgen.next()
                po = K.pgen.next()
                for k in range(8):
                    K.mm(pa, pa.t[:, 0:416], win, win.t[:, k, lo:lo + 128], w_in, w_in.t[:, k, 0:416],
                         start=(k == 0), stop=(k == 7))
                for k in range(8):
                    K.mm(pa, pa.t[:, 416:432], win, win.t[:, k, lo:lo + 128], w_in, w_in.t[:, k, 2464:2480],
                         start=(k == 0), stop=(k == 7))
                for k in range(8):
                    K.mm(pv, pv.t[:, :], win, win.t[:, k, lo:lo + 128], w_in, w_in.t[:, k, 1440:1952],
                         start=(k == 0), stop=(k == 7))
                is_own = t < NTo
                if is_own:
                    for k in range(8):
                        K.mm(po, po.t[:, :], win, win.t[:, k, lo:lo + 128], w_in, w_in.t[:, k, 1952:2464],
                             start=(k == 0), stop=(k == 7))
                vm = vmst.next()
                K.act(vm, vm.t[:, :, 0:128], [pv], pv.t[:, :].rearrange("p (h d) -> p h d", d=128), AF.Copy)
                K.st(vm, VM[t * 128:(t + 1) * 128, :, :], vm.t[:])
                if is_own:
                    so = sor.next()
                    K.act(so, so.t[:], [po], po.t[:, :], AF.Sigmoid)
                    K.st(so, SO[t * 128:(t + 1) * 128, :], so.t[:])
                gt = gtr.next()
                g2 = ge.next()
                K.v("dve", lambda e, gt=gt, pa=pa: e.tensor_tensor(out=gt.t[:], in0=pa.t[:, 416:432], in1=gbias.t[:],
                                                                   op=ALU.add), [pa, gbias], [gt])
                gtv = gt.t[:, :].rearrange("p (a b) -> p a b", b=8)[:, :, 4:8]
                K.act(g2, g2.t[:], [gt], gtv, AF.Exp, scale=-1.0)
                K.v("dve", lambda e, g2=g2: e.tensor_scalar_add(out=g2.t[:], in0=g2.t[:], scalar1=1.0), [g2], [g2])
                K.act(g2, g2.t[:], [g2], g2.t[:], AF.Ln)
                K.v("dve", lambda e, g2=g2, gtv=gtv: e.tensor_scalar_mul(out=gtv, in0=g2.t[:], scalar1=-1.0), [g2], [gt])
                mk = mkr.next()
                K.ld(mk, mk.t[:], mkin[jb][t * 128:(t + 1) * 128, :])
                for (c0, mi) in ((0, 0), (8, 2)):
                    K.v("dve", lambda e, gt=gt, mk=mk, c0=c0, mi=mi: e.scalar_tensor_tensor(
                        out=gt.t[:, c0:c0 + 4], in0=gt.t[:, c0:c0 + 4], scalar=mk.t[:, mi:mi + 1],
                        in1=mk.t[:, mi + 1:mi + 2].to_broadcast([128, 4]), op0=ALU.mult, op1=ALU.add), [gt, mk], [gt])
                    K.v("dve", lambda e, gt=gt, mk=mk, c0=c0, mi=mi: e.tensor_scalar_mul(
                        out=gt.t[:, c0 + 4:c0 + 8], in0=gt.t[:, c0 + 4:c0 + 8], scalar1=mk.t[:, mi:mi + 1]), [gt, mk], [gt])
                K.st(gt, GG[t * 128:(t + 1) * 128, :], gt.t[:])
                ss = ssr.next()
                K.v("dve", lambda e, ss=ss: e.memset(ss.t[:, 0:2], 0.0), [], [ss])
                K.act(junk, junk.t[:, 0:256], [pa, ss], pa.t[:, 0:256], AF.Square, accum=ss.t[:, 0:1], extra_w=[ss])
                K.act(junk, junk.t[:, 0:128], [pa, ss], pa.t[:, 256:384], AF.Square, accum=ss.t[:, 1:2], extra_w=[ss])
                K.v("dve", lambda e, ss=ss: e.tensor_scalar(out=ss.t[:, 2:3], in0=ss.t[:, 0:1], scalar1=1.0 / 256,
                                                            scalar2=EPS, op0=ALU.mult, op1=ALU.add), [ss], [ss])
                K.v("dve", lambda e, ss=ss: e.tensor_scalar(out=ss.t[:, 3:4], in0=ss.t[:, 1:2], scalar1=1.0 / 128,
                                                            scalar2=EPS, op0=ALU.mult, op1=ALU.add), [ss], [ss])
                K.act(ss, ss.t[:, 4:6], [ss], ss.t[:, 2:4], AF.Sqrt)
                K.v("dve", lambda e, ss=ss: e.reciprocal(out=ss.t[:, 6:8], in_=ss.t[:, 4:6]), [ss], [ss])
                cq = cqn.next()
                K.v("dve", lambda e, ss=ss, cq=cq, pa=pa: e.scalar_tensor_tensor(
                    out=cq.t[:, 0:256], in0=pa.t[:, 0:256], scalar=ss.t[:, 6:7], in1=qg.t[:], op0=ALU.mult,
                    op1=ALU.mult), [pa, ss, qg], [cq])
                K.v("dve", lambda e, ss=ss, cq=cq, pa=pa: e.scalar_tensor_tensor(
                    out=cq.t[:, 256:384], in0=pa.t[:, 256:384], scalar=ss.t[:, 7:8], in1=kvg.t[:], op0=ALU.mult,
                    op1=ALU.mult), [pa, ss, kvg], [cq])
                cs = cosr.next()
                K.ld(cs, cs.t[:, 0:32], csin[jb][t * 128:(t + 1) * 128, :])
                kr = krr.next()
                tm = tmp.next()

                def rope(dst1, dst2, x1, x2, cosb, sinb, ta, tb, ins, outs, tmt):
                    K.v("dve", lambda e: e.tensor_tensor(out=ta, in0=x1, in1=cosb, op=ALU.mult), ins, [tmt])
                    K.v("dve", lambda e: e.tensor_tensor(out=tb, in0=x2, in1=sinb, op=ALU.mult), ins, [tmt])
                    K.v("dve", lambda e: e.tensor_tensor(out=dst1, in0=ta, in1=tb, op=ALU.subtract), [tmt], outs)
                    K.v("dve", lambda e: e.tensor_tensor(out=ta, in0=x2, in1=cosb, op=ALU.mult), ins + outs, [tmt])
                    K.v("dve", lambda e: e.tensor_tensor(out=tb, in0=x1, in1=sinb, op=ALU.mult), ins, [tmt])
                    K.v("dve", lambda e: e.tensor_tensor(out=dst2, in0=ta, in1=tb, op=ALU.add), [tmt], outs)

                rope(kr.t[:, 0:16], kr.t[:, 16:32], pa.t[:, 384:400], pa.t[:, 400:416], cs.t[:, 0:16], cs.t[:, 16:32],
                     tm.t[:, 0, 0:16], tm.t[:, 0, 16:32], [pa, cs], [kr], tm)
                pb = K.pgen.next()
                pbv = pb.t[:].bitcast(BF16).rearrange("p (c t) -> p c t", c=8)
                for c in range(3):
                    K.tr(pb, pbv[:, c, :], cq, cq.t[:, c * 128:(c + 1) * 128], ident, ident.t[:])
                cT = cTr.next()
                K.act(cT, cT.t[:], [pb], pbv[:, 0:3, :], AF.Copy)
                if is_own:
                    pq1 = K.pgen.next()
                    pq2 = K.pgen.next()
                    for k in range(2):
                        K.mm(pq1, pq1.t[:, 0:480], cT, cT.t[:, k, :], w_uq, w_uq.t[:, k, 0:480], start=(k == 0), stop=(k == 1))
                    for k in range(2):
                        K.mm(pq2, pq2.t[:, 0:288], cT, cT.t[:, k, :], w_uq, w_uq.t[:, k, 480:768], start=(k == 0), stop=(k == 1))
                    qs = qst.next()
                    qf = qfr.next()
                    K.act(qf, qf.t[:, 0:480], [pq1], pq1.t[:, 0:480], AF.Copy)
                    K.act(qf, qf.t[:, 480:768], [pq2], pq2.t[:, 0:288], AF.Copy)
                    qfv = qf.t[:, :].rearrange("p (h d) -> p h d", d=96)
                    K.v("dve", lambda e, qs=qs, qfv=qfv: e.tensor_copy(out=qs.t[:, :, 0:64], in_=qfv[:, :, 0:64]), [qf], [qs])
                    cosb = cs.t[:, 0:16].unsqueeze(1).to_broadcast([128, 8, 16])
                    sinb = cs.t[:, 16:32].unsqueeze(1).to_broadcast([128, 8, 16])
                    rope(qs.t[:, :, 64:80], qs.t[:, :, 80:96], qfv[:, :, 64:80], qfv[:, :, 80:96],
                         cosb, sinb, tm.t[:, :, 0:16], tm.t[:, :, 16:32], [qf, cs], [qs], tm)
                pk1 = K.pgen.next()
                pk2 = K.pgen.next()
                K.mm(pk1, pk1.t[:, :], cT, cT.t[:, 2, :], w_ukv, w_ukv.t[:, 0:512])
                K.mm(pk2, pk2.t[:, :], cT, cT.t[:, 2, :], w_ukv, w_ukv.t[:, 512:1024])
                ks = kst.next()
                vs = vst.next()
                for (pk, h0) in ((pk1, 0), (pk2, 4)):
                    pkv = pk.t[:, :].rearrange("p (h d) -> p h d", d=128)
                    K.act(ks, ks.t[:, h0:h0 + 4, 0:64], [pk], pkv[:, :, 0:64], AF.Copy)
                    K.v("dve", lambda e, vs=vs, pkv=pkv, h0=h0: e.tensor_copy(out=vs.t[:, h0:h0 + 4, 0:64],
                                                                             in_=pkv[:, :, 64:128]), [pk], [vs])
                K.v("dve", lambda e, ks=ks, kr=kr: e.tensor_copy(
                    out=ks.t[:, :, 64:96], in_=kr.t[:, :].unsqueeze(1).to_broadcast([128, 8, 32])), [kr], [ks])
                K.st(vs, VV[t * 128:(t + 1) * 128, :, :], vs.t[:])
                for (src, dstr, dd) in (((qs, qTs, QT), (ks, kTs, KT)) if is_own else ((ks, kTs, KT),)):
                    pb = K.pgen.next()
                    pbv = pb.t[:].bitcast(BF16).rearrange("p (c t) -> p c t", c=8)
                    for h in range(8):
                        K.tr(pb, pbv[0:96, h, :], src, src.t[:, h, :], ident, ident.t[:])
                    dT = dstr.next()
                    K.act(dT, dT.t[:], [pb], pbv[0:96, :, :], AF.Copy)
                    K.st(dT, dd[:, :, t * 128:(t + 1) * 128].rearrange("h p t -> p h t"), dT.t[:])
        K.pgen = Ring(K.banks[0:6])

        K.begin_sweep()
        kth = K.ring(2, [96, S], BF16, dma=True)
        vh = K.ring(2, [128, NT, 65], BF16, dma=True)
        qbr = K.ring(2, [96, 512], BF16, dma=True)
        ptr = K.ring(3, [128, 512], BF16)
        rcr = K.ring(2, [128, 4], F32)
        yar = K.ring(2, [128, 4, 64], BF16, dma=True)
        for h in range(8):
            kt = kth.next()
            K.ld(kt, kt.t[:], KT[h, :, 0:S])
            vv = vh.next()
            K.ld(vv, vv.t[:], VV[0:S, h, :].rearrange("(n p) d -> p n d", p=128))
            for qb in range(NBo):
                qt = qbr.next()
                K.ld(qt, qt.t[:], QT[h, :, qb * 512:(qb + 1) * 512])
                acc = K.pacc.next()

                def scores(kb, kt=kt, qt=qt):
                    ps = K.pgen.next()
                    K.mm(ps, ps.t[:, :], kt, kt.t[:, kb * 128:(kb + 1) * 128], qt, qt.t[:, :])
                    return ps

                LOOK = 2
                pend = [scores(kb) for kb in range(min(LOOK, NT))]
                for kb in range(NT):
                    ps = pend.pop(0)
                    if kb + LOOK < NT:
                        pend.append(scores(kb + LOOK))
                    pt = ptr.next()
                    K.act(pt, pt.t[:], [ps], ps.t[:, :], AF.Exp, scale=SC_ATT)
                    for j in range(4):
                        K.mm(acc, acc.t[:, j * 65:(j + 1) * 65], pt, pt.t[:, j * 128:(j + 1) * 128], vv, vv.t[:, kb, :],
                             start=(kb == 0 and j == 0), stop=(kb == NT - 1 and j == 3), skip=True)
                accv = acc.t[:, 0:260].rearrange("p (j d) -> p j d", d=65)
                rc = rcr.next()
                K.v("dve", lambda e, rc=rc, accv=accv: e.reciprocal(out=rc.t[:, :].unsqueeze(2), in_=accv[:, :, 64:65]),
                    [acc], [rc])
                ya = yar.next()
                K.v("dve", lambda e, rc=rc, accv=accv, ya=ya: e.tensor_tensor(
                    out=ya.t[:], in0=accv[:, :, 0:64], in1=rc.t[:, :].unsqueeze(2).to_broadcast([128, 4, 64]),
                    op=ALU.mult), [acc, rc], [ya])
                K.st(ya, YY[qb * 512:(qb + 1) * 512, h * 64:(h + 1) * 64].rearrange("(j p) d -> p j d", p=128), ya.t[:])

        for direction in (0, 1):
            K.begin_sweep()
            K.pgen = Ring(K.banks[0:8])
            tri = triu if direction == 0 else tril
            gi0 = 0 if direction == 0 else 8
            qkl = K.ring(3, [128, 8, 128], BF16, dma=True)
            vml = K.ring(3, [128, 4, 129], BF16, dma=True)
            gl = K.ring(3, [128, 16], F32, dma=True)
            ktok = K.ring(3, [128, 4, 128], BF16)
            sm = K.ring(3, [128, 6, 4], F32)
            ptm = K.ring(3, [128, 4, 128], BF16)
            dnr = K.ring(2, [128, 3, 4], F32)
            vwr = K.ring(2, [128, 4, 129], BF16)
            ST = K.tile([128, 4, 129], F32)
            STb = K.tile([128, 4, 129], BF16)
            hfr = K.ring(3, [128, 512], F32, dma=True)
            if direction == 1:
                sol = K.ring(3, [128, 512], BF16, dma=True)
                og = K.tile([128, 512], F32, dma=True)
                bc_load(og, W['mlstm_out_norm'][0], 512)
                sq = K.tile([128, 512], F32)
                ms = K.ring(2, [128, 3, 4], F32)
                ymr = K.ring(2, [128, 512], BF16, dma=True)
            K.v("dve", lambda e, ST=ST: e.memset(ST.t[:], 0.0), [], [ST])
            K.v("dve", lambda e, STb=STb: e.memset(STb.t[:], 0.0), [], [STb])
            own_c = list(range(NTo))
            ctx_c = list(range(NTo, NT))
            if direction == 0:
                order = [(n, True) for n in ctx_c] + [(n, False) for n in own_c]
            else:
                order = [(n, True) for n in reversed(ctx_c)] + [(n, False) for n in reversed(own_c)]

            def stage1(n, ctx_only, direction=direction, tri=tri, gi0=gi0):
                c = dict(n=n, ctx_only=ctx_only)
                r0, r1 = n * 128, (n + 1) * 128
                qk = qkl.next()
                if ctx_only:
                    K.ld(qk, qk.t[:, 4:8, :], QKT[4:8, :, r0:r1].rearrange("c p t -> p c t"))
                else:
                    K.ld(qk, qk.t[:], QKT[:, :, r0:r1].rearrange("c p t -> p c t"))
                vm = vml.next()
                K.ld(vm, vm.t[:], VM[r0:r1, :, :])
                g = gl.next()
                K.ld(g, g.t[:], GG[r0:r1, :])
                c.update(qk=qk, vm=vm, g=g)
                if not ctx_only:
                    hf = hfr.next()
                    c['hf'] = hf
                    if direction == 1:
                        K.ld(hf, hf.t[:], HF[r0:r1, :])
                        so = sol.next()
                        K.ld(so, so.t[:], SO[r0:r1, :])
                        c['so'] = so
                li = g.t[:, gi0:gi0 + 4]
                lf = g.t[:, gi0 + 4:gi0 + 8]
                pb = K.pgen.next()
                pbv = pb.t[:].bitcast(BF16).rearrange("p (c t) -> p c t", c=8)
                for h in range(4):
                    K.tr(pb, pbv[:, h, :], qk, qk.t[:, 4 + h, :], ident, ident.t[:])
                kk = ktok.next()
                K.act(kk, kk.t[:], [pb], pbv[:, 0:4, :], AF.Copy)
                pc = K.pgen.next()
                K.mm(pc, pc.t[:, 0:4], tri, tri.t[:], g, lf)
                K.mm(pc, pc.t[:, 4:8], ones, ones.t[:], g, lf)
                s_ = sm.next()
                K.v("dve", lambda e, s_=s_, pc=pc, li=li: e.scalar_tensor_tensor(
                    out=s_.t[:, 0, :], in0=li, scalar=LNSC_M, in1=pc.t[:, 0:4], op0=ALU.add, op1=ALU.subtract),
                    [g, pc], [s_])
                K.v("dve", lambda e, s_=s_, pc=pc: e.tensor_tensor(out=s_.t[:, 1, :], in0=s_.t[:, 0, :], in1=pc.t[:, 4:8],
                                                                   op=ALU.add), [pc, s_], [s_])
                K.act(s_, s_.t[:, 2:4, :], [pc], pc.t[:, 0:8].rearrange("p (a b) -> p a b", b=4), AF.Exp)
                K.act(s_, s_.t[:, 4:6, :], [s_], s_.t[:, 0:2, :], AF.Exp)
                c.update(kk=kk, s_=s_)
                if not ctx_only:
                    pS = K.pgen.next()
                    for h in range(4):
                        K.mm(pS, pS.t[:, h * 128:(h + 1) * 128], qk, qk.t[:, 4 + h, :], qk, qk.t[:, h, :])
                    pt = ptm.next()
                    for h in range(4):
                        K.v("dve", lambda e, pS=pS, pt=pt, s_=s_, h=h, tri=tri: e.scalar_tensor_tensor(
                            out=pt.t[:, h, :], in0=pS.t[:, h * 128:(h + 1) * 128], scalar=s_.t[:, 4, h:h + 1],
                            in1=tri.t[:], op0=ALU.mult, op1=ALU.mult), [pS, s_, tri], [pt])
                    c['pt'] = pt
                return c

            def stage2(c, direction=direction):
                n, ctx_only = c['n'], c['ctx_only']
                r0, r1 = n * 128, (n + 1) * 128
                qk, vm, kk, s_ = c['qk'], c['vm'], c['kk'], c['s_']
                if not ctx_only:
                    pt, hf = c['pt'], c['hf']
                    pN = [K.pgen.next(), K.pgen.next()]
                    for h in range(4):
                        pp = pN[h // 2]
                        c0 = (h % 2) * 129
                        K.mm(pp, pp.t[:, c0:c0 + 129], pt, pt.t[:, h, :], vm, vm.t[:, h, :], start=True, stop=False)
                        K.mm(pp, pp.t[:, c0:c0 + 129], qk, qk.t[:, h, :], STb, STb.t[:, h, :], start=False, stop=True)
                    dn = dnr.next()
                    for bi in range(2):
                        K.v("dve", lambda e, dn=dn, bi=bi, pp=pN[bi], s_=s_: e.tensor_tensor(
                            out=dn.t[:, 0, 2 * bi:2 * bi + 2], in0=pp.t[:, 128:258:129], in1=s_.t[:, 2, 2 * bi:2 * bi + 2],
                            op=ALU.mult), [pN[bi], s_], [dn])
                    K.v("dve", lambda e, dn=dn: e.tensor_scalar_mul(out=dn.t[:, 1, :], in0=dn.t[:, 0, :], scalar1=-1.0),
                        [dn], [dn])
                    K.v("dve", lambda e, dn=dn: e.tensor_tensor(out=dn.t[:, 0, :], in0=dn.t[:, 0, :], in1=dn.t[:, 1, :],
                                                                op=ALU.max), [dn], [dn])
                    K.v("dve", lambda e, dn=dn: e.tensor_scalar_max(out=dn.t[:, 0, :], in0=dn.t[:, 0, :], scalar1=1.0),
                        [dn], [dn])
                    K.v("dve", lambda e, dn=dn: e.reciprocal(out=dn.t[:, 1, :], in_=dn.t[:, 0, :]), [dn], [dn])
                    K.v("dve", lambda e, dn=dn, s_=s_: e.tensor_tensor(out=dn.t[:, 2, :], in0=dn.t[:, 1, :],
                                                                       in1=s_.t[:, 2, :], op=ALU.mult), [dn, s_], [dn])
                    for h in range(4):
                        pp = pN[h // 2]
                        c0 = (h % 2) * 129
                        hs_ap = hf.t[:, h * 128:(h + 1) * 128]
                        if direction == 0:
                            K.act(hf, hs_ap, [pp, dn], pp.t[:, c0:c0 + 128], AF.Copy, scale=dn.t[:, 2, h:h + 1])
                        else:
                            K.v("dve", lambda e, pp=pp, c0=c0, dn=dn, hs_ap=hs_ap, h=h: e.scalar_tensor_tensor(
                                out=hs_ap, in0=pp.t[:, c0:c0 + 128], scalar=dn.t[:, 2, h:h + 1], in1=hs_ap,
                                op0=ALU.mult, op1=ALU.add), [pp, dn, hf], [hf])
                vw = vwr.next()
                for h in range(4):
                    K.v("pool", lambda e, vw=vw, vm=vm, s_=s_, h=h: e.tensor_scalar_mul(
                        out=vw.t[:, h, :], in0=vm.t[:, h, :], scalar1=s_.t[:, 5, h:h + 1]), [vm, s_], [vw])
                pC = [K.pgen.next(), K.pgen.next()]
                for h in range(4):
                    pp = pC[h // 2]
                    c0 = (h % 2) * 129
                    K.mm(pp, pp.t[:, c0:c0 + 129], kk, kk.t[:, h, :], vw, vw.t[:, h, :])
                for h in range(4):
                    pp = pC[h // 2]
                    c0 = (h % 2) * 129
                    K.v("dve", lambda e, pp=pp, c0=c0, s_=s_, h=h, ST=ST: e.scalar_tensor_tensor(
                        out=ST.t[:, h, :], in0=ST.t[:, h, :], scalar=s_.t[:, 3, h:h + 1], in1=pp.t[:, c0:c0 + 129],
                        op0=ALU.mult, op1=ALU.add), [ST, s_, pp], [ST])
                K.act(STb, STb.t[:], [ST], ST.t[:], AF.Copy)
                if ctx_only:
                    return
                if direction == 0:
                    K.st(hf, HF[r0:r1, :], hf.t[:])
                else:
                    so = c['so']
                    m_ = ms.next()
                    K.v("dve", lambda e, hf=hf, sq=sq: e.tensor_tensor(out=sq.t[:], in0=hf.t[:], in1=hf.t[:], op=ALU.mult),
                        [hf], [sq])
                    K.v("dve", lambda e, m_=m_, sq=sq: e.tensor_reduce(
                        out=m_.t[:, 0, :], in_=sq.t[:, :].rearrange("p (h d) -> p h d", d=128), axis=AX.X, op=ALU.add),
                        [sq], [m_])
                    K.v("dve", lambda e, m_=m_: e.tensor_scalar(out=m_.t[:, 1, :], in0=m_.t[:, 0, :], scalar1=1.0 / 128,
                                                                scalar2=EPS, op0=ALU.mult, op1=ALU.add), [m_], [m_])
                    K.act(m_, m_.t[:, 2, :], [m_], m_.t[:, 1, :], AF.Sqrt)
                    K.v("dve", lambda e, m_=m_: e.reciprocal(out=m_.t[:, 0, :], in_=m_.t[:, 2, :]), [m_], [m_])
                    K.v("dve", lambda e, m_=m_, hf=hf, sq=sq: e.tensor_tensor(
                        out=sq.t[:, :].rearrange("p (h d) -> p h d", d=128),
                        in0=hf.t[:, :].rearrange("p (h d) -> p h d", d=128),
                        in1=m_.t[:, 0, :].unsqueeze(2).to_broadcast([128, 4, 128]), op=ALU.mult), [hf, m_], [sq])
                    K.v("dve", lambda e, sq=sq, og=og: e.tensor_tensor(out=sq.t[:], in0=sq.t[:], in1=og.t[:], op=ALU.mult),
                        [sq, og], [sq])
                    ym = ymr.next()
                    K.v("dve", lambda e, ym=ym, so=so, sq=sq: e.tensor_tensor(out=ym.t[:], in0=sq.t[:], in1=so.t[:],
                                                                              op=ALU.mult), [sq, so], [ym])
                    K.st(ym, YY[r0:r1, 512:1024], ym.t[:])

            cur = stage1(*order[0])
            for i in range(len(order)):
                nxt = stage1(*order[i + 1]) if i + 1 < len(order) else None
                stage2(cur)
                cur = nxt
            K.pgen = Ring(K.banks[0:6])

        def tail(x, pr, t, nt, XTd, final_g=None):
            for half in range(2):
                K.v("dve", lambda e, half=half: e.tensor_tensor(
                    out=x.t[:, half * 512:(half + 1) * 512], in0=x.t[:, half * 512:(half + 1) * 512],
                    in1=pr[half].t[:, :], op=ALU.add), [x, pr[half]], [x])
            if final_g is None:
                K.st(x, X1[t * 128:(t + 1) * 128, :], x.t[:])
                nt.run(x, x.t[:], t, XTd)
            else:
                ss = nt.stats(x, x.t[:], 1024)
                K.v("dve", lambda e, ss=ss: e.scalar_tensor_tensor(out=x.t[:], in0=x.t[:], scalar=ss.t[:, 3:4],
                                                                   in1=final_g.t[:], op0=ALU.mult, op1=ALU.mult),
                    [x, ss, final_g], [x])
                if out_lo <= t < out_hi:
                    K.st(x, y_out[(t - out_lo) * 128:(t - out_lo + 1) * 128, :], x.t[:])

        def project_tm(src_tile, src_ap_k, wt, w_ap_k, nk):
            pr = [K.pgen.next(), K.pgen.next()]
            for half in range(2):
                for k in range(nk):
                    K.mm(pr[half], pr[half].t[:, :], src_tile, src_ap_k(k), wt, w_ap_k(k, half), start=(k == 0),
                         stop=(k == nk - 1))
            return pr

        K.begin_sweep()
        nt = NormT()
        nt.load_gain(W['ffn_norm'][0])
        w_o = K.tile([128, 8, 1024], BF16, dma=True)
        K.ld(w_o, w_o.t[:], wb_out.rearrange("(k p) n -> p k n", p=128))
        xr = K.ring(2, [128, 1024], F32, dma=True)
        yr = K.ring(2, [128, 1024], BF16, dma=True)
        yTr = K.ring(2, [128, 8, 128], BF16)
        for t in range(NTo):
            x = xr.next()
            K.ld(x, x.t[:], x_in[t * 128:(t + 1) * 128, :])
            y = yr.next()
            K.ld(y, y.t[:], YY[t * 128:(t + 1) * 128, :])
            pb = K.pgen.next()
            pbv = pb.t[:].bitcast(BF16).rearrange("p (c t) -> p c t", c=8)
            for c in range(8):
                K.tr(pb, pbv[:, c, :], y, y.t[:, c * 128:(c + 1) * 128], ident, ident.t[:])
            yT = yTr.next()
            K.act(yT, yT.t[:], [pb], pbv, AF.Copy)
            pr = project_tm(yT, lambda k: yT.t[:, k, :], w_o, lambda k, half: w_o.t[:, k, half * 512:(half + 1) * 512], 8)
            tail(x, pr, t, nt, XT[1])

        def ffn_sweep(l, XTs, XTd, gain_next, final):
            K.begin_sweep()
            wup = K.tile([128, 8, 5632], BF16, dma=True)
            for k in range(8):
                K.ld(wup, wup.t[:, k, :], wb_up[l, k * 128:(k + 1) * 128, :])
            wdn = K.tile([128, NFC, 1024], BF16, dma=True)
            K.ld(wdn, wdn.t[:], wb_dn[l].rearrange("(i p) n -> p i n", p=128))
            nt = NormT()
            fg = None
            if final:
                fg = nt.g
            nt.load_gain(gain_next)
            win = K.tile([128, 8, 514], BF16, dma=True)
            hT = K.tile([128, NFC, 512], BF16)
            cgr = K.ring(2, [128, 512], F32)
            cvr = K.ring(2, [128, 512], F32)
            sgr = K.ring(2, [128, 512], BF16)
            xr = K.ring(2, [128, 1024], F32, dma=True)
            K.pgen = Ring(K.banks[0:6])
            cw = cw_f[l]
            hs_i = 0
            nfw_f = seam_setup(cw, 44, 2) if prompt else None
            for b in range(NBo):
                K.ld(win, win.t[:], XTs[:, :, 512 * b:512 * b + 514].rearrange("c p t -> p c t"))
                for i in range(NFC):
                    outs = []
                    for (col0, ci, rr) in ((i * 128, i, cgr), (DFF + i * 128, NFC + i, cvr)):
                        pm = K.pgen.next()
                        hsl = 0
                        halo = K.banks[6 + (hs_i % 2)]
                        hs_i += 1
                        for k in range(8):
                            K.mm(pm, pm.t[:, :], wup, wup.t[:, k, col0:col0 + 128], win, win.t[:, k, 1:513],
                                 start=(k == 0), stop=(k == 7))
                        for k in range(8):
                            K.mm(halo, halo.t[:, hsl:hsl + 2], wup, wup.t[:, k, col0:col0 + 128], win,
                                 win.t[:, k, 0:514:513], start=(k == 0), stop=(k == 7))
                        cc = rr.next()
                        conv_chunk(pm, halo, hsl, cw, ci, cc)
                        if prompt and b in (0, 4):
                            seam_fix(pm, cc, nfw_f, b // 4, ci)
                        outs.append(cc)
                    cg, cv = outs
                    sg = sgr.next()
                    K.act(sg, sg.t[:], [cg], cg.t[:], AF.Silu)
                    K.v("pool", lambda e, sg=sg, cv=cv, i=i: e.tensor_tensor(out=hT.t[:, i, :], in0=sg.t[:], in1=cv.t[:],
                                                                             op=ALU.mult), [sg, cv], [hT])
                for tt in range(4):
                    t = b * 4 + tt
                    x = xr.next()
                    K.ld(x, x.t[:], X1[t * 128:(t + 1) * 128, :])
                    pr = project_tm(hT, lambda k: hT.t[:, k, tt * 128:(tt + 1) * 128], wdn,
                                    lambda k, half: wdn.t[:, k, half * 512:(half + 1) * 512], NFC)
                    tail(x, pr, t, nt, XTd, final_g=fg)
            K.pgen = Ring(K.banks[0:6])

        ffn_sweep(0, XT[1], XT[0], W['c_norm'][0], False)

        K.begin_sweep()
        nt = NormT()
        nt.load_gain(W['ffn_norm'][1])
        cwin = K.tile([128, 8, 2048], BF16, dma=True)
        K.ld(cwin, cwin.t[:], wb_cin.rearrange("(k p) n -> p k n", p=128))
        cwo = K.tile([128, 8, 1024], BF16, dma=True)
        K.ld(cwo, cwo.t[:], wb_cout.rearrange("(k p) n -> p k n", p=128))
        vg = K.tile([128, 1024], F32, dma=True)
        bc_load(vg, W['c_v_norm'][0], 1024)
        xTl = K.ring(2, [128, 8, 128], BF16, dma=True)
        xr = K.ring(2, [128, 1024], F32, dma=True)
        ur = K.ring(2, [128, 2048], F32)
        vnr = K.ring(2, [128, 1024], BF16)
        zr = K.ring(2, [128, 1024], BF16)
        zTr = K.ring(2, [128, 8, 128], BF16)
        for t in range(NTo):
            xT = xTl.next()
            K.ld(xT, xT.t[:], XT[0][:, :, 1 + t * 128:1 + (t + 1) * 128].rearrange("c p t -> p c t"))
            x = xr.next()
            K.ld(x, x.t[:], X1[t * 128:(t + 1) * 128, :])
            uv = ur.next()
            for j in range(4):
                pu = K.pgen.next()
                for k in range(8):
                    K.mm(pu, pu.t[:, :], xT, xT.t[:, k, :], cwin, cwin.t[:, k, j * 512:(j + 1) * 512], start=(k == 0),
                         stop=(k == 7))
                K.act(uv, uv.t[:, j * 512:(j + 1) * 512], [pu], pu.t[:, :], AF.Gelu)
            ss = nt.stats(uv, uv.t[:, 1024:2048], 1024)
            vn = vnr.next()
            K.v("dve", lambda e, ss=ss, uv=uv, vn=vn: e.scalar_tensor_tensor(
                out=vn.t[:], in0=uv.t[:, 1024:2048], scalar=ss.t[:, 3:4], in1=vg.t[:], op0=ALU.mult, op1=ALU.mult),
                [uv, ss, vg], [vn])
            z = zr.next()
            psv = [K.pgen.next(), K.pgen.next()]
            for g in range(8):
                pp = psv[g // 4]
                K.mm(pp, pp.t[:, (g % 4) * 128:(g % 4 + 1) * 128], wsT, wsT.t[:, g, :], vn, vn.t[:, g * 128:(g + 1) * 128])
            for g in range(8):
                pp = psv[g // 4]
                K.v("dve", lambda e, pp=pp, g=g, uv=uv, z=z: e.scalar_tensor_tensor(
                    out=z.t[:, g * 128:(g + 1) * 128], in0=pp.t[:, (g % 4) * 128:(g % 4 + 1) * 128],
                    scalar=bsT.t[:, g:g + 1], in1=uv.t[:, g * 128:(g + 1) * 128], op0=ALU.add, op1=ALU.mult),
                    [pp, bsT, uv], [z])
            if DEBUG and t == 0:
                P.op("sp", lambda e: e.dma_start(out=DBG_wsT, in_=wsT.t[:, :, :].rearrange("p a b -> p (a b)")), reads=[wsT.b], dsem=wsem)
                P.op("sp", lambda e: e.dma_start(out=DBG_bsT, in_=bsT.t[:]), reads=[bsT.b], dsem=wsem)
                P.op("sp", lambda e, uv=uv: e.dma_start(out=DBG_uv, in_=uv.t[:]), reads=[uv.b], dsem=wsem)
                P.op("sp", lambda e, vn=vn: e.dma_start(out=DBG_vn, in_=vn.t[:]), reads=[vn.b], dsem=wsem)
                P.op("sp", lambda e, z=z: e.dma_start(out=DBG_z, in_=z.t[:]), reads=[z.b], dsem=wsem)
            pb = K.pgen.next()
            pbv = pb.t[:].bitcast(BF16).rearrange("p (c t) -> p c t", c=8)
            for c in range(8):
                K.tr(pb, pbv[:, c, :], z, z.t[:, c * 128:(c + 1) * 128], ident, ident.t[:])
            zT = zTr.next()
            K.act(zT, zT.t[:], [pb], pbv, AF.Copy)
            pr = project_tm(zT, lambda k: zT.t[:, k, :], cwo, lambda k, half: cwo.t[:, k, half * 512:(half + 1) * 512], 8)
            tail(x, pr, t, nt, XT[1])

        ffn_sweep(1, XT[1], XT[0], W['final_norm'], True)

    P.emit()
    nc._dbg_names = dbg_names
    return nc


BIG = 30000.0
OWN_EXT = 2560
HALO = 256


def make_consts():
    return {
        "c_ident": np.eye(128, dtype=np.float32),
        "c_triu": np.triu(np.ones((128, 128), np.float32)),
        "c_tril": np.tril(np.ones((128, 128), np.float32)),
        "c_ones": np.ones((128, 128), np.float32),
    }


def rope_table(pos):
    inv = (1.0 / (np.float32(10000.0) ** (np.arange(0, 32, 2, dtype=np.float32) / np.float32(32)))).astype(np.float32)
    ang = pos.astype(np.float32)[:, None] * inv[None, :]
    return np.concatenate([np.cos(ang), np.sin(ang)], axis=1).astype(np.float32)


def mask_table(mf, mb):
    mf = mf.astype(np.float32)
    mb = mb.astype(np.float32)
    return np.stack([mf, (mf - 1) * BIG, mb, (mb - 1) * BIG], axis=1).astype(np.float32)


def aux_full(S):
    ones = np.ones(S, np.float32)
    return rope_table(np.arange(S)), mask_table(ones, ones)


def aux_quarter(S, j):
    q = S // 4
    shift = j * q - HALO
    r = np.arange(S)
    t = r + shift
    pos = t % S
    own = r < OWN_EXT
    valid = (t >= 0) & (t < S)
    t_start = max(shift, 0)
    t_end = min(shift + OWN_EXT, S)
    mf = np.where(own, valid, pos < t_start)
    mb = np.where(own, valid, pos >= t_end)
    sf = np.zeros((1, 4), np.float32)
    sf[0, (4 - j) % 4] = -1.0
    return shift, rope_table(pos), mask_table(mf, mb), sf


_CACHE = {}


def run(core_inputs, params, jobs):
    jobs = norm_jobs(jobs)
    key = repr(jobs)
    if key not in _CACHE:
        _CACHE[key] = build_program(jobs)
    nc = _CACHE[key]
    consts = make_consts()
    in_maps = []
    for ci in core_inputs:
        m = {k: np.ascontiguousarray(v, dtype=np.float32) for k, v in ci.items()}
        for j, jd in enumerate(jobs):
            if ("cs%d" % j) not in m:
                cs, mk = aux_full(jd['S'])
                m["cs%d" % j] = cs
                m["mk%d" % j] = mk
        if "sf" not in m:
            m["sf"] = np.zeros((1, 4), np.float32)
        for n in PARAMS:
            m[n] = np.ascontiguousarray(params[n], dtype=np.float32)
        m.update(consts)
        in_maps.append(m)
    res = run_bass_kernel_spmd(nc, in_maps, core_ids=list(range(len(core_inputs))))
    global LAST_RES
    LAST_RES = res.results
    return [[r["y%d" % j] for j in range(len(jobs))] for r in res.results]


def kernel(**inputs):
    xp = np.asarray(inputs['x_prompt'], dtype=np.float32)
    xs = np.asarray(inputs['x_sample'], dtype=np.float32)
    params = {n: np.asarray(inputs[n], dtype=np.float32) for n in PARAMS}
    SP = xp.shape[1]
    jobs = [4096, 4096, dict(S=SP, NTo=OWN_EXT // 128, out=(HALO // 128, (OWN_EXT - HALO) // 128), prompt=True)]
    core_inputs = []
    for c in range(8):
        p, j = c // 4, c % 4
        shift, cs, mk, sf = aux_quarter(SP, j)
        core_inputs.append({"x0": xs[2 * c], "x1": xs[2 * c + 1], "x2": np.roll(xp[p], -shift, axis=0),
                            "cs2": cs, "mk2": mk, "sf": sf})
    outs = run(core_inputs, params, jobs)
    q = SP // 4
    y_prompt = np.zeros_like(xp)
    for c in range(8):
        p, j = c // 4, c % 4
        y_prompt[p, j * q:(j + 1) * q] = outs[c][2]
    y_sample = np.stack([outs[c][j] for c in range(8) for j in range(2)], axis=0)
    return (y_prompt, y_sample)
```

```python
import math
import numpy as np
import concourse.bass as bass
import concourse.mybir as mybir
from concourse.bass_utils import run_bass_kernel_spmd

F32 = mybir.dt.float32
BF16 = mybir.dt.bfloat16
ALU = mybir.AluOpType
AF = mybir.ActivationFunctionType
AX = mybir.AxisListType

D = 1024
EPS = 1e-6
DFF = 2816
NFC = 22
SC_ATT = 96 ** -0.5
SC_M = 128 ** -0.5
LNSC_M = math.log(SC_M)

PARAMS = ['ab_norm', 'ab_w_in', 'mla_q_norm', 'mla_w_uq', 'mla_kv_norm', 'mla_w_ukv', 'mlstm_conv_w',
          'mlstm_conv_b', 'mlstm_gate_bias', 'mlstm_out_norm', 'ab_w_out', 'c_norm', 'c_w_in', 'c_v_norm',
          'c_w_spatial', 'c_b_spatial', 'c_w_out', 'ffn_norm', 'ffn_w_up', 'ffn_conv_w', 'ffn_conv_b',
          'ffn_w_down', 'final_norm']
PSHAPES = {
    'ab_norm': [1, 1024], 'ab_w_in': [1, 1024, 2480], 'mla_q_norm': [1, 256], 'mla_w_uq': [1, 256, 768],
    'mla_kv_norm': [1, 128], 'mla_w_ukv': [1, 128, 1024], 'mlstm_conv_w': [1, 3, 1024], 'mlstm_conv_b': [1, 1024],
    'mlstm_gate_bias': [1, 16], 'mlstm_out_norm': [1, 512], 'ab_w_out': [1, 1024, 1024], 'c_norm': [1, 1024],
    'c_w_in': [1, 1024, 2048], 'c_v_norm': [1, 1024], 'c_w_spatial': [1, 8, 128, 128], 'c_b_spatial': [1, 8, 128],
    'c_w_out': [1, 1024, 1024], 'ffn_norm': [2, 1024], 'ffn_w_up': [2, 1024, 5632], 'ffn_conv_w': [2, 3, 5632],
    'ffn_conv_b': [2, 5632], 'ffn_w_down': [2, 2816, 1024], 'final_norm': [1024],
}


class Buf:
    __slots__ = ("name", "w", "r", "excl")

    def __init__(self, name, excl=False):
        self.name = name
        self.w = None
        self.r = []
        self.excl = excl


class Prog:
    ENGS = ("pe", "act", "dve", "pool", "sp")

    def __init__(self, nc, n_dma_sems=80):
        self.nc = nc
        self.ops = {e: [] for e in self.ENGS}
        self.count = {}
        self.seen = {e: {} for e in self.ENGS}
        self.pending = {e: [] for e in self.ENGS}
        self.dma_sems = ["d%d" % i for i in range(n_dma_sems)]
        self.nops = 0
        self.disabled = False
        self.sweep_no = 0
        import os
        self.maxops = int(os.environ.get("K_OPS", "1000000000"))

    def op(self, eng, fn, reads=(), writes=(), dsem=None):
        if self.disabled or self.nops >= self.maxops:
            return None
        deps = []
        for b in reads:
            if b.w is not None:
                deps.append(b.w)
            if b.excl:
                for t in b.r:
                    if t[0] != eng:
                        deps.append(t)
        for b in writes:
            if b.w is not None:
                deps.append(b.w)
            deps.extend(b.r)
        if dsem is None:
            key, inc = eng, 1
        else:
            key, inc = dsem, 16
        self.count[key] = self.count.get(key, 0) + inc
        tok = (key, self.count[key])
        seen = self.seen[eng]
        waits = []
        if self.pending[eng]:
            for k, v in self.pending[eng]:
                if seen.get(k, 0) < v:
                    seen[k] = v
                    waits.append((k, v))
            self.pending[eng] = []
        for k, v in deps:
            if k == "pe" and eng == "pe" and dsem is None:
                continue
            if seen.get(k, 0) >= v:
                continue
            seen[k] = v
            waits.append((k, v))
        self.ops[eng].append((fn, waits, key, inc))
        for b in reads:
            b.r.append(tok)
        for b in writes:
            b.w = tok
            b.r = []
        self.nops += 1
        return tok

    def barrier(self):
        snap = [(k, v) for k, v in self.count.items()]
        for e in self.ENGS:
            self.pending[e] = list(snap)

    def emit(self):
        from contextlib import ExitStack
        nc = self.nc
        self.barrier()
        with ExitStack() as st:
            sems = {}
            for k in list(self.ENGS) + self.dma_sems:
                sems[k] = st.enter_context(nc.semaphore("s_" + k))
            block = st.enter_context(nc.Block())

            def mk(e):
                def section(eng):
                    for fn, waits, key, inc in self.ops[e]:
                        for k, v in waits:
                            eng.wait_ge(sems[k], v)
                        ins = fn(eng)
                        ins.then_inc(sems[key], inc)
                    for k, v in self.pending[e]:
                        if self.seen[e].get(k, 0) < v:
                            eng.wait_ge(sems[k], v)
                return section

            block.tensor(mk("pe"))
            block.scalar(mk("act"))
            block.vector(mk("dve"))
            block.gpsimd(mk("pool"))
            block.sync(mk("sp"))


class TL:
    __slots__ = ("t", "b", "sem")

    def __init__(self, t, name, sem=None):
        self.t = t
        self.b = Buf(name)
        self.sem = sem


class Ring:
    def __init__(self, tiles):
        self.tiles = tiles
        self.i = 0

    def next(self):
        t = self.tiles[self.i % len(self.tiles)]
        self.i += 1
        return t


def dsize(dt):
    return 4 if dt == F32 else 2


class Ctx:
    def __init__(self, nc, P):
        self.nc = nc
        self.P = P
        self.uid = 0
        self.pers_off = 16512
        self.off = 16512
        self.semi = 0
        self.banks = [TL(nc.alloc_psum_tensor("pb%d" % i, [128, 512], F32), "pb%d" % i) for i in range(8)]
        for b in self.banks:
            b.b.excl = True
        self.pgen = Ring(self.banks[0:6])
        self.pacc = Ring(self.banks[6:8])

    def _alloc(self, shape, dt, dma, off):
        self.uid += 1
        name = "t%d" % self.uid
        t = self.nc.alloc_sbuf_tensor_at(name, list(shape), dt, offset=off)
        sem = None
        if dma:
            sem = self.P.dma_sems[self.semi % (len(self.P.dma_sems) - 1)]
            self.semi += 1
        return TL(t, name, sem)

    @staticmethod
    def _bytes(shape, dt):
        n = 1
        for s in shape[1:]:
            n *= s
        return (n * dsize(dt) + 63) // 64 * 64

    def pers(self, shape, dt, dma=False):
        tl = self._alloc(shape, dt, dma, self.pers_off)
        self.pers_off += self._bytes(shape, dt)
        return tl

    def begin_sweep(self):
        import os
        self.P.sweep_no += 1
        lim = int(os.environ.get("K_LIMIT", "100000"))
        self.P.disabled = self.P.sweep_no > lim
        if os.environ.get("K_VERBOSE"):
            print("sweep", self.P.sweep_no, "ops so far", self.P.nops, flush=True)
        self.P.barrier()
        self.off = self.pers_off

    def tile(self, shape, dt, dma=False):
        tl = self._alloc(shape, dt, dma, self.off)
        self.off += self._bytes(shape, dt)
        assert self.off <= 229376, ("SBUF overflow", self.off)
        return tl

    def ring(self, n, shape, dt, dma=False):
        return Ring([self.tile(shape, dt, dma) for _ in range(n)])

    def ld(self, dst, dst_ap, src_ap, eng="sp", slow=False, reads=()):
        if slow:
            fn = lambda e: e.dma_start(out=dst_ap, in_=src_ap, allow_slow_non_contiguous=True)
        else:
            fn = lambda e: e.dma_start(out=dst_ap, in_=src_ap)
        self.P.op(eng, fn, reads=list(reads), writes=[dst.b], dsem=dst.sem)

    def st(self, src, dst_ap, src_ap, eng="pool", slow=False):
        if slow:
            fn = lambda e: e.dma_start(out=dst_ap, in_=src_ap, allow_slow_non_contiguous=True)
        else:
            fn = lambda e: e.dma_start(out=dst_ap, in_=src_ap)
        self.P.op(eng, fn, reads=[src.b], dsem=src.sem)

    def mm(self, out, out_ap, lt, lhsT, rt, rhs, start=True, stop=True, skip=False):
        if skip:
            fn = lambda e: e.matmul(out_ap, lhsT=lhsT, rhs=rhs, start=start, stop=stop, skip_group_check=True)
        else:
            fn = lambda e: e.matmul(out_ap, lhsT=lhsT, rhs=rhs, start=start, stop=stop)
        self.P.op("pe", fn, reads=[lt.b, rt.b], writes=[out.b])

    def tr(self, out, out_ap, it, in_ap, idt, id_ap):
        self.P.op("pe", lambda e: e.transpose(out=out_ap, in_=in_ap, identity=id_ap),
                  reads=[it.b, idt.b], writes=[out.b])

    def act(self, out, out_ap, ins, in_ap, func, scale=None, bias=None, accum=None, extra_w=()):
        kw = {}
        if scale is not None:
            kw["scale"] = scale
        if bias is not None:
            kw["bias"] = bias
        if accum is not None:
            kw["accum_out"] = accum
        self.P.op("act", lambda e: e.activation(out=out_ap, in_=in_ap, func=func, **kw),
                  reads=[t.b for t in ins], writes=[out.b] + [t.b for t in extra_w])

    def v(self, eng, fn, ins, outs):
        self.P.op(eng, fn, reads=[t.b for t in ins], writes=[t.b for t in outs])


def norm_jobs(jobs):
    return [dict(S=j, NTo=j // 128, out=(0, j // 128), prompt=False) if isinstance(j, int) else j for j in jobs]


def build_program(jobs):
    jobs = norm_jobs(jobs)
    S_list = [j['S'] for j in jobs]
    nc = bass.Bass("TRN2", target_bir_lowering=False)
    P = Prog(nc)
    K = Ctx(nc, P)
    SM = max(S_list)
    NJ = len(S_list)

    import os
    DEBUG = bool(os.environ.get("K_DEBUG"))
    dbg_names = []

    def dram(name, shape, dt, kind="Internal"):
        if DEBUG and kind == "Internal" and (not name.startswith("wb_") or os.environ.get("K_DEBUGW")):
            kind = "ExternalOutput"
            dbg_names.append(name)
        return nc.dram_tensor(name, list(shape), dt, kind=kind).ap()

    xin = [dram("x%d" % j, [S_list[j], D], F32, "ExternalInput") for j in range(NJ)]
    yout = [dram("y%d" % j, [(jobs[j]['out'][1] - jobs[j]['out'][0]) * 128, D], F32, "ExternalOutput") for j in range(NJ)]
    csin = [dram("cs%d" % j, [S_list[j], 32], F32, "ExternalInput") for j in range(NJ)]
    mkin = [dram("mk%d" % j, [S_list[j], 4], F32, "ExternalInput") for j in range(NJ)]
    sfin = dram("sf", [1, 4], F32, "ExternalInput")
    W = {n: dram(n, PSHAPES[n], F32, "ExternalInput") for n in PARAMS}
    c_ident = dram("c_ident", [128, 128], F32, "ExternalInput")
    c_triu = dram("c_triu", [128, 128], F32, "ExternalInput")
    c_tril = dram("c_tril", [128, 128], F32, "ExternalInput")
    c_ones = dram("c_ones", [128, 128], F32, "ExternalInput")

    wb_in = dram("wb_in", [1024, 2480], BF16)
    wb_uq = dram("wb_uq", [256, 768], BF16)
    wb_ukv = dram("wb_ukv", [128, 1024], BF16)
    wb_out = dram("wb_out", [1024, 1024], BF16)
    wb_cin = dram("wb_cin", [1024, 2048], BF16)
    wb_cout = dram("wb_cout", [1024, 1024], BF16)
    wb_up = dram("wb_up", [2, 1024, 5632], BF16)
    wb_dn = dram("wb_dn", [2, 2816, 1024], BF16)
    XT = [dram("XT%d" % i, [8, 128, SM + 2], BF16) for i in range(2)]
    X1 = dram("X1", [SM, D], F32)
    QT = dram("QT", [8, 96, SM], BF16)
    KT = dram("KT", [8, 96, SM], BF16)
    VV = dram("VV", [SM, 8, 65], BF16)
    QKT = dram("QKT", [8, 128, SM], BF16)
    VM = dram("VM", [SM, 4, 129], BF16)
    SO = dram("SO", [SM, 512], BF16)
    GG = dram("GG", [SM, 16], F32)
    HF = dram("HF", [SM, 512], F32)
    YY = dram("YY", [SM, D], BF16)
    if DEBUG:
        DBG_dm = dram("DBG_dm", [128, 128], F32)
        DBG_pt = dram("DBG_pt", [128, 128], BF16)
        DBG_s = dram("DBG_s", [128, 24], F32)
        DBG_tot = dram("DBG_tot", [128, 129], F32)
        DBG_kk = dram("DBG_kk", [128, 128], BF16)
        DBG_wsT = dram("DBG_wsT", [128, 1024], BF16)
        DBG_bsT = dram("DBG_bsT", [128, 8], F32)
        DBG_uv = dram("DBG_uv", [128, 2048], F32)
        DBG_vn = dram("DBG_vn", [128, 1024], BF16)
        DBG_z = dram("DBG_z", [128, 1024], BF16)

    wsem = P.dma_sems[-1]

    def cast(dst, src, rows):
        for r0 in range(0, rows, 128):
            r1 = min(rows, r0 + 128)
            P.op("pool", lambda e, r0=r0, r1=r1: e.dma_start(out=dst[r0:r1, :], in_=src[r0:r1, :]),
                 writes=[], dsem=wsem)

    cast(wb_in, W['ab_w_in'][0], 1024)
    cast(wb_uq, W['mla_w_uq'][0], 256)
    cast(wb_ukv, W['mla_w_ukv'][0], 128)
    cast(wb_out, W['ab_w_out'][0], 1024)
    cast(wb_cin, W['c_w_in'][0], 1024)
    cast(wb_cout, W['c_w_out'][0], 1024)
    for l in range(2):
        cast(wb_up[l], W['ffn_w_up'][l], 1024)
        cast(wb_dn[l], W['ffn_w_down'][l], 2816)

    identf = K.pers([128, 128], F32, dma=True)
    ident = K.pers([128, 128], BF16)
    triu = K.pers([128, 128], F32, dma=True)
    tril = K.pers([128, 128], F32, dma=True)
    ones = K.pers([128, 128], F32, dma=True)
    cw_m = K.pers([128, 4, 8], F32)
    cw_f = [K.pers([128, 4, 44], F32) for _ in range(2)]
    bsT = K.pers([128, 8], F32)
    wsT = K.pers([128, 8, 128], BF16)
    zero = K.pers([128, 16], BF16, dma=True)
    epsb = K.pers([128, 1], F32)
    K.begin_sweep()
    K.v("dve", lambda e: e.memset(epsb.t[:], EPS), [], [epsb])
    K.ld(identf, identf.t[:], c_ident)
    K.ld(triu, triu.t[:], c_triu)
    K.ld(tril, tril.t[:], c_tril)
    K.ld(ones, ones.t[:], c_ones)
    K.v("dve", lambda e: e.tensor_copy(out=ident.t[:], in_=identf.t[:]), [identf], [ident])
    K.v("dve", lambda e: e.memset(zero.t[:], 0.0), [], [zero])

    def load_T(dst, dst_view, raws, nrow):
        raw = K.tile([nrow, len(raws), 128], F32, dma=True)
        for j, src in enumerate(raws):
            K.ld(raw, raw.t[:, j, :], src)
        for j in range(len(raws)):
            pb = K.pgen.next()
            K.tr(pb, pb.t[:, 0:nrow], raw, raw.t[:, j, :], identf, identf.t[0:nrow, 0:nrow])
            K.v("dve", lambda e, j=j, pb=pb: e.tensor_copy(out=dst_view(j), in_=pb.t[:, 0:nrow]), [pb], [dst])

    mcw = W['mlstm_conv_w'][0]
    load_T(cw_m, lambda j: cw_m.t[:, j, :],
           [mcw[j].rearrange("(c p) -> c p", p=128) for j in range(3)] +
           [W['mlstm_conv_b'][0].rearrange("(c p) -> c p", p=128)], 8)
    for l in range(2):
        load_T(cw_f[l], lambda j, l=l: cw_f[l].t[:, j, :],
               [W['ffn_conv_w'][l, j].rearrange("(c p) -> c p", p=128) for j in range(3)] +
               [W['ffn_conv_b'][l].rearrange("(c p) -> c p", p=128)], 44)
    load_T(bsT, lambda j: bsT.t[:, :], [W['c_b_spatial'][0]], 8)
    wsr = K.tile([128, 8, 128], F32, dma=True)
    wsrb = K.tile([128, 8, 128], BF16)
    K.ld(wsr, wsr.t[:], W['c_w_spatial'][0].rearrange("g t s -> t g s"))
    K.v("dve", lambda e: e.tensor_copy(out=wsrb.t[:], in_=wsr.t[:]), [wsr], [wsrb])
    pb = K.pgen.next()
    pbv = pb.t[:].bitcast(BF16).rearrange("p (c t) -> p c t", c=8)
    for g in range(8):
        K.tr(pb, pbv[:, g, :], wsrb, wsrb.t[:, g, :], ident, ident.t[:])
    K.v("dve", lambda e, pbv=pbv: e.tensor_copy(out=wsT.t[:], in_=pbv), [pb], [wsT])

    def bc_load(dst, src1d, n):
        K.ld(dst, dst.t[:, 0:n], src1d.partition_broadcast(128))

    class NormT:
        def __init__(self):
            self.junk = K.tile([128, 1024], BF16)
            self.ss = K.ring(3, [128, 4], F32)
            self.xn = K.ring(3, [128, 1024], BF16)
            self.xnT = K.ring(2, [128, 8, 128], BF16, dma=True)
            self.g = K.tile([128, 1024], F32, dma=True)

        def load_gain(self, src1d):
            bc_load(self.g, src1d, 1024)

        def stats(self, x, x_ap, n):
            ss = self.ss.next()
            junk = self.junk
            K.v("dve", lambda e: e.memset(ss.t[:, 0:1], 0.0), [], [ss])
            K.act(junk, junk.t[:, 0:n], [x, ss], x_ap, AF.Square, accum=ss.t[:, 0:1], extra_w=[ss])
            K.act(ss, ss.t[:, 2:3], [ss, epsb], ss.t[:, 0:1], AF.Ln, scale=1.0 / n, bias=epsb.t[:, 0:1])
            K.act(ss, ss.t[:, 3:4], [ss], ss.t[:, 2:3], AF.Exp, scale=-0.5)
            return ss

        def run(self, x, x_ap, t, XTd):
            self.run_b(self.run_a(x, x_ap), t, XTd)

        def run_a(self, x, x_ap):
            ss = self.stats(x, x_ap, 1024)
            xn = self.xn.next()
            g = self.g
            K.v("dve", lambda e: e.scalar_tensor_tensor(out=xn.t[:], in0=x_ap, scalar=ss.t[:, 3:4], in1=g.t[:],
                                                        op0=ALU.mult, op1=ALU.mult), [x, ss, g], [xn])
            return xn

        def run_b(self, xn, t, XTd):
            pb = K.pgen.next()
            pbv = pb.t[:].bitcast(BF16).rearrange("p (c t) -> p c t", c=8)
            for c in range(8):
                K.tr(pb, pbv[:, c, :], xn, xn.t[:, c * 128:(c + 1) * 128], ident, ident.t[:])
            xnT = self.xnT.next()
            K.act(xnT, xnT.t[:], [pb], pbv, AF.Copy)
            K.st(xnT, XTd[:, :, 1 + t * 128:1 + (t + 1) * 128].rearrange("c p t -> p c t"), xnT.t[:])

    def conv_chunk(pm, ph, hsl, cw, ci, dst):
        w0 = cw.t[:, 0, ci:ci + 1]
        w1 = cw.t[:, 1, ci:ci + 1]
        w2 = cw.t[:, 2, ci:ci + 1]
        bb = cw.t[:, 3, ci:ci + 1]
        K.act(dst, dst.t[:], [pm, cw], pm.t[:], AF.Identity, scale=w1, bias=bb)
        K.v("dve", lambda e: e.scalar_tensor_tensor(out=dst.t[:, 1:512], in0=pm.t[:, 0:511], scalar=w0,
                                                    in1=dst.t[:, 1:512], op0=ALU.mult, op1=ALU.add),
            [pm, cw, dst], [dst])
        K.v("dve", lambda e: e.scalar_tensor_tensor(out=dst.t[:, 0:511], in0=pm.t[:, 1:512], scalar=w2,
                                                    in1=dst.t[:, 0:511], op0=ALU.mult, op1=ALU.add),
            [pm, cw, dst], [dst])
        K.v("dve", lambda e: e.scalar_tensor_tensor(out=dst.t[:, 0:1], in0=ph.t[:, hsl:hsl + 1], scalar=w0,
                                                    in1=dst.t[:, 0:1], op0=ALU.mult, op1=ALU.add),
            [ph, cw, dst], [dst])
        K.v("dve", lambda e: e.scalar_tensor_tensor(out=dst.t[:, 511:512], in0=ph.t[:, hsl + 1:hsl + 2], scalar=w2,
                                                    in1=dst.t[:, 511:512], op0=ALU.mult, op1=ALU.add),
            [ph, cw, dst], [dst])

    def skewed(n, stages):
        ctxs = {}
        ns = len(stages)
        for step in range(n + ns - 1):
            for k in range(ns):
                i = step - k
                if 0 <= i < n:
                    if k == 0:
                        ctxs[i] = {}
                    stages[k](i, ctxs[i])
                    if k == ns - 1:
                        del ctxs[i]

    for jb in range(NJ):
        S = S_list[jb]
        NT = S // 128
        NB = S // 512
        x_in = xin[jb]
        y_out = yout[jb]
        NTo = jobs[jb]['NTo']
        NBo = NTo // 4
        out_lo, out_hi = jobs[jb]['out']
        prompt = jobs[jb]['prompt']
        SEAM_BLOCKS = (0, 4, 8, 12) if prompt else ()

        def seam_setup(cw, nch, ncand):
            sfb = K.tile([128, 4], F32, dma=True)
            bc_load(sfb, sfin[0], 4)
            nfw = K.tile([128, ncand, 2, nch], F32)
            for c in range(ncand):
                for jj, wi in ((0, 0), (1, 2)):
                    K.v("dve", lambda e, c=c, jj=jj, wi=wi, nfw=nfw, sfb=sfb, cw=cw: e.tensor_scalar_mul(
                        out=nfw.t[:, c, jj, :], in0=cw.t[:, wi, :], scalar1=sfb.t[:, c:c + 1]), [cw, sfb], [nfw])
            return nfw

        def seam_fix(pm, dst, nfw, c, ci):
            K.v("dve", lambda e: e.scalar_tensor_tensor(out=dst.t[:, 256:257], in0=pm.t[:, 255:256],
                                                        scalar=nfw.t[:, c, 0, ci:ci + 1], in1=dst.t[:, 256:257],
                                                        op0=ALU.mult, op1=ALU.add), [pm, nfw, dst], [dst])
            K.v("dve", lambda e: e.scalar_tensor_tensor(out=dst.t[:, 255:256], in0=pm.t[:, 256:257],
                                                        scalar=nfw.t[:, c, 1, ci:ci + 1], in1=dst.t[:, 255:256],
                                                        op0=ALU.mult, op1=ALU.add), [pm, nfw, dst], [dst])

        K.begin_sweep()
        nt = NormT()
        nt.load_gain(W['ab_norm'][0])
        for i in range(2):
            K.st(zero, XT[i][:, :, 0:1].rearrange("c p t -> p c t"), zero.t[:, 0:8].rearrange("p (c t) -> p c t", t=1),
                 slow=True)
            K.st(zero, XT[i][:, :, S + 1:S + 2].rearrange("c p t -> p c t"),
                 zero.t[:, 0:8].rearrange("p (c t) -> p c t", t=1), slow=True)
        xr = K.ring(3, [128, 1024], F32, dma=True)

        def a0(t, c):
            x = xr.next()
            K.ld(x, x.t[:], x_in[t * 128:(t + 1) * 128, :])
            c['xn'] = nt.run_a(x, x.t[:])

        def a1(t, c):
            nt.run_b(c['xn'], t, XT[0])

        skewed(NT, [a0, a1])

        K.begin_sweep()
        w_in = K.tile([128, 8, 2480], BF16, dma=True)
        K.ld(w_in, w_in.t[:], wb_in.rearrange("(k p) n -> p k n", p=128))
        w_uq = K.tile([128, 2, 768], BF16, dma=True)
        K.ld(w_uq, w_uq.t[:], wb_uq.rearrange("(k p) n -> p k n", p=128))
        w_ukv = K.tile([128, 1024], BF16, dma=True)
        K.ld(w_ukv, w_ukv.t[:], wb_ukv)
        qg = K.tile([128, 256], F32, dma=True)
        bc_load(qg, W['mla_q_norm'][0], 256)
        kvg = K.tile([128, 128], F32, dma=True)
        bc_load(kvg, W['mla_kv_norm'][0], 128)
        gbias = K.tile([128, 16], F32, dma=True)
        bc_load(gbias, W['mlstm_gate_bias'][0], 16)
        winr = K.ring(2, [128, 8, 514], BF16, dma=True)
        cgr = K.ring(2, [128, 512], F32)
        qkr = K.ring(2, [128, 512], BF16, dma=True)
        junk = K.tile([128, 256], BF16)
        ssr = K.ring(2, [128, 8], F32)
        cqn = K.ring(2, [128, 384], BF16)
        cTr = K.ring(2, [128, 3, 128], BF16)
        cosr = K.ring(2, [128, 32], F32, dma=True)
        krr = K.ring(2, [128, 32], BF16)
        tmp = K.ring(2, [128, 8, 64], F32)
        qst = K.ring(2, [128, 8, 96], BF16)
        qfr = K.ring(2, [128, 768], F32)
        kst = K.ring(2, [128, 8, 96], BF16)
        vst = K.ring(2, [128, 8, 65], BF16, dma=True)
        qTs = K.ring(2, [96, 8, 128], BF16, dma=True)
        kTs = K.ring(2, [96, 8, 128], BF16, dma=True)
        vmst = K.ring(2, [128, 4, 129], BF16, dma=True)
        sor = K.ring(2, [128, 512], BF16, dma=True)
        sxr = K.ring(2, [128, 512], F32)
        gtr = K.ring(2, [128, 16], F32, dma=True)
        ge = K.ring(2, [128, 2, 4], F32)
        for tl in vst.tiles + vmst.tiles:
            K.v("dve", lambda e, tl=tl: e.memset(tl.t[:], 1.0), [], [tl])
        mkr = K.ring(2, [128, 4], F32, dma=True)
        padb = Buf("xtpad")
        nfw_m = None
        if prompt:
            P.op("pool", lambda e, S=S: e.dma_start(out=XT[0][:, :, 0:1], in_=XT[0][:, :, S:S + 1],
                                                    allow_slow_non_contiguous=True), writes=[padb], dsem=wsem)
            P.op("pool", lambda e, S=S: e.dma_start(out=XT[0][:, :, S + 1:S + 2], in_=XT[0][:, :, 1:2],
                                                    allow_slow_non_contiguous=True), writes=[padb], dsem=wsem)
            nfw_m = seam_setup(cw_m, 8, 4)
        K.pgen = Ring(K.banks[0:6])
        for b in range(NB):
            win = winr.next()
            K.ld(win, win.t[:], XT[0][:, :, 512 * b:512 * b + 514].rearrange("c p t -> p c t"), reads=[padb])
            for c in range(8):
                if c < 4 and b >= NBo:
                    continue
                col0 = 416 + c * 128
                pm = K.pgen.next()
                hsl = 0
                halo = K.banks[6 + (c % 2)]
                for k in range(8):
                    K.mm(pm, pm.t[:, :], w_in, w_in.t[:, k, col0:col0 + 128], win, win.t[:, k, 1:513],
                         start=(k == 0), stop=(k == 7))
                for k in range(8):
                    K.mm(halo, halo.t[:, hsl:hsl + 2], w_in, w_in.t[:, k, col0:col0 + 128], win,
                         win.t[:, k, 0:514:513], start=(k == 0), stop=(k == 7))
                cg = cgr.next()
                conv_chunk(pm, halo, hsl, cw_m, c, cg)
                if b in SEAM_BLOCKS:
                    seam_fix(pm, cg, nfw_m, b // 4, c)
                qk = qkr.next()
                K.act(qk, qk.t[:], [cg], cg.t[:], AF.Silu)
                K.st(qk, QKT[c, :, 512 * b:512 * b + 512], qk.t[:])
            for tt in range(4):
                t = b * 4 + tt
                lo = 1 + tt * 128
                pa = K.pgen.next()
                pv = K.pgen.next()
                po = K.pgen.next()
                for k in range(8):
                    K.mm(pa, pa.t[:, 0:416], win, win.t[:, k, lo:lo + 128], w_in, w_in.t[:, k, 0:416],
                         start=(k == 0), stop=(k == 7))
                for k in range(8):
                    K.mm(pa, pa.t[:, 416:432], win, win.t[:, k, lo:lo + 128], w_in, w_in.t[:, k, 2464:2480],
                         start=(k == 0), stop=(k == 7))
                for k in range(8):
                    K.mm(pv, pv.t[:, :], win, win.t[:, k, lo:lo + 128], w_in, w_in.t[:, k, 1440:1952],
                         start=(k == 0), stop=(k == 7))
                is_own = t < NTo
                if is_own:
                    for k in range(8):
                        K.mm(po, po.t[:, :], win, win.t[:, k, lo:lo + 128], w_in, w_in.t[:, k, 1952:2464],
                             start=(k == 0), stop=(k == 7))
                vm = vmst.next()
                K.act(vm, vm.t[:, :, 0:128], [pv], pv.t[:, :].rearrange("p (h d) -> p h d", d=128), AF.Copy)
                K.st(vm, VM[t * 128:(t + 1) * 128, :, :], vm.t[:])
                if is_own:
                    so = sor.next()
                    sx = sxr.next()
                    K.act(sx, sx.t[:], [po], po.t[:, :], AF.Exp, scale=-1.0)
                    K.v("dve", lambda e, sx=sx: e.tensor_scalar_add(out=sx.t[:], in0=sx.t[:], scalar1=1.0), [sx], [sx])
                    K.v("dve", lambda e, sx=sx: e.reciprocal(out=sx.t[:], in_=sx.t[:]), [sx], [sx])
                    K.v("pool", lambda e, sx=sx, so=so: e.tensor_copy(out=so.t[:], in_=sx.t[:]), [sx], [so])
                    K.st(so, SO[t * 128:(t + 1) * 128, :], so.t[:])
                gt = gtr.next()
                g2 = ge.next()
                K.v("dve", lambda e, gt=gt, pa=pa: e.tensor_tensor(out=gt.t[:], in0=pa.t[:, 416:432], in1=gbias.t[:],
                                                                   op=ALU.add), [pa, gbias], [gt])
                gtv = gt.t[:, :].rearrange("p (a b) -> p a b", b=8)[:, :, 4:8]
                K.act(g2, g2.t[:], [gt], gtv, AF.Exp, scale=-1.0)
                K.v("dve", lambda e, g2=g2: e.tensor_scalar_add(out=g2.t[:], in0=g2.t[:], scalar1=1.0), [g2], [g2])
                K.act(g2, g2.t[:], [g2], g2.t[:], AF.Ln)
                K.v("dve", lambda e, g2=g2, gtv=gtv: e.tensor_scalar_mul(out=gtv, in0=g2.t[:], scalar1=-1.0), [g2], [gt])
                mk = mkr.next()
                K.ld(mk, mk.t[:], mkin[jb][t * 128:(t + 1) * 128, :])
                for (c0, mi) in ((0, 0), (8, 2)):
                    K.v("dve", lambda e, gt=gt, mk=mk, c0=c0, mi=mi: e.scalar_tensor_tensor(
                        out=gt.t[:, c0:c0 + 4], in0=gt.t[:, c0:c0 + 4], scalar=mk.t[:, mi:mi + 1],
                        in1=mk.t[:, mi + 1:mi + 2].to_broadcast([128, 4]), op0=ALU.mult, op1=ALU.add), [gt, mk], [gt])
                    K.v("dve", lambda e, gt=gt, mk=mk, c0=c0, mi=mi: e.tensor_scalar_mul(
                        out=gt.t[:, c0 + 4:c0 + 8], in0=gt.t[:, c0 + 4:c0 + 8], scalar1=mk.t[:, mi:mi + 1]), [gt, mk], [gt])
                K.st(gt, GG[t * 128:(t + 1) * 128, :], gt.t[:])
                ss = ssr.next()
                K.v("dve", lambda e, ss=ss: e.memset(ss.t[:, 0:2], 0.0), [], [ss])
                K.act(junk, junk.t[:, 0:256], [pa, ss], pa.t[:, 0:256], AF.Square, accum=ss.t[:, 0:1], extra_w=[ss])
                K.act(junk, junk.t[:, 0:128], [pa, ss], pa.t[:, 256:384], AF.Square, accum=ss.t[:, 1:2], extra_w=[ss])
                K.act(ss, ss.t[:, 4:5], [ss, epsb], ss.t[:, 0:1], AF.Ln, scale=1.0 / 256, bias=epsb.t[:, 0:1])
                K.act(ss, ss.t[:, 5:6], [ss, epsb], ss.t[:, 1:2], AF.Ln, scale=1.0 / 128, bias=epsb.t[:, 0:1])
                K.act(ss, ss.t[:, 6:8], [ss], ss.t[:, 4:6], AF.Exp, scale=-0.5)
                cq = cqn.next()
                K.v("dve", lambda e, ss=ss, cq=cq, pa=pa: e.scalar_tensor_tensor(
                    out=cq.t[:, 0:256], in0=pa.t[:, 0:256], scalar=ss.t[:, 6:7], in1=qg.t[:], op0=ALU.mult,
                    op1=ALU.mult), [pa, ss, qg], [cq])
                K.v("dve", lambda e, ss=ss, cq=cq, pa=pa: e.scalar_tensor_tensor(
                    out=cq.t[:, 256:384], in0=pa.t[:, 256:384], scalar=ss.t[:, 7:8], in1=kvg.t[:], op0=ALU.mult,
                    op1=ALU.mult), [pa, ss, kvg], [cq])
                cs = cosr.next()
                K.ld(cs, cs.t[:, 0:32], csin[jb][t * 128:(t + 1) * 128, :])
                kr = krr.next()
                tm = tmp.next()

                def rope(dst1, dst2, x1, x2, cosb, sinb, ta, tb, ins, outs, tmt):
                    K.v("dve", lambda e: e.tensor_tensor(out=ta, in0=x1, in1=cosb, op=ALU.mult), ins, [tmt])
                    K.v("dve", lambda e: e.tensor_tensor(out=tb, in0=x2, in1=sinb, op=ALU.mult), ins, [tmt])
                    K.v("dve", lambda e: e.tensor_tensor(out=dst1, in0=ta, in1=tb, op=ALU.subtract), [tmt], outs)
                    K.v("dve", lambda e: e.tensor_tensor(out=ta, in0=x2, in1=cosb, op=ALU.mult), ins + outs, [tmt])
                    K.v("dve", lambda e: e.tensor_tensor(out=tb, in0=x1, in1=sinb, op=ALU.mult), ins, [tmt])
                    K.v("dve", lambda e: e.tensor_tensor(out=dst2, in0=ta, in1=tb, op=ALU.add), [tmt], outs)

                rope(kr.t[:, 0:16], kr.t[:, 16:32], pa.t[:, 384:400], pa.t[:, 400:416], cs.t[:, 0:16], cs.t[:, 16:32],
                     tm.t[:, 0, 0:16], tm.t[:, 0, 16:32], [pa, cs], [kr], tm)
                pb = K.pgen.next()
                pbv = pb.t[:].bitcast(BF16).rearrange("p (c t) -> p c t", c=8)
                for c in range(3):
                    K.tr(pb, pbv[:, c, :], cq, cq.t[:, c * 128:(c + 1) * 128], ident, ident.t[:])
                cT = cTr.next()
                K.act(cT, cT.t[:], [pb], pbv[:, 0:3, :], AF.Copy)
                if is_own:
                    pq1 = K.pgen.next()
                    pq2 = K.pgen.next()
                    for k in range(2):
                        K.mm(pq1, pq1.t[:, 0:480], cT, cT.t[:, k, :], w_uq, w_uq.t[:, k, 0:480], start=(k == 0), stop=(k == 1))
                    for k in range(2):
                        K.mm(pq2, pq2.t[:, 0:288], cT, cT.t[:, k, :], w_uq, w_uq.t[:, k, 480:768], start=(k == 0), stop=(k == 1))
                    qs = qst.next()
                    qf = qfr.next()
                    K.act(qf, qf.t[:, 0:480], [pq1], pq1.t[:, 0:480], AF.Copy)
                    K.act(qf, qf.t[:, 480:768], [pq2], pq2.t[:, 0:288], AF.Copy)
                    qfv = qf.t[:, :].rearrange("p (h d) -> p h d", d=96)
                    K.v("dve", lambda e, qs=qs, qfv=qfv: e.tensor_copy(out=qs.t[:, :, 0:64], in_=qfv[:, :, 0:64]), [qf], [qs])
                    cosb = cs.t[:, 0:16].unsqueeze(1).to_broadcast([128, 8, 16])
                    sinb = cs.t[:, 16:32].unsqueeze(1).to_broadcast([128, 8, 16])
                    rope(qs.t[:, :, 64:80], qs.t[:, :, 80:96], qfv[:, :, 64:80], qfv[:, :, 80:96],
                         cosb, sinb, tm.t[:, :, 0:16], tm.t[:, :, 16:32], [qf, cs], [qs], tm)
                pk1 = K.pgen.next()
                pk2 = K.pgen.next()
                K.mm(pk1, pk1.t[:, :], cT, cT.t[:, 2, :], w_ukv, w_ukv.t[:, 0:512])
                K.mm(pk2, pk2.t[:, :], cT, cT.t[:, 2, :], w_ukv, w_ukv.t[:, 512:1024])
                ks = kst.next()
                vs = vst.next()
                for (pk, h0) in ((pk1, 0), (pk2, 4)):
                    pkv = pk.t[:, :].rearrange("p (h d) -> p h d", d=128)
                    K.act(ks, ks.t[:, h0:h0 + 4, 0:64], [pk], pkv[:, :, 0:64], AF.Copy)
                    K.v("dve", lambda e, vs=vs, pkv=pkv, h0=h0: e.tensor_copy(out=vs.t[:, h0:h0 + 4, 0:64],
                                                                             in_=pkv[:, :, 64:128]), [pk], [vs])
                K.v("dve", lambda e, ks=ks, kr=kr: e.tensor_copy(
                    out=ks.t[:, :, 64:96], in_=kr.t[:, :].unsqueeze(1).to_broadcast([128, 8, 32])), [kr], [ks])
                K.st(vs, VV[t * 128:(t + 1) * 128, :, :], vs.t[:])
                for (src, dstr, dd) in (((qs, qTs, QT), (ks, kTs, KT)) if is_own else ((ks, kTs, KT),)):
                    pb = K.pgen.next()
                    pbv = pb.t[:].bitcast(BF16).rearrange("p (c t) -> p c t", c=8)
                    for h in range(8):
                        K.tr(pb, pbv[0:96, h, :], src, src.t[:, h, :], ident, ident.t[:])
                    dT = dstr.next()
                    K.act(dT, dT.t[:], [pb], pbv[0:96, :, :], AF.Copy)
                    K.st(dT, dd[:, :, t * 128:(t + 1) * 128].rearrange("h p t -> p h t"), dT.t[:])
        K.pgen = Ring(K.banks[0:6])

        K.begin_sweep()
        kth = K.ring(2, [96, S], BF16, dma=True)
        vh = K.ring(2, [128, NT, 65], BF16, dma=True)
        qbr = K.ring(2, [96, 512], BF16, dma=True)
        ptr = K.ring(3, [128, 512], BF16)
        rcr = K.ring(2, [128, 4], F32)
        yar = K.ring(2, [128, 4, 64], BF16, dma=True)
        for h in range(8):
            kt = kth.next()
            K.ld(kt, kt.t[:], KT[h, :, 0:S])
            vv = vh.next()
            K.ld(vv, vv.t[:], VV[0:S, h, :].rearrange("(n p) d -> p n d", p=128))
            for qb in range(NBo):
                qt = qbr.next()
                K.ld(qt, qt.t[:], QT[h, :, qb * 512:(qb + 1) * 512])
                acc = K.pacc.next()

                def scores(kb, kt=kt, qt=qt):
                    ps = K.pgen.next()
                    K.mm(ps, ps.t[:, :], kt, kt.t[:, kb * 128:(kb + 1) * 128], qt, qt.t[:, :])
                    return ps

                LOOK = 2
                pend = [scores(kb) for kb in range(min(LOOK, NT))]
                for kb in range(NT):
                    ps = pend.pop(0)
                    if kb + LOOK < NT:
                        pend.append(scores(kb + LOOK))
                    pt = ptr.next()
                    K.act(pt, pt.t[:], [ps], ps.t[:, :], AF.Exp, scale=SC_ATT)
                    for j in range(4):
                        K.mm(acc, acc.t[:, j * 65:(j + 1) * 65], pt, pt.t[:, j * 128:(j + 1) * 128], vv, vv.t[:, kb, :],
                             start=(kb == 0 and j == 0), stop=(kb == NT - 1 and j == 3), skip=True)
                accv = acc.t[:, 0:260].rearrange("p (j d) -> p j d", d=65)
                rc = rcr.next()
                K.v("dve", lambda e, rc=rc, accv=accv: e.reciprocal(out=rc.t[:, :].unsqueeze(2), in_=accv[:, :, 64:65]),
                    [acc], [rc])
                ya = yar.next()
                K.v("dve", lambda e, rc=rc, accv=accv, ya=ya: e.tensor_tensor(
                    out=ya.t[:], in0=accv[:, :, 0:64], in1=rc.t[:, :].unsqueeze(2).to_broadcast([128, 4, 64]),
                    op=ALU.mult), [acc, rc], [ya])
                K.st(ya, YY[qb * 512:(qb + 1) * 512, h * 64:(h + 1) * 64].rearrange("(j p) d -> p j d", p=128), ya.t[:])

        for direction in (0, 1):
            K.begin_sweep()
            K.pgen = Ring(K.banks[0:8])
            tri = triu if direction == 0 else tril
            gi0 = 0 if direction == 0 else 8
            qkl = K.ring(3, [128, 8, 128], BF16, dma=True)
            vml = K.ring(3, [128, 4, 129], BF16, dma=True)
            gl = K.ring(3, [128, 16], F32, dma=True)
            ktok = K.ring(3, [128, 4, 128], BF16)
            sm = K.ring(3, [128, 6, 4], F32)
            ptm = K.ring(3, [128, 4, 128], BF16)
            dnr = K.ring(2, [128, 3, 4], F32)
            vwr = K.ring(2, [128, 4, 129], BF16)
            ST = K.tile([128, 4, 129], F32)
            STb = K.tile([128, 4, 129], BF16)
            hfr = K.ring(3, [128, 512], F32, dma=True)
            if direction == 1:
                sol = K.ring(3, [128, 512], BF16, dma=True)
                og = K.tile([128, 512], F32, dma=True)
                bc_load(og, W['mlstm_out_norm'][0], 512)
                sq = K.tile([128, 512], F32)
                ms = K.ring(2, [128, 3, 4], F32)
                ymr = K.ring(2, [128, 512], BF16, dma=True)
            K.v("dve", lambda e, ST=ST: e.memset(ST.t[:], 0.0), [], [ST])
            K.v("dve", lambda e, STb=STb: e.memset(STb.t[:], 0.0), [], [STb])
            own_c = list(range(NTo))
            ctx_c = list(range(NTo, NT))
            if direction == 0:
                order = [(n, True) for n in ctx_c] + [(n, False) for n in own_c]
            else:
                order = [(n, True) for n in reversed(ctx_c)] + [(n, False) for n in reversed(own_c)]

            def stage1(n, ctx_only, direction=direction, tri=tri, gi0=gi0):
                c = dict(n=n, ctx_only=ctx_only)
                r0, r1 = n * 128, (n + 1) * 128
                qk = qkl.next()
                if ctx_only:
                    K.ld(qk, qk.t[:, 4:8, :], QKT[4:8, :, r0:r1].rearrange("c p t -> p c t"))
                else:
                    K.ld(qk, qk.t[:], QKT[:, :, r0:r1].rearrange("c p t -> p c t"))
                vm = vml.next()
                K.ld(vm, vm.t[:], VM[r0:r1, :, :])
                g = gl.next()
                K.ld(g, g.t[:], GG[r0:r1, :])
                c.update(qk=qk, vm=vm, g=g)
                if not ctx_only:
                    hf = hfr.next()
                    c['hf'] = hf
                    if direction == 1:
                        K.ld(hf, hf.t[:], HF[r0:r1, :])
                        so = sol.next()
                        K.ld(so, so.t[:], SO[r0:r1, :])
                        c['so'] = so
                li = g.t[:, gi0:gi0 + 4]
                lf = g.t[:, gi0 + 4:gi0 + 8]
                pb = K.pgen.next()
                pbv = pb.t[:].bitcast(BF16).rearrange("p (c t) -> p c t", c=8)
                for h in range(4):
                    K.tr(pb, pbv[:, h, :], qk, qk.t[:, 4 + h, :], ident, ident.t[:])
                kk = ktok.next()
                K.act(kk, kk.t[:], [pb], pbv[:, 0:4, :], AF.Copy)
                pc = K.pgen.next()
                K.mm(pc, pc.t[:, 0:4], tri, tri.t[:], g, lf)
                K.mm(pc, pc.t[:, 4:8], ones, ones.t[:], g, lf)
                s_ = sm.next()
                K.v("dve", lambda e, s_=s_, pc=pc, li=li: e.scalar_tensor_tensor(
                    out=s_.t[:, 0, :], in0=li, scalar=LNSC_M, in1=pc.t[:, 0:4], op0=ALU.add, op1=ALU.subtract),
                    [g, pc], [s_])
                K.v("dve", lambda e, s_=s_, pc=pc: e.tensor_tensor(out=s_.t[:, 1, :], in0=s_.t[:, 0, :], in1=pc.t[:, 4:8],
                                                                   op=ALU.add), [pc, s_], [s_])
                K.act(s_, s_.t[:, 2:4, :], [pc], pc.t[:, 0:8].rearrange("p (a b) -> p a b", b=4), AF.Exp)
                K.act(s_, s_.t[:, 4:6, :], [s_], s_.t[:, 0:2, :], AF.Exp)
                c.update(kk=kk, s_=s_)
                if not ctx_only:
                    pS = K.pgen.next()
                    for h in range(4):
                        K.mm(pS, pS.t[:, h * 128:(h + 1) * 128], qk, qk.t[:, 4 + h, :], qk, qk.t[:, h, :])
                    pt = ptm.next()
                    for h in range(4):
                        K.v("dve", lambda e, pS=pS, pt=pt, s_=s_, h=h, tri=tri: e.scalar_tensor_tensor(
                            out=pt.t[:, h, :], in0=pS.t[:, h * 128:(h + 1) * 128], scalar=s_.t[:, 4, h:h + 1],
                            in1=tri.t[:], op0=ALU.mult, op1=ALU.mult), [pS, s_, tri], [pt])
                    c['pt'] = pt
                return c

            def stage2(c, direction=direction):
                n, ctx_only = c['n'], c['ctx_only']
                r0, r1 = n * 128, (n + 1) * 128
                qk, vm, kk, s_ = c['qk'], c['vm'], c['kk'], c['s_']
                if not ctx_only:
                    pt, hf = c['pt'], c['hf']
                    pN = [K.pgen.next(), K.pgen.next()]
                    for h in range(4):
                        pp = pN[h // 2]
                        c0 = (h % 2) * 129
                        K.mm(pp, pp.t[:, c0:c0 + 129], pt, pt.t[:, h, :], vm, vm.t[:, h, :], start=True, stop=False)
                        K.mm(pp, pp.t[:, c0:c0 + 129], qk, qk.t[:, h, :], STb, STb.t[:, h, :], start=False, stop=True)
                    dn = dnr.next()
                    for bi in range(2):
                        K.v("dve", lambda e, dn=dn, bi=bi, pp=pN[bi], s_=s_: e.tensor_tensor(
                            out=dn.t[:, 0, 2 * bi:2 * bi + 2], in0=pp.t[:, 128:258:129], in1=s_.t[:, 2, 2 * bi:2 * bi + 2],
                            op=ALU.mult), [pN[bi], s_], [dn])
                    K.v("dve", lambda e, dn=dn: e.tensor_scalar_mul(out=dn.t[:, 1, :], in0=dn.t[:, 0, :], scalar1=-1.0),
                        [dn], [dn])
                    K.v("dve", lambda e, dn=dn: e.tensor_tensor(out=dn.t[:, 0, :], in0=dn.t[:, 0, :], in1=dn.t[:, 1, :],
                                                                op=ALU.max), [dn], [dn])
                    K.v("dve", lambda e, dn=dn: e.tensor_scalar_max(out=dn.t[:, 0, :], in0=dn.t[:, 0, :], scalar1=1.0),
                        [dn], [dn])
                    K.v("dve", lambda e, dn=dn: e.reciprocal(out=dn.t[:, 1, :], in_=dn.t[:, 0, :]), [dn], [dn])
                    K.v("dve", lambda e, dn=dn, s_=s_: e.tensor_tensor(out=dn.t[:, 2, :], in0=dn.t[:, 1, :],
                                                                       in1=s_.t[:, 2, :], op=ALU.mult), [dn, s_], [dn])
                    for h in range(4):
                        pp = pN[h // 2]
                        c0 = (h % 2) * 129
                        hs_ap = hf.t[:, h * 128:(h + 1) * 128]
                        if direction == 0:
                            K.act(hf, hs_ap, [pp, dn], pp.t[:, c0:c0 + 128], AF.Copy, scale=dn.t[:, 2, h:h + 1])
                        else:
                            K.v("dve", lambda e, pp=pp, c0=c0, dn=dn, hs_ap=hs_ap, h=h: e.scalar_tensor_tensor(
                                out=hs_ap, in0=pp.t[:, c0:c0 + 128], scalar=dn.t[:, 2, h:h + 1], in1=hs_ap,
                                op0=ALU.mult, op1=ALU.add), [pp, dn, hf], [hf])
                vw = vwr.next()
                for h in range(4):
                    K.v("pool", lambda e, vw=vw, vm=vm, s_=s_, h=h: e.tensor_scalar_mul(
                        out=vw.t[:, h, :], in0=vm.t[:, h, :], scalar1=s_.t[:, 5, h:h + 1]), [vm, s_], [vw])
                pC = [K.pgen.next(), K.pgen.next()]
                for h in range(4):
                    pp = pC[h // 2]
                    c0 = (h % 2) * 129
                    K.mm(pp, pp.t[:, c0:c0 + 129], kk, kk.t[:, h, :], vw, vw.t[:, h, :])
                for h in range(4):
                    pp = pC[h // 2]
                    c0 = (h % 2) * 129
                    K.v("dve", lambda e, pp=pp, c0=c0, s_=s_, h=h, ST=ST: e.scalar_tensor_tensor(
                        out=ST.t[:, h, :], in0=ST.t[:, h, :], scalar=s_.t[:, 3, h:h + 1], in1=pp.t[:, c0:c0 + 129],
                        op0=ALU.mult, op1=ALU.add), [ST, s_, pp], [ST])
                K.act(STb, STb.t[:], [ST], ST.t[:], AF.Copy)
                if ctx_only:
                    return
                if direction == 0:
                    K.st(hf, HF[r0:r1, :], hf.t[:])
                else:
                    so = c['so']
                    m_ = ms.next()
                    K.v("dve", lambda e, hf=hf, sq=sq: e.tensor_tensor(out=sq.t[:], in0=hf.t[:], in1=hf.t[:], op=ALU.mult),
                        [hf], [sq])
                    K.v("dve", lambda e, m_=m_, sq=sq: e.tensor_reduce(
                        out=m_.t[:, 0, :], in_=sq.t[:, :].rearrange("p (h d) -> p h d", d=128), axis=AX.X, op=ALU.add),
                        [sq], [m_])
                    K.act(m_, m_.t[:, 2, :], [m_, epsb], m_.t[:, 0, :], AF.Ln, scale=1.0 / 128, bias=epsb.t[:, 0:1])
                    K.act(m_, m_.t[:, 0, :], [m_], m_.t[:, 2, :], AF.Exp, scale=-0.5)
                    K.v("dve", lambda e, m_=m_, hf=hf, sq=sq: e.tensor_tensor(
                        out=sq.t[:, :].rearrange("p (h d) -> p h d", d=128),
                        in0=hf.t[:, :].rearrange("p (h d) -> p h d", d=128),
                        in1=m_.t[:, 0, :].unsqueeze(2).to_broadcast([128, 4, 128]), op=ALU.mult), [hf, m_], [sq])
                    K.v("dve", lambda e, sq=sq, og=og: e.tensor_tensor(out=sq.t[:], in0=sq.t[:], in1=og.t[:], op=ALU.mult),
                        [sq, og], [sq])
                    ym = ymr.next()
                    K.v("dve", lambda e, ym=ym, so=so, sq=sq: e.tensor_tensor(out=ym.t[:], in0=sq.t[:], in1=so.t[:],
                                                                              op=ALU.mult), [sq, so], [ym])
                    K.st(ym, YY[r0:r1, 512:1024], ym.t[:])

            cur = stage1(*order[0])
            for i in range(len(order)):
                nxt = stage1(*order[i + 1]) if i + 1 < len(order) else None
                stage2(cur)
                cur = nxt
            K.pgen = Ring(K.banks[0:6])

        def tail_a(x, pr, t, nt, final_g=None):
            for half in range(2):
                K.v("dve", lambda e, half=half: e.tensor_tensor(
                    out=x.t[:, half * 512:(half + 1) * 512], in0=x.t[:, half * 512:(half + 1) * 512],
                    in1=pr[half].t[:, :], op=ALU.add), [x, pr[half]], [x])
            if final_g is None:
                K.st(x, X1[t * 128:(t + 1) * 128, :], x.t[:])
                return nt.run_a(x, x.t[:])
            ss = nt.stats(x, x.t[:], 1024)
            K.v("dve", lambda e, ss=ss: e.scalar_tensor_tensor(out=x.t[:], in0=x.t[:], scalar=ss.t[:, 3:4],
                                                               in1=final_g.t[:], op0=ALU.mult, op1=ALU.mult),
                [x, ss, final_g], [x])
            if out_lo <= t < out_hi:
                K.st(x, y_out[(t - out_lo) * 128:(t - out_lo + 1) * 128, :], x.t[:])
            return None

        def tail(x, pr, t, nt, XTd, final_g=None):
            for half in range(2):
                K.v("dve", lambda e, half=half: e.tensor_tensor(
                    out=x.t[:, half * 512:(half + 1) * 512], in0=x.t[:, half * 512:(half + 1) * 512],
                    in1=pr[half].t[:, :], op=ALU.add), [x, pr[half]], [x])
            if final_g is None:
                K.st(x, X1[t * 128:(t + 1) * 128, :], x.t[:])
                nt.run(x, x.t[:], t, XTd)
            else:
                ss = nt.stats(x, x.t[:], 1024)
                K.v("dve", lambda e, ss=ss: e.scalar_tensor_tensor(out=x.t[:], in0=x.t[:], scalar=ss.t[:, 3:4],
                                                                   in1=final_g.t[:], op0=ALU.mult, op1=ALU.mult),
                    [x, ss, final_g], [x])
                if out_lo <= t < out_hi:
                    K.st(x, y_out[(t - out_lo) * 128:(t - out_lo + 1) * 128, :], x.t[:])

        def project_tm(src_tile, src_ap_k, wt, w_ap_k, nk):
            pr = [K.pgen.next(), K.pgen.next()]
            for half in range(2):
                for k in range(nk):
                    K.mm(pr[half], pr[half].t[:, :], src_tile, src_ap_k(k), wt, w_ap_k(k, half), start=(k == 0),
                         stop=(k == nk - 1))
            return pr

        K.begin_sweep()
        nt = NormT()
        nt.load_gain(W['ffn_norm'][0])
        w_o = K.tile([128, 8, 1024], BF16, dma=True)
        K.ld(w_o, w_o.t[:], wb_out.rearrange("(k p) n -> p k n", p=128))
        xr = K.ring(3, [128, 1024], F32, dma=True)
        yr = K.ring(3, [128, 1024], BF16, dma=True)
        yTr = K.ring(3, [128, 8, 128], BF16)
        K.pgen = Ring(K.banks[0:8])

        def f0(t, c):
            y = yr.next()
            K.ld(y, y.t[:], YY[t * 128:(t + 1) * 128, :])
            pb = K.pgen.next()
            pbv = pb.t[:].bitcast(BF16).rearrange("p (c t) -> p c t", c=8)
            for cc in range(8):
                K.tr(pb, pbv[:, cc, :], y, y.t[:, cc * 128:(cc + 1) * 128], ident, ident.t[:])
            yT = yTr.next()
            K.act(yT, yT.t[:], [pb], pbv, AF.Copy)
            c['pr'] = project_tm(yT, lambda k: yT.t[:, k, :], w_o,
                                 lambda k, half: w_o.t[:, k, half * 512:(half + 1) * 512], 8)

        def f1(t, c):
            x = xr.next()
            K.ld(x, x.t[:], x_in[t * 128:(t + 1) * 128, :])
            c['xn'] = tail_a(x, c['pr'], t, nt)

        def f2(t, c):
            nt.run_b(c['xn'], t, XT[1])

        skewed(NTo, [f0, f1, f2])
        K.pgen = Ring(K.banks[0:6])

        def ffn_sweep(l, XTs, XTd, gain_next, final):
            K.begin_sweep()
            wup = K.tile([128, 8, 5632], BF16, dma=True)
            for k in range(8):
                K.ld(wup, wup.t[:, k, :], wb_up[l, k * 128:(k + 1) * 128, :])
            wdn = K.tile([128, NFC, 1024], BF16, dma=True)
            K.ld(wdn, wdn.t[:], wb_dn[l].rearrange("(i p) n -> p i n", p=128))
            nt = NormT()
            fg = None
            if final:
                fg = nt.g
            nt.load_gain(gain_next)
            win = K.tile([128, 8, 514], BF16, dma=True)
            hT = K.tile([128, NFC, 512], BF16)
            cgr = K.ring(2, [128, 512], F32)
            cvr = K.ring(2, [128, 512], F32)
            sgr = K.ring(2, [128, 512], BF16)
            xr = K.ring(2, [128, 1024], F32, dma=True)
            K.pgen = Ring(K.banks[0:6])
            cw = cw_f[l]
            hs_i = 0
            nfw_f = seam_setup(cw, 44, 2) if prompt else None
            pending = []
            for b in range(NBo):
                K.ld(win, win.t[:], XTs[:, :, 512 * b:512 * b + 514].rearrange("c p t -> p c t"))
                for i in range(NFC):
                    outs = []
                    for (col0, ci, rr) in ((i * 128, i, cgr), (DFF + i * 128, NFC + i, cvr)):
                        pm = K.pgen.next()
                        hsl = 0
                        halo = K.banks[6 + (hs_i % 2)]
                        hs_i += 1
                        for k in range(8):
                            K.mm(pm, pm.t[:, :], wup, wup.t[:, k, col0:col0 + 128], win, win.t[:, k, 1:513],
                                 start=(k == 0), stop=(k == 7))
                        for k in range(8):
                            K.mm(halo, halo.t[:, hsl:hsl + 2], wup, wup.t[:, k, col0:col0 + 128], win,
                                 win.t[:, k, 0:514:513], start=(k == 0), stop=(k == 7))
                        cc = rr.next()
                        conv_chunk(pm, halo, hsl, cw, ci, cc)
                        if prompt and b in (0, 4):
                            seam_fix(pm, cc, nfw_f, b // 4, ci)
                        outs.append(cc)
                    cg, cv = outs
                    sg = sgr.next()
                    K.act(sg, sg.t[:], [cg], cg.t[:], AF.Silu)
                    K.v("pool", lambda e, sg=sg, cv=cv, i=i: e.tensor_tensor(out=hT.t[:, i, :], in0=sg.t[:], in1=cv.t[:],
                                                                             op=ALU.mult), [sg, cv], [hT])
                    if pending and i >= 1:
                        pending.pop(0)()
                while pending:
                    pending.pop(0)()
                prs = {}
                xns = {}

                def d0(tt, b=b, prs=prs):
                    prs[tt] = project_tm(hT, lambda k: hT.t[:, k, tt * 128:(tt + 1) * 128], wdn,
                                         lambda k, half: wdn.t[:, k, half * 512:(half + 1) * 512], NFC)

                def d1(tt, b=b, prs=prs, xns=xns):
                    t = b * 4 + tt
                    x = xr.next()
                    K.ld(x, x.t[:], X1[t * 128:(t + 1) * 128, :])
                    xns[tt] = tail_a(x, prs[tt], t, nt, final_g=fg)

                def d2(tt, b=b, xns=xns):
                    if xns[tt] is not None:
                        nt.run_b(xns[tt], b * 4 + tt, XTd)

                for fn_, a_ in ((d0, 0), (d0, 1), (d1, 0), (d0, 2), (d1, 1), (d2, 0), (d0, 3), (d1, 2), (d2, 1), (d1, 3),
                                (d2, 2), (d2, 3)):
                    fn_(a_)
            while pending:
                pending.pop(0)()
            K.pgen = Ring(K.banks[0:6])

        ffn_sweep(0, XT[1], XT[0], W['c_norm'][0], False)

        K.begin_sweep()
        nt = NormT()
        nt.load_gain(W['ffn_norm'][1])
        cwin = K.tile([128, 8, 2048], BF16, dma=True)
        K.ld(cwin, cwin.t[:], wb_cin.rearrange("(k p) n -> p k n", p=128))
        cwo = K.tile([128, 8, 1024], BF16, dma=True)
        K.ld(cwo, cwo.t[:], wb_cout.rearrange("(k p) n -> p k n", p=128))
        vg = K.tile([128, 1024], F32, dma=True)
        bc_load(vg, W['c_v_norm'][0], 1024)
        xTl = K.ring(2, [128, 8, 128], BF16, dma=True)
        xr = K.ring(2, [128, 1024], F32, dma=True)
        ur = K.ring(2, [128, 2048], F32)
        vnr = K.ring(2, [128, 1024], BF16)
        zr = K.ring(2, [128, 1024], BF16)
        zTr = K.ring(2, [128, 8, 128], BF16)
        for t in range(NTo):
            xT = xTl.next()
            K.ld(xT, xT.t[:], XT[0][:, :, 1 + t * 128:1 + (t + 1) * 128].rearrange("c p t -> p c t"))
            x = xr.next()
            K.ld(x, x.t[:], X1[t * 128:(t + 1) * 128, :])
            uv = ur.next()
            for j in range(4):
                pu = K.pgen.next()
                for k in range(8):
                    K.mm(pu, pu.t[:, :], xT, xT.t[:, k, :], cwin, cwin.t[:, k, j * 512:(j + 1) * 512], start=(k == 0),
                         stop=(k == 7))
                K.act(uv, uv.t[:, j * 512:(j + 1) * 512], [pu], pu.t[:, :], AF.Gelu)
            ss = nt.stats(uv, uv.t[:, 1024:2048], 1024)
            vn = vnr.next()
            K.v("dve", lambda e, ss=ss, uv=uv, vn=vn: e.scalar_tensor_tensor(
                out=vn.t[:], in0=uv.t[:, 1024:2048], scalar=ss.t[:, 3:4], in1=vg.t[:], op0=ALU.mult, op1=ALU.mult),
                [uv, ss, vg], [vn])
            z = zr.next()
            psv = [K.pgen.next(), K.pgen.next()]
            for g in range(8):
                pp = psv[g // 4]
                K.mm(pp, pp.t[:, (g % 4) * 128:(g % 4 + 1) * 128], wsT, wsT.t[:, g, :], vn, vn.t[:, g * 128:(g + 1) * 128])
            for g in range(8):
                pp = psv[g // 4]
                K.v("dve", lambda e, pp=pp, g=g, uv=uv, z=z: e.scalar_tensor_tensor(
                    out=z.t[:, g * 128:(g + 1) * 128], in0=pp.t[:, (g % 4) * 128:(g % 4 + 1) * 128],
                    scalar=bsT.t[:, g:g + 1], in1=uv.t[:, g * 128:(g + 1) * 128], op0=ALU.add, op1=ALU.mult),
                    [pp, bsT, uv], [z])
            if DEBUG and t == 0:
                P.op("sp", lambda e: e.dma_start(out=DBG_wsT, in_=wsT.t[:, :, :].rearrange("p a b -> p (a b)")), reads=[wsT.b], dsem=wsem)
                P.op("sp", lambda e: e.dma_start(out=DBG_bsT, in_=bsT.t[:]), reads=[bsT.b], dsem=wsem)
                P.op("sp", lambda e, uv=uv: e.dma_start(out=DBG_uv, in_=uv.t[:]), reads=[uv.b], dsem=wsem)
                P.op("sp", lambda e, vn=vn: e.dma_start(out=DBG_vn, in_=vn.t[:]), reads=[vn.b], dsem=wsem)
                P.op("sp", lambda e, z=z: e.dma_start(out=DBG_z, in_=z.t[:]), reads=[z.b], dsem=wsem)
            pb = K.pgen.next()
            pbv = pb.t[:].bitcast(BF16).rearrange("p (c t) -> p c t", c=8)
            for c in range(8):
                K.tr(pb, pbv[:, c, :], z, z.t[:, c * 128:(c + 1) * 128], ident, ident.t[:])
            zT = zTr.next()
            K.act(zT, zT.t[:], [pb], pbv, AF.Copy)
            pr = project_tm(zT, lambda k: zT.t[:, k, :], cwo, lambda k, half: cwo.t[:, k, half * 512:(half + 1) * 512], 8)
            tail(x, pr, t, nt, XT[1])

        ffn_sweep(1, XT[1], XT[0], W['final_norm'], True)

    P.emit()
    nc._dbg_names = dbg_names
    return nc


BIG = 30000.0
OWN_EXT = 2560
HALO = 256


def make_consts():
    return {
        "c_ident": np.eye(128, dtype=np.float32),
        "c_triu": np.triu(np.ones((128, 128), np.float32)),
        "c_tril": np.tril(np.ones((128, 128), np.float32)),
        "c_ones": np.ones((128, 128), np.float32),
    }


def rope_table(pos):
    inv = (1.0 / (np.float32(10000.0) ** (np.arange(0, 32, 2, dtype=np.float32) / np.float32(32)))).astype(np.float32)
    ang = pos.astype(np.float32)[:, None] * inv[None, :]
    return np.concatenate([np.cos(ang), np.sin(ang)], axis=1).astype(np.float32)


def mask_table(mf, mb):
    mf = mf.astype(np.float32)
    mb = mb.astype(np.float32)
    return np.stack([mf, (mf - 1) * BIG, mb, (mb - 1) * BIG], axis=1).astype(np.float32)


def aux_full(S):
    ones = np.ones(S, np.float32)
    return rope_table(np.arange(S)), mask_table(ones, ones)


def aux_quarter(S, j):
    q = S // 4
    shift = j * q - HALO
    r = np.arange(S)
    t = r + shift
    pos = t % S
    own = r < OWN_EXT
    valid = (t >= 0) & (t < S)
    t_start = max(shift, 0)
    t_end = min(shift + OWN_EXT, S)
    mf = np.where(own, valid, pos < t_start)
    mb = np.where(own, valid, pos >= t_end)
    sf = np.zeros((1, 4), np.float32)
    sf[0, (4 - j) % 4] = -1.0
    return shift, rope_table(pos), mask_table(mf, mb), sf


_CACHE = {}


def run(core_inputs, params, jobs):
    jobs = norm_jobs(jobs)
    key = repr(jobs)
    if key not in _CACHE:
        _CACHE[key] = build_program(jobs)
    nc = _CACHE[key]
    consts = make_consts()
    in_maps = []
    for ci in core_inputs:
        m = {k: np.ascontiguousarray(v, dtype=np.float32) for k, v in ci.items()}
        for j, jd in enumerate(jobs):
            if ("cs%d" % j) not in m:
                cs, mk = aux_full(jd['S'])
                m["cs%d" % j] = cs
                m["mk%d" % j] = mk
        if "sf" not in m:
            m["sf"] = np.zeros((1, 4), np.float32)
        for n in PARAMS:
            m[n] = np.ascontiguousarray(params[n], dtype=np.float32)
        m.update(consts)
        in_maps.append(m)
    res = run_bass_kernel_spmd(nc, in_maps, core_ids=list(range(len(core_inputs))))
    global LAST_RES
    LAST_RES = res.results
    return [[r["y%d" % j] for j in range(len(jobs))] for r in res.results]


def kernel(**inputs):
    xp = np.asarray(inputs['x_prompt'], dtype=np.float32)
    xs = np.asarray(inputs['x_sample'], dtype=np.float32)
    params = {n: np.asarray(inputs[n], dtype=np.float32) for n in PARAMS}
    SP = xp.shape[1]
    jobs = [4096, 4096, dict(S=SP, NTo=OWN_EXT // 128, out=(HALO // 128, (OWN_EXT - HALO) // 128), prompt=True)]
    core_inputs = []
    for c in range(8):
        p, j = c // 4, c % 4
        shift, cs, mk, sf = aux_quarter(SP, j)
        core_inputs.append({"x0": xs[2 * c], "x1": xs[2 * c + 1], "x2": np.roll(xp[p], -shift, axis=0),
                            "cs2": cs, "mk2": mk, "sf": sf})
    outs = run(core_inputs, params, jobs)
    q = SP // 4
    y_prompt = np.zeros_like(xp)
    for c in range(8):
        p, j = c // 4, c % 4
        y_prompt[p, j * q:(j + 1) * q] = outs[c][2]
    y_sample = np.stack([outs[c][j] for c in range(8) for j in range(2)], axis=0)
    return (y_prompt, y_sample)
```

```python
import math
import numpy as np
import concourse.bass as bass
import concourse.mybir as mybir
from concourse.bass_utils import run_bass_kernel_spmd

F32 = mybir.dt.float32
BF16 = mybir.dt.bfloat16
ALU = mybir.AluOpType
AF = mybir.ActivationFunctionType
AX = mybir.AxisListType

D = 1024
EPS = 1e-6
DFF = 2816
NFC = 22
SC_ATT = 96 ** -0.5
SC_M = 128 ** -0.5
LNSC_M = math.log(SC_M)

PARAMS = ['ab_norm', 'ab_w_in', 'mla_q_norm', 'mla_w_uq', 'mla_kv_norm', 'mla_w_ukv', 'mlstm_conv_w',
          'mlstm_conv_b', 'mlstm_gate_bias', 'mlstm_out_norm', 'ab_w_out', 'c_norm', 'c_w_in', 'c_v_norm',
          'c_w_spatial', 'c_b_spatial', 'c_w_out', 'ffn_norm', 'ffn_w_up', 'ffn_conv_w', 'ffn_conv_b',
          'ffn_w_down', 'final_norm']
PSHAPES = {
    'ab_norm': [1, 1024], 'ab_w_in': [1, 1024, 2480], 'mla_q_norm': [1, 256], 'mla_w_uq': [1, 256, 768],
    'mla_kv_norm': [1, 128], 'mla_w_ukv': [1, 128, 1024], 'mlstm_conv_w': [1, 3, 1024], 'mlstm_conv_b': [1, 1024],
    'mlstm_gate_bias': [1, 16], 'mlstm_out_norm': [1, 512], 'ab_w_out': [1, 1024, 1024], 'c_norm': [1, 1024],
    'c_w_in': [1, 1024, 2048], 'c_v_norm': [1, 1024], 'c_w_spatial': [1, 8, 128, 128], 'c_b_spatial': [1, 8, 128],
    'c_w_out': [1, 1024, 1024], 'ffn_norm': [2, 1024], 'ffn_w_up': [2, 1024, 5632], 'ffn_conv_w': [2, 3, 5632],
    'ffn_conv_b': [2, 5632], 'ffn_w_down': [2, 2816, 1024], 'final_norm': [1024],
}


class Buf:
    __slots__ = ("name", "w", "r", "excl")

    def __init__(self, name, excl=False):
        self.name = name
        self.w = None
        self.r = []
        self.excl = excl


class Prog:
    ENGS = ("pe", "act", "dve", "pool", "sp")

    def __init__(self, nc, n_dma_sems=80):
        self.nc = nc
        self.ops = {e: [] for e in self.ENGS}
        self.count = {}
        self.seen = {e: {} for e in self.ENGS}
        self.pending = {e: [] for e in self.ENGS}
        self.dma_sems = ["d%d" % i for i in range(n_dma_sems)]
        self.nops = 0
        self.disabled = False
        self.sweep_no = 0
        import os
        self.maxops = int(os.environ.get("K_OPS", "1000000000"))

    def op(self, eng, fn, reads=(), writes=(), dsem=None):
        if self.disabled or self.nops >= self.maxops:
            return None
        deps = []
        for b in reads:
            if b.w is not None:
                deps.append(b.w)
            if b.excl:
                for t in b.r:
                    if t[0] != eng:
                        deps.append(t)
        for b in writes:
            if b.w is not None:
                deps.append(b.w)
            deps.extend(b.r)
        if dsem is None:
            key, inc = eng, 1
        else:
            key, inc = dsem, 16
        self.count[key] = self.count.get(key, 0) + inc
        tok = (key, self.count[key])
        seen = self.seen[eng]
        waits = []
        if self.pending[eng]:
            for k, v in self.pending[eng]:
                if seen.get(k, 0) < v:
                    seen[k] = v
                    waits.append((k, v))
            self.pending[eng] = []
        for k, v in deps:
            if k == "pe" and eng == "pe" and dsem is None:
                continue
            if seen.get(k, 0) >= v:
                continue
            seen[k] = v
            waits.append((k, v))
        self.ops[eng].append((fn, waits, key, inc))
        for b in reads:
            b.r.append(tok)
        for b in writes:
            b.w = tok
            b.r = []
        self.nops += 1
        return tok

    def barrier(self):
        snap = [(k, v) for k, v in self.count.items()]
        for e in self.ENGS:
            self.pending[e] = list(snap)

    def emit(self):
        from contextlib import ExitStack
        nc = self.nc
        self.barrier()
        with ExitStack() as st:
            sems = {}
            for k in list(self.ENGS) + self.dma_sems:
                sems[k] = st.enter_context(nc.semaphore("s_" + k))
            block = st.enter_context(nc.Block())

            def mk(e):
                def section(eng):
                    for fn, waits, key, inc in self.ops[e]:
                        for k, v in waits:
                            eng.wait_ge(sems[k], v)
                        ins = fn(eng)
                        ins.then_inc(sems[key], inc)
                    for k, v in self.pending[e]:
                        if self.seen[e].get(k, 0) < v:
                            eng.wait_ge(sems[k], v)
                return section

            block.tensor(mk("pe"))
            block.scalar(mk("act"))
            block.vector(mk("dve"))
            block.gpsimd(mk("pool"))
            block.sync(mk("sp"))


class TL:
    __slots__ = ("t", "b", "sem")

    def __init__(self, t, name, sem=None):
        self.t = t
        self.b = Buf(name)
        self.sem = sem


class Ring:
    def __init__(self, tiles):
        self.tiles = tiles
        self.i = 0

    def next(self):
        t = self.tiles[self.i % len(self.tiles)]
        self.i += 1
        return t


def dsize(dt):
    return 4 if dt == F32 else 2


class Ctx:
    def __init__(self, nc, P):
        self.nc = nc
        self.P = P
        self.uid = 0
        self.pers_off = 16512
        self.off = 16512
        self.semi = 0
        self.banks = [TL(nc.alloc_psum_tensor("pb%d" % i, [128, 512], F32), "pb%d" % i) for i in range(8)]
        for b in self.banks:
            b.b.excl = True
        self.pgen = Ring(self.banks[0:6])
        self.pacc = Ring(self.banks[6:8])

    def _alloc(self, shape, dt, dma, off):
        self.uid += 1
        name = "t%d" % self.uid
        t = self.nc.alloc_sbuf_tensor_at(name, list(shape), dt, offset=off)
        sem = None
        if dma:
            sem = self.P.dma_sems[self.semi % (len(self.P.dma_sems) - 1)]
            self.semi += 1
        return TL(t, name, sem)

    @staticmethod
    def _bytes(shape, dt):
        n = 1
        for s in shape[1:]:
            n *= s
        return (n * dsize(dt) + 63) // 64 * 64

    def pers(self, shape, dt, dma=False):
        tl = self._alloc(shape, dt, dma, self.pers_off)
        self.pers_off += self._bytes(shape, dt)
        return tl

    def begin_sweep(self):
        import os
        self.P.sweep_no += 1
        lim = int(os.environ.get("K_LIMIT", "100000"))
        self.P.disabled = self.P.sweep_no > lim
        if os.environ.get("K_VERBOSE"):
            print("sweep", self.P.sweep_no, "ops so far", self.P.nops, flush=True)
        self.P.barrier()
        self.off = self.pers_off

    def tile(self, shape, dt, dma=False):
        tl = self._alloc(shape, dt, dma, self.off)
        self.off += self._bytes(shape, dt)
        assert self.off <= 229376, ("SBUF overflow", self.off)
        return tl

    def ring(self, n, shape, dt, dma=False):
        return Ring([self.tile(shape, dt, dma) for _ in range(n)])

    def ld(self, dst, dst_ap, src_ap, eng="sp", slow=False, reads=()):
        if slow:
            fn = lambda e: e.dma_start(out=dst_ap, in_=src_ap, allow_slow_non_contiguous=True)
        else:
            fn = lambda e: e.dma_start(out=dst_ap, in_=src_ap)
        self.P.op(eng, fn, reads=list(reads), writes=[dst.b], dsem=dst.sem)

    def st(self, src, dst_ap, src_ap, eng="pool", slow=False):
        if slow:
            fn = lambda e: e.dma_start(out=dst_ap, in_=src_ap, allow_slow_non_contiguous=True)
        else:
            fn = lambda e: e.dma_start(out=dst_ap, in_=src_ap)
        self.P.op(eng, fn, reads=[src.b], dsem=src.sem)

    def mm(self, out, out_ap, lt, lhsT, rt, rhs, start=True, stop=True, skip=False):
        if skip:
            fn = lambda e: e.matmul(out_ap, lhsT=lhsT, rhs=rhs, start=start, stop=stop, skip_group_check=True)
        else:
            fn = lambda e: e.matmul(out_ap, lhsT=lhsT, rhs=rhs, start=start, stop=stop)
        self.P.op("pe", fn, reads=[lt.b, rt.b], writes=[out.b])

    def tr(self, out, out_ap, it, in_ap, idt, id_ap):
        self.P.op("pe", lambda e: e.transpose(out=out_ap, in_=in_ap, identity=id_ap),
                  reads=[it.b, idt.b], writes=[out.b])

    def act(self, out, out_ap, ins, in_ap, func, scale=None, bias=None, accum=None, extra_w=()):
        kw = {}
        if scale is not None:
            kw["scale"] = scale
        if bias is not None:
            kw["bias"] = bias
        if accum is not None:
            kw["accum_out"] = accum
        self.P.op("act", lambda e: e.activation(out=out_ap, in_=in_ap, func=func, **kw),
                  reads=[t.b for t in ins], writes=[out.b] + [t.b for t in extra_w])

    def v(self, eng, fn, ins, outs):
        self.P.op(eng, fn, reads=[t.b for t in ins], writes=[t.b for t in outs])


def norm_jobs(jobs):
    return [dict(S=j, NTo=j // 128, out=(0, j // 128), prompt=False) if isinstance(j, int) else j for j in jobs]


def build_program(jobs):
    jobs = norm_jobs(jobs)
    S_list = [j['S'] for j in jobs]
    nc = bass.Bass("TRN2", target_bir_lowering=False)
    P = Prog(nc)
    K = Ctx(nc, P)
    SM = max(S_list)
    NJ = len(S_list)

    import os
    DEBUG = bool(os.environ.get("K_DEBUG"))
    dbg_names = []

    def dram(name, shape, dt, kind="Internal"):
        if DEBUG and kind == "Internal" and (not name.startswith("wb_") or os.environ.get("K_DEBUGW")):
            kind = "ExternalOutput"
            dbg_names.append(name)
        return nc.dram_tensor(name, list(shape), dt, kind=kind).ap()

    xin = [dram("x%d" % j, [S_list[j], D], F32, "ExternalInput") for j in range(NJ)]
    yout = [dram("y%d" % j, [(jobs[j]['out'][1] - jobs[j]['out'][0]) * 128, D], F32, "ExternalOutput") for j in range(NJ)]
    csin = [dram("cs%d" % j, [S_list[j], 32], F32, "ExternalInput") for j in range(NJ)]
    mkin = [dram("mk%d" % j, [S_list[j], 4], F32, "ExternalInput") for j in range(NJ)]
    sfin = dram("sf", [1, 4], F32, "ExternalInput")
    W = {n: dram(n, PSHAPES[n], F32, "ExternalInput") for n in PARAMS}
    c_ident = dram("c_ident", [128, 128], F32, "ExternalInput")
    c_triu = dram("c_triu", [128, 128], F32, "ExternalInput")
    c_tril = dram("c_tril", [128, 128], F32, "ExternalInput")
    c_ones = dram("c_ones", [128, 128], F32, "ExternalInput")

    wb_in = dram("wb_in", [1024, 2480], BF16)
    wb_uq = dram("wb_uq", [256, 768], BF16)
    wb_ukv = dram("wb_ukv", [128, 1024], BF16)
    wb_out = dram("wb_out", [1024, 1024], BF16)
    wb_cin = dram("wb_cin", [1024, 2048], BF16)
    wb_cout = dram("wb_cout", [1024, 1024], BF16)
    wb_up = dram("wb_up", [2, 1024, 5632], BF16)
    wb_dn = dram("wb_dn", [2, 2816, 1024], BF16)
    XT = [dram("XT%d" % i, [8, 128, SM + 2], BF16) for i in range(2)]
    X1 = dram("X1", [SM, D], F32)
    QT = dram("QT", [8, 96, SM], BF16)
    KT = dram("KT", [8, 96, SM], BF16)
    VV = dram("VV", [SM, 8, 65], BF16)
    QKT = dram("QKT", [8, 128, SM], BF16)
    VM = dram("VM", [SM, 4, 129], BF16)
    SO = dram("SO", [SM, 512], BF16)
    GG = dram("GG", [SM, 16], F32)
    HF = dram("HF", [SM, 512], F32)
    YY = dram("YY", [SM, D], BF16)
    if DEBUG:
        DBG_dm = dram("DBG_dm", [128, 128], F32)
        DBG_pt = dram("DBG_pt", [128, 128], BF16)
        DBG_s = dram("DBG_s", [128, 24], F32)
        DBG_tot = dram("DBG_tot", [128, 129], F32)
        DBG_kk = dram("DBG_kk", [128, 128], BF16)
        DBG_wsT = dram("DBG_wsT", [128, 1024], BF16)
        DBG_bsT = dram("DBG_bsT", [128, 8], F32)
        DBG_uv = dram("DBG_uv", [128, 2048], F32)
        DBG_vn = dram("DBG_vn", [128, 1024], BF16)
        DBG_z = dram("DBG_z", [128, 1024], BF16)

    wsem = P.dma_sems[-1]

    def cast(dst, src, rows):
        for r0 in range(0, rows, 128):
            r1 = min(rows, r0 + 128)
            P.op("pool", lambda e, r0=r0, r1=r1: e.dma_start(out=dst[r0:r1, :], in_=src[r0:r1, :]),
                 writes=[], dsem=wsem)

    cast(wb_in, W['ab_w_in'][0], 1024)
    cast(wb_uq, W['mla_w_uq'][0], 256)
    cast(wb_ukv, W['mla_w_ukv'][0], 128)
    cast(wb_out, W['ab_w_out'][0], 1024)
    cast(wb_cin, W['c_w_in'][0], 1024)
    cast(wb_cout, W['c_w_out'][0], 1024)
    for l in range(2):
        cast(wb_up[l], W['ffn_w_up'][l], 1024)
        cast(wb_dn[l], W['ffn_w_down'][l], 2816)

    identf = K.pers([128, 128], F32, dma=True)
    ident = K.pers([128, 128], BF16)
    triu = K.pers([128, 128], F32, dma=True)
    tril = K.pers([128, 128], F32, dma=True)
    ones = K.pers([128, 128], F32, dma=True)
    cw_m = K.pers([128, 4, 8], F32)
    cw_f = [K.pers([128, 4, 44], F32) for _ in range(2)]
    bsT = K.pers([128, 8], F32)
    wsT = K.pers([128, 8, 128], BF16)
    zero = K.pers([128, 16], BF16, dma=True)
    epsb = K.pers([128, 1], F32)
    K.begin_sweep()
    K.v("dve", lambda e: e.memset(epsb.t[:], EPS), [], [epsb])
    K.ld(identf, identf.t[:], c_ident)
    K.ld(triu, triu.t[:], c_triu)
    K.ld(tril, tril.t[:], c_tril)
    K.ld(ones, ones.t[:], c_ones)
    K.v("dve", lambda e: e.tensor_copy(out=ident.t[:], in_=identf.t[:]), [identf], [ident])
    K.v("dve", lambda e: e.memset(zero.t[:], 0.0), [], [zero])

    def load_T(dst, dst_view, raws, nrow):
        raw = K.tile([nrow, len(raws), 128], F32, dma=True)
        for j, src in enumerate(raws):
            K.ld(raw, raw.t[:, j, :], src)
        for j in range(len(raws)):
            pb = K.pgen.next()
            K.tr(pb, pb.t[:, 0:nrow], raw, raw.t[:, j, :], identf, identf.t[0:nrow, 0:nrow])
            K.v("dve", lambda e, j=j, pb=pb: e.tensor_copy(out=dst_view(j), in_=pb.t[:, 0:nrow]), [pb], [dst])

    mcw = W['mlstm_conv_w'][0]
    load_T(cw_m, lambda j: cw_m.t[:, j, :],
           [mcw[j].rearrange("(c p) -> c p", p=128) for j in range(3)] +
           [W['mlstm_conv_b'][0].rearrange("(c p) -> c p", p=128)], 8)
    for l in range(2):
        load_T(cw_f[l], lambda j, l=l: cw_f[l].t[:, j, :],
               [W['ffn_conv_w'][l, j].rearrange("(c p) -> c p", p=128) for j in range(3)] +
               [W['ffn_conv_b'][l].rearrange("(c p) -> c p", p=128)], 44)
    load_T(bsT, lambda j: bsT.t[:, :], [W['c_b_spatial'][0]], 8)
    wsr = K.tile([128, 8, 128], F32, dma=True)
    wsrb = K.tile([128, 8, 128], BF16)
    K.ld(wsr, wsr.t[:], W['c_w_spatial'][0].rearrange("g t s -> t g s"))
    K.v("dve", lambda e: e.tensor_copy(out=wsrb.t[:], in_=wsr.t[:]), [wsr], [wsrb])
    pb = K.pgen.next()
    pbv = pb.t[:].bitcast(BF16).rearrange("p (c t) -> p c t", c=8)
    for g in range(8):
        K.tr(pb, pbv[:, g, :], wsrb, wsrb.t[:, g, :], ident, ident.t[:])
    K.v("dve", lambda e, pbv=pbv: e.tensor_copy(out=wsT.t[:], in_=pbv), [pb], [wsT])

    def bc_load(dst, src1d, n):
        K.ld(dst, dst.t[:, 0:n], src1d.partition_broadcast(128))

    class NormT:
        def __init__(self):
            self.ss = K.ring(3, [128, 4], F32)
            self.xn = K.ring(2, [128, 1024], BF16)
            self.xnT = K.ring(2, [128, 8, 128], BF16, dma=True)
            self.g = K.tile([128, 1024], F32, dma=True)

        def load_gain(self, src1d):
            bc_load(self.g, src1d, 1024)

        def stats(self, x, x_ap, n, junk=None):
            ss = self.ss.next()
            if junk is None:
                junk = self.xn.next()
            K.v("dve", lambda e: e.memset(ss.t[:, 0:1], 0.0), [], [ss])
            K.act(junk, junk.t[:, 0:n], [x, ss], x_ap, AF.Square, accum=ss.t[:, 0:1], extra_w=[ss])
            K.act(ss, ss.t[:, 2:3], [ss, epsb], ss.t[:, 0:1], AF.Ln, scale=1.0 / n, bias=epsb.t[:, 0:1])
            K.act(ss, ss.t[:, 3:4], [ss], ss.t[:, 2:3], AF.Exp, scale=-0.5)
            return ss

        def run(self, x, x_ap, t, XTd):
            self.run_b(self.run_a(x, x_ap), t, XTd)

        def run_a(self, x, x_ap):
            xn = self.xn.next()
            ss = self.stats(x, x_ap, 1024, junk=xn)
            g = self.g
            K.v("dve", lambda e: e.scalar_tensor_tensor(out=xn.t[:], in0=x_ap, scalar=ss.t[:, 3:4], in1=g.t[:],
                                                        op0=ALU.mult, op1=ALU.mult), [x, ss, g], [xn])
            return xn

        def run_b(self, xn, t, XTd):
            pb = K.pgen.next()
            pbv = pb.t[:].bitcast(BF16).rearrange("p (c t) -> p c t", c=8)
            for c in range(8):
                K.tr(pb, pbv[:, c, :], xn, xn.t[:, c * 128:(c + 1) * 128], ident, ident.t[:])
            xnT = self.xnT.next()
            K.act(xnT, xnT.t[:], [pb], pbv, AF.Copy)
            K.st(xnT, XTd[:, :, 1 + t * 128:1 + (t + 1) * 128].rearrange("c p t -> p c t"), xnT.t[:])

    def conv_chunk(pm, ph, hsl, cw, ci, dst):
        w0 = cw.t[:, 0, ci:ci + 1]
        w1 = cw.t[:, 1, ci:ci + 1]
        w2 = cw.t[:, 2, ci:ci + 1]
        bb = cw.t[:, 3, ci:ci + 1]
        K.act(dst, dst.t[:], [pm, cw], pm.t[:], AF.Identity, scale=w1, bias=bb)
        K.act(dst, dst.t[:, 0:1], [ph, cw, dst], ph.t[:, hsl:hsl + 1], AF.Identity, scale=w0, bias=dst.t[:, 0:1])
        K.act(dst, dst.t[:, 511:512], [ph, cw, dst], ph.t[:, hsl + 1:hsl + 2], AF.Identity, scale=w2,
              bias=dst.t[:, 511:512])
        K.v("dve", lambda e: e.scalar_tensor_tensor(out=dst.t[:, 1:512], in0=pm.t[:, 0:511], scalar=w0,
                                                    in1=dst.t[:, 1:512], op0=ALU.mult, op1=ALU.add),
            [pm, cw, dst], [dst])
        K.v("dve", lambda e: e.scalar_tensor_tensor(out=dst.t[:, 0:511], in0=pm.t[:, 1:512], scalar=w2,
                                                    in1=dst.t[:, 0:511], op0=ALU.mult, op1=ALU.add),
            [pm, cw, dst], [dst])

    def skewed(n, stages):
        ctxs = {}
        ns = len(stages)
        for step in range(n + ns - 1):
            for k in range(ns):
                i = step - k
                if 0 <= i < n:
                    if k == 0:
                        ctxs[i] = {}
                    stages[k](i, ctxs[i])
                    if k == ns - 1:
                        del ctxs[i]

    for jb in range(NJ):
        S = S_list[jb]
        NT = S // 128
        NB = S // 512
        x_in = xin[jb]
        y_out = yout[jb]
        NTo = jobs[jb]['NTo']
        NBo = NTo // 4
        out_lo, out_hi = jobs[jb]['out']
        prompt = jobs[jb]['prompt']
        SEAM_BLOCKS = (0, 4, 8, 12) if prompt else ()

        def seam_setup(cw, nch, ncand):
            sfb = K.tile([128, 4], F32, dma=True)
            bc_load(sfb, sfin[0], 4)
            nfw = K.tile([128, ncand, 2, nch], F32)
            for c in range(ncand):
                for jj, wi in ((0, 0), (1, 2)):
                    K.v("dve", lambda e, c=c, jj=jj, wi=wi, nfw=nfw, sfb=sfb, cw=cw: e.tensor_scalar_mul(
                        out=nfw.t[:, c, jj, :], in0=cw.t[:, wi, :], scalar1=sfb.t[:, c:c + 1]), [cw, sfb], [nfw])
            return nfw

        def seam_fix(pm, dst, nfw, c, ci):
            K.v("dve", lambda e: e.scalar_tensor_tensor(out=dst.t[:, 256:257], in0=pm.t[:, 255:256],
                                                        scalar=nfw.t[:, c, 0, ci:ci + 1], in1=dst.t[:, 256:257],
                                                        op0=ALU.mult, op1=ALU.add), [pm, nfw, dst], [dst])
            K.v("dve", lambda e: e.scalar_tensor_tensor(out=dst.t[:, 255:256], in0=pm.t[:, 256:257],
                                                        scalar=nfw.t[:, c, 1, ci:ci + 1], in1=dst.t[:, 255:256],
                                                        op0=ALU.mult, op1=ALU.add), [pm, nfw, dst], [dst])

        K.begin_sweep()
        nt = NormT()
        nt.load_gain(W['ab_norm'][0])
        for i in range(2):
            K.st(zero, XT[i][:, :, 0:1].rearrange("c p t -> p c t"), zero.t[:, 0:8].rearrange("p (c t) -> p c t", t=1),
                 slow=True)
            K.st(zero, XT[i][:, :, S + 1:S + 2].rearrange("c p t -> p c t"),
                 zero.t[:, 0:8].rearrange("p (c t) -> p c t", t=1), slow=True)
        xr = K.ring(3, [128, 1024], F32, dma=True)

        def a0(t, c):
            x = xr.next()
            K.ld(x, x.t[:], x_in[t * 128:(t + 1) * 128, :])
            c['xn'] = nt.run_a(x, x.t[:])

        def a1(t, c):
            nt.run_b(c['xn'], t, XT[0])

        skewed(NT, [a0, a1])

        K.begin_sweep()
        w_in = K.tile([128, 8, 2480], BF16, dma=True)
        K.ld(w_in, w_in.t[:], wb_in.rearrange("(k p) n -> p k n", p=128))
        w_uq = K.tile([128, 2, 768], BF16, dma=True)
        K.ld(w_uq, w_uq.t[:], wb_uq.rearrange("(k p) n -> p k n", p=128))
        w_ukv = K.tile([128, 1024], BF16, dma=True)
        K.ld(w_ukv, w_ukv.t[:], wb_ukv)
        qg = K.tile([128, 256], F32, dma=True)
        bc_load(qg, W['mla_q_norm'][0], 256)
        kvg = K.tile([128, 128], F32, dma=True)
        bc_load(kvg, W['mla_kv_norm'][0], 128)
        gbias = K.tile([128, 16], F32, dma=True)
        bc_load(gbias, W['mlstm_gate_bias'][0], 16)
        winr = K.ring(2, [128, 8, 514], BF16, dma=True)
        cgr = K.ring(2, [128, 512], F32)
        qkr = K.ring(2, [128, 512], BF16, dma=True)
        junk = K.tile([128, 256], BF16)
        ssr = K.ring(2, [128, 8], F32)
        cqn = K.ring(2, [128, 384], BF16)
        cTr = K.ring(2, [128, 3, 128], BF16)
        cosr = K.ring(2, [128, 32], F32, dma=True)
        krr = K.ring(2, [128, 32], BF16)
        tmp = K.ring(2, [128, 8, 64], F32)
        qst = K.ring(2, [128, 8, 96], BF16)
        qfr = K.ring(2, [128, 768], F32)
        kst = K.ring(2, [128, 8, 96], BF16)
        vst = K.ring(2, [128, 8, 65], BF16, dma=True)
        qTs = K.ring(2, [96, 8, 128], BF16, dma=True)
        kTs = K.ring(2, [96, 8, 128], BF16, dma=True)
        vmst = K.ring(2, [128, 4, 129], BF16, dma=True)
        sor = K.ring(2, [128, 512], BF16, dma=True)
        sxr = K.ring(2, [128, 512], F32)
        gtr = K.ring(2, [128, 16], F32, dma=True)
        ge = K.ring(2, [128, 2, 4], F32)
        for tl in vst.tiles + vmst.tiles:
            K.v("dve", lambda e, tl=tl: e.memset(tl.t[:], 1.0), [], [tl])
        mkr = K.ring(2, [128, 4], F32, dma=True)
        padb = Buf("xtpad")
        nfw_m = None
        if prompt:
            P.op("pool", lambda e, S=S: e.dma_start(out=XT[0][:, :, 0:1], in_=XT[0][:, :, S:S + 1],
                                                    allow_slow_non_contiguous=True), writes=[padb], dsem=wsem)
            P.op("pool", lambda e, S=S: e.dma_start(out=XT[0][:, :, S + 1:S + 2], in_=XT[0][:, :, 1:2],
                                                    allow_slow_non_contiguous=True), writes=[padb], dsem=wsem)
            nfw_m = seam_setup(cw_m, 8, 4)
        K.pgen = Ring(K.banks[0:6])
        for b in range(NB):
            win = winr.next()
            K.ld(win, win.t[:], XT[0][:, :, 512 * b:512 * b + 514].rearrange("c p t -> p c t"), reads=[padb])
            for c in range(8):
                if c < 4 and b >= NBo:
                    continue
                col0 = 416 + c * 128
                pm = K.pgen.next()
                hsl = 0
                halo = K.banks[6 + (c % 2)]
                for k in range(8):
                    K.mm(pm, pm.t[:, :], w_in, w_in.t[:, k, col0:col0 + 128], win, win.t[:, k, 1:513],
                         start=(k == 0), stop=(k == 7))
                for k in range(8):
                    K.mm(halo, halo.t[:, hsl:hsl + 2], w_in, w_in.t[:, k, col0:col0 + 128], win,
                         win.t[:, k, 0:514:513], start=(k == 0), stop=(k == 7))
                cg = cgr.next()
                conv_chunk(pm, halo, hsl, cw_m, c, cg)
                if b in SEAM_BLOCKS:
                    seam_fix(pm, cg, nfw_m, b // 4, c)
                qk = qkr.next()
                K.act(qk, qk.t[:], [cg], cg.t[:], AF.Silu)
                K.st(qk, QKT[c, :, 512 * b:512 * b + 512], qk.t[:])
            for tt in range(4):
                t = b * 4 + tt
                lo = 1 + tt * 128
                pa = K.pgen.next()
                pv = K.pgen.next()
                po = K.pgen.next()
                for k in range(8):
                    K.mm(pa, pa.t[:, 0:416], win, win.t[:, k, lo:lo + 128], w_in, w_in.t[:, k, 0:416],
                         start=(k == 0), stop=(k == 7))
                for k in range(8):
                    K.mm(pa, pa.t[:, 416:432], win, win.t[:, k, lo:lo + 128], w_in, w_in.t[:, k, 2464:2480],
                         start=(k == 0), stop=(k == 7))
                for k in range(8):
                    K.mm(pv, pv.t[:, :], win, win.t[:, k, lo:lo + 128], w_in, w_in.t[:, k, 1440:1952],
                         start=(k == 0), stop=(k == 7))
                is_own = t < NTo
                if is_own:
                    for k in range(8):
                        K.mm(po, po.t[:, :], win, win.t[:, k, lo:lo + 128], w_in, w_in.t[:, k, 1952:2464],
                             start=(k == 0), stop=(k == 7))
                vm = vmst.next()
                K.act(vm, vm.t[:, :, 0:128], [pv], pv.t[:, :].rearrange("p (h d) -> p h d", d=128), AF.Copy)
                K.st(vm, VM[t * 128:(t + 1) * 128, :, :], vm.t[:])
                if is_own:
                    so = sor.next()
                    sx = sxr.next()
                    K.act(sx, sx.t[:], [po], po.t[:, :], AF.Exp, scale=-1.0)
                    K.v("dve", lambda e, sx=sx: e.tensor_scalar_add(out=sx.t[:], in0=sx.t[:], scalar1=1.0), [sx], [sx])
                    K.v("dve", lambda e, sx=sx: e.reciprocal(out=sx.t[:], in_=sx.t[:]), [sx], [sx])
                    K.v("pool", lambda e, sx=sx, so=so: e.tensor_copy(out=so.t[:], in_=sx.t[:]), [sx], [so])
                    K.st(so, SO[t * 128:(t + 1) * 128, :], so.t[:])
                gt = gtr.next()
                g2 = ge.next()
                K.v("dve", lambda e, gt=gt, pa=pa: e.tensor_tensor(out=gt.t[:], in0=pa.t[:, 416:432], in1=gbias.t[:],
                                                                   op=ALU.add), [pa, gbias], [gt])
                gtv = gt.t[:, :].rearrange("p (a b) -> p a b", b=8)[:, :, 4:8]
                K.act(g2, g2.t[:], [gt], gtv, AF.Exp, scale=-1.0)
                K.v("dve", lambda e, g2=g2: e.tensor_scalar_add(out=g2.t[:], in0=g2.t[:], scalar1=1.0), [g2], [g2])
                K.act(g2, g2.t[:], [g2], g2.t[:], AF.Ln)
                K.v("dve", lambda e, g2=g2, gtv=gtv: e.tensor_scalar_mul(out=gtv, in0=g2.t[:], scalar1=-1.0), [g2], [gt])
                mk = mkr.next()
                K.ld(mk, mk.t[:], mkin[jb][t * 128:(t + 1) * 128, :])
                for (c0, mi) in ((0, 0), (8, 2)):
                    K.v("dve", lambda e, gt=gt, mk=mk, c0=c0, mi=mi: e.scalar_tensor_tensor(
                        out=gt.t[:, c0:c0 + 4], in0=gt.t[:, c0:c0 + 4], scalar=mk.t[:, mi:mi + 1],
                        in1=mk.t[:, mi + 1:mi + 2].to_broadcast([128, 4]), op0=ALU.mult, op1=ALU.add), [gt, mk], [gt])
                    K.v("dve", lambda e, gt=gt, mk=mk, c0=c0, mi=mi: e.tensor_scalar_mul(
                        out=gt.t[:, c0 + 4:c0 + 8], in0=gt.t[:, c0 + 4:c0 + 8], scalar1=mk.t[:, mi:mi + 1]), [gt, mk], [gt])
                K.st(gt, GG[t * 128:(t + 1) * 128, :], gt.t[:])
                ss = ssr.next()
                K.v("dve", lambda e, ss=ss: e.memset(ss.t[:, 0:2], 0.0), [], [ss])
                K.act(junk, junk.t[:, 0:256], [pa, ss], pa.t[:, 0:256], AF.Square, accum=ss.t[:, 0:1], extra_w=[ss])
                K.act(junk, junk.t[:, 0:128], [pa, ss], pa.t[:, 256:384], AF.Square, accum=ss.t[:, 1:2], extra_w=[ss])
                K.act(ss, ss.t[:, 4:5], [ss, epsb], ss.t[:, 0:1], AF.Ln, scale=1.0 / 256, bias=epsb.t[:, 0:1])
                K.act(ss, ss.t[:, 5:6], [ss, epsb], ss.t[:, 1:2], AF.Ln, scale=1.0 / 128, bias=epsb.t[:, 0:1])
                K.act(ss, ss.t[:, 6:8], [ss], ss.t[:, 4:6], AF.Exp, scale=-0.5)
                cq = cqn.next()
                K.v("dve", lambda e, ss=ss, cq=cq, pa=pa: e.scalar_tensor_tensor(
                    out=cq.t[:, 0:256], in0=pa.t[:, 0:256], scalar=ss.t[:, 6:7], in1=qg.t[:], op0=ALU.mult,
                    op1=ALU.mult), [pa, ss, qg], [cq])
                K.v("dve", lambda e, ss=ss, cq=cq, pa=pa: e.scalar_tensor_tensor(
                    out=cq.t[:, 256:384], in0=pa.t[:, 256:384], scalar=ss.t[:, 7:8], in1=kvg.t[:], op0=ALU.mult,
                    op1=ALU.mult), [pa, ss, kvg], [cq])
                cs = cosr.next()
                K.ld(cs, cs.t[:, 0:32], csin[jb][t * 128:(t + 1) * 128, :])
                kr = krr.next()
                tm = tmp.next()

                def rope(dst1, dst2, x1, x2, cosb, sinb, ta, tb, ins, outs, tmt):
                    K.v("dve", lambda e: e.tensor_tensor(out=ta, in0=x1, in1=cosb, op=ALU.mult), ins, [tmt])
                    K.v("dve", lambda e: e.tensor_tensor(out=tb, in0=x2, in1=sinb, op=ALU.mult), ins, [tmt])
                    K.v("dve", lambda e: e.tensor_tensor(out=dst1, in0=ta, in1=tb, op=ALU.subtract), [tmt], outs)
                    K.v("dve", lambda e: e.tensor_tensor(out=ta, in0=x2, in1=cosb, op=ALU.mult), ins + outs, [tmt])
                    K.v("dve", lambda e: e.tensor_tensor(out=tb, in0=x1, in1=sinb, op=ALU.mult), ins, [tmt])
                    K.v("dve", lambda e: e.tensor_tensor(out=dst2, in0=ta, in1=tb, op=ALU.add), [tmt], outs)

                rope(kr.t[:, 0:16], kr.t[:, 16:32], pa.t[:, 384:400], pa.t[:, 400:416], cs.t[:, 0:16], cs.t[:, 16:32],
                     tm.t[:, 0, 0:16], tm.t[:, 0, 16:32], [pa, cs], [kr], tm)
                pb = K.pgen.next()
                pbv = pb.t[:].bitcast(BF16).rearrange("p (c t) -> p c t", c=8)
                for c in range(3):
                    K.tr(pb, pbv[:, c, :], cq, cq.t[:, c * 128:(c + 1) * 128], ident, ident.t[:])
                cT = cTr.next()
                K.act(cT, cT.t[:], [pb], pbv[:, 0:3, :], AF.Copy)
                if is_own:
                    pq1 = K.pgen.next()
                    pq2 = K.pgen.next()
                    for k in range(2):
                        K.mm(pq1, pq1.t[:, 0:480], cT, cT.t[:, k, :], w_uq, w_uq.t[:, k, 0:480], start=(k == 0), stop=(k == 1))
                    for k in range(2):
                        K.mm(pq2, pq2.t[:, 0:288], cT, cT.t[:, k, :], w_uq, w_uq.t[:, k, 480:768], start=(k == 0), stop=(k == 1))
                    qs = qst.next()
                    qf = qfr.next()
                    K.act(qf, qf.t[:, 0:480], [pq1], pq1.t[:, 0:480], AF.Copy)
                    K.act(qf, qf.t[:, 480:768], [pq2], pq2.t[:, 0:288], AF.Copy)
                    qfv = qf.t[:, :].rearrange("p (h d) -> p h d", d=96)
                    K.v("dve", lambda e, qs=qs, qfv=qfv: e.tensor_copy(out=qs.t[:, :, 0:64], in_=qfv[:, :, 0:64]), [qf], [qs])
                    cosb = cs.t[:, 0:16].unsqueeze(1).to_broadcast([128, 8, 16])
                    sinb = cs.t[:, 16:32].unsqueeze(1).to_broadcast([128, 8, 16])
                    rope(qs.t[:, :, 64:80], qs.t[:, :, 80:96], qfv[:, :, 64:80], qfv[:, :, 80:96],
                         cosb, sinb, tm.t[:, :, 0:16], tm.t[:, :, 16:32], [qf, cs], [qs], tm)
                pk1 = K.pgen.next()
                pk2 = K.pgen.next()
                K.mm(pk1, pk1.t[:, :], cT, cT.t[:, 2, :], w_ukv, w_ukv.t[:, 0:512])
                K.mm(pk2, pk2.t[:, :], cT, cT.t[:, 2, :], w_ukv, w_ukv.t[:, 512:1024])
                ks = kst.next()
                vs = vst.next()
                for (pk, h0) in ((pk1, 0), (pk2, 4)):
                    pkv = pk.t[:, :].rearrange("p (h d) -> p h d", d=128)
                    K.act(ks, ks.t[:, h0:h0 + 4, 0:64], [pk], pkv[:, :, 0:64], AF.Copy)
                    K.v("dve", lambda e, vs=vs, pkv=pkv, h0=h0: e.tensor_copy(out=vs.t[:, h0:h0 + 4, 0:64],
                                                                             in_=pkv[:, :, 64:128]), [pk], [vs])
                K.v("dve", lambda e, ks=ks, kr=kr: e.tensor_copy(
                    out=ks.t[:, :, 64:96], in_=kr.t[:, :].unsqueeze(1).to_broadcast([128, 8, 32])), [kr], [ks])
                K.st(vs, VV[t * 128:(t + 1) * 128, :, :], vs.t[:])
                for (src, dstr, dd) in (((qs, qTs, QT), (ks, kTs, KT)) if is_own else ((ks, kTs, KT),)):
                    pb = K.pgen.next()
                    pbv = pb.t[:].bitcast(BF16).rearrange("p (c t) -> p c t", c=8)
                    for h in range(8):
                        K.tr(pb, pbv[0:96, h, :], src, src.t[:, h, :], ident, ident.t[:])
                    dT = dstr.next()
                    K.act(dT, dT.t[:], [pb], pbv[0:96, :, :], AF.Copy)
                    K.st(dT, dd[:, :, t * 128:(t + 1) * 128].rearrange("h p t -> p h t"), dT.t[:])
        K.pgen = Ring(K.banks[0:6])

        K.begin_sweep()
        kth = K.ring(2, [96, S], BF16, dma=True)
        vh = K.ring(2, [128, NT, 65], BF16, dma=True)
        qbr = K.ring(2, [96, 512], BF16, dma=True)
        ptr = K.ring(3, [128, 512], BF16)
        rcr = K.ring(2, [128, 4], F32)
        yar = K.ring(2, [128, 4, 64], BF16, dma=True)
        for h in range(8):
            kt = kth.next()
            K.ld(kt, kt.t[:], KT[h, :, 0:S])
            vv = vh.next()
            K.ld(vv, vv.t[:], VV[0:S, h, :].rearrange("(n p) d -> p n d", p=128))
            for qb in range(NBo):
                qt = qbr.next()
                K.ld(qt, qt.t[:], QT[h, :, qb * 512:(qb + 1) * 512])
                acc = K.pacc.next()

                def scores(kb, kt=kt, qt=qt):
                    ps = K.pgen.next()
                    K.mm(ps, ps.t[:, :], kt, kt.t[:, kb * 128:(kb + 1) * 128], qt, qt.t[:, :])
                    return ps

                LOOK = 2
                pend = [scores(kb) for kb in range(min(LOOK, NT))]
                for kb in range(NT):
                    ps = pend.pop(0)
                    if kb + LOOK < NT:
                        pend.append(scores(kb + LOOK))
                    pt = ptr.next()
                    K.act(pt, pt.t[:], [ps], ps.t[:, :], AF.Exp, scale=SC_ATT)
                    for j in range(4):
                        K.mm(acc, acc.t[:, j * 65:(j + 1) * 65], pt, pt.t[:, j * 128:(j + 1) * 128], vv, vv.t[:, kb, :],
                             start=(kb == 0 and j == 0), stop=(kb == NT - 1 and j == 3), skip=True)
                accv = acc.t[:, 0:260].rearrange("p (j d) -> p j d", d=65)
                rc = rcr.next()
                K.v("dve", lambda e, rc=rc, accv=accv: e.reciprocal(out=rc.t[:, :].unsqueeze(2), in_=accv[:, :, 64:65]),
                    [acc], [rc])
                ya = yar.next()
                K.v("dve", lambda e, rc=rc, accv=accv, ya=ya: e.tensor_tensor(
                    out=ya.t[:], in0=accv[:, :, 0:64], in1=rc.t[:, :].unsqueeze(2).to_broadcast([128, 4, 64]),
                    op=ALU.mult), [acc, rc], [ya])
                K.st(ya, YY[qb * 512:(qb + 1) * 512, h * 64:(h + 1) * 64].rearrange("(j p) d -> p j d", p=128), ya.t[:])

        for direction in (0, 1):
            K.begin_sweep()
            K.pgen = Ring(K.banks[0:8])
            tri = triu if direction == 0 else tril
            gi0 = 0 if direction == 0 else 8
            qkl = K.ring(3, [128, 8, 128], BF16, dma=True)
            vml = K.ring(3, [128, 4, 129], BF16, dma=True)
            gl = K.ring(3, [128, 16], F32, dma=True)
            ktok = K.ring(3, [128, 4, 128], BF16)
            sm = K.ring(3, [128, 6, 4], F32)
            ptm = K.ring(3, [128, 4, 128], BF16)
            dnr = K.ring(2, [128, 3, 4], F32)
            vwr = K.ring(2, [128, 4, 129], BF16)
            ST = K.tile([128, 4, 129], F32)
            STb = K.tile([128, 4, 129], BF16)
            hfr = K.ring(3, [128, 512], F32, dma=True)
            if direction == 1:
                sol = K.ring(3, [128, 512], BF16, dma=True)
                og = K.tile([128, 512], F32, dma=True)
                bc_load(og, W['mlstm_out_norm'][0], 512)
                sq = K.tile([128, 512], F32)
                ms = K.ring(2, [128, 3, 4], F32)
                ymr = K.ring(2, [128, 512], BF16, dma=True)
            K.v("dve", lambda e, ST=ST: e.memset(ST.t[:], 0.0), [], [ST])
            K.v("dve", lambda e, STb=STb: e.memset(STb.t[:], 0.0), [], [STb])
            own_c = list(range(NTo))
            ctx_c = list(range(NTo, NT))
            if direction == 0:
                order = [(n, True) for n in ctx_c] + [(n, False) for n in own_c]
            else:
                order = [(n, True) for n in reversed(ctx_c)] + [(n, False) for n in reversed(own_c)]

            def stage1(n, ctx_only, direction=direction, tri=tri, gi0=gi0):
                c = dict(n=n, ctx_only=ctx_only)
                r0, r1 = n * 128, (n + 1) * 128
                qk = qkl.next()
                if ctx_only:
                    K.ld(qk, qk.t[:, 4:8, :], QKT[4:8, :, r0:r1].rearrange("c p t -> p c t"))
                else:
                    K.ld(qk, qk.t[:], QKT[:, :, r0:r1].rearrange("c p t -> p c t"))
                vm = vml.next()
                K.ld(vm, vm.t[:], VM[r0:r1, :, :])
                g = gl.next()
                K.ld(g, g.t[:], GG[r0:r1, :])
                c.update(qk=qk, vm=vm, g=g)
                if not ctx_only:
                    hf = hfr.next()
                    c['hf'] = hf
                    if direction == 1:
                        K.ld(hf, hf.t[:], HF[r0:r1, :])
                        so = sol.next()
                        K.ld(so, so.t[:], SO[r0:r1, :])
                        c['so'] = so
                li = g.t[:, gi0:gi0 + 4]
                lf = g.t[:, gi0 + 4:gi0 + 8]
                pb = K.pgen.next()
                pbv = pb.t[:].bitcast(BF16).rearrange("p (c t) -> p c t", c=8)
                for h in range(4):
                    K.tr(pb, pbv[:, h, :], qk, qk.t[:, 4 + h, :], ident, ident.t[:])
                kk = ktok.next()
                K.act(kk, kk.t[:], [pb], pbv[:, 0:4, :], AF.Copy)
                pc = K.pgen.next()
                K.mm(pc, pc.t[:, 0:4], tri, tri.t[:], g, lf)
                K.mm(pc, pc.t[:, 4:8], ones, ones.t[:], g, lf)
                s_ = sm.next()
                K.v("dve", lambda e, s_=s_, pc=pc, li=li: e.scalar_tensor_tensor(
                    out=s_.t[:, 0, :], in0=li, scalar=LNSC_M, in1=pc.t[:, 0:4], op0=ALU.add, op1=ALU.subtract),
                    [g, pc], [s_])
                K.v("dve", lambda e, s_=s_, pc=pc: e.tensor_tensor(out=s_.t[:, 1, :], in0=s_.t[:, 0, :], in1=pc.t[:, 4:8],
                                                                   op=ALU.add), [pc, s_], [s_])
                K.act(s_, s_.t[:, 2:4, :], [pc], pc.t[:, 0:8].rearrange("p (a b) -> p a b", b=4), AF.Exp)
                K.act(s_, s_.t[:, 4:6, :], [s_], s_.t[:, 0:2, :], AF.Exp)
                c.update(kk=kk, s_=s_)
                if not ctx_only:
                    pS = K.pgen.next()
                    for h in range(4):
                        K.mm(pS, pS.t[:, h * 128:(h + 1) * 128], qk, qk.t[:, 4 + h, :], qk, qk.t[:, h, :])
                    pt = ptm.next()
                    for h in range(4):
                        K.v("dve", lambda e, pS=pS, pt=pt, s_=s_, h=h, tri=tri: e.scalar_tensor_tensor(
                            out=pt.t[:, h, :], in0=pS.t[:, h * 128:(h + 1) * 128], scalar=s_.t[:, 4, h:h + 1],
                            in1=tri.t[:], op0=ALU.mult, op1=ALU.mult), [pS, s_, tri], [pt])
                    c['pt'] = pt
                return c

            def stage2(c, direction=direction):
                n, ctx_only = c['n'], c['ctx_only']
                r0, r1 = n * 128, (n + 1) * 128
                qk, vm, kk, s_ = c['qk'], c['vm'], c['kk'], c['s_']
                if not ctx_only:
                    pt, hf = c['pt'], c['hf']
                    pN = [K.pgen.next(), K.pgen.next()]
                    for h in range(4):
                        pp = pN[h // 2]
                        c0 = (h % 2) * 129
                        K.mm(pp, pp.t[:, c0:c0 + 129], pt, pt.t[:, h, :], vm, vm.t[:, h, :], start=True, stop=False)
                        K.mm(pp, pp.t[:, c0:c0 + 129], qk, qk.t[:, h, :], STb, STb.t[:, h, :], start=False, stop=True)
                    dn = dnr.next()
                    for bi in range(2):
                        K.v("dve", lambda e, dn=dn, bi=bi, pp=pN[bi], s_=s_: e.tensor_tensor(
                            out=dn.t[:, 0, 2 * bi:2 * bi + 2], in0=pp.t[:, 128:258:129], in1=s_.t[:, 2, 2 * bi:2 * bi + 2],
                            op=ALU.mult), [pN[bi], s_], [dn])
                    K.v("dve", lambda e, dn=dn: e.tensor_scalar_mul(out=dn.t[:, 1, :], in0=dn.t[:, 0, :], scalar1=-1.0),
                        [dn], [dn])
                    K.v("dve", lambda e, dn=dn: e.tensor_tensor(out=dn.t[:, 0, :], in0=dn.t[:, 0, :], in1=dn.t[:, 1, :],
                                                                op=ALU.max), [dn], [dn])
                    K.v("dve", lambda e, dn=dn: e.tensor_scalar_max(out=dn.t[:, 0, :], in0=dn.t[:, 0, :], scalar1=1.0),
                        [dn], [dn])
                    K.v("dve", lambda e, dn=dn: e.reciprocal(out=dn.t[:, 1, :], in_=dn.t[:, 0, :]), [dn], [dn])
                    K.v("dve", lambda e, dn=dn, s_=s_: e.tensor_tensor(out=dn.t[:, 2, :], in0=dn.t[:, 1, :],
                                                                       in1=s_.t[:, 2, :], op=ALU.mult), [dn, s_], [dn])
                    for h in range(4):
                        pp = pN[h // 2]
                        c0 = (h % 2) * 129
                        hs_ap = hf.t[:, h * 128:(h + 1) * 128]
                        if direction == 0:
                            K.act(hf, hs_ap, [pp, dn], pp.t[:, c0:c0 + 128], AF.Copy, scale=dn.t[:, 2, h:h + 1])
                        else:
                            K.v("dve", lambda e, pp=pp, c0=c0, dn=dn, hs_ap=hs_ap, h=h: e.scalar_tensor_tensor(
                                out=hs_ap, in0=pp.t[:, c0:c0 + 128], scalar=dn.t[:, 2, h:h + 1], in1=hs_ap,
                                op0=ALU.mult, op1=ALU.add), [pp, dn, hf], [hf])
                vw = vwr.next()
                for h in range(4):
                    K.v("pool", lambda e, vw=vw, vm=vm, s_=s_, h=h: e.tensor_scalar_mul(
                        out=vw.t[:, h, :], in0=vm.t[:, h, :], scalar1=s_.t[:, 5, h:h + 1]), [vm, s_], [vw])
                pC = [K.pgen.next(), K.pgen.next()]
                for h in range(4):
                    pp = pC[h // 2]
                    c0 = (h % 2) * 129
                    K.mm(pp, pp.t[:, c0:c0 + 129], kk, kk.t[:, h, :], vw, vw.t[:, h, :])
                for h in range(4):
                    pp = pC[h // 2]
                    c0 = (h % 2) * 129
                    K.v("dve", lambda e, pp=pp, c0=c0, s_=s_, h=h, ST=ST: e.scalar_tensor_tensor(
                        out=ST.t[:, h, :], in0=ST.t[:, h, :], scalar=s_.t[:, 3, h:h + 1], in1=pp.t[:, c0:c0 + 129],
                        op0=ALU.mult, op1=ALU.add), [ST, s_, pp], [ST])
                K.act(STb, STb.t[:], [ST], ST.t[:], AF.Copy)
                if ctx_only:
                    return
                if direction == 0:
                    K.st(hf, HF[r0:r1, :], hf.t[:])
                else:
                    so = c['so']
                    m_ = ms.next()
                    K.v("dve", lambda e, hf=hf, sq=sq: e.tensor_tensor(out=sq.t[:], in0=hf.t[:], in1=hf.t[:], op=ALU.mult),
                        [hf], [sq])
                    K.v("dve", lambda e, m_=m_, sq=sq: e.tensor_reduce(
                        out=m_.t[:, 0, :], in_=sq.t[:, :].rearrange("p (h d) -> p h d", d=128), axis=AX.X, op=ALU.add),
                        [sq], [m_])
                    K.act(m_, m_.t[:, 2, :], [m_, epsb], m_.t[:, 0, :], AF.Ln, scale=1.0 / 128, bias=epsb.t[:, 0:1])
                    K.act(m_, m_.t[:, 0, :], [m_], m_.t[:, 2, :], AF.Exp, scale=-0.5)
                    K.v("dve", lambda e, m_=m_, hf=hf, sq=sq: e.tensor_tensor(
                        out=sq.t[:, :].rearrange("p (h d) -> p h d", d=128),
                        in0=hf.t[:, :].rearrange("p (h d) -> p h d", d=128),
                        in1=m_.t[:, 0, :].unsqueeze(2).to_broadcast([128, 4, 128]), op=ALU.mult), [hf, m_], [sq])
                    K.v("dve", lambda e, sq=sq, og=og: e.tensor_tensor(out=sq.t[:], in0=sq.t[:], in1=og.t[:], op=ALU.mult),
                        [sq, og], [sq])
                    ym = ymr.next()
                    K.v("dve", lambda e, ym=ym, so=so, sq=sq: e.tensor_tensor(out=ym.t[:], in0=sq.t[:], in1=so.t[:],
                                                                              op=ALU.mult), [sq, so], [ym])
                    K.st(ym, YY[r0:r1, 512:1024], ym.t[:])

            cur = stage1(*order[0])
            for i in range(len(order)):
                nxt = stage1(*order[i + 1]) if i + 1 < len(order) else None
                stage2(cur)
                cur = nxt
            K.pgen = Ring(K.banks[0:6])

        def tail_a(x, pr, t, nt, final_g=None):
            for half in range(2):
                K.v("dve", lambda e, half=half: e.tensor_tensor(
                    out=x.t[:, half * 512:(half + 1) * 512], in0=x.t[:, half * 512:(half + 1) * 512],
                    in1=pr[half].t[:, :], op=ALU.add), [x, pr[half]], [x])
            if final_g is None:
                K.st(x, X1[t * 128:(t + 1) * 128, :], x.t[:])
                return nt.run_a(x, x.t[:])
            ss = nt.stats(x, x.t[:], 1024)
            K.v("dve", lambda e, ss=ss: e.scalar_tensor_tensor(out=x.t[:], in0=x.t[:], scalar=ss.t[:, 3:4],
                                                               in1=final_g.t[:], op0=ALU.mult, op1=ALU.mult),
                [x, ss, final_g], [x])
            if out_lo <= t < out_hi:
                K.st(x, y_out[(t - out_lo) * 128:(t - out_lo + 1) * 128, :], x.t[:])
            return None

        def tail(x, pr, t, nt, XTd, final_g=None):
            for half in range(2):
                K.v("dve", lambda e, half=half: e.tensor_tensor(
                    out=x.t[:, half * 512:(half + 1) * 512], in0=x.t[:, half * 512:(half + 1) * 512],
                    in1=pr[half].t[:, :], op=ALU.add), [x, pr[half]], [x])
            if final_g is None:
                K.st(x, X1[t * 128:(t + 1) * 128, :], x.t[:])
                nt.run(x, x.t[:], t, XTd)
            else:
                ss = nt.stats(x, x.t[:], 1024)
                K.v("dve", lambda e, ss=ss: e.scalar_tensor_tensor(out=x.t[:], in0=x.t[:], scalar=ss.t[:, 3:4],
                                                                   in1=final_g.t[:], op0=ALU.mult, op1=ALU.mult),
                    [x, ss, final_g], [x])
                if out_lo <= t < out_hi:
                    K.st(x, y_out[(t - out_lo) * 128:(t - out_lo + 1) * 128, :], x.t[:])

        def project_tm(src_tile, src_ap_k, wt, w_ap_k, nk):
            pr = [K.pgen.next(), K.pgen.next()]
            for half in range(2):
                for k in range(nk):
                    K.mm(pr[half], pr[half].t[:, :], src_tile, src_ap_k(k), wt, w_ap_k(k, half), start=(k == 0),
                         stop=(k == nk - 1))
            return pr

        K.begin_sweep()
        nt = NormT()
        nt.load_gain(W['ffn_norm'][0])
        w_o = K.tile([128, 8, 1024], BF16, dma=True)
        K.ld(w_o, w_o.t[:], wb_out.rearrange("(k p) n -> p k n", p=128))
        xr = K.ring(3, [128, 1024], F32, dma=True)
        yr = K.ring(3, [128, 1024], BF16, dma=True)
        yTr = K.ring(3, [128, 8, 128], BF16)
        K.pgen = Ring(K.banks[0:8])

        def f0(t, c):
            y = yr.next()
            K.ld(y, y.t[:], YY[t * 128:(t + 1) * 128, :])
            pb = K.pgen.next()
            pbv = pb.t[:].bitcast(BF16).rearrange("p (c t) -> p c t", c=8)
            for cc in range(8):
                K.tr(pb, pbv[:, cc, :], y, y.t[:, cc * 128:(cc + 1) * 128], ident, ident.t[:])
            yT = yTr.next()
            K.act(yT, yT.t[:], [pb], pbv, AF.Copy)
            c['pr'] = project_tm(yT, lambda k: yT.t[:, k, :], w_o,
                                 lambda k, half: w_o.t[:, k, half * 512:(half + 1) * 512], 8)

        def f1(t, c):
            x = xr.next()
            K.ld(x, x.t[:], x_in[t * 128:(t + 1) * 128, :])
            c['xn'] = tail_a(x, c['pr'], t, nt)

        def f2(t, c):
            nt.run_b(c['xn'], t, XT[1])

        skewed(NTo, [f0, f1, f2])
        K.pgen = Ring(K.banks[0:6])

        def ffn_sweep(l, XTs, XTd, gain_next, final):
            K.begin_sweep()
            wup = K.tile([128, 8, 5632], BF16, dma=True)
            for k in range(8):
                K.ld(wup, wup.t[:, k, :], wb_up[l, k * 128:(k + 1) * 128, :])
            wdn = K.tile([128, NFC, 1024], BF16, dma=True)
            K.ld(wdn, wdn.t[:], wb_dn[l].rearrange("(i p) n -> p i n", p=128))
            nt = NormT()
            fg = None
            if final:
                fg = nt.g
            nt.load_gain(gain_next)
            winr = K.ring(2, [128, 8, 514], BF16, dma=True)
            hT = K.tile([128, NFC, 512], BF16)
            cgr = K.ring(2, [128, 512], F32)
            cvr = K.ring(2, [128, 512], F32)
            sgr = K.ring(2, [128, 512], BF16)
            xr = K.ring(2, [128, 1024], F32, dma=True)
            K.pgen = Ring(K.banks[0:6])
            cw = cw_f[l]
            hs_i = 0
            nfw_f = seam_setup(cw, 44, 2) if prompt else None
            pending = []
            for b in range(NBo):
                win = winr.next()
                K.ld(win, win.t[:], XTs[:, :, 512 * b:512 * b + 514].rearrange("c p t -> p c t"))
                for i in range(NFC):
                    outs = []
                    for (col0, ci, rr) in ((i * 128, i, cgr), (DFF + i * 128, NFC + i, cvr)):
                        pm = K.pgen.next()
                        hsl = 0
                        halo = K.banks[6 + (hs_i % 2)]
                        hs_i += 1
                        for k in range(8):
                            K.mm(pm, pm.t[:, :], wup, wup.t[:, k, col0:col0 + 128], win, win.t[:, k, 1:513],
                                 start=(k == 0), stop=(k == 7))
                        for k in range(8):
                            K.mm(halo, halo.t[:, hsl:hsl + 2], wup, wup.t[:, k, col0:col0 + 128], win,
                                 win.t[:, k, 0:514:513], start=(k == 0), stop=(k == 7))
                        cc = rr.next()
                        conv_chunk(pm, halo, hsl, cw, ci, cc)
                        if prompt and b in (0, 4):
                            seam_fix(pm, cc, nfw_f, b // 4, ci)
                        outs.append(cc)
                    cg, cv = outs
                    sg = sgr.next()
                    K.act(sg, sg.t[:], [cg], cg.t[:], AF.Silu)
                    K.v("pool", lambda e, sg=sg, cv=cv, i=i: e.tensor_tensor(out=hT.t[:, i, :], in0=sg.t[:], in1=cv.t[:],
                                                                             op=ALU.mult), [sg, cv], [hT])
                    if pending and i >= 1:
                        pending.pop(0)()
                while pending:
                    pending.pop(0)()
                prs = {}
                xns = {}

                def d0(tt, b=b, prs=prs):
                    prs[tt] = project_tm(hT, lambda k: hT.t[:, k, tt * 128:(tt + 1) * 128], wdn,
                                         lambda k, half: wdn.t[:, k, half * 512:(half + 1) * 512], NFC)

                def d1(tt, b=b, prs=prs, xns=xns):
                    t = b * 4 + tt
                    x = xr.next()
                    K.ld(x, x.t[:], X1[t * 128:(t + 1) * 128, :])
                    xns[tt] = tail_a(x, prs[tt], t, nt, final_g=fg)

                def d2(tt, b=b, xns=xns):
                    if xns[tt] is not None:
                        nt.run_b(xns[tt], b * 4 + tt, XTd)

                for fn_, a_ in ((d0, 0), (d0, 1), (d1, 0), (d0, 2), (d1, 1), (d2, 0), (d0, 3), (d1, 2), (d2, 1), (d1, 3)):
                    fn_(a_)
                pending.extend([lambda d2=d2: d2(2), lambda d2=d2: d2(3)])
            while pending:
                pending.pop(0)()
            K.pgen = Ring(K.banks[0:6])

        ffn_sweep(0, XT[1], XT[0], W['c_norm'][0], False)

        K.begin_sweep()
        nt = NormT()
        nt.load_gain(W['ffn_norm'][1])
        cwin = K.tile([128, 8, 2048], BF16, dma=True)
        K.ld(cwin, cwin.t[:], wb_cin.rearrange("(k p) n -> p k n", p=128))
        cwo = K.tile([128, 8, 1024], BF16, dma=True)
        K.ld(cwo, cwo.t[:], wb_cout.rearrange("(k p) n -> p k n", p=128))
        vg = K.tile([128, 1024], F32, dma=True)
        bc_load(vg, W['c_v_norm'][0], 1024)
        xTl = K.ring(2, [128, 8, 128], BF16, dma=True)
        xr = K.ring(2, [128, 1024], F32, dma=True)
        ur = K.ring(2, [128, 2048], F32)
        vnr = K.ring(2, [128, 1024], BF16)
        zr = K.ring(2, [128, 1024], BF16)
        zTr = K.ring(2, [128, 8, 128], BF16)
        for t in range(NTo):
            xT = xTl.next()
            K.ld(xT, xT.t[:], XT[0][:, :, 1 + t * 128:1 + (t + 1) * 128].rearrange("c p t -> p c t"))
            x = xr.next()
            K.ld(x, x.t[:], X1[t * 128:(t + 1) * 128, :])
            uv = ur.next()
            for j in range(4):
                pu = K.pgen.next()
                for k in range(8):
                    K.mm(pu, pu.t[:, :], xT, xT.t[:, k, :], cwin, cwin.t[:, k, j * 512:(j + 1) * 512], start=(k == 0),
                         stop=(k == 7))
                K.act(uv, uv.t[:, j * 512:(j + 1) * 512], [pu], pu.t[:, :], AF.Gelu)
            ss = nt.stats(uv, uv.t[:, 1024:2048], 1024)
            vn = vnr.next()
            K.v("dve", lambda e, ss=ss, uv=uv, vn=vn: e.scalar_tensor_tensor(
                out=vn.t[:], in0=uv.t[:, 1024:2048], scalar=ss.t[:, 3:4], in1=vg.t[:], op0=ALU.mult, op1=ALU.mult),
                [uv, ss, vg], [vn])
            z = zr.next()
            psv = [K.pgen.next(), K.pgen.next()]
            for g in range(8):
                pp = psv[g // 4]
                K.mm(pp, pp.t[:, (g % 4) * 128:(g % 4 + 1) * 128], wsT, wsT.t[:, g, :], vn, vn.t[:, g * 128:(g + 1) * 128])
            for g in range(8):
                pp = psv[g // 4]
                K.v("dve", lambda e, pp=pp, g=g, uv=uv, z=z: e.scalar_tensor_tensor(
                    out=z.t[:, g * 128:(g + 1) * 128], in0=pp.t[:, (g % 4) * 128:(g % 4 + 1) * 128],
                    scalar=bsT.t[:, g:g + 1], in1=uv.t[:, g * 128:(g + 1) * 128], op0=ALU.add, op1=ALU.mult),
                    [pp, bsT, uv], [z])
            if DEBUG and t == 0:
                P.op("sp", lambda e: e.dma_start(out=DBG_wsT, in_=wsT.t[:, :, :].rearrange("p a b -> p (a b)")), reads=[wsT.b], dsem=wsem)
                P.op("sp", lambda e: e.dma_start(out=DBG_bsT, in_=bsT.t[:]), reads=[bsT.b], dsem=wsem)
                P.op("sp", lambda e, uv=uv: e.dma_start(out=DBG_uv, in_=uv.t[:]), reads=[uv.b], dsem=wsem)
                P.op("sp", lambda e, vn=vn: e.dma_start(out=DBG_vn, in_=vn.t[:]), reads=[vn.b], dsem=wsem)
                P.op("sp", lambda e, z=z: e.dma_start(out=DBG_z, in_=z.t[:]), reads=[z.b], dsem=wsem)
            pb = K.pgen.next()
            pbv = pb.t[:].bitcast(BF16).rearrange("p (c t) -> p c t", c=8)
            for c in range(8):
                K.tr(pb, pbv[:, c, :], z, z.t[:, c * 128:(c + 1) * 128], ident, ident.t[:])
            zT = zTr.next()
            K.act(zT, zT.t[:], [pb], pbv, AF.Copy)
            pr = project_tm(zT, lambda k: zT.t[:, k, :], cwo, lambda k, half: cwo.t[:, k, half * 512:(half + 1) * 512], 8)
            tail(x, pr, t, nt, XT[1])

        ffn_sweep(1, XT[1], XT[0], W['final_norm'], True)

    P.emit()
    nc._dbg_names = dbg_names
    return nc


BIG = 30000.0
OWN_EXT = 2560
HALO = 256


def make_consts():
    return {
        "c_ident": np.eye(128, dtype=np.float32),
        "c_triu": np.triu(np.ones((128, 128), np.float32)),
        "c_tril": np.tril(np.ones((128, 128), np.float32)),
        "c_ones": np.ones((128, 128), np.float32),
    }


def rope_table(pos):
    inv = (1.0 / (np.float32(10000.0) ** (np.arange(0, 32, 2, dtype=np.float32) / np.float32(32)))).astype(np.float32)
    ang = pos.astype(np.float32)[:, None] * inv[None, :]
    return np.concatenate([np.cos(ang), np.sin(ang)], axis=1).astype(np.float32)


def mask_table(mf, mb):
    mf = mf.astype(np.float32)
    mb = mb.astype(np.float32)
    return np.stack([mf, (mf - 1) * BIG, mb, (mb - 1) * BIG], axis=1).astype(np.float32)


def aux_full(S):
    ones = np.ones(S, np.float32)
    return rope_table(np.arange(S)), mask_table(ones, ones)


def aux_quarter(S, j):
    q = S // 4
    shift = j * q - HALO
    r = np.arange(S)
    t = r + shift
    pos = t % S
    own = r < OWN_EXT
    valid = (t >= 0) & (t < S)
    t_start = max(shift, 0)
    t_end = min(shift + OWN_EXT, S)
    mf = np.where(own, valid, pos < t_start)
    mb = np.where(own, valid, pos >= t_end)
    sf = np.zeros((1, 4), np.float32)
    sf[0, (4 - j) % 4] = -1.0
    return shift, rope_table(pos), mask_table(mf, mb), sf


_CACHE = {}


def run(core_inputs, params, jobs):
    jobs = norm_jobs(jobs)
    key = repr(jobs)
    if key not in _CACHE:
        _CACHE[key] = build_program(jobs)
    nc = _CACHE[key]
    consts = make_consts()
    in_maps = []
    for ci in core_inputs:
        m = {k: np.ascontiguousarray(v, dtype=np.float32) for k, v in ci.items()}
        for j, jd in enumerate(jobs):
            if ("cs%d" % j) not in m:
                cs, mk = aux_full(jd['S'])
                m["cs%d" % j] = cs
                m["mk%d" % j] = mk
        if "sf" not in m:
            m["sf"] = np.zeros((1, 4), np.float32)
        for n in PARAMS:
            m[n] = np.ascontiguousarray(params[n], dtype=np.float32)
        m.update(consts)
        in_maps.append(m)
    res = run_bass_kernel_spmd(nc, in_maps, core_ids=list(range(len(core_inputs))))
    global LAST_RES
    LAST_RES = res.results
    return [[r["y%d" % j] for j in range(len(jobs))] for r in res.results]


def kernel(**inputs):
    xp = np.asarray(inputs['x_prompt'], dtype=np.float32)
    xs = np.asarray(inputs['x_sample'], dtype=np.float32)
    params = {n: np.asarray(inputs[n], dtype=np.float32) for n in PARAMS}
    SP = xp.shape[1]
    jobs = [4096, 4096, dict(S=SP, NTo=OWN_EXT // 128, out=(HALO // 128, (OWN_EXT - HALO) // 128), prompt=True)]
    core_inputs = []
    for c in range(8):
        p, j = c // 4, c % 4
        shift, cs, mk, sf = aux_quarter(SP, j)
        core_inputs.append({"x0": xs[2 * c], "x1": xs[2 * c + 1], "x2": np.roll(xp[p], -shift, axis=0),
                            "cs2": cs, "mk2": mk, "sf": sf})
    outs = run(core_inputs, params, jobs)
    q = SP // 4
    y_prompt = np.zeros_like(xp)
    for c in range(8):
        p, j = c // 4, c % 4
        y_prompt[p, j * q:(j + 1) * q] = outs[c][2]
    y_sample = np.stack([outs[c][j] for c in range(8) for j in range(2)], axis=0)
    return (y_prompt, y_sample)
```

```python
import math
import numpy as np
import concourse.bass as bass
import concourse.mybir as mybir
from concourse.bass_utils import run_bass_kernel_spmd

F32 = mybir.dt.float32
BF16 = mybir.dt.bfloat16
ALU = mybir.AluOpType
AF = mybir.ActivationFunctionType
AX = mybir.AxisListType

D = 1024
EPS = 1e-6
DFF = 2816
NFC = 22
SC_ATT = 96 ** -0.5
SC_M = 128 ** -0.5
LNSC_M = math.log(SC_M)

PARAMS = ['ab_norm', 'ab_w_in', 'mla_q_norm', 'mla_w_uq', 'mla_kv_norm', 'mla_w_ukv', 'mlstm_conv_w',
          'mlstm_conv_b', 'mlstm_gate_bias', 'mlstm_out_norm', 'ab_w_out', 'c_norm', 'c_w_in', 'c_v_norm',
          'c_w_spatial', 'c_b_spatial', 'c_w_out', 'ffn_norm', 'ffn_w_up', 'ffn_conv_w', 'ffn_conv_b',
          'ffn_w_down', 'final_norm']
PSHAPES = {
    'ab_norm': [1, 1024], 'ab_w_in': [1, 1024, 2480], 'mla_q_norm': [1, 256], 'mla_w_uq': [1, 256, 768],
    'mla_kv_norm': [1, 128], 'mla_w_ukv': [1, 128, 1024], 'mlstm_conv_w': [1, 3, 1024], 'mlstm_conv_b': [1, 1024],
    'mlstm_gate_bias': [1, 16], 'mlstm_out_norm': [1, 512], 'ab_w_out': [1, 1024, 1024], 'c_norm': [1, 1024],
    'c_w_in': [1, 1024, 2048], 'c_v_norm': [1, 1024], 'c_w_spatial': [1, 8, 128, 128], 'c_b_spatial': [1, 8, 128],
    'c_w_out': [1, 1024, 1024], 'ffn_norm': [2, 1024], 'ffn_w_up': [2, 1024, 5632], 'ffn_conv_w': [2, 3, 5632],
    'ffn_conv_b': [2, 5632], 'ffn_w_down': [2, 2816, 1024], 'final_norm': [1024],
}


class Buf:
    __slots__ = ("name", "w", "r", "excl")

    def __init__(self, name, excl=False):
        self.name = name
        self.w = None
        self.r = []
        self.excl = excl


class Prog:
    ENGS = ("pe", "act", "dve", "pool", "sp")

    def __init__(self, nc, n_dma_sems=80):
        self.nc = nc
        self.ops = {e: [] for e in self.ENGS}
        self.count = {}
        self.seen = {e: {} for e in self.ENGS}
        self.pending = {e: [] for e in self.ENGS}
        self.dma_sems = ["d%d" % i for i in range(n_dma_sems)]
        self.nops = 0
        self.disabled = False
        self.sweep_no = 0
        import os
        self.maxops = int(os.environ.get("K_OPS", "1000000000"))

    def op(self, eng, fn, reads=(), writes=(), dsem=None):
        if self.disabled or self.nops >= self.maxops:
            return None
        deps = []
        for b in reads:
            if b.w is not None:
                deps.append(b.w)
            if b.excl:
                for t in b.r:
                    if t[0] != eng:
                        deps.append(t)
        for b in writes:
            if b.w is not None:
                deps.append(b.w)
            deps.extend(b.r)
        if dsem is None:
            key, inc = eng, 1
        else:
            key, inc = dsem, 16
        self.count[key] = self.count.get(key, 0) + inc
        tok = (key, self.count[key])
        seen = self.seen[eng]
        waits = []
        if self.pending[eng]:
            for k, v in self.pending[eng]:
                if seen.get(k, 0) < v:
                    seen[k] = v
                    waits.append((k, v))
            self.pending[eng] = []
        for k, v in deps:
            if k == "pe" and eng == "pe" and dsem is None:
                continue
            if seen.get(k, 0) >= v:
                continue
            seen[k] = v
            waits.append((k, v))
        self.ops[eng].append((fn, waits, key, inc))
        for b in reads:
            b.r.append(tok)
        for b in writes:
            b.w = tok
            b.r = []
        self.nops += 1
        return tok

    def barrier(self):
        snap = [(k, v) for k, v in self.count.items()]
        for e in self.ENGS:
            self.pending[e] = list(snap)

    def emit(self):
        from contextlib import ExitStack
        nc = self.nc
        self.barrier()
        with ExitStack() as st:
            sems = {}
            for k in list(self.ENGS) + self.dma_sems:
                sems[k] = st.enter_context(nc.semaphore("s_" + k))
            block = st.enter_context(nc.Block())

            def mk(e):
                def section(eng):
                    for fn, waits, key, inc in self.ops[e]:
                        for k, v in waits:
                            eng.wait_ge(sems[k], v)
                        ins = fn(eng)
                        ins.then_inc(sems[key], inc)
                    for k, v in self.pending[e]:
                        if self.seen[e].get(k, 0) < v:
                            eng.wait_ge(sems[k], v)
                return section

            block.tensor(mk("pe"))
            block.scalar(mk("act"))
            block.vector(mk("dve"))
            block.gpsimd(mk("pool"))
            block.sync(mk("sp"))


class TL:
    __slots__ = ("t", "b", "sem")

    def __init__(self, t, name, sem=None):
        self.t = t
        self.b = Buf(name)
        self.sem = sem


class Ring:
    def __init__(self, tiles):
        self.tiles = tiles
        self.i = 0

    def next(self):
        t = self.tiles[self.i % len(self.tiles)]
        self.i += 1
        return t


def dsize(dt):
    return 4 if dt == F32 else 2


class Ctx:
    def __init__(self, nc, P):
        self.nc = nc
        self.P = P
        self.uid = 0
        self.pers_off = 16512
        self.off = 16512
        self.semi = 0
        self.banks = [TL(nc.alloc_psum_tensor("pb%d" % i, [128, 512], F32), "pb%d" % i) for i in range(8)]
        for b in self.banks:
            b.b.excl = True
        self.pgen = Ring(self.banks[0:6])
        self.pacc = Ring(self.banks[6:8])

    def _alloc(self, shape, dt, dma, off):
        self.uid += 1
        name = "t%d" % self.uid
        t = self.nc.alloc_sbuf_tensor_at(name, list(shape), dt, offset=off)
        sem = None
        if dma:
            sem = self.P.dma_sems[self.semi % (len(self.P.dma_sems) - 1)]
            self.semi += 1
        return TL(t, name, sem)

    @staticmethod
    def _bytes(shape, dt):
        n = 1
        for s in shape[1:]:
            n *= s
        return (n * dsize(dt) + 63) // 64 * 64

    def pers(self, shape, dt, dma=False):
        tl = self._alloc(shape, dt, dma, self.pers_off)
        self.pers_off += self._bytes(shape, dt)
        return tl

    def begin_sweep(self):
        import os
        self.P.sweep_no += 1
        lim = int(os.environ.get("K_LIMIT", "100000"))
        self.P.disabled = self.P.sweep_no > lim
        if os.environ.get("K_VERBOSE"):
            print("sweep", self.P.sweep_no, "ops so far", self.P.nops, flush=True)
        self.P.barrier()
        self.off = self.pers_off

    def tile(self, shape, dt, dma=False):
        tl = self._alloc(shape, dt, dma, self.off)
        self.off += self._bytes(shape, dt)
        assert self.off <= 229376, ("SBUF overflow", self.off)
        return tl

    def ring(self, n, shape, dt, dma=False):
        return Ring([self.tile(shape, dt, dma) for _ in range(n)])

    def ld(self, dst, dst_ap, src_ap, eng="sp", slow=False, reads=()):
        if slow:
            fn = lambda e: e.dma_start(out=dst_ap, in_=src_ap, allow_slow_non_contiguous=True)
        else:
            fn = lambda e: e.dma_start(out=dst_ap, in_=src_ap)
        self.P.op(eng, fn, reads=list(reads), writes=[dst.b], dsem=dst.sem)

    def st(self, src, dst_ap, src_ap, eng="pool", slow=False):
        if slow:
            fn = lambda e: e.dma_start(out=dst_ap, in_=src_ap, allow_slow_non_contiguous=True)
        else:
            fn = lambda e: e.dma_start(out=dst_ap, in_=src_ap)
        self.P.op(eng, fn, reads=[src.b], dsem=src.sem)

    def mm(self, out, out_ap, lt, lhsT, rt, rhs, start=True, stop=True, skip=False):
        if skip:
            fn = lambda e: e.matmul(out_ap, lhsT=lhsT, rhs=rhs, start=start, stop=stop, skip_group_check=True)
        else:
            fn = lambda e: e.matmul(out_ap, lhsT=lhsT, rhs=rhs, start=start, stop=stop)
        self.P.op("pe", fn, reads=[lt.b, rt.b], writes=[out.b])

    def tr(self, out, out_ap, it, in_ap, idt, id_ap):
        self.P.op("pe", lambda e: e.transpose(out=out_ap, in_=in_ap, identity=id_ap),
                  reads=[it.b, idt.b], writes=[out.b])

    def act(self, out, out_ap, ins, in_ap, func, scale=None, bias=None, accum=None, extra_w=()):
        kw = {}
        if scale is not None:
            kw["scale"] = scale
        if bias is not None:
            kw["bias"] = bias
        if accum is not None:
            kw["accum_out"] = accum
        self.P.op("act", lambda e: e.activation(out=out_ap, in_=in_ap, func=func, **kw),
                  reads=[t.b for t in ins], writes=[out.b] + [t.b for t in extra_w])

    def v(self, eng, fn, ins, outs):
        self.P.op(eng, fn, reads=[t.b for t in ins], writes=[t.b for t in outs])


def norm_jobs(jobs):
    return [dict(S=j, NTo=j // 128, out=(0, j // 128), prompt=False) if isinstance(j, int) else j for j in jobs]


def build_program(jobs):
    jobs = norm_jobs(jobs)
    S_list = [j['S'] for j in jobs]
    nc = bass.Bass("TRN2", target_bir_lowering=False)
    P = Prog(nc)
    K = Ctx(nc, P)
    SM = max(S_list)
    NJ = len(S_list)

    import os
    DEBUG = bool(os.environ.get("K_DEBUG"))
    dbg_names = []

    def dram(name, shape, dt, kind="Internal"):
        if DEBUG and kind == "Internal" and (not name.startswith("wb_") or os.environ.get("K_DEBUGW")):
            kind = "ExternalOutput"
            dbg_names.append(name)
        return nc.dram_tensor(name, list(shape), dt, kind=kind).ap()

    xin = [dram("x%d" % j, [S_list[j], D], F32, "ExternalInput") for j in range(NJ)]
    yout = [dram("y%d" % j, [(jobs[j]['out'][1] - jobs[j]['out'][0]) * 128, D], F32, "ExternalOutput") for j in range(NJ)]
    csin = [dram("cs%d" % j, [S_list[j], 32], F32, "ExternalInput") for j in range(NJ)]
    mkin = [dram("mk%d" % j, [S_list[j], 4], F32, "ExternalInput") for j in range(NJ)]
    sfin = dram("sf", [1, 4], F32, "ExternalInput")
    W = {n: dram(n, PSHAPES[n], F32, "ExternalInput") for n in PARAMS}
    c_ident = dram("c_ident", [128, 128], F32, "ExternalInput")
    c_triu = dram("c_triu", [128, 128], F32, "ExternalInput")
    c_tril = dram("c_tril", [128, 128], F32, "ExternalInput")
    c_ones = dram("c_ones", [128, 128], F32, "ExternalInput")

    wb_in = dram("wb_in", [1024, 2480], BF16)
    wb_uq = dram("wb_uq", [256, 768], BF16)
    wb_ukv = dram("wb_ukv", [128, 1024], BF16)
    wb_out = dram("wb_out", [1024, 1024], BF16)
    wb_cin = dram("wb_cin", [1024, 2048], BF16)
    wb_cout = dram("wb_cout", [1024, 1024], BF16)
    wb_up = dram("wb_up", [2, 1024, 5632], BF16)
    wb_dn = dram("wb_dn", [2, 2816, 1024], BF16)
    XT = [dram("XT%d" % i, [8, 128, SM + 2], BF16) for i in range(2)]
    X1 = dram("X1", [SM, D], F32)
    QT = dram("QT", [8, 96, SM], BF16)
    KT = dram("KT", [8, 96, SM], BF16)
    VV = dram("VV", [SM, 8, 65], BF16)
    QKT = dram("QKT", [8, 128, SM], BF16)
    VM = dram("VM", [SM, 4, 129], BF16)
    SO = dram("SO", [SM, 512], BF16)
    GG = dram("GG", [SM, 16], F32)
    HF = dram("HF", [SM, 512], F32)
    YY = dram("YY", [SM, D], BF16)
    if DEBUG:
        DBG_dm = dram("DBG_dm", [128, 128], F32)
        DBG_pt = dram("DBG_pt", [128, 128], BF16)
        DBG_s = dram("DBG_s", [128, 24], F32)
        DBG_tot = dram("DBG_tot", [128, 129], F32)
        DBG_kk = dram("DBG_kk", [128, 128], BF16)
        DBG_wsT = dram("DBG_wsT", [128, 1024], BF16)
        DBG_bsT = dram("DBG_bsT", [128, 8], F32)
        DBG_uv = dram("DBG_uv", [128, 2048], F32)
        DBG_vn = dram("DBG_vn", [128, 1024], BF16)
        DBG_z = dram("DBG_z", [128, 1024], BF16)

    wsem = P.dma_sems[-1]

    def cast(dst, src, rows):
        for r0 in range(0, rows, 128):
            r1 = min(rows, r0 + 128)
            P.op("pool", lambda e, r0=r0, r1=r1: e.dma_start(out=dst[r0:r1, :], in_=src[r0:r1, :]),
                 writes=[], dsem=wsem)

    cast(wb_in, W['ab_w_in'][0], 1024)
    cast(wb_uq, W['mla_w_uq'][0], 256)
    cast(wb_ukv, W['mla_w_ukv'][0], 128)
    cast(wb_out, W['ab_w_out'][0], 1024)
    cast(wb_cin, W['c_w_in'][0], 1024)
    cast(wb_cout, W['c_w_out'][0], 1024)
    for l in range(2):
        cast(wb_up[l], W['ffn_w_up'][l], 1024)
        cast(wb_dn[l], W['ffn_w_down'][l], 2816)

    identf = K.pers([128, 128], F32, dma=True)
    ident = K.pers([128, 128], BF16)
    triu = K.pers([128, 128], F32, dma=True)
    tril = K.pers([128, 128], F32, dma=True)
    ones = K.pers([128, 128], F32, dma=True)
    cw_m = K.pers([128, 4, 8], F32)
    cw_f = [K.pers([128, 4, 44], F32) for _ in range(2)]
    bsT = K.pers([128, 8], F32)
    wsT = K.pers([128, 8, 128], BF16)
    zero = K.pers([128, 16], BF16, dma=True)
    epsb = K.pers([128, 1], F32)
    K.begin_sweep()
    K.v("dve", lambda e: e.memset(epsb.t[:], EPS), [], [epsb])
    K.ld(identf, identf.t[:], c_ident)
    K.ld(triu, triu.t[:], c_triu)
    K.ld(tril, tril.t[:], c_tril)
    K.ld(ones, ones.t[:], c_ones)
    K.v("dve", lambda e: e.tensor_copy(out=ident.t[:], in_=identf.t[:]), [identf], [ident])
    K.v("dve", lambda e: e.memset(zero.t[:], 0.0), [], [zero])

    def load_T(dst, dst_view, raws, nrow):
        raw = K.tile([nrow, len(raws), 128], F32, dma=True)
        for j, src in enumerate(raws):
            K.ld(raw, raw.t[:, j, :], src)
        for j in range(len(raws)):
            pb = K.pgen.next()
            K.tr(pb, pb.t[:, 0:nrow], raw, raw.t[:, j, :], identf, identf.t[0:nrow, 0:nrow])
            K.v("dve", lambda e, j=j, pb=pb: e.tensor_copy(out=dst_view(j), in_=pb.t[:, 0:nrow]), [pb], [dst])

    mcw = W['mlstm_conv_w'][0]
    load_T(cw_m, lambda j: cw_m.t[:, j, :],
           [mcw[j].rearrange("(c p) -> c p", p=128) for j in range(3)] +
           [W['mlstm_conv_b'][0].rearrange("(c p) -> c p", p=128)], 8)
    for l in range(2):
        load_T(cw_f[l], lambda j, l=l: cw_f[l].t[:, j, :],
               [W['ffn_conv_w'][l, j].rearrange("(c p) -> c p", p=128) for j in range(3)] +
               [W['ffn_conv_b'][l].rearrange("(c p) -> c p", p=128)], 44)
    load_T(bsT, lambda j: bsT.t[:, :], [W['c_b_spatial'][0]], 8)
    wsr = K.tile([128, 8, 128], F32, dma=True)
    wsrb = K.tile([128, 8, 128], BF16)
    K.ld(wsr, wsr.t[:], W['c_w_spatial'][0].rearrange("g t s -> t g s"))
    K.v("dve", lambda e: e.tensor_copy(out=wsrb.t[:], in_=wsr.t[:]), [wsr], [wsrb])
    pb = K.pgen.next()
    pbv = pb.t[:].bitcast(BF16).rearrange("p (c t) -> p c t", c=8)
    for g in range(8):
        K.tr(pb, pbv[:, g, :], wsrb, wsrb.t[:, g, :], ident, ident.t[:])
    K.v("dve", lambda e, pbv=pbv: e.tensor_copy(out=wsT.t[:], in_=pbv), [pb], [wsT])

    def bc_load(dst, src1d, n):
        K.ld(dst, dst.t[:, 0:n], src1d.partition_broadcast(128))

    class NormT:
        def __init__(self):
            self.ss = K.ring(3, [128, 4], F32)
            self.xn = K.ring(2, [128, 1024], BF16)
            self.xnT = K.ring(2, [128, 8, 128], BF16, dma=True)
            self.g = K.tile([128, 1024], F32, dma=True)

        def load_gain(self, src1d):
            bc_load(self.g, src1d, 1024)

        def stats(self, x, x_ap, n, junk=None):
            ss = self.ss.next()
            if junk is None:
                junk = self.xn.next()
            K.v("dve", lambda e: e.memset(ss.t[:, 0:1], 0.0), [], [ss])
            K.act(junk, junk.t[:, 0:n], [x, ss], x_ap, AF.Square, accum=ss.t[:, 0:1], extra_w=[ss])
            K.act(ss, ss.t[:, 2:3], [ss, epsb], ss.t[:, 0:1], AF.Ln, scale=1.0 / n, bias=epsb.t[:, 0:1])
            K.act(ss, ss.t[:, 3:4], [ss], ss.t[:, 2:3], AF.Exp, scale=-0.5)
            return ss

        def run(self, x, x_ap, t, XTd):
            self.run_b(self.run_a(x, x_ap), t, XTd)

        def run_a(self, x, x_ap):
            xn = self.xn.next()
            ss = self.stats(x, x_ap, 1024, junk=xn)
            g = self.g
            K.v("dve", lambda e: e.scalar_tensor_tensor(out=xn.t[:], in0=x_ap, scalar=ss.t[:, 3:4], in1=g.t[:],
                                                        op0=ALU.mult, op1=ALU.mult), [x, ss, g], [xn])
            return xn

        def run_b(self, xn, t, XTd):
            pb = K.pgen.next()
            pbv = pb.t[:].bitcast(BF16).rearrange("p (c t) -> p c t", c=8)
            for c in range(8):
                K.tr(pb, pbv[:, c, :], xn, xn.t[:, c * 128:(c + 1) * 128], ident, ident.t[:])
            xnT = self.xnT.next()
            K.act(xnT, xnT.t[:], [pb], pbv, AF.Copy)
            K.st(xnT, XTd[:, :, 1 + t * 128:1 + (t + 1) * 128].rearrange("c p t -> p c t"), xnT.t[:])

    def conv_chunk(pm, ph, hsl, cw, ci, dst):
        w0 = cw.t[:, 0, ci:ci + 1]
        w1 = cw.t[:, 1, ci:ci + 1]
        w2 = cw.t[:, 2, ci:ci + 1]
        bb = cw.t[:, 3, ci:ci + 1]
        K.act(dst, dst.t[:], [pm, cw], pm.t[:], AF.Identity, scale=w1, bias=bb)
        K.act(dst, dst.t[:, 0:1], [ph, cw, dst], ph.t[:, hsl:hsl + 1], AF.Identity, scale=w0, bias=dst.t[:, 0:1])
        K.act(dst, dst.t[:, 511:512], [ph, cw, dst], ph.t[:, hsl + 1:hsl + 2], AF.Identity, scale=w2,
              bias=dst.t[:, 511:512])
        K.v("dve", lambda e: e.scalar_tensor_tensor(out=dst.t[:, 1:512], in0=pm.t[:, 0:511], scalar=w0,
                                                    in1=dst.t[:, 1:512], op0=ALU.mult, op1=ALU.add),
            [pm, cw, dst], [dst])
        K.v("dve", lambda e: e.scalar_tensor_tensor(out=dst.t[:, 0:511], in0=pm.t[:, 1:512], scalar=w2,
                                                    in1=dst.t[:, 0:511], op0=ALU.mult, op1=ALU.add),
            [pm, cw, dst], [dst])

    def skewed(n, stages, reverse=False):
        ctxs = {}
        ns = len(stages)
        for step in range(n + ns - 1):
            for k in (range(ns - 1, -1, -1) if reverse else range(ns)):
                i = step - k
                if 0 <= i < n:
                    if k == 0:
                        ctxs[i] = {}
                    stages[k](i, ctxs[i])
                    if k == ns - 1:
                        del ctxs[i]

    for jb in range(NJ):
        S = S_list[jb]
        NT = S // 128
        NB = S // 512
        x_in = xin[jb]
        y_out = yout[jb]
        NTo = jobs[jb]['NTo']
        NBo = NTo // 4
        out_lo, out_hi = jobs[jb]['out']
        prompt = jobs[jb]['prompt']
        SEAM_BLOCKS = (0, 4, 8, 12) if prompt else ()

        def seam_setup(cw, nch, ncand):
            sfb = K.tile([128, 4], F32, dma=True)
            bc_load(sfb, sfin[0], 4)
            nfw = K.tile([128, ncand, 2, nch], F32)
            for c in range(ncand):
                for jj, wi in ((0, 0), (1, 2)):
                    K.v("dve", lambda e, c=c, jj=jj, wi=wi, nfw=nfw, sfb=sfb, cw=cw: e.tensor_scalar_mul(
                        out=nfw.t[:, c, jj, :], in0=cw.t[:, wi, :], scalar1=sfb.t[:, c:c + 1]), [cw, sfb], [nfw])
            return nfw

        def seam_fix(pm, dst, nfw, c, ci):
            K.v("dve", lambda e: e.scalar_tensor_tensor(out=dst.t[:, 256:257], in0=pm.t[:, 255:256],
                                                        scalar=nfw.t[:, c, 0, ci:ci + 1], in1=dst.t[:, 256:257],
                                                        op0=ALU.mult, op1=ALU.add), [pm, nfw, dst], [dst])
            K.v("dve", lambda e: e.scalar_tensor_tensor(out=dst.t[:, 255:256], in0=pm.t[:, 256:257],
                                                        scalar=nfw.t[:, c, 1, ci:ci + 1], in1=dst.t[:, 255:256],
                                                        op0=ALU.mult, op1=ALU.add), [pm, nfw, dst], [dst])

        K.begin_sweep()
        nt = NormT()
        nt.load_gain(W['ab_norm'][0])
        for i in range(2):
            K.st(zero, XT[i][:, :, 0:1].rearrange("c p t -> p c t"), zero.t[:, 0:8].rearrange("p (c t) -> p c t", t=1),
                 slow=True)
            K.st(zero, XT[i][:, :, S + 1:S + 2].rearrange("c p t -> p c t"),
                 zero.t[:, 0:8].rearrange("p (c t) -> p c t", t=1), slow=True)
        xr = K.ring(3, [128, 1024], F32, dma=True)

        def a0(t, c):
            x = xr.next()
            K.ld(x, x.t[:], x_in[t * 128:(t + 1) * 128, :])
            c['xn'] = nt.run_a(x, x.t[:])

        def a1(t, c):
            nt.run_b(c['xn'], t, XT[0])

        skewed(NT, [a0, a1])

        K.begin_sweep()
        w_in = K.tile([128, 8, 2480], BF16, dma=True)
        K.ld(w_in, w_in.t[:], wb_in.rearrange("(k p) n -> p k n", p=128))
        w_uq = K.tile([128, 2, 768], BF16, dma=True)
        K.ld(w_uq, w_uq.t[:], wb_uq.rearrange("(k p) n -> p k n", p=128))
        w_ukv = K.tile([128, 1024], BF16, dma=True)
        K.ld(w_ukv, w_ukv.t[:], wb_ukv)
        qg = K.tile([128, 256], F32, dma=True)
        bc_load(qg, W['mla_q_norm'][0], 256)
        kvg = K.tile([128, 128], F32, dma=True)
        bc_load(kvg, W['mla_kv_norm'][0], 128)
        gbias = K.tile([128, 16], F32, dma=True)
        bc_load(gbias, W['mlstm_gate_bias'][0], 16)
        winr = K.ring(2, [128, 8, 514], BF16, dma=True)
        cgr = K.ring(2, [128, 512], F32)
        qkr = K.ring(2, [128, 512], BF16, dma=True)
        junk = K.tile([128, 256], BF16)
        ssr = K.ring(2, [128, 8], F32)
        cqn = K.ring(2, [128, 384], BF16)
        cTr = K.ring(2, [128, 3, 128], BF16)
        cosr = K.ring(2, [128, 32], F32, dma=True)
        krr = K.ring(2, [128, 32], BF16)
        tmp = K.ring(2, [128, 8, 64], F32)
        tmp2 = K.ring(2, [128, 8, 32], F32)
        qst = K.ring(2, [128, 8, 96], BF16)
        qfr = K.ring(2, [128, 768], F32)
        kst = K.ring(2, [128, 8, 96], BF16)
        vst = K.ring(2, [128, 8, 65], BF16, dma=True)
        qTs = K.ring(2, [96, 8, 128], BF16, dma=True)
        kTs = K.ring(2, [96, 8, 128], BF16, dma=True)
        vmst = K.ring(2, [128, 4, 129], BF16, dma=True)
        sor = K.ring(2, [128, 512], BF16, dma=True)
        sxr = K.ring(2, [128, 512], F32)
        gtr = K.ring(2, [128, 16], F32, dma=True)
        ge = K.ring(2, [128, 2, 4], F32)
        for tl in vst.tiles + vmst.tiles:
            K.v("dve", lambda e, tl=tl: e.memset(tl.t[:], 1.0), [], [tl])
        mkr = K.ring(2, [128, 4], F32, dma=True)
        padb = Buf("xtpad")
        nfw_m = None
        if prompt:
            P.op("pool", lambda e, S=S: e.dma_start(out=XT[0][:, :, 0:1], in_=XT[0][:, :, S:S + 1],
                                                    allow_slow_non_contiguous=True), writes=[padb], dsem=wsem)
            P.op("pool", lambda e, S=S: e.dma_start(out=XT[0][:, :, S + 1:S + 2], in_=XT[0][:, :, 1:2],
                                                    allow_slow_non_contiguous=True), writes=[padb], dsem=wsem)
            nfw_m = seam_setup(cw_m, 8, 4)
        K.pgen = Ring(K.banks[0:6])
        for b in range(NB):
            win = winr.next()
            K.ld(win, win.t[:], XT[0][:, :, 512 * b:512 * b + 514].rearrange("c p t -> p c t"), reads=[padb])
            for c in range(8):
                if c < 4 and b >= NBo:
                    continue
                col0 = 416 + c * 128
                pm = K.pgen.next()
                hsl = 0
                halo = K.banks[6 + (c % 2)]
                for k in range(8):
                    K.mm(pm, pm.t[:, :], w_in, w_in.t[:, k, col0:col0 + 128], win, win.t[:, k, 1:513],
                         start=(k == 0), stop=(k == 7))
                for k in range(8):
                    K.mm(halo, halo.t[:, hsl:hsl + 2], w_in, w_in.t[:, k, col0:col0 + 128], win,
                         win.t[:, k, 0:514:513], start=(k == 0), stop=(k == 7))
                cg = cgr.next()
                conv_chunk(pm, halo, hsl, cw_m, c, cg)
                if b in SEAM_BLOCKS:
                    seam_fix(pm, cg, nfw_m, b // 4, c)
                qk = qkr.next()
                K.act(qk, qk.t[:], [cg], cg.t[:], AF.Silu)
                K.st(qk, QKT[c, :, 512 * b:512 * b + 512], qk.t[:])
            for tt in range(4):
                t = b * 4 + tt
                lo = 1 + tt * 128
                pa = K.pgen.next()
                pv = K.pgen.next()
                po = K.pgen.next()
                for k in range(8):
                    K.mm(pa, pa.t[:, 0:416], win, win.t[:, k, lo:lo + 128], w_in, w_in.t[:, k, 0:416],
                         start=(k == 0), stop=(k == 7))
                for k in range(8):
                    K.mm(pa, pa.t[:, 416:432], win, win.t[:, k, lo:lo + 128], w_in, w_in.t[:, k, 2464:2480],
                         start=(k == 0), stop=(k == 7))
                for k in range(8):
                    K.mm(pv, pv.t[:, :], win, win.t[:, k, lo:lo + 128], w_in, w_in.t[:, k, 1440:1952],
                         start=(k == 0), stop=(k == 7))
                is_own = t < NTo
                if is_own:
                    for k in range(8):
                        K.mm(po, po.t[:, :], win, win.t[:, k, lo:lo + 128], w_in, w_in.t[:, k, 1952:2464],
                             start=(k == 0), stop=(k == 7))
                vm = vmst.next()
                K.act(vm, vm.t[:, :, 0:128], [pv], pv.t[:, :].rearrange("p (h d) -> p h d", d=128), AF.Copy)
                K.st(vm, VM[t * 128:(t + 1) * 128, :, :], vm.t[:])
                if is_own:
                    so = sor.next()
                    K.act(so, so.t[:], [po], po.t[:, :], AF.Sigmoid)
                    K.st(so, SO[t * 128:(t + 1) * 128, :], so.t[:])
                gt = gtr.next()
                g2 = ge.next()
                K.v("dve", lambda e, gt=gt, pa=pa: e.tensor_tensor(out=gt.t[:], in0=pa.t[:, 416:432], in1=gbias.t[:],
                                                                   op=ALU.add), [pa, gbias], [gt])
                gtv = gt.t[:, :].rearrange("p (a b) -> p a b", b=8)[:, :, 4:8]
                K.act(g2, g2.t[:], [gt], gtv, AF.Exp, scale=-1.0)
                K.v("dve", lambda e, g2=g2: e.tensor_scalar_add(out=g2.t[:], in0=g2.t[:], scalar1=1.0), [g2], [g2])
                K.act(g2, g2.t[:], [g2], g2.t[:], AF.Ln)
                K.v("dve", lambda e, g2=g2, gtv=gtv: e.tensor_scalar_mul(out=gtv, in0=g2.t[:], scalar1=-1.0), [g2], [gt])
                mk = mkr.next()
                K.ld(mk, mk.t[:], mkin[jb][t * 128:(t + 1) * 128, :])
                for (c0, mi) in ((0, 0), (8, 2)):
                    K.v("dve", lambda e, gt=gt, mk=mk, c0=c0, mi=mi: e.scalar_tensor_tensor(
                        out=gt.t[:, c0:c0 + 4], in0=gt.t[:, c0:c0 + 4], scalar=mk.t[:, mi:mi + 1],
                        in1=mk.t[:, mi + 1:mi + 2].to_broadcast([128, 4]), op0=ALU.mult, op1=ALU.add), [gt, mk], [gt])
                    K.v("dve", lambda e, gt=gt, mk=mk, c0=c0, mi=mi: e.tensor_scalar_mul(
                        out=gt.t[:, c0 + 4:c0 + 8], in0=gt.t[:, c0 + 4:c0 + 8], scalar1=mk.t[:, mi:mi + 1]), [gt, mk], [gt])
                K.st(gt, GG[t * 128:(t + 1) * 128, :], gt.t[:])
                ss = ssr.next()
                K.v("dve", lambda e, ss=ss: e.memset(ss.t[:, 0:2], 0.0), [], [ss])
                K.act(junk, junk.t[:, 0:256], [pa, ss], pa.t[:, 0:256], AF.Square, accum=ss.t[:, 0:1], extra_w=[ss])
                K.act(junk, junk.t[:, 0:128], [pa, ss], pa.t[:, 256:384], AF.Square, accum=ss.t[:, 1:2], extra_w=[ss])
                K.act(ss, ss.t[:, 4:5], [ss, epsb], ss.t[:, 0:1], AF.Ln, scale=1.0 / 256, bias=epsb.t[:, 0:1])
                K.act(ss, ss.t[:, 5:6], [ss, epsb], ss.t[:, 1:2], AF.Ln, scale=1.0 / 128, bias=epsb.t[:, 0:1])
                K.act(ss, ss.t[:, 6:8], [ss], ss.t[:, 4:6], AF.Exp, scale=-0.5)
                cq = cqn.next()
                K.v("dve", lambda e, ss=ss, cq=cq, pa=pa: e.scalar_tensor_tensor(
                    out=cq.t[:, 0:256], in0=pa.t[:, 0:256], scalar=ss.t[:, 6:7], in1=qg.t[:], op0=ALU.mult,
                    op1=ALU.mult), [pa, ss, qg], [cq])
                K.v("dve", lambda e, ss=ss, cq=cq, pa=pa: e.scalar_tensor_tensor(
                    out=cq.t[:, 256:384], in0=pa.t[:, 256:384], scalar=ss.t[:, 7:8], in1=kvg.t[:], op0=ALU.mult,
                    op1=ALU.mult), [pa, ss, kvg], [cq])
                cs = cosr.next()
                K.ld(cs, cs.t[:, 0:32], csin[jb][t * 128:(t + 1) * 128, :])
                kr = krr.next()
                tm = tmp.next()

                def rope(dst1, dst2, x1, x2, cosb, sinb, ta, tb, ins, outs, tmt, eng="dve"):
                    K.v(eng, lambda e: e.tensor_tensor(out=ta, in0=x1, in1=cosb, op=ALU.mult), ins, [tmt])
                    K.v(eng, lambda e: e.tensor_tensor(out=tb, in0=x2, in1=sinb, op=ALU.mult), ins, [tmt])
                    K.v(eng, lambda e: e.tensor_tensor(out=dst1, in0=ta, in1=tb, op=ALU.subtract), [tmt], outs)
                    K.v(eng, lambda e: e.tensor_tensor(out=ta, in0=x2, in1=cosb, op=ALU.mult), ins + outs, [tmt])
                    K.v(eng, lambda e: e.tensor_tensor(out=tb, in0=x1, in1=sinb, op=ALU.mult), ins, [tmt])
                    K.v(eng, lambda e: e.tensor_tensor(out=dst2, in0=ta, in1=tb, op=ALU.add), [tmt], outs)

                rope(kr.t[:, 0:16], kr.t[:, 16:32], pa.t[:, 384:400], pa.t[:, 400:416], cs.t[:, 0:16], cs.t[:, 16:32],
                     tm.t[:, 0, 0:16], tm.t[:, 0, 16:32], [pa, cs], [kr], tm)
                pb = K.pgen.next()
                pbv = pb.t[:].bitcast(BF16).rearrange("p (c t) -> p c t", c=8)
                for c in range(3):
                    K.tr(pb, pbv[:, c, :], cq, cq.t[:, c * 128:(c + 1) * 128], ident, ident.t[:])
                cT = cTr.next()
                K.act(cT, cT.t[:], [pb], pbv[:, 0:3, :], AF.Copy)
                if is_own:
                    pq1 = K.pgen.next()
                    pq2 = K.pgen.next()
                    for k in range(2):
                        K.mm(pq1, pq1.t[:, 0:480], cT, cT.t[:, k, :], w_uq, w_uq.t[:, k, 0:480], start=(k == 0), stop=(k == 1))
                    for k in range(2):
                        K.mm(pq2, pq2.t[:, 0:288], cT, cT.t[:, k, :], w_uq, w_uq.t[:, k, 480:768], start=(k == 0), stop=(k == 1))
                    qs = qst.next()
                    qf = qfr.next()
                    K.act(qf, qf.t[:, 0:480], [pq1], pq1.t[:, 0:480], AF.Copy)
                    K.act(qf, qf.t[:, 480:768], [pq2], pq2.t[:, 0:288], AF.Copy)
                    qfv = qf.t[:, :].rearrange("p (h d) -> p h d", d=96)
                    tm2 = tmp2.next()
                    K.v("pool", lambda e, qs=qs, qfv=qfv: e.tensor_copy(out=qs.t[:, :, 0:64], in_=qfv[:, :, 0:64]), [qf], [qs])
                    cosb = cs.t[:, 0:16].unsqueeze(1).to_broadcast([128, 8, 16])
                    sinb = cs.t[:, 16:32].unsqueeze(1).to_broadcast([128, 8, 16])
                    rope(qs.t[:, :, 64:80], qs.t[:, :, 80:96], qfv[:, :, 64:80], qfv[:, :, 80:96],
                         cosb, sinb, tm2.t[:, :, 0:16], tm2.t[:, :, 16:32], [qf, cs], [qs], tm2, eng="pool")
                pk1 = K.pgen.next()
                pk2 = K.pgen.next()
                K.mm(pk1, pk1.t[:, :], cT, cT.t[:, 2, :], w_ukv, w_ukv.t[:, 0:512])
                K.mm(pk2, pk2.t[:, :], cT, cT.t[:, 2, :], w_ukv, w_ukv.t[:, 512:1024])
                ks = kst.next()
                vs = vst.next()
                for (pk, h0) in ((pk1, 0), (pk2, 4)):
                    pkv = pk.t[:, :].rearrange("p (h d) -> p h d", d=128)
                    K.act(ks, ks.t[:, h0:h0 + 4, 0:64], [pk], pkv[:, :, 0:64], AF.Copy)
                    K.v("dve", lambda e, vs=vs, pkv=pkv, h0=h0: e.tensor_copy(out=vs.t[:, h0:h0 + 4, 0:64],
                                                                             in_=pkv[:, :, 64:128]), [pk], [vs])
                K.v("dve", lambda e, ks=ks, kr=kr: e.tensor_copy(
                    out=ks.t[:, :, 64:96], in_=kr.t[:, :].unsqueeze(1).to_broadcast([128, 8, 32])), [kr], [ks])
                K.st(vs, VV[t * 128:(t + 1) * 128, :, :], vs.t[:])
                for (src, dstr, dd) in (((qs, qTs, QT), (ks, kTs, KT)) if is_own else ((ks, kTs, KT),)):
                    pb = K.pgen.next()
                    pbv = pb.t[:].bitcast(BF16).rearrange("p (c t) -> p c t", c=8)
                    for h in range(8):
                        K.tr(pb, pbv[0:96, h, :], src, src.t[:, h, :], ident, ident.t[:])
                    dT = dstr.next()
                    K.act(dT, dT.t[:], [pb], pbv[0:96, :, :], AF.Copy)
                    K.st(dT, dd[:, :, t * 128:(t + 1) * 128].rearrange("h p t -> p h t"), dT.t[:])
        K.pgen = Ring(K.banks[0:6])

        K.begin_sweep()
        kth = K.ring(2, [96, S], BF16, dma=True)
        vh = K.ring(2, [128, NT, 65], BF16, dma=True)
        qbr = K.ring(2, [96, 512], BF16, dma=True)
        ptr = K.ring(3, [128, 512], BF16)
        rcr = K.ring(2, [128, 4], F32)
        yar = K.ring(2, [128, 4, 64], BF16, dma=True)
        for h in range(8):
            kt = kth.next()
            K.ld(kt, kt.t[:], KT[h, :, 0:S])
            vv = vh.next()
            K.ld(vv, vv.t[:], VV[0:S, h, :].rearrange("(n p) d -> p n d", p=128))
            for qb in range(NBo):
                qt = qbr.next()
                K.ld(qt, qt.t[:], QT[h, :, qb * 512:(qb + 1) * 512])
                acc = K.pacc.next()

                def scores(kb, kt=kt, qt=qt):
                    ps = K.pgen.next()
                    K.mm(ps, ps.t[:, :], kt, kt.t[:, kb * 128:(kb + 1) * 128], qt, qt.t[:, :])
                    return ps

                LOOK = 2
                pend = [scores(kb) for kb in range(min(LOOK, NT))]
                for kb in range(NT):
                    ps = pend.pop(0)
                    if kb + LOOK < NT:
                        pend.append(scores(kb + LOOK))
                    pt = ptr.next()
                    K.act(pt, pt.t[:], [ps], ps.t[:, :], AF.Exp, scale=SC_ATT)
                    for j in range(4):
                        K.mm(acc, acc.t[:, j * 65:(j + 1) * 65], pt, pt.t[:, j * 128:(j + 1) * 128], vv, vv.t[:, kb, :],
                             start=(kb == 0 and j == 0), stop=(kb == NT - 1 and j == 3), skip=True)
                accv = acc.t[:, 0:260].rearrange("p (j d) -> p j d", d=65)
                rc = rcr.next()
                K.v("dve", lambda e, rc=rc, accv=accv: e.reciprocal(out=rc.t[:, :].unsqueeze(2), in_=accv[:, :, 64:65]),
                    [acc], [rc])
                ya = yar.next()
                K.v("dve", lambda e, rc=rc, accv=accv, ya=ya: e.tensor_tensor(
                    out=ya.t[:], in0=accv[:, :, 0:64], in1=rc.t[:, :].unsqueeze(2).to_broadcast([128, 4, 64]),
                    op=ALU.mult), [acc, rc], [ya])
                K.st(ya, YY[qb * 512:(qb + 1) * 512, h * 64:(h + 1) * 64].rearrange("(j p) d -> p j d", p=128), ya.t[:])

        for direction in (0, 1):
            K.begin_sweep()
            K.pgen = Ring(K.banks[0:8])
            tri = triu if direction == 0 else tril
            gi0 = 0 if direction == 0 else 8
            qkl = K.ring(3, [128, 8, 128], BF16, dma=True)
            vml = K.ring(3, [128, 4, 129], BF16, dma=True)
            gl = K.ring(3, [128, 16], F32, dma=True)
            ktok = K.ring(3, [128, 4, 128], BF16)
            sm = K.ring(3, [128, 6, 4], F32)
            ptm = K.ring(3, [128, 4, 128], BF16)
            dnr = K.ring(2, [128, 3, 4], F32)
            vwr = K.ring(2, [128, 4, 129], BF16)
            ST = K.tile([128, 4, 129], F32)
            STb = K.tile([128, 4, 129], BF16)
            hfr = K.ring(3, [128, 512], F32, dma=True)
            if direction == 1:
                sol = K.ring(3, [128, 512], BF16, dma=True)
                og = K.tile([128, 512], F32, dma=True)
                bc_load(og, W['mlstm_out_norm'][0], 512)
                sq = K.tile([128, 512], F32)
                ms = K.ring(2, [128, 3, 4], F32)
                ymr = K.ring(2, [128, 512], BF16, dma=True)
            K.v("dve", lambda e, ST=ST: e.memset(ST.t[:], 0.0), [], [ST])
            K.v("dve", lambda e, STb=STb: e.memset(STb.t[:], 0.0), [], [STb])
            own_c = list(range(NTo))
            ctx_c = list(range(NTo, NT))
            if direction == 0:
                order = [(n, True) for n in ctx_c] + [(n, False) for n in own_c]
            else:
                order = [(n, True) for n in reversed(ctx_c)] + [(n, False) for n in reversed(own_c)]

            def stage1(n, ctx_only, direction=direction, tri=tri, gi0=gi0):
                c = dict(n=n, ctx_only=ctx_only)
                r0, r1 = n * 128, (n + 1) * 128
                qk = qkl.next()
                if ctx_only:
                    K.ld(qk, qk.t[:, 4:8, :], QKT[4:8, :, r0:r1].rearrange("c p t -> p c t"))
                else:
                    K.ld(qk, qk.t[:], QKT[:, :, r0:r1].rearrange("c p t -> p c t"))
                vm = vml.next()
                K.ld(vm, vm.t[:], VM[r0:r1, :, :])
                g = gl.next()
                K.ld(g, g.t[:], GG[r0:r1, :])
                c.update(qk=qk, vm=vm, g=g)
                if not ctx_only:
                    hf = hfr.next()
                    c['hf'] = hf
                    if direction == 1:
                        K.ld(hf, hf.t[:], HF[r0:r1, :])
                        so = sol.next()
                        K.ld(so, so.t[:], SO[r0:r1, :])
                        c['so'] = so
                li = g.t[:, gi0:gi0 + 4]
                lf = g.t[:, gi0 + 4:gi0 + 8]
                pb = K.pgen.next()
                pbv = pb.t[:].bitcast(BF16).rearrange("p (c t) -> p c t", c=8)
                for h in range(4):
                    K.tr(pb, pbv[:, h, :], qk, qk.t[:, 4 + h, :], ident, ident.t[:])
                kk = ktok.next()
                K.act(kk, kk.t[:], [pb], pbv[:, 0:4, :], AF.Copy)
                pc = K.pgen.next()
                K.mm(pc, pc.t[:, 0:4], tri, tri.t[:], g, lf)
                K.mm(pc, pc.t[:, 4:8], ones, ones.t[:], g, lf)
                s_ = sm.next()
                K.v("dve", lambda e, s_=s_, pc=pc, li=li: e.scalar_tensor_tensor(
                    out=s_.t[:, 0, :], in0=li, scalar=LNSC_M, in1=pc.t[:, 0:4], op0=ALU.add, op1=ALU.subtract),
                    [g, pc], [s_])
                K.v("dve", lambda e, s_=s_, pc=pc: e.tensor_tensor(out=s_.t[:, 1, :], in0=s_.t[:, 0, :], in1=pc.t[:, 4:8],
                                                                   op=ALU.add), [pc, s_], [s_])
                K.act(s_, s_.t[:, 2:4, :], [pc], pc.t[:, 0:8].rearrange("p (a b) -> p a b", b=4), AF.Exp)
                K.act(s_, s_.t[:, 4:6, :], [s_], s_.t[:, 0:2, :], AF.Exp)
                c.update(kk=kk, s_=s_)
                if not ctx_only:
                    pS = K.pgen.next()
                    for h in range(4):
                        K.mm(pS, pS.t[:, h * 128:(h + 1) * 128], qk, qk.t[:, 4 + h, :], qk, qk.t[:, h, :])
                    pt = ptm.next()
                    for h in range(4):
                        K.v("dve", lambda e, pS=pS, pt=pt, s_=s_, h=h, tri=tri: e.scalar_tensor_tensor(
                            out=pt.t[:, h, :], in0=pS.t[:, h * 128:(h + 1) * 128], scalar=s_.t[:, 4, h:h + 1],
                            in1=tri.t[:], op0=ALU.mult, op1=ALU.mult), [pS, s_, tri], [pt])
                    c['pt'] = pt
                return c

            def stage2(c, direction=direction):
                n, ctx_only = c['n'], c['ctx_only']
                r0, r1 = n * 128, (n + 1) * 128
                qk, vm, kk, s_ = c['qk'], c['vm'], c['kk'], c['s_']
                if not ctx_only:
                    pt, hf = c['pt'], c['hf']
                    pN = [K.pgen.next(), K.pgen.next()]
                    for h in range(4):
                        pp = pN[h // 2]
                        c0 = (h % 2) * 129
                        K.mm(pp, pp.t[:, c0:c0 + 129], pt, pt.t[:, h, :], vm, vm.t[:, h, :], start=True, stop=False)
                        K.mm(pp, pp.t[:, c0:c0 + 129], qk, qk.t[:, h, :], STb, STb.t[:, h, :], start=False, stop=True)
                    dn = dnr.next()
                    for bi in range(2):
                        K.v("dve", lambda e, dn=dn, bi=bi, pp=pN[bi], s_=s_: e.tensor_tensor(
                            out=dn.t[:, 0, 2 * bi:2 * bi + 2], in0=pp.t[:, 128:258:129], in1=s_.t[:, 2, 2 * bi:2 * bi + 2],
                            op=ALU.mult), [pN[bi], s_], [dn])
                    K.v("dve", lambda e, dn=dn: e.tensor_scalar_mul(out=dn.t[:, 1, :], in0=dn.t[:, 0, :], scalar1=-1.0),
                        [dn], [dn])
                    K.v("dve", lambda e, dn=dn: e.tensor_tensor(out=dn.t[:, 0, :], in0=dn.t[:, 0, :], in1=dn.t[:, 1, :],
                                                                op=ALU.max), [dn], [dn])
                    K.v("dve", lambda e, dn=dn: e.tensor_scalar_max(out=dn.t[:, 0, :], in0=dn.t[:, 0, :], scalar1=1.0),
                        [dn], [dn])
                    K.v("dve", lambda e, dn=dn: e.reciprocal(out=dn.t[:, 1, :], in_=dn.t[:, 0, :]), [dn], [dn])
                    K.v("dve", lambda e, dn=dn, s_=s_: e.tensor_tensor(out=dn.t[:, 2, :], in0=dn.t[:, 1, :],
                                                                       in1=s_.t[:, 2, :], op=ALU.mult), [dn, s_], [dn])
                    for h in range(4):
                        pp = pN[h // 2]
                        c0 = (h % 2) * 129
                        hs_ap = hf.t[:, h * 128:(h + 1) * 128]
                        if direction == 0:
                            K.act(hf, hs_ap, [pp, dn], pp.t[:, c0:c0 + 128], AF.Copy, scale=dn.t[:, 2, h:h + 1])
                        else:
                            K.v("dve", lambda e, pp=pp, c0=c0, dn=dn, hs_ap=hs_ap, h=h: e.scalar_tensor_tensor(
                                out=hs_ap, in0=pp.t[:, c0:c0 + 128], scalar=dn.t[:, 2, h:h + 1], in1=hs_ap,
                                op0=ALU.mult, op1=ALU.add), [pp, dn, hf], [hf])
                vw = vwr.next()
                for h in range(4):
                    K.v("pool", lambda e, vw=vw, vm=vm, s_=s_, h=h: e.tensor_scalar_mul(
                        out=vw.t[:, h, :], in0=vm.t[:, h, :], scalar1=s_.t[:, 5, h:h + 1]), [vm, s_], [vw])
                pC = [K.pgen.next(), K.pgen.next()]
                for h in range(4):
                    pp = pC[h // 2]
                    c0 = (h % 2) * 129
                    K.mm(pp, pp.t[:, c0:c0 + 129], kk, kk.t[:, h, :], vw, vw.t[:, h, :])
                for h in range(4):
                    pp = pC[h // 2]
                    c0 = (h % 2) * 129
                    K.v("dve", lambda e, pp=pp, c0=c0, s_=s_, h=h, ST=ST: e.scalar_tensor_tensor(
                        out=ST.t[:, h, :], in0=ST.t[:, h, :], scalar=s_.t[:, 3, h:h + 1], in1=pp.t[:, c0:c0 + 129],
                        op0=ALU.mult, op1=ALU.add), [ST, s_, pp], [ST])
                K.act(STb, STb.t[:], [ST], ST.t[:], AF.Copy)
                if ctx_only:
                    return
                if direction == 0:
                    K.st(hf, HF[r0:r1, :], hf.t[:])
                else:
                    so = c['so']
                    m_ = ms.next()
                    K.v("dve", lambda e, hf=hf, sq=sq: e.tensor_tensor(out=sq.t[:], in0=hf.t[:], in1=hf.t[:], op=ALU.mult),
                        [hf], [sq])
                    K.v("dve", lambda e, m_=m_, sq=sq: e.tensor_reduce(
                        out=m_.t[:, 0, :], in_=sq.t[:, :].rearrange("p (h d) -> p h d", d=128), axis=AX.X, op=ALU.add),
                        [sq], [m_])
                    K.act(m_, m_.t[:, 2, :], [m_, epsb], m_.t[:, 0, :], AF.Ln, scale=1.0 / 128, bias=epsb.t[:, 0:1])
                    K.act(m_, m_.t[:, 0, :], [m_], m_.t[:, 2, :], AF.Exp, scale=-0.5)
                    K.v("dve", lambda e, m_=m_, hf=hf, sq=sq: e.tensor_tensor(
                        out=sq.t[:, :].rearrange("p (h d) -> p h d", d=128),
                        in0=hf.t[:, :].rearrange("p (h d) -> p h d", d=128),
                        in1=m_.t[:, 0, :].unsqueeze(2).to_broadcast([128, 4, 128]), op=ALU.mult), [hf, m_], [sq])
                    K.v("dve", lambda e, sq=sq, og=og: e.tensor_tensor(out=sq.t[:], in0=sq.t[:], in1=og.t[:], op=ALU.mult),
                        [sq, og], [sq])
                    ym = ymr.next()
                    K.v("dve", lambda e, ym=ym, so=so, sq=sq: e.tensor_tensor(out=ym.t[:], in0=sq.t[:], in1=so.t[:],
                                                                              op=ALU.mult), [sq, so], [ym])
                    K.st(ym, YY[r0:r1, 512:1024], ym.t[:])

            cur = stage1(*order[0])
            for i in range(len(order)):
                nxt = stage1(*order[i + 1]) if i + 1 < len(order) else None
                stage2(cur)
                cur = nxt
            K.pgen = Ring(K.banks[0:6])

        def tail_a(x, pr, t, nt, final_g=None):
            for half in range(2):
                K.v("dve", lambda e, half=half: e.tensor_tensor(
                    out=x.t[:, half * 512:(half + 1) * 512], in0=x.t[:, half * 512:(half + 1) * 512],
                    in1=pr[half].t[:, :], op=ALU.add), [x, pr[half]], [x])
            if final_g is None:
                K.st(x, X1[t * 128:(t + 1) * 128, :], x.t[:])
                return nt.run_a(x, x.t[:])
            ss = nt.stats(x, x.t[:], 1024)
            K.v("dve", lambda e, ss=ss: e.scalar_tensor_tensor(out=x.t[:], in0=x.t[:], scalar=ss.t[:, 3:4],
                                                               in1=final_g.t[:], op0=ALU.mult, op1=ALU.mult),
                [x, ss, final_g], [x])
            if out_lo <= t < out_hi:
                K.st(x, y_out[(t - out_lo) * 128:(t - out_lo + 1) * 128, :], x.t[:])
            return None

        def tail(x, pr, t, nt, XTd, final_g=None):
            for half in range(2):
                K.v("dve", lambda e, half=half: e.tensor_tensor(
                    out=x.t[:, half * 512:(half + 1) * 512], in0=x.t[:, half * 512:(half + 1) * 512],
                    in1=pr[half].t[:, :], op=ALU.add), [x, pr[half]], [x])
            if final_g is None:
                K.st(x, X1[t * 128:(t + 1) * 128, :], x.t[:])
                nt.run(x, x.t[:], t, XTd)
            else:
                ss = nt.stats(x, x.t[:], 1024)
                K.v("dve", lambda e, ss=ss: e.scalar_tensor_tensor(out=x.t[:], in0=x.t[:], scalar=ss.t[:, 3:4],
                                                                   in1=final_g.t[:], op0=ALU.mult, op1=ALU.mult),
                    [x, ss, final_g], [x])
                if out_lo <= t < out_hi:
                    K.st(x, y_out[(t - out_lo) * 128:(t - out_lo + 1) * 128, :], x.t[:])

        def project_tm(src_tile, src_ap_k, wt, w_ap_k, nk):
            pr = [K.pgen.next(), K.pgen.next()]
            for half in range(2):
                for k in range(nk):
                    K.mm(pr[half], pr[half].t[:, :], src_tile, src_ap_k(k), wt, w_ap_k(k, half), start=(k == 0),
                         stop=(k == nk - 1))
            return pr

        K.begin_sweep()
        nt = NormT()
        nt.load_gain(W['ffn_norm'][0])
        w_o = K.tile([128, 8, 1024], BF16, dma=True)
        K.ld(w_o, w_o.t[:], wb_out.rearrange("(k p) n -> p k n", p=128))
        xr = K.ring(3, [128, 1024], F32, dma=True)
        yr = K.ring(3, [128, 1024], BF16, dma=True)
        yTr = K.ring(3, [128, 8, 128], BF16)
        K.pgen = Ring(K.banks[0:8])

        def f0(t, c):
            y = yr.next()
            K.ld(y, y.t[:], YY[t * 128:(t + 1) * 128, :])
            pb = K.pgen.next()
            pbv = pb.t[:].bitcast(BF16).rearrange("p (c t) -> p c t", c=8)
            for cc in range(8):
                K.tr(pb, pbv[:, cc, :], y, y.t[:, cc * 128:(cc + 1) * 128], ident, ident.t[:])
            yT = yTr.next()
            K.act(yT, yT.t[:], [pb], pbv, AF.Copy)
            c['pr'] = project_tm(yT, lambda k: yT.t[:, k, :], w_o,
                                 lambda k, half: w_o.t[:, k, half * 512:(half + 1) * 512], 8)

        def f1(t, c):
            x = xr.next()
            K.ld(x, x.t[:], x_in[t * 128:(t + 1) * 128, :])
            c['xn'] = tail_a(x, c['pr'], t, nt)

        def f2(t, c):
            nt.run_b(c['xn'], t, XT[1])

        skewed(NTo, [f0, f1, f2])
        K.pgen = Ring(K.banks[0:6])

        def ffn_sweep(l, XTs, XTd, gain_next, final):
            K.begin_sweep()
            wup = K.tile([128, 8, 5632], BF16, dma=True)
            for k in range(8):
                K.ld(wup, wup.t[:, k, :], wb_up[l, k * 128:(k + 1) * 128, :])
            wdn = K.tile([128, NFC, 1024], BF16, dma=True)
            K.ld(wdn, wdn.t[:], wb_dn[l].rearrange("(i p) n -> p i n", p=128))
            nt = NormT()
            fg = None
            if final:
                fg = nt.g
            nt.load_gain(gain_next)
            winr = K.ring(2, [128, 8, 514], BF16, dma=True)
            hT = K.tile([128, NFC, 512], BF16)
            cgr = K.ring(2, [128, 512], F32)
            cvr = K.ring(2, [128, 512], F32)
            sgr = K.ring(2, [128, 512], BF16)
            xr = K.ring(2, [128, 1024], F32, dma=True)
            K.pgen = Ring(K.banks[0:6])
            cw = cw_f[l]
            hs_i = 0
            nfw_f = seam_setup(cw, 44, 2) if prompt else None
            pending = []
            for b in range(NBo):
                win = winr.next()
                K.ld(win, win.t[:], XTs[:, :, 512 * b:512 * b + 514].rearrange("c p t -> p c t"))
                for i in range(NFC):
                    outs = []
                    for (col0, ci, rr) in ((i * 128, i, cgr), (DFF + i * 128, NFC + i, cvr)):
                        pm = K.pgen.next()
                        hsl = 0
                        halo = K.banks[6 + (hs_i % 2)]
                        hs_i += 1
                        for k in range(8):
                            K.mm(pm, pm.t[:, :], wup, wup.t[:, k, col0:col0 + 128], win, win.t[:, k, 1:513],
                                 start=(k == 0), stop=(k == 7))
                        for k in range(8):
                            K.mm(halo, halo.t[:, hsl:hsl + 2], wup, wup.t[:, k, col0:col0 + 128], win,
                                 win.t[:, k, 0:514:513], start=(k == 0), stop=(k == 7))
                        cc = rr.next()
                        conv_chunk(pm, halo, hsl, cw, ci, cc)
                        if prompt and b in (0, 4):
                            seam_fix(pm, cc, nfw_f, b // 4, ci)
                        outs.append(cc)
                    cg, cv = outs
                    sg = sgr.next()
                    K.act(sg, sg.t[:], [cg], cg.t[:], AF.Silu)
                    K.v("pool", lambda e, sg=sg, cv=cv, i=i: e.tensor_tensor(out=hT.t[:, i, :], in0=sg.t[:], in1=cv.t[:],
                                                                             op=ALU.mult), [sg, cv], [hT])
                    if pending and i >= 1:
                        pending.pop(0)()
                while pending:
                    pending.pop(0)()
                prs = {}
                xns = {}

                def d0(tt, b=b, prs=prs):
                    prs[tt] = project_tm(hT, lambda k: hT.t[:, k, tt * 128:(tt + 1) * 128], wdn,
                                         lambda k, half: wdn.t[:, k, half * 512:(half + 1) * 512], NFC)

                def d1(tt, b=b, prs=prs, xns=xns):
                    t = b * 4 + tt
                    x = xr.next()
                    K.ld(x, x.t[:], X1[t * 128:(t + 1) * 128, :])
                    xns[tt] = tail_a(x, prs[tt], t, nt, final_g=fg)

                def d2(tt, b=b, xns=xns):
                    if xns[tt] is not None:
                        nt.run_b(xns[tt], b * 4 + tt, XTd)

                for fn_, a_ in ((d0, 0), (d0, 1), (d1, 0), (d0, 2), (d1, 1), (d2, 0), (d0, 3), (d1, 2), (d2, 1), (d1, 3)):
                    fn_(a_)
                pending.extend([lambda d2=d2: d2(2), lambda d2=d2: d2(3)])
            while pending:
                pending.pop(0)()
            K.pgen = Ring(K.banks[0:6])

        ffn_sweep(0, XT[1], XT[0], W['c_norm'][0], False)

        K.begin_sweep()
        nt = NormT()
        nt.load_gain(W['ffn_norm'][1])
        cwin = K.tile([128, 8, 2048], BF16, dma=True)
        K.ld(cwin, cwin.t[:], wb_cin.rearrange("(k p) n -> p k n", p=128))
        cwo = K.tile([128, 8, 1024], BF16, dma=True)
        K.ld(cwo, cwo.t[:], wb_cout.rearrange("(k p) n -> p k n", p=128))
        vg = K.tile([128, 1024], F32, dma=True)
        bc_load(vg, W['c_v_norm'][0], 1024)
        xTl = K.ring(3, [128, 8, 128], BF16, dma=True)
        xr = K.ring(2, [128, 1024], F32, dma=True)
        ur = K.ring(3, [128, 2048], F32)
        vnr = K.ring(2, [128, 1024], BF16)
        zr = K.ring(2, [128, 1024], BF16)
        zTr = K.ring(2, [128, 8, 128], BF16)
        vjunk = K.tile([128, 1024], BF16)
        K.pgen = Ring(K.banks[0:8])

        def h0(t, c):
            xT = xTl.next()
            K.ld(xT, xT.t[:], XT[0][:, :, 1 + t * 128:1 + (t + 1) * 128].rearrange("c p t -> p c t"))
            uv = ur.next()
            for j in range(4):
                pu = K.pgen.next()
                for k in range(8):
                    K.mm(pu, pu.t[:, :], xT, xT.t[:, k, :], cwin, cwin.t[:, k, j * 512:(j + 1) * 512], start=(k == 0),
                         stop=(k == 7))
                K.act(uv, uv.t[:, j * 512:(j + 1) * 512], [pu], pu.t[:, :], AF.Gelu)
            c['uv'] = uv

        def h1(t, c):
            uv = c['uv']
            ss = nt.stats(uv, uv.t[:, 1024:2048], 1024, junk=vjunk)
            vn = vnr.next()
            K.v("dve", lambda e, ss=ss, uv=uv, vn=vn: e.scalar_tensor_tensor(
                out=vn.t[:], in0=uv.t[:, 1024:2048], scalar=ss.t[:, 3:4], in1=vg.t[:], op0=ALU.mult, op1=ALU.mult),
                [uv, ss, vg], [vn])
            z = zr.next()
            psv = [K.pgen.next(), K.pgen.next()]
            for g in range(8):
                pp = psv[g // 4]
                K.mm(pp, pp.t[:, (g % 4) * 128:(g % 4 + 1) * 128], wsT, wsT.t[:, g, :], vn, vn.t[:, g * 128:(g + 1) * 128])
            for g in range(8):
                pp = psv[g // 4]
                K.v("dve", lambda e, pp=pp, g=g, uv=uv, z=z: e.scalar_tensor_tensor(
                    out=z.t[:, g * 128:(g + 1) * 128], in0=pp.t[:, (g % 4) * 128:(g % 4 + 1) * 128],
                    scalar=bsT.t[:, g:g + 1], in1=uv.t[:, g * 128:(g + 1) * 128], op0=ALU.add, op1=ALU.mult),
                    [pp, bsT, uv], [z])
            pb = K.pgen.next()
            pbv = pb.t[:].bitcast(BF16).rearrange("p (c t) -> p c t", c=8)
            for cc in range(8):
                K.tr(pb, pbv[:, cc, :], z, z.t[:, cc * 128:(cc + 1) * 128], ident, ident.t[:])
            zT = zTr.next()
            K.act(zT, zT.t[:], [pb], pbv, AF.Copy)
            c['pr'] = project_tm(zT, lambda k: zT.t[:, k, :], cwo,
                                 lambda k, half: cwo.t[:, k, half * 512:(half + 1) * 512], 8)

        def h2(t, c):
            x = xr.next()
            K.ld(x, x.t[:], X1[t * 128:(t + 1) * 128, :])
            c['xn'] = tail_a(x, c['pr'], t, nt)

        def h3(t, c):
            nt.run_b(c['xn'], t, XT[1])

        skewed(NTo, [h0, h1, h2, h3], reverse=True)
        K.pgen = Ring(K.banks[0:6])

        ffn_sweep(1, XT[1], XT[0], W['final_norm'], True)

    P.emit()
    nc._dbg_names = dbg_names
    return nc


BIG = 30000.0
OWN_EXT = 2560
HALO = 256


def make_consts():
    return {
        "c_ident": np.eye(128, dtype=np.float32),
        "c_triu": np.triu(np.ones((128, 128), np.float32)),
        "c_tril": np.tril(np.ones((128, 128), np.float32)),
        "c_ones": np.ones((128, 128), np.float32),
    }


def rope_table(pos):
    inv = (1.0 / (np.float32(10000.0) ** (np.arange(0, 32, 2, dtype=np.float32) / np.float32(32)))).astype(np.float32)
    ang = pos.astype(np.float32)[:, None] * inv[None, :]
    return np.concatenate([np.cos(ang), np.sin(ang)], axis=1).astype(np.float32)


def mask_table(mf, mb):
    mf = mf.astype(np.float32)
    mb = mb.astype(np.float32)
    return np.stack([mf, (mf - 1) * BIG, mb, (mb - 1) * BIG], axis=1).astype(np.float32)


def aux_full(S):
    ones = np.ones(S, np.float32)
    return rope_table(np.arange(S)), mask_table(ones, ones)


def aux_quarter(S, j):
    q = S // 4
    shift = j * q - HALO
    r = np.arange(S)
    t = r + shift
    pos = t % S
    own = r < OWN_EXT
    valid = (t >= 0) & (t < S)
    t_start = max(shift, 0)
    t_end = min(shift + OWN_EXT, S)
    mf = np.where(own, valid, pos < t_start)
    mb = np.where(own, valid, pos >= t_end)
    sf = np.zeros((1, 4), np.float32)
    sf[0, (4 - j) % 4] = -1.0
    return shift, rope_table(pos), mask_table(mf, mb), sf


_CACHE = {}


def run(core_inputs, params, jobs):
    jobs = norm_jobs(jobs)
    key = repr(jobs)
    if key not in _CACHE:
        _CACHE[key] = build_program(jobs)
    nc = _CACHE[key]
    consts = make_consts()
    in_maps = []
    for ci in core_inputs:
        m = {k: np.ascontiguousarray(v, dtype=np.float32) for k, v in ci.items()}
        for j, jd in enumerate(jobs):
            if ("cs%d" % j) not in m:
                cs, mk = aux_full(jd['S'])
                m["cs%d" % j] = cs
                m["mk%d" % j] = mk
        if "sf" not in m:
            m["sf"] = np.zeros((1, 4), np.float32)
        for n in PARAMS:
            m[n] = np.ascontiguousarray(params[n], dtype=np.float32)
        m.update(consts)
        in_maps.append(m)
    res = run_bass_kernel_spmd(nc, in_maps, core_ids=list(range(len(core_inputs))))
    global LAST_RES
    LAST_RES = res.results
    return [[r["y%d" % j] for j in range(len(jobs))] for r in res.results]


def kernel(**inputs):
    xp = np.asarray(inputs['x_prompt'], dtype=np.float32)
    xs = np.asarray(inputs['x_sample'], dtype=np.float32)
    params = {n: np.asarray(inputs[n], dtype=np.float32) for n in PARAMS}
    SP = xp.shape[1]
    jobs = [4096, 4096, dict(S=SP, NTo=OWN_EXT // 128, out=(HALO // 128, (OWN_EXT - HALO) // 128), prompt=True)]
    core_inputs = []
    for c in range(8):
        p, j = c // 4, c % 4
        shift, cs, mk, sf = aux_quarter(SP, j)
        core_inputs.append({"x0": xs[2 * c], "x1": xs[2 * c + 1], "x2": np.roll(xp[p], -shift, axis=0),
                            "cs2": cs, "mk2": mk, "sf": sf})
    outs = run(core_inputs, params, jobs)
    q = SP // 4
    y_prompt = np.zeros_like(xp)
    for c in range(8):
        p, j = c // 4, c % 4
        y_prompt[p, j * q:(j + 1) * q] = outs[c][2]
    y_sample = np.stack([outs[c][j] for c in range(8) for j in range(2)], axis=0)
    return (y_prompt, y_sample)
```

```python
import math
import numpy as np
import concourse.bass as bass
import concourse.mybir as mybir
from concourse.bass_utils import run_bass_kernel_spmd

F32 = mybir.dt.float32
BF16 = mybir.dt.bfloat16
ALU = mybir.AluOpType
AF = mybir.ActivationFunctionType
AX = mybir.AxisListType

D = 1024
EPS = 1e-6
DFF = 2816
NFC = 22
SC_ATT = 96 ** -0.5
SC_M = 128 ** -0.5
LNSC_M = math.log(SC_M)

PARAMS = ['ab_norm', 'ab_w_in', 'mla_q_norm', 'mla_w_uq', 'mla_kv_norm', 'mla_w_ukv', 'mlstm_conv_w',
          'mlstm_conv_b', 'mlstm_gate_bias', 'mlstm_out_norm', 'ab_w_out', 'c_norm', 'c_w_in', 'c_v_norm',
          'c_w_spatial', 'c_b_spatial', 'c_w_out', 'ffn_norm', 'ffn_w_up', 'ffn_conv_w', 'ffn_conv_b',
          'ffn_w_down', 'final_norm']
PSHAPES = {
    'ab_norm': [1, 1024], 'ab_w_in': [1, 1024, 2480], 'mla_q_norm': [1, 256], 'mla_w_uq': [1, 256, 768],
    'mla_kv_norm': [1, 128], 'mla_w_ukv': [1, 128, 1024], 'mlstm_conv_w': [1, 3, 1024], 'mlstm_conv_b': [1, 1024],
    'mlstm_gate_bias': [1, 16], 'mlstm_out_norm': [1, 512], 'ab_w_out': [1, 1024, 1024], 'c_norm': [1, 1024],
    'c_w_in': [1, 1024, 2048], 'c_v_norm': [1, 1024], 'c_w_spatial': [1, 8, 128, 128], 'c_b_spatial': [1, 8, 128],
    'c_w_out': [1, 1024, 1024], 'ffn_norm': [2, 1024], 'ffn_w_up': [2, 1024, 5632], 'ffn_conv_w': [2, 3, 5632],
    'ffn_conv_b': [2, 5632], 'ffn_w_down': [2, 2816, 1024], 'final_norm': [1024],
}


class Buf:
    __slots__ = ("name", "w", "r", "excl")

    def __init__(self, name, excl=False):
        self.name = name
        self.w = None
        self.r = []
        self.excl = excl


class Prog:
    ENGS = ("pe", "act", "dve", "pool", "sp")

    def __init__(self, nc, n_dma_sems=80):
        self.nc = nc
        self.ops = {e: [] for e in self.ENGS}
        self.count = {}
        self.seen = {e: {} for e in self.ENGS}
        self.pending = {e: [] for e in self.ENGS}
        self.dma_sems = ["d%d" % i for i in range(n_dma_sems)]
        self.nops = 0
        self.disabled = False
        self.sweep_no = 0
        import os
        self.maxops = int(os.environ.get("K_OPS", "1000000000"))

    def op(self, eng, fn, reads=(), writes=(), dsem=None):
        if self.disabled or self.nops >= self.maxops:
            return None
        deps = []
        for b in reads:
            if b.w is not None:
                deps.append(b.w)
            if b.excl:
                for t in b.r:
                    if t[0] != eng:
                        deps.append(t)
        for b in writes:
            if b.w is not None:
                deps.append(b.w)
            deps.extend(b.r)
        if dsem is None:
            key, inc = eng, 1
        else:
            key, inc = dsem, 16
        self.count[key] = self.count.get(key, 0) + inc
        tok = (key, self.count[key])
        seen = self.seen[eng]
        waits = []
        if self.pending[eng]:
            for k, v in self.pending[eng]:
                if seen.get(k, 0) < v:
                    seen[k] = v
                    waits.append((k, v))
            self.pending[eng] = []
        for k, v in deps:
            if k == "pe" and eng == "pe" and dsem is None:
                continue
            if seen.get(k, 0) >= v:
                continue
            seen[k] = v
            waits.append((k, v))
        self.ops[eng].append((fn, waits, key, inc))
        for b in reads:
            b.r.append(tok)
        for b in writes:
            b.w = tok
            b.r = []
        self.nops += 1
        return tok

    def barrier(self, skip=()):
        snap = [(k, v) for k, v in self.count.items() if k not in skip]
        for e in self.ENGS:
            self.pending[e] = list(snap)

    def emit(self):
        from contextlib import ExitStack
        nc = self.nc
        self.barrier()
        with ExitStack() as st:
            sems = {}
            for k in list(self.ENGS) + self.dma_sems:
                sems[k] = st.enter_context(nc.semaphore("s_" + k))
            block = st.enter_context(nc.Block())

            def mk(e):
                def section(eng):
                    for fn, waits, key, inc in self.ops[e]:
                        for k, v in waits:
                            eng.wait_ge(sems[k], v)
                        ins = fn(eng)
                        ins.then_inc(sems[key], inc)
                    for k, v in self.pending[e]:
                        if self.seen[e].get(k, 0) < v:
                            eng.wait_ge(sems[k], v)
                return section

            block.tensor(mk("pe"))
            block.scalar(mk("act"))
            block.vector(mk("dve"))
            block.gpsimd(mk("pool"))
            block.sync(mk("sp"))


class TL:
    __slots__ = ("t", "b", "sem")

    def __init__(self, t, name, sem=None):
        self.t = t
        self.b = Buf(name)
        self.sem = sem


class Ring:
    def __init__(self, tiles):
        self.tiles = tiles
        self.i = 0

    def next(self):
        t = self.tiles[self.i % len(self.tiles)]
        self.i += 1
        return t


def dsize(dt):
    return 4 if dt == F32 else 2


class Ctx:
    def __init__(self, nc, P):
        self.nc = nc
        self.P = P
        self.uid = 0
        self.pers_off = 16512
        self.off = 16512
        self.semi = 0
        self.banks = [TL(nc.alloc_psum_tensor("pb%d" % i, [128, 512], F32), "pb%d" % i) for i in range(8)]
        for b in self.banks:
            b.b.excl = True
        self.pgen = Ring(self.banks[0:6])
        self.pacc = Ring(self.banks[6:8])

    def _alloc(self, shape, dt, dma, off):
        self.uid += 1
        name = "t%d" % self.uid
        t = self.nc.alloc_sbuf_tensor_at(name, list(shape), dt, offset=off)
        sem = None
        if dma:
            sem = self.P.dma_sems[self.semi % (len(self.P.dma_sems) - 1)]
            self.semi += 1
        return TL(t, name, sem)

    @staticmethod
    def _bytes(shape, dt):
        n = 1
        for s in shape[1:]:
            n *= s
        return (n * dsize(dt) + 63) // 64 * 64

    def pers(self, shape, dt, dma=False):
        tl = self._alloc(shape, dt, dma, self.pers_off)
        self.pers_off += self._bytes(shape, dt)
        return tl

    def begin_sweep(self):
        import os
        self.P.sweep_no += 1
        lim = int(os.environ.get("K_LIMIT", "100000"))
        self.P.disabled = self.P.sweep_no > lim
        if os.environ.get("K_VERBOSE"):
            print("sweep", self.P.sweep_no, "ops so far", self.P.nops, flush=True)
        self.P.barrier(skip=(self.P.dma_sems[-1],) if self.P.sweep_no == 2 else ())
        self.off = self.pers_off

    def tile(self, shape, dt, dma=False):
        tl = self._alloc(shape, dt, dma, self.off)
        self.off += self._bytes(shape, dt)
        assert self.off <= 229376, ("SBUF overflow", self.off)
        return tl

    def ring(self, n, shape, dt, dma=False):
        return Ring([self.tile(shape, dt, dma) for _ in range(n)])

    def ld(self, dst, dst_ap, src_ap, eng="sp", slow=False, reads=()):
        if slow:
            fn = lambda e: e.dma_start(out=dst_ap, in_=src_ap, allow_slow_non_contiguous=True)
        else:
            fn = lambda e: e.dma_start(out=dst_ap, in_=src_ap)
        self.P.op(eng, fn, reads=list(reads), writes=[dst.b], dsem=dst.sem)

    def st(self, src, dst_ap, src_ap, eng="pool", slow=False):
        if slow:
            fn = lambda e: e.dma_start(out=dst_ap, in_=src_ap, allow_slow_non_contiguous=True)
        else:
            fn = lambda e: e.dma_start(out=dst_ap, in_=src_ap)
        self.P.op(eng, fn, reads=[src.b], dsem=src.sem)

    def mm(self, out, out_ap, lt, lhsT, rt, rhs, start=True, stop=True, skip=False):
        if skip:
            fn = lambda e: e.matmul(out_ap, lhsT=lhsT, rhs=rhs, start=start, stop=stop, skip_group_check=True)
        else:
            fn = lambda e: e.matmul(out_ap, lhsT=lhsT, rhs=rhs, start=start, stop=stop)
        self.P.op("pe", fn, reads=[lt.b, rt.b], writes=[out.b])

    def tr(self, out, out_ap, it, in_ap, idt, id_ap):
        self.P.op("pe", lambda e: e.transpose(out=out_ap, in_=in_ap, identity=id_ap),
                  reads=[it.b, idt.b], writes=[out.b])

    def act(self, out, out_ap, ins, in_ap, func, scale=None, bias=None, accum=None, extra_w=()):
        kw = {}
        if scale is not None:
            kw["scale"] = scale
        if bias is not None:
            kw["bias"] = bias
        if accum is not None:
            kw["accum_out"] = accum
        self.P.op("act", lambda e: e.activation(out=out_ap, in_=in_ap, func=func, **kw),
                  reads=[t.b for t in ins], writes=[out.b] + [t.b for t in extra_w])

    def v(self, eng, fn, ins, outs):
        self.P.op(eng, fn, reads=[t.b for t in ins], writes=[t.b for t in outs])


def norm_jobs(jobs):
    return [dict(S=j, NTo=j // 128, out=(0, j // 128), prompt=False) if isinstance(j, int) else j for j in jobs]


def build_program(jobs):
    jobs = norm_jobs(jobs)
    S_list = [j['S'] for j in jobs]
    nc = bass.Bass("TRN2", target_bir_lowering=False)
    P = Prog(nc)
    K = Ctx(nc, P)
    SM = max(S_list)
    NJ = len(S_list)

    import os
    DEBUG = bool(os.environ.get("K_DEBUG"))
    dbg_names = []

    def dram(name, shape, dt, kind="Internal"):
        if DEBUG and kind == "Internal" and (not name.startswith("wb_") or os.environ.get("K_DEBUGW")):
            kind = "ExternalOutput"
            dbg_names.append(name)
        return nc.dram_tensor(name, list(shape), dt, kind=kind).ap()

    xin = [dram("x%d" % j, [S_list[j], D], F32, "ExternalInput") for j in range(NJ)]
    yout = [dram("y%d" % j, [(jobs[j]['out'][1] - jobs[j]['out'][0]) * 128, D], F32, "ExternalOutput") for j in range(NJ)]
    csin = [dram("cs%d" % j, [S_list[j], 32], F32, "ExternalInput") for j in range(NJ)]
    mkin = [dram("mk%d" % j, [S_list[j], 4], F32, "ExternalInput") for j in range(NJ)]
    sfin = dram("sf", [1, 4], F32, "ExternalInput")
    W = {n: dram(n, PSHAPES[n], F32, "ExternalInput") for n in PARAMS}
    c_ident = dram("c_ident", [128, 128], F32, "ExternalInput")
    c_triu = dram("c_triu", [128, 128], F32, "ExternalInput")
    c_tril = dram("c_tril", [128, 128], F32, "ExternalInput")
    c_ones = dram("c_ones", [128, 128], F32, "ExternalInput")

    wb_in = dram("wb_in", [1024, 2480], BF16)
    wb_uq = dram("wb_uq", [256, 768], BF16)
    wb_ukv = dram("wb_ukv", [128, 1024], BF16)
    wb_out = dram("wb_out", [1024, 1024], BF16)
    wb_cin = dram("wb_cin", [1024, 2048], BF16)
    wb_cout = dram("wb_cout", [1024, 1024], BF16)
    wb_up = dram("wb_up", [2, 1024, 5632], BF16)
    wb_dn = dram("wb_dn", [2, 2816, 1024], BF16)
    XT = [dram("XT%d" % i, [8, 128, SM + 2], BF16) for i in range(2)]
    X1 = dram("X1", [SM, D], F32)
    QT = dram("QT", [8, 96, SM], BF16)
    KT = dram("KT", [8, 96, SM], BF16)
    VV = dram("VV", [SM, 8, 65], BF16)
    QKT = dram("QKT", [8, 128, SM], BF16)
    VM = dram("VM", [SM, 4, 129], BF16)
    SO = dram("SO", [SM, 512], BF16)
    GG = dram("GG", [SM, 16], F32)
    HF = dram("HF", [SM, 512], F32)
    YY = dram("YY", [SM, D], BF16)
    if DEBUG:
        DBG_dm = dram("DBG_dm", [128, 128], F32)
        DBG_pt = dram("DBG_pt", [128, 128], BF16)
        DBG_s = dram("DBG_s", [128, 24], F32)
        DBG_tot = dram("DBG_tot", [128, 129], F32)
        DBG_kk = dram("DBG_kk", [128, 128], BF16)
        DBG_wsT = dram("DBG_wsT", [128, 1024], BF16)
        DBG_bsT = dram("DBG_bsT", [128, 8], F32)
        DBG_uv = dram("DBG_uv", [128, 2048], F32)
        DBG_vn = dram("DBG_vn", [128, 1024], BF16)
        DBG_z = dram("DBG_z", [128, 1024], BF16)

    wsem = P.dma_sems[-1]

    def cast(dst, src, rows):
        for r0 in range(0, rows, 128):
            r1 = min(rows, r0 + 128)
            P.op("pool", lambda e, r0=r0, r1=r1: e.dma_start(out=dst[r0:r1, :], in_=src[r0:r1, :]),
                 writes=[], dsem=wsem)

    cast(wb_in, W['ab_w_in'][0], 1024)
    cast(wb_uq, W['mla_w_uq'][0], 256)
    cast(wb_ukv, W['mla_w_ukv'][0], 128)
    cast(wb_out, W['ab_w_out'][0], 1024)
    cast(wb_cin, W['c_w_in'][0], 1024)
    cast(wb_cout, W['c_w_out'][0], 1024)
    for l in range(2):
        cast(wb_up[l], W['ffn_w_up'][l], 1024)
        cast(wb_dn[l], W['ffn_w_down'][l], 2816)

    identf = K.pers([128, 128], F32, dma=True)
    ident = K.pers([128, 128], BF16)
    triu = K.pers([128, 128], F32, dma=True)
    tril = K.pers([128, 128], F32, dma=True)
    ones = K.pers([128, 128], F32, dma=True)
    cw_m = K.pers([128, 4, 8], F32)
    cw_f = [K.pers([128, 4, 44], F32) for _ in range(2)]
    bsT = K.pers([128, 8], F32)
    wsT = K.pers([128, 8, 128], BF16)
    zero = K.pers([128, 16], BF16, dma=True)
    epsb = K.pers([128, 1], F32)
    K.begin_sweep()
    K.v("dve", lambda e: e.memset(epsb.t[:], EPS), [], [epsb])
    K.ld(identf, identf.t[:], c_ident)
    K.ld(triu, triu.t[:], c_triu)
    K.ld(tril, tril.t[:], c_tril)
    K.ld(ones, ones.t[:], c_ones)
    K.v("dve", lambda e: e.tensor_copy(out=ident.t[:], in_=identf.t[:]), [identf], [ident])
    K.v("dve", lambda e: e.memset(zero.t[:], 0.0), [], [zero])

    def load_T(dst, dst_view, raws, nrow):
        raw = K.tile([nrow, len(raws), 128], F32, dma=True)
        for j, src in enumerate(raws):
            K.ld(raw, raw.t[:, j, :], src)
        for j in range(len(raws)):
            pb = K.pgen.next()
            K.tr(pb, pb.t[:, 0:nrow], raw, raw.t[:, j, :], identf, identf.t[0:nrow, 0:nrow])
            K.v("dve", lambda e, j=j, pb=pb: e.tensor_copy(out=dst_view(j), in_=pb.t[:, 0:nrow]), [pb], [dst])

    mcw = W['mlstm_conv_w'][0]
    load_T(cw_m, lambda j: cw_m.t[:, j, :],
           [mcw[j].rearrange("(c p) -> c p", p=128) for j in range(3)] +
           [W['mlstm_conv_b'][0].rearrange("(c p) -> c p", p=128)], 8)
    for l in range(2):
        load_T(cw_f[l], lambda j, l=l: cw_f[l].t[:, j, :],
               [W['ffn_conv_w'][l, j].rearrange("(c p) -> c p", p=128) for j in range(3)] +
               [W['ffn_conv_b'][l].rearrange("(c p) -> c p", p=128)], 44)
    load_T(bsT, lambda j: bsT.t[:, :], [W['c_b_spatial'][0]], 8)
    wsr = K.tile([128, 8, 128], F32, dma=True)
    wsrb = K.tile([128, 8, 128], BF16)
    K.ld(wsr, wsr.t[:], W['c_w_spatial'][0].rearrange("g t s -> t g s"))
    K.v("dve", lambda e: e.tensor_copy(out=wsrb.t[:], in_=wsr.t[:]), [wsr], [wsrb])
    pb = K.pgen.next()
    pbv = pb.t[:].bitcast(BF16).rearrange("p (c t) -> p c t", c=8)
    for g in range(8):
        K.tr(pb, pbv[:, g, :], wsrb, wsrb.t[:, g, :], ident, ident.t[:])
    K.v("dve", lambda e, pbv=pbv: e.tensor_copy(out=wsT.t[:], in_=pbv), [pb], [wsT])

    def bc_load(dst, src1d, n):
        K.ld(dst, dst.t[:, 0:n], src1d.partition_broadcast(128))

    class NormT:
        def __init__(self):
            self.ss = K.ring(3, [128, 4], F32)
            self.xn = K.ring(2, [128, 1024], BF16)
            self.xnT = K.ring(2, [128, 8, 128], BF16, dma=True)
            self.g = K.tile([128, 1024], F32, dma=True)

        def load_gain(self, src1d):
            bc_load(self.g, src1d, 1024)

        def stats(self, x, x_ap, n, junk=None):
            ss = self.ss.next()
            if junk is None:
                junk = self.xn.next()
            K.v("dve", lambda e: e.memset(ss.t[:, 0:1], 0.0), [], [ss])
            K.act(junk, junk.t[:, 0:n], [x, ss], x_ap, AF.Square, accum=ss.t[:, 0:1], extra_w=[ss])
            K.act(ss, ss.t[:, 2:3], [ss, epsb], ss.t[:, 0:1], AF.Ln, scale=1.0 / n, bias=epsb.t[:, 0:1])
            K.act(ss, ss.t[:, 3:4], [ss], ss.t[:, 2:3], AF.Exp, scale=-0.5)
            return ss

        def run(self, x, x_ap, t, XTd):
            self.run_b(self.run_a(x, x_ap), t, XTd)

        def run_a(self, x, x_ap):
            xn = self.xn.next()
            ss = self.stats(x, x_ap, 1024, junk=xn)
            g = self.g
            K.v("dve", lambda e: e.scalar_tensor_tensor(out=xn.t[:], in0=x_ap, scalar=ss.t[:, 3:4], in1=g.t[:],
                                                        op0=ALU.mult, op1=ALU.mult), [x, ss, g], [xn])
            return xn

        def run_b(self, xn, t, XTd):
            pb = K.pgen.next()
            pbv = pb.t[:].bitcast(BF16).rearrange("p (c t) -> p c t", c=8)
            for c in range(8):
                K.tr(pb, pbv[:, c, :], xn, xn.t[:, c * 128:(c + 1) * 128], ident, ident.t[:])
            xnT = self.xnT.next()
            K.act(xnT, xnT.t[:], [pb], pbv, AF.Copy)
            K.st(xnT, XTd[:, :, 1 + t * 128:1 + (t + 1) * 128].rearrange("c p t -> p c t"), xnT.t[:])

    def conv_chunk(pm, ph, hsl, cw, ci, dst):
        w0 = cw.t[:, 0, ci:ci + 1]
        w1 = cw.t[:, 1, ci:ci + 1]
        w2 = cw.t[:, 2, ci:ci + 1]
        bb = cw.t[:, 3, ci:ci + 1]
        K.act(dst, dst.t[:], [pm, cw], pm.t[:], AF.Identity, scale=w1, bias=bb)
        K.act(dst, dst.t[:, 0:1], [ph, cw, dst], ph.t[:, hsl:hsl + 1], AF.Identity, scale=w0, bias=dst.t[:, 0:1])
        K.act(dst, dst.t[:, 511:512], [ph, cw, dst], ph.t[:, hsl + 1:hsl + 2], AF.Identity, scale=w2,
              bias=dst.t[:, 511:512])
        K.v("dve", lambda e: e.scalar_tensor_tensor(out=dst.t[:, 1:512], in0=pm.t[:, 0:511], scalar=w0,
                                                    in1=dst.t[:, 1:512], op0=ALU.mult, op1=ALU.add),
            [pm, cw, dst], [dst])
        K.v("dve", lambda e: e.scalar_tensor_tensor(out=dst.t[:, 0:511], in0=pm.t[:, 1:512], scalar=w2,
                                                    in1=dst.t[:, 0:511], op0=ALU.mult, op1=ALU.add),
            [pm, cw, dst], [dst])

    def skewed(n, stages, reverse=False):
        ctxs = {}
        ns = len(stages)
        for step in range(n + ns - 1):
            for k in (range(ns - 1, -1, -1) if reverse else range(ns)):
                i = step - k
                if 0 <= i < n:
                    if k == 0:
                        ctxs[i] = {}
                    stages[k](i, ctxs[i])
                    if k == ns - 1:
                        del ctxs[i]

    for jb in range(NJ):
        S = S_list[jb]
        NT = S // 128
        NB = S // 512
        x_in = xin[jb]
        y_out = yout[jb]
        NTo = jobs[jb]['NTo']
        NBo = NTo // 4
        out_lo, out_hi = jobs[jb]['out']
        prompt = jobs[jb]['prompt']
        SEAM_BLOCKS = (0, 4, 8, 12) if prompt else ()

        def seam_setup(cw, nch, ncand):
            sfb = K.tile([128, 4], F32, dma=True)
            bc_load(sfb, sfin[0], 4)
            nfw = K.tile([128, ncand, 2, nch], F32)
            for c in range(ncand):
                for jj, wi in ((0, 0), (1, 2)):
                    K.v("dve", lambda e, c=c, jj=jj, wi=wi, nfw=nfw, sfb=sfb, cw=cw: e.tensor_scalar_mul(
                        out=nfw.t[:, c, jj, :], in0=cw.t[:, wi, :], scalar1=sfb.t[:, c:c + 1]), [cw, sfb], [nfw])
            return nfw

        def seam_fix(pm, dst, nfw, c, ci):
            K.v("dve", lambda e: e.scalar_tensor_tensor(out=dst.t[:, 256:257], in0=pm.t[:, 255:256],
                                                        scalar=nfw.t[:, c, 0, ci:ci + 1], in1=dst.t[:, 256:257],
                                                        op0=ALU.mult, op1=ALU.add), [pm, nfw, dst], [dst])
            K.v("dve", lambda e: e.scalar_tensor_tensor(out=dst.t[:, 255:256], in0=pm.t[:, 256:257],
                                                        scalar=nfw.t[:, c, 1, ci:ci + 1], in1=dst.t[:, 255:256],
                                                        op0=ALU.mult, op1=ALU.add), [pm, nfw, dst], [dst])

        K.begin_sweep()
        nt = NormT()
        nt.load_gain(W['ab_norm'][0])
        for i in range(2):
            K.st(zero, XT[i][:, :, 0:1].rearrange("c p t -> p c t"), zero.t[:, 0:8].rearrange("p (c t) -> p c t", t=1),
                 slow=True)
            K.st(zero, XT[i][:, :, S + 1:S + 2].rearrange("c p t -> p c t"),
                 zero.t[:, 0:8].rearrange("p (c t) -> p c t", t=1), slow=True)
        xr = K.ring(3, [128, 1024], F32, dma=True)

        def a0(t, c):
            x = xr.next()
            K.ld(x, x.t[:], x_in[t * 128:(t + 1) * 128, :])
            c['xn'] = nt.run_a(x, x.t[:])

        def a1(t, c):
            nt.run_b(c['xn'], t, XT[0])

        skewed(NT, [a0, a1])

        K.begin_sweep()
        w_in = K.tile([128, 8, 2480], BF16, dma=True)
        K.ld(w_in, w_in.t[:], wb_in.rearrange("(k p) n -> p k n", p=128))
        w_uq = K.tile([128, 2, 768], BF16, dma=True)
        K.ld(w_uq, w_uq.t[:], wb_uq.rearrange("(k p) n -> p k n", p=128))
        w_ukv = K.tile([128, 1024], BF16, dma=True)
        K.ld(w_ukv, w_ukv.t[:], wb_ukv)
        qg = K.tile([128, 256], F32, dma=True)
        bc_load(qg, W['mla_q_norm'][0], 256)
        kvg = K.tile([128, 128], F32, dma=True)
        bc_load(kvg, W['mla_kv_norm'][0], 128)
        gbias = K.tile([128, 16], F32, dma=True)
        bc_load(gbias, W['mlstm_gate_bias'][0], 16)
        winr = K.ring(2, [128, 8, 514], BF16, dma=True)
        cgr = K.ring(2, [128, 512], F32)
        qkr = K.ring(2, [128, 512], BF16, dma=True)
        junk = K.tile([128, 256], BF16)
        ssr = K.ring(2, [128, 8], F32)
        cqn = K.ring(2, [128, 384], BF16)
        cTr = K.ring(2, [128, 3, 128], BF16)
        cosr = K.ring(2, [128, 32], F32, dma=True)
        krr = K.ring(2, [128, 32], BF16)
        tmp = K.ring(2, [128, 8, 64], F32)
        tmp2 = K.ring(2, [128, 8, 32], F32)
        qst = K.ring(2, [128, 8, 96], BF16)
        qfr = K.ring(2, [128, 768], F32)
        kst = K.ring(2, [128, 8, 96], BF16)
        vst = K.ring(2, [128, 8, 65], BF16, dma=True)
        qTs = K.ring(2, [96, 8, 128], BF16, dma=True)
        kTs = K.ring(2, [96, 8, 128], BF16, dma=True)
        vmst = K.ring(2, [128, 4, 129], BF16, dma=True)
        sor = K.ring(2, [128, 512], BF16, dma=True)
        sxr = K.ring(2, [128, 512], F32)
        gtr = K.ring(2, [128, 16], F32, dma=True)
        ge = K.ring(2, [128, 2, 4], F32)
        for tl in vst.tiles + vmst.tiles:
            K.v("dve", lambda e, tl=tl: e.memset(tl.t[:], 1.0), [], [tl])
        mkr = K.ring(2, [128, 4], F32, dma=True)
        padb = Buf("xtpad")
        nfw_m = None
        if prompt:
            P.op("pool", lambda e, S=S: e.dma_start(out=XT[0][:, :, 0:1], in_=XT[0][:, :, S:S + 1],
                                                    allow_slow_non_contiguous=True), writes=[padb], dsem=wsem)
            P.op("pool", lambda e, S=S: e.dma_start(out=XT[0][:, :, S + 1:S + 2], in_=XT[0][:, :, 1:2],
                                                    allow_slow_non_contiguous=True), writes=[padb], dsem=wsem)
            nfw_m = seam_setup(cw_m, 8, 4)
        K.pgen = Ring(K.banks[0:6])
        for b in range(NB):
            win = winr.next()
            K.ld(win, win.t[:], XT[0][:, :, 512 * b:512 * b + 514].rearrange("c p t -> p c t"), reads=[padb])
            for c in range(8):
                if c < 4 and b >= NBo:
                    continue
                col0 = 416 + c * 128
                pm = K.pgen.next()
                hsl = 0
                halo = K.banks[6 + (c % 2)]
                for k in range(8):
                    K.mm(pm, pm.t[:, :], w_in, w_in.t[:, k, col0:col0 + 128], win, win.t[:, k, 1:513],
                         start=(k == 0), stop=(k == 7))
                for k in range(8):
                    K.mm(halo, halo.t[:, hsl:hsl + 2], w_in, w_in.t[:, k, col0:col0 + 128], win,
                         win.t[:, k, 0:514:513], start=(k == 0), stop=(k == 7))
                cg = cgr.next()
                conv_chunk(pm, halo, hsl, cw_m, c, cg)
                if b in SEAM_BLOCKS:
                    seam_fix(pm, cg, nfw_m, b // 4, c)
                qk = qkr.next()
                K.act(qk, qk.t[:], [cg], cg.t[:], AF.Silu)
                K.st(qk, QKT[c, :, 512 * b:512 * b + 512], qk.t[:])
            for tt in range(4):
                t = b * 4 + tt
                lo = 1 + tt * 128
                pa = K.pgen.next()
                pv = K.pgen.next()
                po = K.pgen.next()
                for k in range(8):
                    K.mm(pa, pa.t[:, 0:416], win, win.t[:, k, lo:lo + 128], w_in, w_in.t[:, k, 0:416],
                         start=(k == 0), stop=(k == 7))
                for k in range(8):
                    K.mm(pa, pa.t[:, 416:432], win, win.t[:, k, lo:lo + 128], w_in, w_in.t[:, k, 2464:2480],
                         start=(k == 0), stop=(k == 7))
                for k in range(8):
                    K.mm(pv, pv.t[:, :], win, win.t[:, k, lo:lo + 128], w_in, w_in.t[:, k, 1440:1952],
                         start=(k == 0), stop=(k == 7))
                is_own = t < NTo
                if is_own:
                    for k in range(8):
                        K.mm(po, po.t[:, :], win, win.t[:, k, lo:lo + 128], w_in, w_in.t[:, k, 1952:2464],
                             start=(k == 0), stop=(k == 7))
                vm = vmst.next()
                K.act(vm, vm.t[:, :, 0:128], [pv], pv.t[:, :].rearrange("p (h d) -> p h d", d=128), AF.Copy)
                K.st(vm, VM[t * 128:(t + 1) * 128, :, :], vm.t[:])
                if is_own:
                    so = sor.next()
                    K.act(so, so.t[:], [po], po.t[:, :], AF.Sigmoid)
                    K.st(so, SO[t * 128:(t + 1) * 128, :], so.t[:])
                gt = gtr.next()
                g2 = ge.next()
                K.v("dve", lambda e, gt=gt, pa=pa, gbias=gbias: e.tensor_tensor(out=gt.t[:], in0=pa.t[:, 416:432], in1=gbias.t[:],
                                                                   op=ALU.add), [pa, gbias], [gt])
                gtv = gt.t[:, :].rearrange("p (a b) -> p a b", b=8)[:, :, 4:8]
                K.act(g2, g2.t[:], [gt], gtv, AF.Exp, scale=-1.0)
                K.v("dve", lambda e, g2=g2: e.tensor_scalar_add(out=g2.t[:], in0=g2.t[:], scalar1=1.0), [g2], [g2])
                K.act(g2, g2.t[:], [g2], g2.t[:], AF.Ln)
                K.v("dve", lambda e, g2=g2, gtv=gtv: e.tensor_scalar_mul(out=gtv, in0=g2.t[:], scalar1=-1.0), [g2], [gt])
                mk = mkr.next()
                if prompt:
                    K.ld(mk, mk.t[:], mkin[jb][t * 128:(t + 1) * 128, :])
                for (c0, mi) in (((0, 0), (8, 2)) if prompt else ()):
                    K.v("dve", lambda e, gt=gt, mk=mk, c0=c0, mi=mi: e.scalar_tensor_tensor(
                        out=gt.t[:, c0:c0 + 4], in0=gt.t[:, c0:c0 + 4], scalar=mk.t[:, mi:mi + 1],
                        in1=mk.t[:, mi + 1:mi + 2].to_broadcast([128, 4]), op0=ALU.mult, op1=ALU.add), [gt, mk], [gt])
                    K.v("dve", lambda e, gt=gt, mk=mk, c0=c0, mi=mi: e.tensor_scalar_mul(
                        out=gt.t[:, c0 + 4:c0 + 8], in0=gt.t[:, c0 + 4:c0 + 8], scalar1=mk.t[:, mi:mi + 1]), [gt, mk], [gt])
                K.st(gt, GG[t * 128:(t + 1) * 128, :], gt.t[:])
                ss = ssr.next()
                K.v("dve", lambda e, ss=ss: e.memset(ss.t[:, 0:2], 0.0), [], [ss])
                K.act(junk, junk.t[:, 0:256], [pa, ss], pa.t[:, 0:256], AF.Square, accum=ss.t[:, 0:1], extra_w=[ss])
                K.act(junk, junk.t[:, 0:128], [pa, ss], pa.t[:, 256:384], AF.Square, accum=ss.t[:, 1:2], extra_w=[ss])
                K.act(ss, ss.t[:, 4:5], [ss, epsb], ss.t[:, 0:1], AF.Ln, scale=1.0 / 256, bias=epsb.t[:, 0:1])
                K.act(ss, ss.t[:, 5:6], [ss, epsb], ss.t[:, 1:2], AF.Ln, scale=1.0 / 128, bias=epsb.t[:, 0:1])
                K.act(ss, ss.t[:, 6:8], [ss], ss.t[:, 4:6], AF.Exp, scale=-0.5)
                cq = cqn.next()
                K.v("dve", lambda e, ss=ss, cq=cq, pa=pa, qg=qg: e.scalar_tensor_tensor(
                    out=cq.t[:, 0:256], in0=pa.t[:, 0:256], scalar=ss.t[:, 6:7], in1=qg.t[:], op0=ALU.mult,
                    op1=ALU.mult), [pa, ss, qg], [cq])
                K.v("dve", lambda e, ss=ss, cq=cq, pa=pa, kvg=kvg: e.scalar_tensor_tensor(
                    out=cq.t[:, 256:384], in0=pa.t[:, 256:384], scalar=ss.t[:, 7:8], in1=kvg.t[:], op0=ALU.mult,
                    op1=ALU.mult), [pa, ss, kvg], [cq])
                cs = cosr.next()
                K.ld(cs, cs.t[:, 0:32], csin[jb][t * 128:(t + 1) * 128, :])
                kr = krr.next()
                tm = tmp.next()

                def rope(dst1, dst2, x1, x2, cosb, sinb, ta, tb, ins, outs, tmt, eng="dve"):
                    K.v(eng, lambda e: e.tensor_tensor(out=ta, in0=x1, in1=cosb, op=ALU.mult), ins, [tmt])
                    K.v(eng, lambda e: e.tensor_tensor(out=tb, in0=x2, in1=sinb, op=ALU.mult), ins, [tmt])
                    K.v(eng, lambda e: e.tensor_tensor(out=dst1, in0=ta, in1=tb, op=ALU.subtract), [tmt], outs)
                    K.v(eng, lambda e: e.tensor_tensor(out=ta, in0=x2, in1=cosb, op=ALU.mult), ins + outs, [tmt])
                    K.v(eng, lambda e: e.tensor_tensor(out=tb, in0=x1, in1=sinb, op=ALU.mult), ins, [tmt])
                    K.v(eng, lambda e: e.tensor_tensor(out=dst2, in0=ta, in1=tb, op=ALU.add), [tmt], outs)

                rope(kr.t[:, 0:16], kr.t[:, 16:32], pa.t[:, 384:400], pa.t[:, 400:416], cs.t[:, 0:16], cs.t[:, 16:32],
                     tm.t[:, 0, 0:16], tm.t[:, 0, 16:32], [pa, cs], [kr], tm)
                pb = K.pgen.next()
                pbv = pb.t[:].bitcast(BF16).rearrange("p (c t) -> p c t", c=8)
                for c in range(3):
                    K.tr(pb, pbv[:, c, :], cq, cq.t[:, c * 128:(c + 1) * 128], ident, ident.t[:])
                cT = cTr.next()
                K.act(cT, cT.t[:], [pb], pbv[:, 0:3, :], AF.Copy)
                if is_own:
                    pq1 = K.pgen.next()
                    pq2 = K.pgen.next()
                    for k in range(2):
                        K.mm(pq1, pq1.t[:, 0:480], cT, cT.t[:, k, :], w_uq, w_uq.t[:, k, 0:480], start=(k == 0), stop=(k == 1))
                    for k in range(2):
                        K.mm(pq2, pq2.t[:, 0:288], cT, cT.t[:, k, :], w_uq, w_uq.t[:, k, 480:768], start=(k == 0), stop=(k == 1))
                    qs = qst.next()
                    qf = qfr.next()
                    K.act(qf, qf.t[:, 0:480], [pq1], pq1.t[:, 0:480], AF.Copy)
                    K.act(qf, qf.t[:, 480:768], [pq2], pq2.t[:, 0:288], AF.Copy)
                    qfv = qf.t[:, :].rearrange("p (h d) -> p h d", d=96)
                    tm2 = tmp2.next()
                    K.v("pool", lambda e, qs=qs, qfv=qfv: e.tensor_copy(out=qs.t[:, :, 0:64], in_=qfv[:, :, 0:64]), [qf], [qs])
                    cosb = cs.t[:, 0:16].unsqueeze(1).to_broadcast([128, 8, 16])
                    sinb = cs.t[:, 16:32].unsqueeze(1).to_broadcast([128, 8, 16])
                    rope(qs.t[:, :, 64:80], qs.t[:, :, 80:96], qfv[:, :, 64:80], qfv[:, :, 80:96],
                         cosb, sinb, tm2.t[:, :, 0:16], tm2.t[:, :, 16:32], [qf, cs], [qs], tm2, eng="pool")
                pk1 = K.pgen.next()
                pk2 = K.pgen.next()
                K.mm(pk1, pk1.t[:, :], cT, cT.t[:, 2, :], w_ukv, w_ukv.t[:, 0:512])
                K.mm(pk2, pk2.t[:, :], cT, cT.t[:, 2, :], w_ukv, w_ukv.t[:, 512:1024])
                ks = kst.next()
                vs = vst.next()
                for (pk, h0) in ((pk1, 0), (pk2, 4)):
                    pkv = pk.t[:, :].rearrange("p (h d) -> p h d", d=128)
                    K.act(ks, ks.t[:, h0:h0 + 4, 0:64], [pk], pkv[:, :, 0:64], AF.Copy)
                    K.v("dve", lambda e, vs=vs, pkv=pkv, h0=h0: e.tensor_copy(out=vs.t[:, h0:h0 + 4, 0:64],
                                                                             in_=pkv[:, :, 64:128]), [pk], [vs])
                K.v("dve", lambda e, ks=ks, kr=kr: e.tensor_copy(
                    out=ks.t[:, :, 64:96], in_=kr.t[:, :].unsqueeze(1).to_broadcast([128, 8, 32])), [kr], [ks])
                K.st(vs, VV[t * 128:(t + 1) * 128, :, :], vs.t[:])
                for (src, dstr, dd) in (((qs, qTs, QT), (ks, kTs, KT)) if is_own else ((ks, kTs, KT),)):
                    pb = K.pgen.next()
                    pbv = pb.t[:].bitcast(BF16).rearrange("p (c t) -> p c t", c=8)
                    for h in range(8):
                        K.tr(pb, pbv[0:96, h, :], src, src.t[:, h, :], ident, ident.t[:])
                    dT = dstr.next()
                    K.act(dT, dT.t[:], [pb], pbv[0:96, :, :], AF.Copy)
                    K.st(dT, dd[:, :, t * 128:(t + 1) * 128].rearrange("h p t -> p h t"), dT.t[:])
        K.pgen = Ring(K.banks[0:6])

        K.begin_sweep()
        kth = K.ring(2, [96, S], BF16, dma=True)
        vh = K.ring(2, [128, NT, 65], BF16, dma=True)
        qbr = K.ring(2, [96, 512], BF16, dma=True)
        ptr = K.ring(3, [128, 512], BF16)
        rcr = K.ring(2, [128, 4], F32)
        yar = K.ring(2, [128, 4, 64], BF16, dma=True)
        for h in range(8):
            kt = kth.next()
            K.ld(kt, kt.t[:], KT[h, :, 0:S])
            vv = vh.next()
            K.ld(vv, vv.t[:], VV[0:S, h, :].rearrange("(n p) d -> p n d", p=128))
            for qb in range(NBo):
                qt = qbr.next()
                K.ld(qt, qt.t[:], QT[h, :, qb * 512:(qb + 1) * 512])
                acc = K.pacc.next()

                def scores(kb, kt=kt, qt=qt):
                    ps = K.pgen.next()
                    K.mm(ps, ps.t[:, :], kt, kt.t[:, kb * 128:(kb + 1) * 128], qt, qt.t[:, :])
                    return ps

                LOOK = 2
                pend = [scores(kb) for kb in range(min(LOOK, NT))]
                for kb in range(NT):
                    ps = pend.pop(0)
                    if kb + LOOK < NT:
                        pend.append(scores(kb + LOOK))
                    pt = ptr.next()
                    K.act(pt, pt.t[:], [ps], ps.t[:, :], AF.Exp, scale=SC_ATT)
                    for j in range(4):
                        K.mm(acc, acc.t[:, j * 65:(j + 1) * 65], pt, pt.t[:, j * 128:(j + 1) * 128], vv, vv.t[:, kb, :],
                             start=(kb == 0 and j == 0), stop=(kb == NT - 1 and j == 3), skip=True)
                accv = acc.t[:, 0:260].rearrange("p (j d) -> p j d", d=65)
                rc = rcr.next()
                K.v("dve", lambda e, rc=rc, accv=accv: e.reciprocal(out=rc.t[:, :].unsqueeze(2), in_=accv[:, :, 64:65]),
                    [acc], [rc])
                ya = yar.next()
                K.v("dve", lambda e, rc=rc, accv=accv, ya=ya: e.tensor_tensor(
                    out=ya.t[:], in0=accv[:, :, 0:64], in1=rc.t[:, :].unsqueeze(2).to_broadcast([128, 4, 64]),
                    op=ALU.mult), [acc, rc], [ya])
                K.st(ya, YY[qb * 512:(qb + 1) * 512, h * 64:(h + 1) * 64].rearrange("(j p) d -> p j d", p=128), ya.t[:])

        for direction in (0, 1):
            K.begin_sweep()
            K.pgen = Ring(K.banks[0:8])
            tri = triu if direction == 0 else tril
            gi0 = 0 if direction == 0 else 8
            qkl = K.ring(3, [128, 8, 128], BF16, dma=True)
            vml = K.ring(3, [128, 4, 129], BF16, dma=True)
            gl = K.ring(3, [128, 16], F32, dma=True)
            ktok = K.ring(3, [128, 4, 128], BF16)
            sm = K.ring(3, [128, 6, 4], F32)
            ptm = K.ring(3, [128, 4, 128], BF16)
            dnr = K.ring(2, [128, 3, 4], F32)
            vwr = K.ring(2, [128, 4, 129], BF16)
            ST = K.tile([128, 4, 129], F32)
            STb = K.tile([128, 4, 129], BF16)
            hfr = K.ring(3, [128, 512], F32, dma=True)
            if direction == 1:
                sol = K.ring(3, [128, 512], BF16, dma=True)
                og = K.tile([128, 512], F32, dma=True)
                bc_load(og, W['mlstm_out_norm'][0], 512)
                sq = K.tile([128, 512], F32)
                ms = K.ring(2, [128, 3, 4], F32)
                ymr = K.ring(2, [128, 512], BF16, dma=True)
            K.v("dve", lambda e, ST=ST: e.memset(ST.t[:], 0.0), [], [ST])
            K.v("dve", lambda e, STb=STb: e.memset(STb.t[:], 0.0), [], [STb])
            own_c = list(range(NTo))
            ctx_c = list(range(NTo, NT))
            if direction == 0:
                order = [(n, True) for n in ctx_c] + [(n, False) for n in own_c]
            else:
                order = [(n, True) for n in reversed(ctx_c)] + [(n, False) for n in reversed(own_c)]

            def stage1(n, ctx_only, direction=direction, tri=tri, gi0=gi0):
                c = dict(n=n, ctx_only=ctx_only)
                r0, r1 = n * 128, (n + 1) * 128
                qk = qkl.next()
                if ctx_only:
                    K.ld(qk, qk.t[:, 4:8, :], QKT[4:8, :, r0:r1].rearrange("c p t -> p c t"))
                else:
                    K.ld(qk, qk.t[:], QKT[:, :, r0:r1].rearrange("c p t -> p c t"))
                vm = vml.next()
                K.ld(vm, vm.t[:], VM[r0:r1, :, :])
                g = gl.next()
                K.ld(g, g.t[:], GG[r0:r1, :])
                c.update(qk=qk, vm=vm, g=g)
                if not ctx_only:
                    hf = hfr.next()
                    c['hf'] = hf
                    if direction == 1:
                        K.ld(hf, hf.t[:], HF[r0:r1, :])
                        so = sol.next()
                        K.ld(so, so.t[:], SO[r0:r1, :])
                        c['so'] = so
                li = g.t[:, gi0:gi0 + 4]
                lf = g.t[:, gi0 + 4:gi0 + 8]
                pb = K.pgen.next()
                pbv = pb.t[:].bitcast(BF16).rearrange("p (c t) -> p c t", c=8)
                for h in range(4):
                    K.tr(pb, pbv[:, h, :], qk, qk.t[:, 4 + h, :], ident, ident.t[:])
                kk = ktok.next()
                K.act(kk, kk.t[:], [pb], pbv[:, 0:4, :], AF.Copy)
                pc = K.pgen.next()
                K.mm(pc, pc.t[:, 0:4], tri, tri.t[:], g, lf)
                K.mm(pc, pc.t[:, 4:8], ones, ones.t[:], g, lf)
                s_ = sm.next()
                K.v("dve", lambda e, s_=s_, pc=pc, li=li: e.scalar_tensor_tensor(
                    out=s_.t[:, 0, :], in0=li, scalar=LNSC_M, in1=pc.t[:, 0:4], op0=ALU.add, op1=ALU.subtract),
                    [g, pc], [s_])
                K.v("dve", lambda e, s_=s_, pc=pc: e.tensor_tensor(out=s_.t[:, 1, :], in0=s_.t[:, 0, :], in1=pc.t[:, 4:8],
                                                                   op=ALU.add), [pc, s_], [s_])
                K.act(s_, s_.t[:, 2:4, :], [pc], pc.t[:, 0:8].rearrange("p (a b) -> p a b", b=4), AF.Exp)
                K.act(s_, s_.t[:, 4:6, :], [s_], s_.t[:, 0:2, :], AF.Exp)
                c.update(kk=kk, s_=s_)
                if not ctx_only:
                    pS = K.pgen.next()
                    for h in range(4):
                        K.mm(pS, pS.t[:, h * 128:(h + 1) * 128], qk, qk.t[:, 4 + h, :], qk, qk.t[:, h, :])
                    pt = ptm.next()
                    for h in range(4):
                        K.v("dve", lambda e, pS=pS, pt=pt, s_=s_, h=h, tri=tri: e.scalar_tensor_tensor(
                            out=pt.t[:, h, :], in0=pS.t[:, h * 128:(h + 1) * 128], scalar=s_.t[:, 4, h:h + 1],
                            in1=tri.t[:], op0=ALU.mult, op1=ALU.mult), [pS, s_, tri], [pt])
                    c['pt'] = pt
                return c

            def stage2(c, direction=direction):
                n, ctx_only = c['n'], c['ctx_only']
                r0, r1 = n * 128, (n + 1) * 128
                qk, vm, kk, s_ = c['qk'], c['vm'], c['kk'], c['s_']
                if not ctx_only:
                    pt, hf = c['pt'], c['hf']
                    pN = [K.pgen.next(), K.pgen.next()]
                    for h in range(4):
                        pp = pN[h // 2]
                        c0 = (h % 2) * 129
                        K.mm(pp, pp.t[:, c0:c0 + 129], pt, pt.t[:, h, :], vm, vm.t[:, h, :], start=True, stop=False)
                        K.mm(pp, pp.t[:, c0:c0 + 129], qk, qk.t[:, h, :], STb, STb.t[:, h, :], start=False, stop=True)
                    dn = dnr.next()
                    for bi in range(2):
                        K.v("dve", lambda e, dn=dn, bi=bi, pp=pN[bi], s_=s_: e.tensor_tensor(
                            out=dn.t[:, 0, 2 * bi:2 * bi + 2], in0=pp.t[:, 128:258:129], in1=s_.t[:, 2, 2 * bi:2 * bi + 2],
                            op=ALU.mult), [pN[bi], s_], [dn])
                    K.v("dve", lambda e, dn=dn: e.tensor_scalar_mul(out=dn.t[:, 1, :], in0=dn.t[:, 0, :], scalar1=-1.0),
                        [dn], [dn])
                    K.v("dve", lambda e, dn=dn: e.tensor_tensor(out=dn.t[:, 0, :], in0=dn.t[:, 0, :], in1=dn.t[:, 1, :],
                                                                op=ALU.max), [dn], [dn])
                    K.v("dve", lambda e, dn=dn: e.tensor_scalar_max(out=dn.t[:, 0, :], in0=dn.t[:, 0, :], scalar1=1.0),
                        [dn], [dn])
                    K.v("dve", lambda e, dn=dn: e.reciprocal(out=dn.t[:, 1, :], in_=dn.t[:, 0, :]), [dn], [dn])
                    K.v("dve", lambda e, dn=dn, s_=s_: e.tensor_tensor(out=dn.t[:, 2, :], in0=dn.t[:, 1, :],
                                                                       in1=s_.t[:, 2, :], op=ALU.mult), [dn, s_], [dn])
                    for h in range(4):
                        pp = pN[h // 2]
                        c0 = (h % 2) * 129
                        hs_ap = hf.t[:, h * 128:(h + 1) * 128]
                        if direction == 0:
                            K.act(hf, hs_ap, [pp, dn], pp.t[:, c0:c0 + 128], AF.Copy, scale=dn.t[:, 2, h:h + 1])
                        else:
                            K.v("dve", lambda e, pp=pp, c0=c0, dn=dn, hs_ap=hs_ap, h=h: e.scalar_tensor_tensor(
                                out=hs_ap, in0=pp.t[:, c0:c0 + 128], scalar=dn.t[:, 2, h:h + 1], in1=hs_ap,
                                op0=ALU.mult, op1=ALU.add), [pp, dn, hf], [hf])
                vw = vwr.next()
                for h in range(4):
                    K.v("pool", lambda e, vw=vw, vm=vm, s_=s_, h=h: e.tensor_scalar_mul(
                        out=vw.t[:, h, :], in0=vm.t[:, h, :], scalar1=s_.t[:, 5, h:h + 1]), [vm, s_], [vw])
                pC = [K.pgen.next(), K.pgen.next()]
                for h in range(4):
                    pp = pC[h // 2]
                    c0 = (h % 2) * 129
                    K.mm(pp, pp.t[:, c0:c0 + 129], kk, kk.t[:, h, :], vw, vw.t[:, h, :])
                for h in range(4):
                    pp = pC[h // 2]
                    c0 = (h % 2) * 129
                    K.v("dve", lambda e, pp=pp, c0=c0, s_=s_, h=h, ST=ST: e.scalar_tensor_tensor(
                        out=ST.t[:, h, :], in0=ST.t[:, h, :], scalar=s_.t[:, 3, h:h + 1], in1=pp.t[:, c0:c0 + 129],
                        op0=ALU.mult, op1=ALU.add), [ST, s_, pp], [ST])
                K.act(STb, STb.t[:], [ST], ST.t[:], AF.Copy)
                if ctx_only:
                    return
                if direction == 0:
                    K.st(hf, HF[r0:r1, :], hf.t[:])
                else:
                    so = c['so']
                    m_ = ms.next()
                    K.v("dve", lambda e, hf=hf, sq=sq: e.tensor_tensor(out=sq.t[:], in0=hf.t[:], in1=hf.t[:], op=ALU.mult),
                        [hf], [sq])
                    K.v("dve", lambda e, m_=m_, sq=sq: e.tensor_reduce(
                        out=m_.t[:, 0, :], in_=sq.t[:, :].rearrange("p (h d) -> p h d", d=128), axis=AX.X, op=ALU.add),
                        [sq], [m_])
                    K.act(m_, m_.t[:, 2, :], [m_, epsb], m_.t[:, 0, :], AF.Ln, scale=1.0 / 128, bias=epsb.t[:, 0:1])
                    K.act(m_, m_.t[:, 0, :], [m_], m_.t[:, 2, :], AF.Exp, scale=-0.5)
                    K.v("dve", lambda e, m_=m_, hf=hf, sq=sq: e.tensor_tensor(
                        out=sq.t[:, :].rearrange("p (h d) -> p h d", d=128),
                        in0=hf.t[:, :].rearrange("p (h d) -> p h d", d=128),
                        in1=m_.t[:, 0, :].unsqueeze(2).to_broadcast([128, 4, 128]), op=ALU.mult), [hf, m_], [sq])
                    K.v("dve", lambda e, sq=sq, og=og: e.tensor_tensor(out=sq.t[:], in0=sq.t[:], in1=og.t[:], op=ALU.mult),
                        [sq, og], [sq])
                    ym = ymr.next()
                    K.v("dve", lambda e, ym=ym, so=so, sq=sq: e.tensor_tensor(out=ym.t[:], in0=sq.t[:], in1=so.t[:],
                                                                              op=ALU.mult), [sq, so], [ym])
                    K.st(ym, YY[r0:r1, 512:1024], ym.t[:])

            cur = stage1(*order[0])
            for i in range(len(order)):
                nxt = stage1(*order[i + 1]) if i + 1 < len(order) else None
                stage2(cur)
                cur = nxt
            K.pgen = Ring(K.banks[0:6])

        def tail_a(x, pr, t, nt, final_g=None):
            for half in range(2):
                K.v("dve", lambda e, half=half: e.tensor_tensor(
                    out=x.t[:, half * 512:(half + 1) * 512], in0=x.t[:, half * 512:(half + 1) * 512],
                    in1=pr[half].t[:, :], op=ALU.add), [x, pr[half]], [x])
            if final_g is None:
                K.st(x, X1[t * 128:(t + 1) * 128, :], x.t[:])
                return nt.run_a(x, x.t[:])
            ss = nt.stats(x, x.t[:], 1024)
            K.v("dve", lambda e, ss=ss: e.scalar_tensor_tensor(out=x.t[:], in0=x.t[:], scalar=ss.t[:, 3:4],
                                                               in1=final_g.t[:], op0=ALU.mult, op1=ALU.mult),
                [x, ss, final_g], [x])
            if out_lo <= t < out_hi:
                K.st(x, y_out[(t - out_lo) * 128:(t - out_lo + 1) * 128, :], x.t[:])
            return None

        def tail(x, pr, t, nt, XTd, final_g=None):
            for half in range(2):
                K.v("dve", lambda e, half=half: e.tensor_tensor(
                    out=x.t[:, half * 512:(half + 1) * 512], in0=x.t[:, half * 512:(half + 1) * 512],
                    in1=pr[half].t[:, :], op=ALU.add), [x, pr[half]], [x])
            if final_g is None:
                K.st(x, X1[t * 128:(t + 1) * 128, :], x.t[:])
                nt.run(x, x.t[:], t, XTd)
            else:
                ss = nt.stats(x, x.t[:], 1024)
                K.v("dve", lambda e, ss=ss: e.scalar_tensor_tensor(out=x.t[:], in0=x.t[:], scalar=ss.t[:, 3:4],
                                                                   in1=final_g.t[:], op0=ALU.mult, op1=ALU.mult),
                    [x, ss, final_g], [x])
                if out_lo <= t < out_hi:
                    K.st(x, y_out[(t - out_lo) * 128:(t - out_lo + 1) * 128, :], x.t[:])

        def project_tm(src_tile, src_ap_k, wt, w_ap_k, nk):
            pr = [K.pgen.next(), K.pgen.next()]
            for half in range(2):
                for k in range(nk):
                    K.mm(pr[half], pr[half].t[:, :], src_tile, src_ap_k(k), wt, w_ap_k(k, half), start=(k == 0),
                         stop=(k == nk - 1))
            return pr

        K.begin_sweep()
        nt = NormT()
        nt.load_gain(W['ffn_norm'][0])
        w_o = K.tile([128, 8, 1024], BF16, dma=True)
        K.ld(w_o, w_o.t[:], wb_out.rearrange("(k p) n -> p k n", p=128))
        xr = K.ring(3, [128, 1024], F32, dma=True)
        yr = K.ring(3, [128, 1024], BF16, dma=True)
        yTr = K.ring(3, [128, 8, 128], BF16)
        K.pgen = Ring(K.banks[0:8])

        def f0(t, c):
            y = yr.next()
            K.ld(y, y.t[:], YY[t * 128:(t + 1) * 128, :])
            pb = K.pgen.next()
            pbv = pb.t[:].bitcast(BF16).rearrange("p (c t) -> p c t", c=8)
            for cc in range(8):
                K.tr(pb, pbv[:, cc, :], y, y.t[:, cc * 128:(cc + 1) * 128], ident, ident.t[:])
            yT = yTr.next()
            K.act(yT, yT.t[:], [pb], pbv, AF.Copy)
            c['pr'] = project_tm(yT, lambda k: yT.t[:, k, :], w_o,
                                 lambda k, half: w_o.t[:, k, half * 512:(half + 1) * 512], 8)

        def f1(t, c):
            x = xr.next()
            K.ld(x, x.t[:], x_in[t * 128:(t + 1) * 128, :])
            c['xn'] = tail_a(x, c['pr'], t, nt)

        def f2(t, c):
            nt.run_b(c['xn'], t, XT[1])

        skewed(NTo, [f0, f1, f2])
        K.pgen = Ring(K.banks[0:6])

        def ffn_sweep(l, XTs, XTd, gain_next, final):
            K.begin_sweep()
            wup = K.tile([128, 8, 5632], BF16, dma=True)
            for k in range(8):
                K.ld(wup, wup.t[:, k, :], wb_up[l, k * 128:(k + 1) * 128, :])
            wdn = K.tile([128, NFC, 1024], BF16, dma=True)
            K.ld(wdn, wdn.t[:], wb_dn[l].rearrange("(i p) n -> p i n", p=128))
            nt = NormT()
            fg = None
            if final:
                fg = nt.g
            nt.load_gain(gain_next)
            winr = K.ring(2, [128, 8, 514], BF16, dma=True)
            hT = K.tile([128, NFC, 512], BF16)
            cgr = K.ring(2, [128, 512], F32)
            cvr = K.ring(2, [128, 512], F32)
            sgr = K.ring(2, [128, 512], BF16)
            xr = K.ring(2, [128, 1024], F32, dma=True)
            K.pgen = Ring(K.banks[0:6])
            cw = cw_f[l]
            hs_i = 0
            nfw_f = seam_setup(cw, 44, 2) if prompt else None
            pending = []
            for b in range(NBo):
                win = winr.next()
                K.ld(win, win.t[:], XTs[:, :, 512 * b:512 * b + 514].rearrange("c p t -> p c t"))
                for i in range(NFC):
                    outs = []
                    for (col0, ci, rr) in ((i * 128, i, cgr), (DFF + i * 128, NFC + i, cvr)):
                        pm = K.pgen.next()
                        hsl = 0
                        halo = K.banks[6 + (hs_i % 2)]
                        hs_i += 1
                        for k in range(8):
                            K.mm(pm, pm.t[:, :], wup, wup.t[:, k, col0:col0 + 128], win, win.t[:, k, 1:513],
                                 start=(k == 0), stop=(k == 7))
                        for k in range(8):
                            K.mm(halo, halo.t[:, hsl:hsl + 2], wup, wup.t[:, k, col0:col0 + 128], win,
                                 win.t[:, k, 0:514:513], start=(k == 0), stop=(k == 7))
                        cc = rr.next()
                        conv_chunk(pm, halo, hsl, cw, ci, cc)
                        if prompt and b in (0, 4):
                            seam_fix(pm, cc, nfw_f, b // 4, ci)
                        outs.append(cc)
                    cg, cv = outs
                    sg = sgr.next()
                    K.act(sg, sg.t[:], [cg], cg.t[:], AF.Silu)
                    K.v("pool", lambda e, sg=sg, cv=cv, i=i: e.tensor_tensor(out=hT.t[:, i, :], in0=sg.t[:], in1=cv.t[:],
                                                                             op=ALU.mult), [sg, cv], [hT])
                    if pending and i >= 1:
                        pending.pop(0)()
                while pending:
                    pending.pop(0)()
                prs = {}
                xns = {}

                def d0(tt, b=b, prs=prs):
                    prs[tt] = project_tm(hT, lambda k: hT.t[:, k, tt * 128:(tt + 1) * 128], wdn,
                                         lambda k, half: wdn.t[:, k, half * 512:(half + 1) * 512], NFC)

                def d1(tt, b=b, prs=prs, xns=xns):
                    t = b * 4 + tt
                    x = xr.next()
                    K.ld(x, x.t[:], X1[t * 128:(t + 1) * 128, :])
                    xns[tt] = tail_a(x, prs[tt], t, nt, final_g=fg)

                def d2(tt, b=b, xns=xns):
                    if xns[tt] is not None:
                        nt.run_b(xns[tt], b * 4 + tt, XTd)

                for fn_, a_ in ((d0, 0), (d0, 1), (d1, 0), (d0, 2), (d1, 1), (d2, 0), (d0, 3), (d1, 2), (d2, 1), (d1, 3)):
                    fn_(a_)
                pending.extend([lambda d2=d2: d2(2), lambda d2=d2: d2(3)])
            while pending:
                pending.pop(0)()
            K.pgen = Ring(K.banks[0:6])

        ffn_sweep(0, XT[1], XT[0], W['c_norm'][0], False)

        K.begin_sweep()
        nt = NormT()
        nt.load_gain(W['ffn_norm'][1])
        cwin = K.tile([128, 8, 2048], BF16, dma=True)
        K.ld(cwin, cwin.t[:], wb_cin.rearrange("(k p) n -> p k n", p=128))
        cwo = K.tile([128, 8, 1024], BF16, dma=True)
        K.ld(cwo, cwo.t[:], wb_cout.rearrange("(k p) n -> p k n", p=128))
        vg = K.tile([128, 1024], F32, dma=True)
        bc_load(vg, W['c_v_norm'][0], 1024)
        xTl = K.ring(3, [128, 8, 128], BF16, dma=True)
        xr = K.ring(2, [128, 1024], F32, dma=True)
        ur = K.ring(3, [128, 2048], F32)
        vnr = K.ring(2, [128, 1024], BF16)
        zr = K.ring(2, [128, 1024], BF16)
        zTr = K.ring(2, [128, 8, 128], BF16)
        vjunk = K.tile([128, 1024], BF16)
        K.pgen = Ring(K.banks[0:8])

        def h0(t, c):
            xT = xTl.next()
            K.ld(xT, xT.t[:], XT[0][:, :, 1 + t * 128:1 + (t + 1) * 128].rearrange("c p t -> p c t"))
            uv = ur.next()
            for j in range(4):
                pu = K.pgen.next()
                for k in range(8):
                    K.mm(pu, pu.t[:, :], xT, xT.t[:, k, :], cwin, cwin.t[:, k, j * 512:(j + 1) * 512], start=(k == 0),
                         stop=(k == 7))
                K.act(uv, uv.t[:, j * 512:(j + 1) * 512], [pu], pu.t[:, :], AF.Gelu)
            c['uv'] = uv

        def h1(t, c):
            uv = c['uv']
            ss = nt.stats(uv, uv.t[:, 1024:2048], 1024, junk=vjunk)
            vn = vnr.next()
            K.v("dve", lambda e, ss=ss, uv=uv, vn=vn, vg=vg: e.scalar_tensor_tensor(
                out=vn.t[:], in0=uv.t[:, 1024:2048], scalar=ss.t[:, 3:4], in1=vg.t[:], op0=ALU.mult, op1=ALU.mult),
                [uv, ss, vg], [vn])
            z = zr.next()
            psv = [K.pgen.next(), K.pgen.next()]
            for g in range(8):
                pp = psv[g // 4]
                K.mm(pp, pp.t[:, (g % 4) * 128:(g % 4 + 1) * 128], wsT, wsT.t[:, g, :], vn, vn.t[:, g * 128:(g + 1) * 128])
            for g in range(8):
                pp = psv[g // 4]
                K.v("dve", lambda e, pp=pp, g=g, uv=uv, z=z: e.scalar_tensor_tensor(
                    out=z.t[:, g * 128:(g + 1) * 128], in0=pp.t[:, (g % 4) * 128:(g % 4 + 1) * 128],
                    scalar=bsT.t[:, g:g + 1], in1=uv.t[:, g * 128:(g + 1) * 128], op0=ALU.add, op1=ALU.mult),
                    [pp, bsT, uv], [z])
            pb = K.pgen.next()
            pbv = pb.t[:].bitcast(BF16).rearrange("p (c t) -> p c t", c=8)
            for cc in range(8):
                K.tr(pb, pbv[:, cc, :], z, z.t[:, cc * 128:(cc + 1) * 128], ident, ident.t[:])
            zT = zTr.next()
            K.act(zT, zT.t[:], [pb], pbv, AF.Copy)
            c['pr'] = project_tm(zT, lambda k: zT.t[:, k, :], cwo,
                                 lambda k, half: cwo.t[:, k, half * 512:(half + 1) * 512], 8)

        def h2(t, c):
            x = xr.next()
            K.ld(x, x.t[:], X1[t * 128:(t + 1) * 128, :])
            c['xn'] = tail_a(x, c['pr'], t, nt)

        def h3(t, c):
            nt.run_b(c['xn'], t, XT[1])

        skewed(NTo, [h0, h1, h2, h3], reverse=True)
        K.pgen = Ring(K.banks[0:6])

        ffn_sweep(1, XT[1], XT[0], W['final_norm'], True)

    P.emit()
    nc._dbg_names = dbg_names
    return nc


BIG = 30000.0
OWN_EXT = 2560
HALO = 256


def make_consts():
    return {
        "c_ident": np.eye(128, dtype=np.float32),
        "c_triu": np.triu(np.ones((128, 128), np.float32)),
        "c_tril": np.tril(np.ones((128, 128), np.float32)),
        "c_ones": np.ones((128, 128), np.float32),
    }


def rope_table(pos):
    inv = (1.0 / (np.float32(10000.0) ** (np.arange(0, 32, 2, dtype=np.float32) / np.float32(32)))).astype(np.float32)
    ang = pos.astype(np.float32)[:, None] * inv[None, :]
    return np.concatenate([np.cos(ang), np.sin(ang)], axis=1).astype(np.float32)


def mask_table(mf, mb):
    mf = mf.astype(np.float32)
    mb = mb.astype(np.float32)
    return np.stack([mf, (mf - 1) * BIG, mb, (mb - 1) * BIG], axis=1).astype(np.float32)


def aux_full(S):
    ones = np.ones(S, np.float32)
    return rope_table(np.arange(S)), mask_table(ones, ones)


def aux_quarter(S, j):
    q = S // 4
    shift = j * q - HALO
    r = np.arange(S)
    t = r + shift
    pos = t % S
    own = r < OWN_EXT
    valid = (t >= 0) & (t < S)
    t_start = max(shift, 0)
    t_end = min(shift + OWN_EXT, S)
    mf = np.where(own, valid, pos < t_start)
    mb = np.where(own, valid, pos >= t_end)
    sf = np.zeros((1, 4), np.float32)
    sf[0, (4 - j) % 4] = -1.0
    return shift, rope_table(pos), mask_table(mf, mb), sf


_CACHE = {}


def run(core_inputs, params, jobs):
    jobs = norm_jobs(jobs)
    key = repr(jobs)
    if key not in _CACHE:
        _CACHE[key] = build_program(jobs)
    nc = _CACHE[key]
    consts = make_consts()
    in_maps = []
    for ci in core_inputs:
        m = {k: np.ascontiguousarray(v, dtype=np.float32) for k, v in ci.items()}
        for j, jd in enumerate(jobs):
            if ("cs%d" % j) not in m:
                cs, mk = aux_full(jd['S'])
                m["cs%d" % j] = cs
                m["mk%d" % j] = mk
        if "sf" not in m:
            m["sf"] = np.zeros((1, 4), np.float32)
        for n in PARAMS:
            m[n] = np.ascontiguousarray(params[n], dtype=np.float32)
        m.update(consts)
        in_maps.append(m)
    res = run_bass_kernel_spmd(nc, in_maps, core_ids=list(range(len(core_inputs))))
    global LAST_RES
    LAST_RES = res.results
    return [[r["y%d" % j] for j in range(len(jobs))] for r in res.results]


def kernel(**inputs):
    xp = np.asarray(inputs['x_prompt'], dtype=np.float32)
    xs = np.asarray(inputs['x_sample'], dtype=np.float32)
    params = {n: np.asarray(inputs[n], dtype=np.float32) for n in PARAMS}
    SP = xp.shape[1]
    jobs = [4096, 4096, dict(S=SP, NTo=OWN_EXT // 128, out=(HALO // 128, (OWN_EXT - HALO) // 128), prompt=True)]
    core_inputs = []
    for c in range(8):
        p, j = c // 4, c % 4
        shift, cs, mk, sf = aux_quarter(SP, j)
        core_inputs.append({"x0": xs[2 * c], "x1": xs[2 * c + 1], "x2": np.roll(xp[p], -shift, axis=0),
                            "cs2": cs, "mk2": mk, "sf": sf})
    outs = run(core_inputs, params, jobs)
    q = SP // 4
    y_prompt = np.zeros_like(xp)
    for c in range(8):
        p, j = c // 4, c % 4
        y_prompt[p, j * q:(j + 1) * q] = outs[c][2]
    y_sample = np.stack([outs[c][j] for c in range(8) for j in range(2)], axis=0)
    return (y_prompt, y_sample)
```
